# Optimizing a Trainium2 kernel written in Bass

```python
import jax
import jax.numpy as jnp
from jax import lax
import numpy as np

D_MODEL = 1024
BATCH = 4
SEQ = 4096
DEPTH = 2

N_EVEN = (DEPTH + 1) // 2
N_ODD = DEPTH // 2
D_FF = 2816
RMS_EPS = 1e-6
MIX_WIDTH = D_MODEL
GMLP_WIDTH = MIX_WIDTH // 2
GMLP_GROUPS = 4
GMLP_GROUP_DIM = GMLP_WIDTH // GMLP_GROUPS
GMLP_CHUNK = 128
MOBA_WIDTH = MIX_WIDTH - GMLP_WIDTH
MOBA_HEAD_DIM = 128
MOBA_HEADS = MOBA_WIDTH // MOBA_HEAD_DIM
MOBA_BLOCK = 256
MOBA_TOPK = 3
MOBA_Q_CHUNK = 64
ROPE_THETA = 500000.0
ROT_DIM = MOBA_HEAD_DIM // 4
HYB_IN_WIDTH = 2 * GMLP_WIDTH + 3 * MOBA_WIDTH
CONV_WIDTH = 3
CONV_DIM = D_MODEL
NEG_INF = -1e30

kernel_name = 'hybrid_gmlp_moba_shortconv_macaron'


def rms_norm(x, g):
    xf = x.astype(jnp.float32)
    y = xf * lax.rsqrt(jnp.mean(xf * xf, axis=-1, keepdims=True) + RMS_EPS)
    return (y * g.astype(jnp.float32)).astype(x.dtype)


def swiglu(x, w_gate, w_up, w_down):
    return (jax.nn.silu(x @ w_gate) * (x @ w_up)) @ w_down


def partial_rotary(x, positions):
    half = ROT_DIM // 2
    inv_freq = ROPE_THETA ** (-jnp.arange(half, dtype=jnp.float32) / half)
    ang = positions.astype(jnp.float32)[..., None] * inv_freq
    cos = jnp.cos(ang)[:, :, None, :]
    sin = jnp.sin(ang)[:, :, None, :]
    xf = x.astype(jnp.float32)
    x1, x2, rest = xf[..., :half], xf[..., half:ROT_DIM], xf[..., ROT_DIM:]
    out = jnp.concatenate([x1 * cos - x2 * sin, x2 * cos + x1 * sin, rest], axis=-1)
    return out.astype(x.dtype)


def chunked_gmlp(u, v, v_norm, w_s, b_s):
    bsz, seq, _ = u.shape
    v = rms_norm(v, v_norm)
    v = v.reshape(bsz, seq // GMLP_CHUNK, GMLP_CHUNK, GMLP_GROUPS, GMLP_GROUP_DIM)
    causal = jnp.tril(jnp.ones((GMLP_CHUNK, GMLP_CHUNK), dtype=bool))
    w = jnp.where(causal, w_s, 0)
    mixed = jnp.einsum('gts,bnsgc->bntgc', w, v) + b_s.T[:, :, None]
    return u * mixed.reshape(bsz, seq, GMLP_WIDTH)


def moba_attention(q, k, v):
    bsz, seq, nh, hd = q.shape
    seq_p = -(-seq // MOBA_BLOCK) * MOBA_BLOCK
    pad = ((0, 0), (0, seq_p - seq), (0, 0), (0, 0))
    q, k, v = [jnp.pad(t, pad).transpose(0, 2, 1, 3) for t in (q, k, v)]
    nb = seq_p // MOBA_BLOCK
    k_blk = k.reshape(bsz, nh, nb, MOBA_BLOCK, hd)
    v_blk = v.reshape(bsz, nh, nb, MOBA_BLOCK, hd)
    k_mean = jnp.mean(k_blk.astype(jnp.float32), axis=3)
    n_sel = min(MOBA_TOPK, nb - 1)
    scale = hd ** -0.5
    nq = seq_p // MOBA_Q_CHUNK
    q_chunks = q.reshape(bsz, nh, nq, MOBA_Q_CHUNK, hd).transpose(2, 0, 1, 3, 4)
    gather_blocks = jax.vmap(jax.vmap(lambda blk, ix: blk[ix]))

    def one_chunk(args):
        qc, ci = args
        q0 = ci * MOBA_Q_CHUNK
        qblk = q0 // MOBA_BLOCK
        qpos = q0 + jnp.arange(MOBA_Q_CHUNK)
        qf = qc.astype(jnp.float32)
        k_own = lax.dynamic_index_in_dim(k_blk, qblk, axis=2, keepdims=False).astype(jnp.float32)
        v_own = lax.dynamic_index_in_dim(v_blk, qblk, axis=2, keepdims=False).astype(jnp.float32)
        kpos = qblk * MOBA_BLOCK + jnp.arange(MOBA_BLOCK)
        s_own = jnp.einsum('bhqd,bhkd->bhqk', qf, k_own) * scale
        s_own = jnp.where(kpos[None, :] <= qpos[:, None], s_own, NEG_INF)
        if n_sel > 0:
            gate = jnp.einsum('bhqd,bhnd->bhqn', qf, k_mean)
            gate = jnp.where(jnp.arange(nb) < qblk, gate, NEG_INF)
            _, sel = lax.top_k(gate, n_sel)
            valid = sel < qblk
            k_sel = gather_blocks(k_blk, sel).astype(jnp.float32)
            v_sel = gather_blocks(v_blk, sel).astype(jnp.float32)
            s_sel = jnp.einsum('bhqd,bhqnkd->bhqnk', qf, k_sel) * scale
            s_sel = jnp.where(valid[..., None], s_sel, NEG_INF)
            s_sel = s_sel.reshape(bsz, nh, MOBA_Q_CHUNK, n_sel * MOBA_BLOCK)
            p = jax.nn.softmax(jnp.concatenate([s_sel, s_own], axis=-1), axis=-1)
            p_sel = p[..., :n_sel * MOBA_BLOCK].reshape(bsz, nh, MOBA_Q_CHUNK, n_sel, MOBA_BLOCK)
            p_own = p[..., n_sel * MOBA_BLOCK:]
            out = (jnp.einsum('bhqnk,bhqnkd->bhqd', p_sel, v_sel)
                   + jnp.einsum('bhqk,bhkd->bhqd', p_own, v_own))
        else:
            p_own = jax.nn.softmax(s_own, axis=-1)
            out = jnp.einsum('bhqk,bhkd->bhqd', p_own, v_own)
        return out.astype(qc.dtype)

    out = lax.map(one_chunk, (q_chunks, jnp.arange(nq)))
    out = out.transpose(1, 0, 3, 2, 4).reshape(bsz, seq_p, nh, hd)[:, :seq]
    return out.reshape(bsz, seq, nh * hd)


def hybrid_ab_mixer(h, positions, w_in, v_norm, w_s, b_s, w_out):
    bsz, seq, _ = h.shape
    proj = h @ w_in
    cuts = [GMLP_WIDTH, 2 * GMLP_WIDTH, 2 * GMLP_WIDTH + MOBA_WIDTH, 2 * GMLP_WIDTH + 2 * MOBA_WIDTH]
    u, v_g, q, k, v = jnp.split(proj, cuts, axis=-1)
    a_out = chunked_gmlp(jax.nn.gelu(u), jax.nn.gelu(v_g), v_norm, w_s, b_s)
    heads = (bsz, seq, MOBA_HEADS, MOBA_HEAD_DIM)
    q = partial_rotary(q.reshape(heads), positions)
    k = partial_rotary(k.reshape(heads), positions)
    b_out = moba_attention(q, k, v.reshape(heads))
    return jnp.concatenate([a_out, b_out], axis=-1) @ w_out


def short_conv_mixer(h, w_in, conv_w, w_out):
    proj = h @ w_in
    b_gate, c_gate, xin = jnp.split(proj, 3, axis=-1)
    z = c_gate * xin
    y = lax.conv_general_dilated(z, conv_w[:, None, :], window_strides=(1,),
                                 padding=[(CONV_WIDTH - 1, 0)],
                                 dimension_numbers=('NWC', 'WIO', 'NWC'),
                                 feature_group_count=CONV_DIM)
    return (b_gate * y) @ w_out


def setup_inputs(seed: int = 0) -> dict:
    key = jax.random.key(seed)
    ks = jax.random.split(key, 16)

    def nrm(k, shape, fan_in):
        return jax.random.normal(k, shape, jnp.float32) * (fan_in ** -0.5)

    def gain(k, shape):
        return 1.0 + 0.02 * jax.random.normal(k, shape, jnp.float32)

    return {
        'x': jax.random.normal(ks[0], (BATCH, SEQ, D_MODEL), jnp.float32),
        'positions': jnp.broadcast_to(jnp.arange(SEQ, dtype=jnp.int32)[None, :], (BATCH, SEQ)),
        'ffn_norm': gain(ks[1], (DEPTH, 2, D_MODEL)),
        'ffn_w_gate': nrm(ks[2], (DEPTH, 2, D_MODEL, D_FF), D_MODEL),
        'ffn_w_up': nrm(ks[3], (DEPTH, 2, D_MODEL, D_FF), D_MODEL),
        'ffn_w_down': nrm(ks[4], (DEPTH, 2, D_FF, D_MODEL), D_FF),
        'mix_norm': gain(ks[5], (DEPTH, D_MODEL)),
        'hyb_w_in': nrm(ks[6], (N_EVEN, D_MODEL, HYB_IN_WIDTH), D_MODEL),
        'gmlp_v_norm': gain(ks[7], (N_EVEN, GMLP_WIDTH)),
        'gmlp_w_s': nrm(ks[8], (N_EVEN, GMLP_GROUPS, GMLP_CHUNK, GMLP_CHUNK), GMLP_CHUNK),
        'gmlp_b_s': 1.0 + 0.1 * jax.random.normal(ks[9], (N_EVEN, GMLP_GROUPS, GMLP_CHUNK), jnp.float32),
        'hyb_w_out': nrm(ks[10], (N_EVEN, MIX_WIDTH, D_MODEL), MIX_WIDTH),
        'conv_w_in': nrm(ks[11], (N_ODD, D_MODEL, 3 * CONV_DIM), D_MODEL),
        'conv_w': nrm(ks[12], (N_ODD, CONV_WIDTH, CONV_DIM), CONV_WIDTH),
        'conv_w_out': nrm(ks[13], (N_ODD, CONV_DIM, D_MODEL), CONV_DIM),
        'final_norm': gain(ks[14], (D_MODEL,)),
    }


def reference(x, positions, ffn_norm, ffn_w_gate, ffn_w_up, ffn_w_down, mix_norm,
              hyb_w_in, gmlp_v_norm, gmlp_w_s, gmlp_b_s, hyb_w_out,
              conv_w_in, conv_w, conv_w_out, final_norm):
    for layer in range(DEPTH):
        x = x + 0.5 * swiglu(rms_norm(x, ffn_norm[layer, 0]), ffn_w_gate[layer, 0],
                             ffn_w_up[layer, 0], ffn_w_down[layer, 0])
        h = rms_norm(x, mix_norm[layer])
        i = layer // 2
        if layer % 2 == 0:
            x = x + hybrid_ab_mixer(h, positions, hyb_w_in[i], gmlp_v_norm[i], gmlp_w_s[i],
                                    gmlp_b_s[i], hyb_w_out[i])
        else:
            x = x + short_conv_mixer(h, conv_w_in[i], conv_w[i], conv_w_out[i])
        x = x + 0.5 * swiglu(rms_norm(x, ffn_norm[layer, 1]), ffn_w_gate[layer, 1],
                             ffn_w_up[layer, 1], ffn_w_down[layer, 1])
    return rms_norm(x, final_norm)
```

```python
import numpy as np
import ml_dtypes
import concourse.bass as bass
import concourse.mybir as mybir
from concourse.bass_utils import run_bass_kernel_spmd

F32 = mybir.dt.float32
BF16 = mybir.dt.bfloat16
I32 = mybir.dt.int32
ALU = mybir.AluOpType
AF = mybir.ActivationFunctionType

N_CORES = 8
D_MODEL = 1024
SEQ = 4096
BATCH = 4
T_CORE = 2048
DCH = D_MODEL // 128
D_FF = 2816
FCH = D_FF // 128
RMS_EPS = 1e-6


from contextlib import ExitStack

AX = mybir.AxisListType.X
GROUPS = [(0, 6), (6, 12), (12, 17), (17, 22)]
NT = T_CORE // 512
GELU_C = 0.044715
GELU_S = 1.5957691216057308


def bf16_np(a):
    return np.asarray(a, dtype=np.float32).astype(ml_dtypes.bfloat16)


class Trk:
    def __init__(self, nc, es, dma_streams, sfx=""):
        self.nc = nc
        self.eng = {"pe": nc.tensor, "act": nc.scalar, "dve": nc.vector, "pool": nc.gpsimd, "sp": nc.sync}
        names = list(self.eng) + list(dma_streams)
        self.dma_streams = set(dma_streams)
        self.sem = {n: es.enter_context(nc.semaphore("s_" + n + sfx)) for n in names}
        self.cnt = {k: 0 for k in names}
        self.known = {e: {} for e in self.eng}
        self.lastw = {}
        self.readers = {}

    def _wait(self, e, semname, val):
        if val <= 0:
            return
        if semname in self.dma_streams:
            val = self.cnt[semname]
        if self.known[e].get(semname, 0) >= val:
            return
        if e == "pe" and semname == "pe":
            return
        self.eng[e].wait_ge(self.sem[semname], val)
        self.known[e][semname] = val

    def deps(self, e, reads=(), writes=()):
        for k in reads:
            if k in self.lastw:
                self._wait(e, *self.lastw[k])
        for k in writes:
            if k in self.lastw:
                self._wait(e, *self.lastw[k])
            for r in self.readers.get(k, ()):
                self._wait(e, *r)

    def done(self, ins, semname, inc, reads=(), writes=()):
        self.cnt[semname] += inc
        ins.then_inc(self.sem[semname], inc)
        tag = (semname, self.cnt[semname])
        for k in reads:
            self.readers.setdefault(k, []).append(tag)
        for k in writes:
            self.lastw[k] = tag
            self.readers[k] = []
        return tag

    @staticmethod
    def _excl(reads, writes):
        r = [k for k in reads if not k.startswith("ps")]
        w = list(writes) + [k for k in reads if k.startswith("ps") and k not in writes]
        return r, w

    def op(self, e, fn, reads=(), writes=()):
        reads, writes = self._excl(reads, writes)
        self.deps(e, reads, writes)
        return self.done(fn(), e, 1, reads, writes)

    def barrier(self):
        for e in self.eng:
            for sname, c in self.cnt.items():
                if c and not (e == sname):
                    if self.known[e].get(sname, 0) < c:
                        self.eng[e].wait_ge(self.sem[sname], c)
                        self.known[e][sname] = c

    def dma(self, e, stream, fn, reads=(), writes=()):
        self.deps(e, reads, writes)
        return self.done(fn(), stream, 16, reads, writes)

    def finish(self, e, streams):
        for s in streams:
            if self.cnt[s]:
                self.eng[e].wait_ge(self.sem[s], self.cnt[s])


def tsl(tt):
    return slice(tt * 512, (tt + 1) * 512)


def emit_rmsnorm(tk, nc, xT, g, gkey, hn, rstd, ps, ones):
    for c in range(DCH):
        tk.op("act", lambda c=c: nc.scalar.activation(out=hn[:, c, :], in_=xT[:, c, :], func=AF.Square),
              reads=[f"xT{c}"], writes=[f"hn{c}"])
    for tt in range(NT):
        for c in range(DCH):
            tk.op("pe", lambda c=c, tt=tt: nc.tensor.matmul(ps[:, tt, :], lhsT=ones[:, :], rhs=hn[:, c, tsl(tt)],
                                                             start=(c == 0), stop=(c == DCH - 1)),
                  reads=[f"hn{c}", "ones"], writes=[f"ps{tt}"])
    for tt in range(NT):
        sl = tsl(tt)
        tk.op("dve", lambda tt=tt, sl=sl: nc.vector.tensor_scalar(out=rstd[:, sl], in0=ps[:, tt, :], scalar1=1.0 / D_MODEL, scalar2=RMS_EPS,
                                                                    op0=ALU.mult, op1=ALU.add),
              reads=[f"ps{tt}"], writes=[f"rstd{tt}"])
        tk.op("act", lambda sl=sl: nc.scalar.sqrt(out=rstd[:, sl], in_=rstd[:, sl]), reads=[f"rstd{tt}"], writes=[f"rstd{tt}"])
        tk.op("dve", lambda sl=sl: nc.vector.reciprocal(out=rstd[:, sl], in_=rstd[:, sl]), reads=[f"rstd{tt}"], writes=[f"rstd{tt}"])
    for c in range(DCH):
        for tt in range(NT):
            sl = tsl(tt)
            tk.op("dve", lambda c=c, sl=sl: nc.vector.scalar_tensor_tensor(out=hn[:, c, sl], in0=xT[:, c, sl], scalar=g[:, c:c + 1], in1=rstd[:, sl],
                                                                             op0=ALU.mult, op1=ALU.mult),
                  reads=[f"xT{c}", f"rstd{tt}", gkey], writes=[f"hn{c}"])


def emit_ffn(tk, nc, st, xT, hn, H, sg, ps, wbuf, wdbuf, wg_d, wu_d, wd_d):
    NW = len(wbuf)
    for (f0, f1) in GROUPS:
        nf = f1 - f0
        for f in range(f0, f1):
            for which, wsrc in ((0, wg_d), (1, wu_d)):
                wi = st["w"] % NW
                st["w"] += 1
                wt = wbuf[wi]
                tk.dma("pool", f"dw{wi}", lambda wt=wt, wsrc=wsrc, f=f: nc.gpsimd.dma_start(out=wt[:, :, :], in_=wsrc[f]),
                       writes=[f"w{wi}"])
                for c in range(DCH):
                    for tt in range(NT):
                        b = which * 4 + tt
                        tk.op("pe", lambda wt=wt, c=c, tt=tt, b=b: nc.tensor.matmul(
                            ps[:, b, :], lhsT=wt[:, c, :], rhs=hn[:, c, tsl(tt)], start=(c == 0), stop=(c == DCH - 1)),
                            reads=[f"w{wi}", f"hn{c}"], writes=[f"ps{b}"])
            si = st["sg"] % 2
            st["sg"] += 1
            for tt in range(NT):
                tk.op("act", lambda tt=tt, si=si: nc.scalar.activation(out=sg[:, si, tsl(tt)], in_=ps[:, tt, :], func=AF.Silu),
                      reads=[f"ps{tt}"], writes=[f"sg{si}_{tt}"])
            for tt in range(NT):
                tk.op("dve", lambda tt=tt, si=si, f=f: nc.vector.tensor_tensor(out=H[:, f - f0, tsl(tt)], in0=sg[:, si, tsl(tt)], in1=ps[:, 4 + tt, :], op=ALU.mult),
                      reads=[f"sg{si}_{tt}", f"ps{4 + tt}"], writes=[f"H{f - f0}"])
        for j in range(DCH):
            di = st["wd"] % len(wdbuf)
            st["wd"] += 1
            wt = wdbuf[di]
            tk.dma("pool", f"dd{di}", lambda wt=wt, j=j, f0=f0, f1=f1, nf=nf: nc.gpsimd.dma_start(out=wt[:, 0:nf, :], in_=wd_d[j, :, f0:f1, :]),
                   writes=[f"wd{di}"])
            bs = (st["dn"] % 2) * 4
            st["dn"] += 1
            for k in range(nf):
                for tt in range(NT):
                    b = bs + tt
                    tk.op("pe", lambda wt=wt, k=k, tt=tt, b=b: nc.tensor.matmul(
                        ps[:, b, :], lhsT=wt[:, k, :], rhs=H[:, k, tsl(tt)], start=(k == 0), stop=(k == nf - 1)),
                        reads=[f"wd{di}", f"H{k}"], writes=[f"ps{b}"])
            for tt in range(NT):
                b = bs + tt
                tk.op("dve", lambda j=j, tt=tt, b=b: nc.vector.scalar_tensor_tensor(out=xT[:, j, tsl(tt)], in0=ps[:, b, :], scalar=0.5, in1=xT[:, j, tsl(tt)],
                                                                                    op0=ALU.mult, op1=ALU.add),
                      reads=[f"ps{b}", f"xT{j}"], writes=[f"xT{j}"])


def emit_gelu(tk, nc, src, srckey, tmp, tmpkey, out, outkey):
    tk.op("act", lambda: nc.scalar.activation(out=tmp, in_=src, func=AF.Square), reads=[srckey], writes=[tmpkey])
    tk.op("dve", lambda: nc.vector.tensor_scalar(out=tmp, in0=tmp, scalar1=GELU_C, scalar2=1.0, op0=ALU.mult, op1=ALU.add),
          reads=[tmpkey], writes=[tmpkey])
    tk.op("dve", lambda: nc.vector.tensor_tensor(out=tmp, in0=tmp, in1=src, op=ALU.mult), reads=[tmpkey, srckey], writes=[tmpkey])
    tk.op("act", lambda: nc.scalar.activation(out=tmp, in_=tmp, func=AF.Sigmoid, scale=GELU_S), reads=[tmpkey], writes=[tmpkey])
    tk.op("dve", lambda: nc.vector.tensor_tensor(out=out, in0=tmp, in1=src, op=ALU.mult), reads=[tmpkey, srckey], writes=[outkey])


FFN_STREAMS = [f"dw{i}" for i in range(4)] + [f"dd{i}" for i in range(3)]


def ffn_dram(nc, sfx, ov=None):
    ov = ov or {}
    mk = lambda name, shape: ov[name] if name in ov else nc.dram_tensor(name, shape, F32, kind="ExternalInput").ap()
    wg = mk("wg" + sfx, [FCH, 128, DCH, 128])
    wu = mk("wu" + sfx, [FCH, 128, DCH, 128])
    wd = mk("wd" + sfx, [DCH, 128, FCH, 128])
    return wg, wu, wd


def io_makers(nc, ov):
    di = lambda name, shape, dt: ov[name] if name in ov else nc.dram_tensor(name, shape, dt, kind="ExternalInput").ap()
    do = lambda name, shape, dt: ov[name] if name in ov else nc.dram_tensor(name, shape, dt, kind="ExternalOutput").ap()
    return di, do


def build_l1(nc=None, ov=None, sfx="", sem_es=None):
    nc = nc or bass.Bass("TRN2", target_bir_lowering=False)
    ov = ov or {}
    di, do = io_makers(nc, ov)
    x_d = di("xT", [128, DCH, T_CORE], F32)
    gf_d = di("g_ffn", [128, DCH], F32)
    gm_d = di("g_mix", [128, DCH], F32)
    wg_d, wu_d, wd_d = ffn_dram(nc, "0", ov)
    wfm_d = di("win_fm", [12, 128, DCH, 128], F32)
    wtm_d = di("win_tm", [2, 128, DCH, 512], F32)
    pos_d = di("pos", [32, T_CORE], I32)
    rc_d = di("ropec", [32, 4], F32)
    pt_d = di("PT", [128, 128], F32)
    vn_d = di("vnorm", [128, 512], F32)
    ws_d = di("wsT", [128, 4, 128], F32)
    tr_d = di("trilT", [128, 4, 128], F32)
    bs_d = di("bsB", [128, 4, 128], F32)
    xo_d = do("xT_o", [128, DCH, T_CORE], F32)
    ao_d = do("aT_o", [128, 4, T_CORE], BF16)
    qk_d = do("QKT_o", [128, 8, T_CORE], BF16)
    vo_d = do("V_o", [16, 128, 512], BF16)
    with ExitStack() as es:
        sb = lambda name, shape, dt: es.enter_context(nc.sbuf_tensor(name + sfx, shape, dt))
        xT = sb("xT_s", [128, DCH, T_CORE], F32)
        hn = sb("hn", [128, DCH, T_CORE], BF16)
        H = sb("H", [128, 8, T_CORE], BF16)
        sg = sb("sg", [128, 2, T_CORE], F32)
        rstd = sb("rstd", [128, T_CORE], F32)
        gf = sb("gf", [128, DCH], F32)
        gm = sb("gm", [128, DCH], F32)
        ones = sb("ones", [128, 128], BF16)
        wbuf = [sb(f"wb{i}", [128, DCH, 128], BF16) for i in range(4)]
        wdbuf = [sb(f"wdb{i}", [128, 6, 128], BF16) for i in range(3)]
        posi = sb("posi", [32, T_CORE], I32)
        rc = sb("rc", [32, 4], F32)
        PT = sb("PTs", [128, 128], BF16)
        vnorm = sb("vnorm_s", [128, 512], F32)
        wsT = sb("wsT_s", [128, 4, 128], F32)
        trT = sb("trT_s", [128, 4, 128], F32)
        WmT = sb("WmT", [128, 4, 128], BF16)
        bsB = sb("bsB_s", [128, 4, 128], F32)
        tmpA = sb("tmpA", [128, 2, 512], F32)
        tmpB = sb("tmpB", [128, 2, 512], F32)
        vg = sb("vg", [128, 2, 512], F32)
        vnb = sb("vnb", [128, 2, 512], BF16)
        vt = sb("vt", [128, 2, 512], BF16)
        ao = sb("ao", [128, 2, 4, 128], BF16)
        tmpm = sb("tmpm", [128, 2, 4, 128], F32)
        ss = sb("ss", [128, 4], F32)
        ps = es.enter_context(nc.psum_tensor("ps" + sfx, [128, 8, 512], F32))
        uT = H[:, 4:8, :]
        wtm = H[:, 0:4, :].rearrange("p a (b t) -> p (a b) t", t=512)
        tk = Trk(nc, sem_es or es, ["dx", "dout", "dwt"] + FFN_STREAMS, sfx)
        for c in range(DCH):
            tk.dma("sp", "dx", lambda c=c: nc.sync.dma_start(out=xT[:, c, :], in_=x_d[:, c, :]), writes=[f"xT{c}"])
        for (t_s, t_d, key) in ((gf, gf_d, "gf"), (gm, gm_d, "gm"), (posi, pos_d, "posi"), (rc, rc_d, "rc"), (vnorm, vn_d, "vnorm"),
                                (wsT, ws_d, "wsT"), (trT, tr_d, "trT"), (bsB, bs_d, "bsB")):
            tk.dma("sp", "dx", lambda t_s=t_s, t_d=t_d: nc.sync.dma_start(out=t_s[:], in_=t_d), writes=[key])
        tk.dma("pool", "dx", lambda: nc.gpsimd.dma_start(out=PT[:, :], in_=pt_d[:, :]), writes=["PT"])
        tk.op("dve", lambda: nc.vector.memset(ones[:, :], 1.0), writes=["ones"])
        tk.op("dve", lambda: nc.vector.tensor_tensor(out=WmT[:, :, :], in0=wsT[:, :, :], in1=trT[:, :, :], op=ALU.mult),
              reads=["wsT", "trT"], writes=["WmT"])
        st = {"w": 0, "sg": 0, "wd": 0, "dn": 0}
        emit_rmsnorm(tk, nc, xT, gf, "gf", hn, rstd, ps, ones)
        emit_ffn(tk, nc, st, xT, hn, H, sg, ps, wbuf, wdbuf, wg_d, wu_d, wd_d)
        emit_rmsnorm(tk, nc, xT, gm, "gm", hn, rstd, ps, ones)
        for c in range(DCH):
            tk.dma("sp", "dout", lambda c=c: nc.sync.dma_start(out=xo_d[:, c, :], in_=xT[:, c, :]), reads=[f"xT{c}"])
        sgkeys = [f"sg{si}_{tt}" for si in range(2) for tt in range(NT)]
        cosT = sg[0:32, 0, :]
        sinT = sg[0:32, 1, :]
        ang = rstd[0:32, :]
        rkeys = [f"rstd{tt}" for tt in range(NT)]
        tk.op("dve", lambda: nc.vector.tensor_copy(out=ang, in_=posi[:, :]), reads=["posi"] + rkeys, writes=rkeys)
        tk.op("dve", lambda: nc.vector.tensor_scalar(out=ang, in0=ang, scalar1=rc[:, 0:1], scalar2=None, op0=ALU.mult), reads=rkeys + ["rc"], writes=rkeys)
        for (tab, off) in ((sinT, 0.0), (cosT, 0.25)):
            tk.op("dve", lambda tab=tab, off=off: nc.vector.tensor_scalar(out=tab, in0=ang, scalar1=float(1.0 / (2 * np.pi)), scalar2=float(off),
                                                                          op0=ALU.mult, op1=ALU.add), reads=rkeys, writes=sgkeys)
            tk.op("dve", lambda tab=tab: nc.vector.tensor_copy(out=posi[:, :], in_=tab), reads=sgkeys, writes=["posi"])
            tk.op("dve", lambda: nc.vector.tensor_copy(out=tmpA[0:32, :, :].rearrange("p a b -> p (a b)"), in_=posi[:, 0:1024]), reads=["posi"], writes=["tmpA0", "tmpA1"])
            tk.op("dve", lambda: nc.vector.tensor_copy(out=tmpB[0:32, :, :].rearrange("p a b -> p (a b)"), in_=posi[:, 1024:2048]), reads=["posi"], writes=["tmpB0", "tmpB1"])
            tk.op("dve", lambda tab=tab: nc.vector.tensor_tensor(out=tab[:, 0:1024], in0=tab[:, 0:1024], in1=tmpA[0:32, :, :].rearrange("p a b -> p (a b)"), op=ALU.subtract),
                  reads=sgkeys + ["tmpA0", "tmpA1"], writes=sgkeys)
            tk.op("dve", lambda tab=tab: nc.vector.tensor_tensor(out=tab[:, 1024:2048], in0=tab[:, 1024:2048], in1=tmpB[0:32, :, :].rearrange("p a b -> p (a b)"), op=ALU.subtract),
                  reads=sgkeys + ["tmpB0", "tmpB1"], writes=sgkeys)
            tk.op("dve", lambda tab=tab: nc.vector.tensor_scalar(out=tmpA[0:32, :, :].rearrange("p a b -> p (a b)"), in0=tab[:, 0:1024], scalar1=0.5, scalar2=None, op0=ALU.is_gt),
                  reads=sgkeys, writes=["tmpA0", "tmpA1"])
            tk.op("dve", lambda tab=tab: nc.vector.tensor_scalar(out=tmpB[0:32, :, :].rearrange("p a b -> p (a b)"), in0=tab[:, 1024:2048], scalar1=0.5, scalar2=None, op0=ALU.is_gt),
                  reads=sgkeys, writes=["tmpB0", "tmpB1"])
            tk.op("dve", lambda tab=tab: nc.vector.tensor_tensor(out=tab[:, 0:1024], in0=tab[:, 0:1024], in1=tmpA[0:32, :, :].rearrange("p a b -> p (a b)"), op=ALU.subtract),
                  reads=sgkeys + ["tmpA0", "tmpA1"], writes=sgkeys)
            tk.op("dve", lambda tab=tab: nc.vector.tensor_tensor(out=tab[:, 1024:2048], in0=tab[:, 1024:2048], in1=tmpB[0:32, :, :].rearrange("p a b -> p (a b)"), op=ALU.subtract),
                  reads=sgkeys + ["tmpB0", "tmpB1"], writes=sgkeys)
            tk.op("act", lambda tab=tab: nc.scalar.activation(out=tab, in_=tab, func=AF.Sin, scale=float(2 * np.pi)), reads=sgkeys, writes=sgkeys)
        tk.op("dve", lambda: nc.vector.tensor_scalar(out=sinT, in0=sinT, scalar1=rc[:, 1:2], scalar2=None, op0=ALU.mult), reads=sgkeys + ["rc"], writes=sgkeys)
        t4 = [(tmpA[:, 0, :], "tmpA0"), (tmpA[:, 1, :], "tmpA1"), (tmpB[:, 0, :], "tmpB0"), (tmpB[:, 1, :], "tmpB1")]
        pending = None
        for j in range(12):
            wi = st["w"] % 4
            st["w"] += 1
            wt = wbuf[wi]
            tk.dma("pool", f"dw{wi}", lambda wt=wt, j=j: nc.gpsimd.dma_start(out=wt[:, :, :], in_=wfm_d[j]), writes=[f"w{wi}"])
            bs = (j % 2) * 4
            for c in range(DCH):
                for tt in range(NT):
                    tk.op("pe", lambda wt=wt, c=c, tt=tt, bs=bs: nc.tensor.matmul(
                        ps[:, bs + tt, :], lhsT=wt[:, c, :], rhs=hn[:, c, tsl(tt)], start=(c == 0), stop=(c == DCH - 1)),
                        reads=[f"w{wi}", f"hn{c}"], writes=[f"ps{bs + tt}"])
            if pending is not None:
                pending()
                pending = None
            if j < 4:
                for tt in range(NT):
                    b = bs + tt
                    emit_gelu(tk, nc, ps[:, b, :], f"ps{b}", t4[tt][0], t4[tt][1], uT[:, j, tsl(tt)], f"H{4 + j}")
            else:
                idx = (j - 4) % 4
                for tt in range(NT):
                    b = bs + tt
                    tk.op("act", lambda idx=idx, tt=tt, b=b: nc.scalar.copy(out=H[:, idx, tsl(tt)], in_=ps[:, b, :]),
                          reads=[f"ps{b}"], writes=[f"H{idx}"])
                    tk.op("dve", lambda tt=tt, b=b: nc.vector.tensor_tensor(out=t4[tt][0][0:32, :], in0=ps[0:32, b, :], in1=cosT[:, tsl(tt)], op=ALU.mult),
                          reads=[f"ps{b}"] + sgkeys, writes=[t4[tt][1]])

                def part2(j=j, idx=idx, bs=bs):
                    for tt in range(NT):
                        b = bs + tt
                        tk.op("pe", lambda: nc.tensor.matmul(ps[:, b, :], lhsT=PT[:, :], rhs=H[:, idx, tsl(tt)], start=True, stop=True),
                              reads=["PT", f"H{idx}"], writes=[f"ps{b}"])
                        tk.op("dve", lambda: nc.vector.tensor_tensor(out=vg[0:32, tt % 2, :], in0=ps[0:32, b, :], in1=sinT[:, tsl(tt)], op=ALU.mult),
                              reads=[f"ps{b}"] + sgkeys, writes=[f"vg{tt % 2}"])
                        tk.op("dve", lambda: nc.vector.tensor_tensor(out=H[0:32, idx, tsl(tt)], in0=t4[tt][0][0:32, :], in1=vg[0:32, tt % 2, :], op=ALU.add),
                              reads=[t4[tt][1], f"vg{tt % 2}"], writes=[f"H{idx}"])
                    tk.dma("sp", "dout", lambda: nc.sync.dma_start(out=qk_d[:, j - 4, :], in_=H[:, idx, :]), reads=[f"H{idx}"])
                pending = part2
        if pending is not None:
            pending()
        for w in range(2):
            tk.dma("pool", "dwt", lambda w=w: nc.gpsimd.dma_start(out=wtm[:, w * 8:(w + 1) * 8, :], in_=wtm_d[w]),
                   writes=["H0", "H1", "H2", "H3"])
        wkeys = ["H0", "H1", "H2", "H3"]
        for i in range(16):
            par = i % 2
            bA, bB, bC = par * 3, par * 3 + 1, par * 3 + 2
            tok = slice(i * 128, (i + 1) * 128)
            for w, b in ((0, bA), (1, bB)):
                for c in range(DCH):
                    tk.op("pe", lambda w=w, b=b, c=c, tok=tok: nc.tensor.matmul(ps[:, b, :], lhsT=hn[:, c, tok], rhs=wtm[:, w * 8 + c, :],
                                                                                  start=(c == 0), stop=(c == DCH - 1)),
                          reads=[f"hn{c}"] + wkeys, writes=[f"ps{b}"])
            tk.op("act", lambda par=par, bB=bB: nc.scalar.copy(out=vt[:, par, :], in_=ps[:, bB, :]), reads=[f"ps{bB}"], writes=[f"vt{par}"])
            tk.dma("sp", "dout", lambda i=i, par=par: nc.sync.dma_start(out=vo_d[i], in_=vt[:, par, :]), reads=[f"vt{par}"])
            emit_gelu(tk, nc, ps[:, bA, :], f"ps{bA}", tmpA[:, par, :], f"tmpA{par}", vg[:, par, :], f"vg{par}")
            tk.op("act", lambda par=par: nc.scalar.activation(out=tmpB[:, par, :], in_=vg[:, par, :], func=AF.Square), reads=[f"vg{par}"], writes=[f"tmpB{par}"])
            tk.op("dve", lambda par=par: nc.vector.reduce_sum(out=ss[:, par:par + 1], in_=tmpB[:, par, :], axis=AX), reads=[f"tmpB{par}"], writes=[f"ss{par}"])
            tk.op("dve", lambda par=par: nc.vector.tensor_scalar(out=ss[:, par:par + 1], in0=ss[:, par:par + 1], scalar1=1.0 / 512.0, scalar2=RMS_EPS,
                                                                   op0=ALU.mult, op1=ALU.add), reads=[f"ss{par}"], writes=[f"ss{par}"])
            tk.op("act", lambda par=par: nc.scalar.sqrt(out=ss[:, par:par + 1], in_=ss[:, par:par + 1]), reads=[f"ss{par}"], writes=[f"ss{par}"])
            tk.op("dve", lambda par=par: nc.vector.reciprocal(out=ss[:, par:par + 1], in_=ss[:, par:par + 1]), reads=[f"ss{par}"], writes=[f"ss{par}"])
            tk.op("dve", lambda par=par: nc.vector.scalar_tensor_tensor(out=vnb[:, par, :], in0=vg[:, par, :], scalar=ss[:, par:par + 1], in1=vnorm[:, :],
                                                                         op0=ALU.mult, op1=ALU.mult),
                  reads=[f"vg{par}", f"ss{par}", "vnorm"], writes=[f"vnb{par}"])
            for g in range(4):
                tk.op("pe", lambda par=par, g=g, bC=bC: nc.tensor.matmul(ps[:, bC, g * 128:(g + 1) * 128], lhsT=vnb[:, par, g * 128:(g + 1) * 128], rhs=WmT[:, g, :],
                                                                          start=True, stop=True),
                      reads=[f"vnb{par}", "WmT"], writes=[f"ps{bC}"])
            tk.op("dve", lambda par=par, bC=bC: nc.vector.tensor_tensor(out=tmpm[:, par, :, :], in0=ps[:, bC, :].rearrange("p (g t) -> p g t", g=4), in1=bsB[:, :, :], op=ALU.add),
                  reads=[f"ps{bC}", "bsB"], writes=[f"tmpm{par}"])
            tk.op("dve", lambda par=par, tok=tok: nc.vector.tensor_tensor(out=ao[:, par, :, :], in0=tmpm[:, par, :, :], in1=uT[:, :, tok], op=ALU.mult),
                  reads=[f"tmpm{par}", "H4", "H5", "H6", "H7"], writes=[f"ao{par}"])
            tk.dma("sp", "dout", lambda par=par, tok=tok: nc.sync.dma_start(out=ao_d[:, :, tok], in_=ao[:, par, :, :]), reads=[f"ao{par}"])
        tk.finish("sp", ["dout"])
        tk.barrier()
    return nc


def relayout_ffn(wg, wu, wd):
    a = np.ascontiguousarray(np.asarray(wg, np.float32).reshape(DCH, 128, FCH, 128).transpose(2, 1, 0, 3))
    b = np.ascontiguousarray(np.asarray(wu, np.float32).reshape(DCH, 128, FCH, 128).transpose(2, 1, 0, 3))
    c = np.ascontiguousarray(np.asarray(wd, np.float32).reshape(FCH, 128, DCH, 128).transpose(2, 1, 0, 3))
    return a, b, c


def fm_vec(v):
    return np.ascontiguousarray(np.asarray(v, np.float32).reshape(DCH, 128).T)


def rope_consts():
    half = 16
    inv = (np.float32(500000.0) ** (-np.arange(half, dtype=np.float32) / np.float32(half))).astype(np.float32)
    rc = np.zeros((32, 4), np.float32)
    rc[:, 0] = np.concatenate([inv, inv])
    rc[:16, 1] = -1.0
    rc[16:, 1] = 1.0
    rc[:, 2] = -np.pi
    PT = np.zeros((128, 128), np.float32)
    for i in range(16):
        PT[i + 16, i] = 1.0
        PT[i, i + 16] = 1.0
    return rc, PT


def l1_inputs(inputs, core):
    b, half = core // 2, core % 2
    tok = slice(half * T_CORE, (half + 1) * T_CORE)
    x = np.asarray(inputs["x"], np.float32)[b, tok]
    xT = np.ascontiguousarray(x.T.reshape(DCH, 128, T_CORE).transpose(1, 0, 2))
    return xT


def l1_shared(inputs):
    wg, wu, wd = relayout_ffn(inputs["ffn_w_gate"][0, 0], inputs["ffn_w_up"][0, 0], inputs["ffn_w_down"][0, 0])
    w_in = np.asarray(inputs["hyb_w_in"], np.float32)[0]
    cols_fm = np.concatenate([w_in[:, 0:512], w_in[:, 1024:1536], w_in[:, 1536:2048]], axis=1)
    wfm = np.ascontiguousarray(cols_fm.reshape(DCH, 128, 12, 128).transpose(2, 1, 0, 3))
    cols_tm = np.stack([w_in[:, 512:1024], w_in[:, 2048:2560]], axis=0)
    wtm = np.ascontiguousarray(cols_tm.reshape(2, DCH, 128, 512).transpose(0, 2, 1, 3))
    rc, PT = rope_consts()
    ws = np.asarray(inputs["gmlp_w_s"], np.float32)[0]
    wsT = np.ascontiguousarray(ws.transpose(2, 0, 1))
    tril = np.tril(np.ones((128, 128), np.float32))
    trilT = np.ascontiguousarray(np.broadcast_to(tril.T[:, None, :], (128, 4, 128)))
    bsB = np.ascontiguousarray(np.broadcast_to(np.asarray(inputs["gmlp_b_s"], np.float32)[0][None], (128, 4, 128)))
    vnorm = np.ascontiguousarray(np.broadcast_to(np.asarray(inputs["gmlp_v_norm"], np.float32)[0][None], (128, 512)))
    return {
        "g_ffn": fm_vec(inputs["ffn_norm"][0, 0]), "g_mix": fm_vec(inputs["mix_norm"][0]),
        "wg0": wg, "wu0": wu, "wd0": wd, "win_fm": wfm, "win_tm": wtm,
        "ropec": rc, "PT": PT, "vnorm": vnorm, "wsT": wsT, "trilT": trilT, "bsB": bsB,
    }


def run_l1(inputs, cores=None):
    cores = list(range(N_CORES)) if cores is None else cores
    nc = build_l1()
    shared = l1_shared(inputs)
    pos = np.asarray(inputs["positions"], np.int32)
    maps = []
    for c in cores:
        b, half = c // 2, c % 2
        m = dict(shared)
        m["xT"] = l1_inputs(inputs, c)
        m["pos"] = np.ascontiguousarray(np.broadcast_to(pos[b, half * T_CORE:(half + 1) * T_CORE][None], (32, T_CORE)))
        maps.append(m)
    res = run_bass_kernel_spmd(nc, maps, core_ids=list(range(len(cores))))
    return res.results


ATT_SCALE = 128.0 ** -0.5
NEG = -1e30


def emit_proj_fm(tk, nc, st, ps, wbuf, w_chunks, in_chunks, epilogue):
    KC = len(in_chunks)
    for j, wsrc in enumerate(w_chunks):
        wi = st["w"] % len(wbuf)
        st["w"] += 1
        wt = wbuf[wi]
        tk.dma("pool", f"dw{wi}", lambda wt=wt, wsrc=wsrc: nc.gpsimd.dma_start(out=wt[:, 0:KC, :], in_=wsrc), writes=[f"w{wi}"])
        bs = (st["pj"] % 2) * 4
        st["pj"] += 1
        for c, (iap, ikey) in enumerate(in_chunks):
            for tt in range(NT):
                tk.op("pe", lambda wt=wt, c=c, tt=tt, bs=bs, iap=iap: nc.tensor.matmul(
                    ps[:, bs + tt, :], lhsT=wt[:, c, :], rhs=iap[:, tsl(tt)], start=(c == 0), stop=(c == KC - 1)),
                    reads=[f"w{wi}", ikey], writes=[f"ps{bs + tt}"])
        epilogue(j, bs)


def emit_resid_epilogue(tk, nc, ps, xT):
    def ep(j, bs):
        for tt in range(NT):
            tk.op("dve", lambda j=j, tt=tt, bs=bs: nc.vector.tensor_tensor(out=xT[:, j, tsl(tt)], in0=ps[:, bs + tt, :], in1=xT[:, j, tsl(tt)], op=ALU.add),
                  reads=[f"ps{bs + tt}", f"xT{j}"], writes=[f"xT{j}"])
    return ep


def emit_final_norm(tk, nc, xT, g, gkey, hn, rstd, ps, ones, sg, out_d):
    for c in range(DCH):
        tk.op("act", lambda c=c: nc.scalar.activation(out=hn[:, c, :], in_=xT[:, c, :], func=AF.Square), reads=[f"xT{c}"], writes=[f"hn{c}"])
    for tt in range(NT):
        for c in range(DCH):
            tk.op("pe", lambda c=c, tt=tt: nc.tensor.matmul(ps[:, tt, :], lhsT=ones[:, :], rhs=hn[:, c, tsl(tt)], start=(c == 0), stop=(c == DCH - 1)),
                  reads=[f"hn{c}", "ones"], writes=[f"ps{tt}"])
    for tt in range(NT):
        sl = tsl(tt)
        tk.op("dve", lambda tt=tt, sl=sl: nc.vector.tensor_scalar(out=rstd[:, sl], in0=ps[:, tt, :], scalar1=1.0 / D_MODEL, scalar2=RMS_EPS, op0=ALU.mult, op1=ALU.add),
              reads=[f"ps{tt}"], writes=[f"rstd{tt}"])
        tk.op("act", lambda sl=sl: nc.scalar.sqrt(out=rstd[:, sl], in_=rstd[:, sl]), reads=[f"rstd{tt}"], writes=[f"rstd{tt}"])
        tk.op("dve", lambda sl=sl: nc.vector.reciprocal(out=rstd[:, sl], in_=rstd[:, sl]), reads=[f"rstd{tt}"], writes=[f"rstd{tt}"])
    for c in range(DCH):
        si = c % 2
        for tt in range(NT):
            sl = tsl(tt)
            tk.op("dve", lambda c=c, sl=sl, si=si: nc.vector.scalar_tensor_tensor(out=sg[:, si, sl], in0=xT[:, c, sl], scalar=g[:, c:c + 1], in1=rstd[:, sl],
                                                                                   op0=ALU.mult, op1=ALU.mult),
                  reads=[f"xT{c}", f"rstd{tt}", gkey], writes=[f"sg{si}_{tt}"])
        tk.dma("sp", "dout", lambda c=c, si=si: nc.sync.dma_start(out=out_d[:, c, :], in_=sg[:, si, :]), reads=[f"sg{si}_{tt}" for tt in range(NT)])


def alloc_ffn_set(nc, es, hchunks=6, sfx=""):
    sb = lambda name, shape, dt: es.enter_context(nc.sbuf_tensor(name + sfx, shape, dt))
    d = {}
    d["xT"] = sb("xT_s", [128, DCH, T_CORE], F32)
    d["hn"] = sb("hn", [128, DCH, T_CORE], BF16)
    d["H"] = sb("H", [128, hchunks, T_CORE], BF16)
    d["sg"] = sb("sg", [128, 2, T_CORE], F32)
    d["rstd"] = sb("rstd", [128, T_CORE], F32)
    d["ones"] = sb("ones", [128, 128], BF16)
    d["wbuf"] = [sb(f"wb{i}", [128, DCH, 128], BF16) for i in range(4)]
    d["wdbuf"] = [sb(f"wdb{i}", [128, 6, 128], BF16) for i in range(3)]
    return d


def build_l2(nc=None, ov=None, sfx="", nblocks=lambda j: 9 + j, sem_es=None):
    nc = nc or bass.Bass("TRN2", target_bir_lowering=False)
    ov = ov or {}
    di, do = io_makers(nc, ov)
    x_d = di("xT", [128, DCH, T_CORE], F32)
    a_d = di("aT", [128, 4, T_CORE], BF16)
    q_d = di("QT", [128, 4, T_CORE], BF16)
    kparts = ov.get("KT_parts")
    vparts = ov.get("V_parts")
    k_d = None if kparts else di("KTp", [128, 4, 2 * T_CORE], BF16)
    v_d = None if vparts else di("Vp", [128, 32, 512], BF16)
    vb_d = di("vbias", [128, 8, 16], F32)
    no_d = di("notown", [128, 8, 16], F32)
    e_d = di("Esel", [32, 16, 128], BF16)
    tri_d = di("tri", [128, 4, 256], BF16)
    idf_d = di("identf", [128, 128], F32)
    wo_d = di("wo", [8, 128, DCH, 128], F32)
    gA_d = di("g_ffn_a", [128, DCH], F32)
    gB_d = di("g_ffn_b", [128, DCH], F32)
    gM_d = di("g_mix", [128, DCH], F32)
    wgA, wuA, wdA = ffn_dram(nc, "A", ov)
    wgB, wuB, wdB = ffn_dram(nc, "B", ov)
    wc_d = di("wc", [24, 128, DCH, 128], F32)
    xo_d = do("xT_o", [128, DCH, T_CORE], F32)
    z_d = do("zT_o", [128, DCH, T_CORE], F32)
    bg_d = do("bgT_o", [128, DCH, T_CORE], F32)
    with ExitStack() as es:
        sb = lambda name, shape, dt: es.enter_context(nc.sbuf_tensor(name + sfx, shape, dt))
        tk = Trk(nc, sem_es or es, ["dx", "dout"] + FFN_STREAMS, sfx)
        bT = sb("bT", [128, 4, T_CORE], BF16)
        ps = es.enter_context(nc.psum_tensor("ps" + sfx, [128, 8, 512], F32))
        st = {"w": 0, "sg": 0, "wd": 0, "dn": 0, "pj": 0}
        with ExitStack() as ea:
            sa = lambda name, shape, dt: ea.enter_context(nc.sbuf_tensor(name + sfx, shape, dt))
            QT = sa("QT_s", [128, 4, T_CORE], BF16)
            KT = sa("KT_s", [128, 4, 2 * T_CORE], BF16)
            V = sa("V_s", [128, 32, 512], BF16)
            vbias = sa("vbias_s", [128, 8, 16], F32)
            notown = sa("notown_s", [128, 8, 16], F32)
            E = sa("E_s", [32, 16, 128], BF16)
            tri = sa("tri_s", [128, 4, 256], BF16)
            identf = sa("identf_s", [128, 128], F32)
            identb = sa("identb", [128, 128], BF16)
            kms = sa("kms", [128, 4, 16], F32)
            kmT = sa("kmT", [128, 4, 16], BF16)
            gm = sa("gm", [128, 2, 16], F32)
            top8 = sa("top8", [128, 2, 8], F32)
            thr = sa("thr", [128, 2, 1], F32)
            G32 = sa("G32", [128, 2, 32], F32)
            biasT = sa("biasT", [32, 2, 256], BF16)
            PTs = sa("PTs", [128, 3, 256], BF16)
            rL = sa("rL", [128, 2, 256], F32)
            accL = sa("accL", [128, 2, 256], F32)
            onesf = sa("onesf", [128, 128], F32)
            for h in range(4):
                tk.dma("sp", "dx", lambda h=h: nc.sync.dma_start(out=QT[:, h, :], in_=q_d[:, h, :]), writes=[f"QT{h}"])
                if kparts:
                    for hf in range(2):
                        tk.dma("sp", "dx", lambda h=h, hf=hf: nc.sync.dma_start(out=KT[:, h, hf * T_CORE:(hf + 1) * T_CORE], in_=kparts[hf][:, h, :]), writes=[f"KT{h}"])
                else:
                    tk.dma("sp", "dx", lambda h=h: nc.sync.dma_start(out=KT[:, h, :], in_=k_d[:, h, :]), writes=[f"KT{h}"])
            for v4 in range(4):
                if vparts:
                    src = vparts[v4 // 2][(v4 % 2) * 8:(v4 % 2 + 1) * 8].rearrange("t p c -> p t c")
                    tk.dma("sp", "dx", lambda v4=v4, src=src: nc.sync.dma_start(out=V[:, v4 * 8:(v4 + 1) * 8, :], in_=src), writes=[f"V{v4}"])
                else:
                    tk.dma("sp", "dx", lambda v4=v4: nc.sync.dma_start(out=V[:, v4 * 8:(v4 + 1) * 8, :], in_=v_d[:, v4 * 8:(v4 + 1) * 8, :]), writes=[f"V{v4}"])
            for (t_s, t_d, key) in ((vbias, vb_d, "vbias"), (notown, no_d, "notown"), (E, e_d, "E"), (tri, tri_d, "tri"), (identf, idf_d, "identf")):
                tk.dma("sp", "dx", lambda t_s=t_s, t_d=t_d: nc.sync.dma_start(out=t_s[:], in_=t_d), writes=[key])
            tk.op("dve", lambda: nc.vector.tensor_copy(out=identb[:, :], in_=identf[:, :]), reads=["identf"], writes=["identb"])
            tk.op("dve", lambda: nc.vector.memset(onesf[:, :], 1.0), writes=["onesf"])
            tk.op("dve", lambda: nc.vector.memset(G32[:, :, :], 0.0), writes=["G32_0", "G32_1"])
            for h in range(4):
                tk.op("dve", lambda h=h: nc.vector.reduce_sum(out=kms[:, h, :], in_=KT[:, h, :].rearrange("p (n k) -> p n k", k=256), axis=AX),
                      reads=[f"KT{h}"], writes=["kms"])
            tk.op("dve", lambda: nc.vector.tensor_scalar(out=kmT[:, :, :], in0=kms[:, :, :], scalar1=1.0 / 256.0, scalar2=None, op0=ALU.mult),
                  reads=["kms"], writes=["kmT"])
            sctr = [0]
            pctr = [0]
            for h in range(4):
                for j in range(8):
                    ctr = h * 8 + j
                    ob, lb = 3 + 2 * (ctr % 2), 4 + 2 * (ctr % 2)
                    bb = ctr % 2
                    qs = slice(j * 256, (j + 1) * 256)
                    for qt in range(2):
                        qsl = slice(j * 256 + qt * 128, j * 256 + (qt + 1) * 128)
                        tk.op("pe", lambda h=h, qt=qt, qsl=qsl: nc.tensor.matmul(ps[:, 7, qt * 16:(qt + 1) * 16], lhsT=QT[:, h, qsl], rhs=kmT[:, h, :], start=True, stop=True),
                              reads=[f"QT{h}", "kmT"], writes=["ps7"])
                    for qt in range(2):
                        tk.op("dve", lambda qt=qt, j=j: nc.vector.tensor_tensor(out=gm[:, qt, :], in0=ps[:, 7, qt * 16:(qt + 1) * 16], in1=vbias[:, j, :], op=ALU.add),
                              reads=["ps7", "vbias"], writes=[f"gm{qt}"])
                        tk.op("dve", lambda qt=qt: nc.vector.max(out=top8[:, qt, :], in_=gm[:, qt, :]), reads=[f"gm{qt}"], writes=[f"top8{qt}"])
                        tk.op("dve", lambda qt=qt: nc.vector.tensor_scalar(out=thr[:, qt, :], in0=top8[:, qt, 2:3], scalar1=-1e29, scalar2=None, op0=ALU.max),
                              reads=[f"top8{qt}"], writes=[f"thr{qt}"])
                        tk.op("dve", lambda qt=qt: nc.vector.tensor_scalar(out=G32[:, qt, 0:16], in0=gm[:, qt, :], scalar1=thr[:, qt, 0:1], scalar2=None, op0=ALU.is_ge),
                              reads=[f"gm{qt}", f"thr{qt}"], writes=[f"G32_{qt}"])
                        tk.op("dve", lambda qt=qt: nc.vector.tensor_scalar(out=G32[:, qt, 0:16], in0=G32[:, qt, 0:16], scalar1=-1.0, scalar2=1e30, op0=ALU.add, op1=ALU.mult),
                              reads=[f"G32_{qt}"], writes=[f"G32_{qt}"])
                        tk.op("dve", lambda qt=qt, j=j: nc.vector.tensor_tensor(out=G32[:, qt, 0:16], in0=G32[:, qt, 0:16], in1=notown[:, j, :], op=ALU.mult),
                              reads=[f"G32_{qt}", "notown"], writes=[f"G32_{qt}"])
                        tk.op("pe", lambda qt=qt: nc.tensor.transpose(out=ps[0:32, 7, 64 + qt * 128:64 + (qt + 1) * 128], in_=G32[:, qt, :], identity=identf[:, :]),
                              reads=[f"G32_{qt}", "identf"], writes=["ps7"])
                    tk.op("act", lambda bb=bb: nc.scalar.copy(out=biasT[:, bb, :], in_=ps[0:32, 7, 64:320]), reads=["ps7"], writes=[f"biasT{bb}"])
                    nb = nblocks(j)
                    tiles = [(n, kt) for n in range(nb) for kt in range(2)]
                    NTI = len(tiles)
                    sbase, pbase = sctr[0], pctr[0]
                    sctr[0] += NTI
                    pctr[0] += NTI

                    def emit_S(i, h=h, j=j, qs=qs, bb=bb, tiles=tiles, sbase=sbase, nb=nb):
                        n, kt = tiles[i]
                        sbk = (sbase + i) % 3
                        ktile = 2 * n + kt
                        extra = (n == j) or (n == 8 + j)
                        tk.op("pe", lambda: nc.tensor.matmul(ps[:, sbk, 0:256], lhsT=KT[:, h, ktile * 128:(ktile + 1) * 128], rhs=QT[:, h, qs], start=True, stop=False),
                              reads=[f"KT{h}", f"QT{h}"], writes=[f"ps{sbk}"])
                        tk.op("pe", lambda: nc.tensor.matmul(ps[:, sbk, 0:256], lhsT=E[:, n, :], rhs=biasT[:, bb, :], start=False, stop=(not extra)),
                              reads=["E", f"biasT{bb}"], writes=[f"ps{sbk}"])
                        if extra:
                            ti = (0 if n == j else 2) + kt
                            tk.op("pe", lambda: nc.tensor.matmul(ps[:, sbk, 0:256], lhsT=identb[:, :], rhs=tri[:, ti, :], start=False, stop=True),
                                  reads=["identb", "tri"], writes=[f"ps{sbk}"])

                    def emit_EXP(i, sbase=sbase, pbase=pbase):
                        sbk = (sbase + i) % 3
                        pb = (pbase + i) % 3
                        tk.op("act", lambda: nc.scalar.activation(out=PTs[:, pb, :], in_=ps[:, sbk, 0:256], func=AF.Exp, scale=ATT_SCALE),
                              reads=[f"ps{sbk}"], writes=[f"PTs{pb}"])

                    rb = ctr % 2

                    def emit_PV(i, h=h, tiles=tiles, pbase=pbase, ob=ob, NTI=NTI, rb=rb):
                        n, kt = tiles[i]
                        pb = (pbase + i) % 3
                        ktile = 2 * n + kt
                        tk.op("pe", lambda: nc.tensor.matmul(ps[:, ob, 0:256], lhsT=V[:, ktile, h * 128:(h + 1) * 128], rhs=PTs[:, pb, :], start=(i == 0), stop=(i == NTI - 1)),
                              reads=[f"V{ktile // 8}", f"PTs{pb}"], writes=[f"ps{ob}"])
                        if i == 0:
                            tk.op("pool", lambda: nc.gpsimd.tensor_copy(out=accL[:, rb, :], in_=PTs[:, pb, :]), reads=[f"PTs{pb}"], writes=[f"accL{rb}"])
                        else:
                            tk.op("pool", lambda: nc.gpsimd.tensor_tensor(out=accL[:, rb, :], in0=accL[:, rb, :], in1=PTs[:, pb, :], op=ALU.add),
                                  reads=[f"PTs{pb}", f"accL{rb}"], writes=[f"accL{rb}"])

                    emit_S(0)
                    emit_S(1)
                    for i in range(NTI):
                        if i + 2 < NTI:
                            emit_S(i + 2)
                        emit_EXP(i)
                        emit_PV(i)
                    tk.op("pe", lambda rb=rb, lb=lb: nc.tensor.matmul(ps[:, lb, 0:256], lhsT=onesf[:, :], rhs=accL[:, rb, :], start=True, stop=True),
                          reads=["onesf", f"accL{rb}"], writes=[f"ps{lb}"])
                    tk.op("dve", lambda rb=rb, lb=lb: nc.vector.reciprocal(out=rL[:, rb, :], in_=ps[:, lb, 0:256]), reads=[f"ps{lb}"], writes=[f"rL{rb}"])
                    tk.op("dve", lambda rb=rb, ob=ob, h=h, qs=qs: nc.vector.tensor_tensor(out=bT[:, h, qs], in0=ps[:, ob, 0:256], in1=rL[:, rb, :], op=ALU.mult),
                          reads=[f"ps{ob}", f"rL{rb}"], writes=[f"bT{h}"])
            tk.barrier()
        with ExitStack() as eb:
            f = alloc_ffn_set(nc, eb, hchunks=6, sfx=sfx)
            xT, hn, H, sg, rstd, ones, wbuf, wdbuf = (f[k] for k in ("xT", "hn", "H", "sg", "rstd", "ones", "wbuf", "wdbuf"))
            sbb = lambda name, shape, dt: eb.enter_context(nc.sbuf_tensor(name + sfx, shape, dt))
            gA = sbb("gA", [128, DCH], F32)
            gB = sbb("gB", [128, DCH], F32)
            gM = sbb("gM", [128, DCH], F32)
            for c in range(DCH):
                tk.dma("sp", "dx", lambda c=c: nc.sync.dma_start(out=xT[:, c, :], in_=x_d[:, c, :]), writes=[f"xT{c}"])
            for c in range(4):
                tk.dma("sp", "dx", lambda c=c: nc.sync.dma_start(out=H[:, c, :], in_=a_d[:, c, :]), writes=[f"H{c}"])
            for (t_s, t_d, key) in ((gA, gA_d, "gA"), (gB, gB_d, "gB"), (gM, gM_d, "gM")):
                tk.dma("sp", "dx", lambda t_s=t_s, t_d=t_d: nc.sync.dma_start(out=t_s[:], in_=t_d), writes=[key])
            tk.op("dve", lambda: nc.vector.memset(ones[:, :], 1.0), writes=["ones"])
            in_chunks = [(H[:, c, :], f"H{c}") for c in range(4)] + [(bT[:, h, :], f"bT{h}") for h in range(4)]
            emit_proj_fm(tk, nc, st, ps, wbuf, [wo_d[j] for j in range(8)], in_chunks, emit_resid_epilogue(tk, nc, ps, xT))
            emit_rmsnorm(tk, nc, xT, gA, "gA", hn, rstd, ps, ones)
            emit_ffn(tk, nc, st, xT, hn, H, sg, ps, wbuf, wdbuf, wgA, wuA, wdA)
            emit_rmsnorm(tk, nc, xT, gB, "gB", hn, rstd, ps, ones)
            emit_ffn(tk, nc, st, xT, hn, H, sg, ps, wbuf, wdbuf, wgB, wuB, wdB)
            emit_rmsnorm(tk, nc, xT, gM, "gM", hn, rstd, ps, ones)
            for c in range(DCH):
                tk.dma("sp", "dout", lambda c=c: nc.sync.dma_start(out=xo_d[:, c, :], in_=xT[:, c, :]), reads=[f"xT{c}"])
            sgk = lambda si: [f"sg{si}_{tt}" for tt in range(NT)]

            def conv_ep(jj, bs):
                if jj < 16:
                    i, si = jj // 2, (jj // 2) % 2
                    if jj % 2 == 0:
                        for tt in range(NT):
                            tk.op("act", lambda tt=tt, si=si, bs=bs: nc.scalar.copy(out=sg[:, si, tsl(tt)], in_=ps[:, bs + tt, :]),
                                  reads=[f"ps{bs + tt}"], writes=[f"sg{si}_{tt}"])
                    else:
                        for tt in range(NT):
                            tk.op("dve", lambda tt=tt, si=si, bs=bs: nc.vector.tensor_tensor(out=sg[:, si, tsl(tt)], in0=sg[:, si, tsl(tt)], in1=ps[:, bs + tt, :], op=ALU.mult),
                                  reads=[f"sg{si}_{tt}", f"ps{bs + tt}"], writes=[f"sg{si}_{tt}"])
                        tk.dma("sp", "dout", lambda i=i, si=si: nc.sync.dma_start(out=z_d[:, i, :], in_=sg[:, si, :]), reads=sgk(si))
                else:
                    i, si = jj - 16, jj % 2
                    for tt in range(NT):
                        tk.op("act", lambda tt=tt, si=si, bs=bs: nc.scalar.copy(out=sg[:, si, tsl(tt)], in_=ps[:, bs + tt, :]),
                              reads=[f"ps{bs + tt}"], writes=[f"sg{si}_{tt}"])
                    tk.dma("sp", "dout", lambda i=i, si=si: nc.sync.dma_start(out=bg_d[:, i, :], in_=sg[:, si, :]), reads=sgk(si))

            emit_proj_fm(tk, nc, st, ps, wbuf, [wc_d[j] for j in range(24)], [(hn[:, c, :], f"hn{c}") for c in range(DCH)], conv_ep)
            tk.finish("sp", ["dout"])
            tk.barrier()
    return nc


def build_l3(nc=None, ov=None, sfx="", sem_es=None):
    nc = nc or bass.Bass("TRN2", target_bir_lowering=False)
    ov = ov or {}
    di, do = io_makers(nc, ov)
    x_d = di("xT", [128, DCH, T_CORE], F32)
    z_src = ov.get("z_src")
    halo_src = ov.get("halo_src")
    ze_d = None if z_src is not None else di("zext", [128, DCH, T_CORE + 2], F32)
    bg_d = di("bgT", [128, DCH, T_CORE], F32)
    cw_d = di("cw", [128, DCH, 3], F32)
    wco_d = di("wco", [8, 128, DCH, 128], F32)
    gC_d = di("g_ffn_c", [128, DCH], F32)
    gF_d = di("g_fin", [128, DCH], F32)
    wgC, wuC, wdC = ffn_dram(nc, "C", ov)
    out_d = do("outT", [128, DCH, T_CORE], F32)
    with ExitStack() as es:
        sb = lambda name, shape, dt: es.enter_context(nc.sbuf_tensor(name + sfx, shape, dt))
        tk = Trk(nc, sem_es or es, ["dx", "dz", "dout"] + FFN_STREAMS, sfx)
        f = alloc_ffn_set(nc, es, hchunks=6, sfx=sfx)
        xT, hn, H, sg, rstd, ones, wbuf, wdbuf = (f[k] for k in ("xT", "hn", "H", "sg", "rstd", "ones", "wbuf", "wdbuf"))
        gC = sb("gC", [128, DCH], F32)
        gF = sb("gF", [128, DCH], F32)
        cw = sb("cw_s", [128, DCH, 3], F32)
        zc = [sb(f"zc{i}", [128, T_CORE + 2], F32) for i in range(2)]
        bgc = [sb(f"bgc{i}", [128, T_CORE], F32) for i in range(2)]
        ps = es.enter_context(nc.psum_tensor("ps" + sfx, [128, 8, 512], F32))
        st = {"w": 0, "sg": 0, "wd": 0, "dn": 0, "pj": 0}
        for c in range(DCH):
            tk.dma("sp", "dx", lambda c=c: nc.sync.dma_start(out=xT[:, c, :], in_=x_d[:, c, :]), writes=[f"xT{c}"])
        for (t_s, t_d, key) in ((gC, gC_d, "gC"), (gF, gF_d, "gF"), (cw, cw_d, "cw")):
            tk.dma("sp", "dx", lambda t_s=t_s, t_d=t_d: nc.sync.dma_start(out=t_s[:], in_=t_d), writes=[key])
        tk.op("dve", lambda: nc.vector.memset(ones[:, :], 1.0), writes=["ones"])
        for c in range(DCH):
            ci = c % 2
            if z_src is None:
                tk.dma("sp", f"dz", lambda c=c, ci=ci: nc.sync.dma_start(out=zc[ci][:, :], in_=ze_d[:, c, :]), writes=[f"zc{ci}"])
            else:
                tk.dma("sp", f"dz", lambda c=c, ci=ci: nc.sync.dma_start(out=zc[ci][:, 2:T_CORE + 2], in_=z_src[:, c, :]), writes=[f"zc{ci}"])
                if halo_src is None:
                    tk.op("dve", lambda ci=ci: nc.vector.memset(zc[ci][:, 0:2], 0.0), writes=[f"zc{ci}"])
                else:
                    tk.dma("sp", f"dz", lambda c=c, ci=ci: nc.sync.dma_start(out=zc[ci][:, 0:2], in_=halo_src[:, c, T_CORE - 2:T_CORE]), writes=[f"zc{ci}"])
            tk.dma("sp", f"dz", lambda c=c, ci=ci: nc.sync.dma_start(out=bgc[ci][:, :], in_=bg_d[:, c, :]), writes=[f"bgc{ci}"])
            en, eo = "dve", nc.vector
            for tt in range(NT):
                sl = tsl(tt)
                o = tt * 512
                key = f"sg{ci}_{tt}"
                tk.op(en, lambda c=c, ci=ci, sl=sl, o=o, eo=eo: eo.tensor_scalar(out=sg[:, ci, sl], in0=zc[ci][:, o:o + 512], scalar1=cw[:, c, 0:1], scalar2=None, op0=ALU.mult),
                      reads=[f"zc{ci}", "cw"], writes=[key])
                tk.op(en, lambda c=c, ci=ci, sl=sl, o=o, eo=eo: eo.scalar_tensor_tensor(out=sg[:, ci, sl], in0=zc[ci][:, o + 1:o + 513], scalar=cw[:, c, 1:2], in1=sg[:, ci, sl],
                                                                                        op0=ALU.mult, op1=ALU.add),
                      reads=[f"zc{ci}", "cw", key], writes=[key])
                tk.op(en, lambda c=c, ci=ci, sl=sl, o=o, eo=eo: eo.scalar_tensor_tensor(out=sg[:, ci, sl], in0=zc[ci][:, o + 2:o + 514], scalar=cw[:, c, 2:3], in1=sg[:, ci, sl],
                                                                                        op0=ALU.mult, op1=ALU.add),
                      reads=[f"zc{ci}", "cw", key], writes=[key])
                tk.op(en, lambda c=c, ci=ci, sl=sl, eo=eo: eo.tensor_tensor(out=hn[:, c, sl], in0=sg[:, ci, sl], in1=bgc[ci][:, sl], op=ALU.mult),
                      reads=[key, f"bgc{ci}"], writes=[f"hn{c}"])
        emit_proj_fm(tk, nc, st, ps, wbuf, [wco_d[j] for j in range(8)], [(hn[:, c, :], f"hn{c}") for c in range(DCH)], emit_resid_epilogue(tk, nc, ps, xT))
        emit_rmsnorm(tk, nc, xT, gC, "gC", hn, rstd, ps, ones)
        emit_ffn(tk, nc, st, xT, hn, H, sg, ps, wbuf, wdbuf, wgC, wuC, wdC)
        emit_final_norm(tk, nc, xT, gF, "gF", hn, rstd, ps, ones, sg, out_d)
        tk.finish("sp", ["dout"])
        tk.barrier()
    return nc


def relayout_fm(W):
    W = np.asarray(W, np.float32)
    nch = W.shape[1] // 128
    return np.ascontiguousarray(W.reshape(DCH, 128, nch, 128).transpose(2, 1, 0, 3))


def attn_consts(half):
    vb = np.zeros((8, 16), np.float32)
    no = np.ones((8, 16), np.float32)
    for j in range(8):
        qblk = 8 * half + j
        vb[j, qblk:] = NEG
        no[j, qblk] = 0.0
    E = np.zeros((32, 16, 128), np.float32)
    for n in range(16):
        E[n, n, :] = 1.0
    tri = np.zeros((128, 4, 256), np.float32)
    k = np.arange(128)[:, None]
    q = np.arange(256)[None, :]
    for kt in range(2):
        m = np.where(kt * 128 + k > q, NEG, 0.0).astype(np.float32)
        tri[:, (0 if half == 0 else 2) + kt, :] = m
    return {
        "vbias": np.ascontiguousarray(np.broadcast_to(vb[None], (128, 8, 16))),
        "notown": np.ascontiguousarray(np.broadcast_to(no[None], (128, 8, 16))),
        "Esel": bf16_np(E), "tri": bf16_np(tri), "identf": np.eye(128, dtype=np.float32),
    }


def run_l2(inputs, r1, cores):
    nc = build_l2()
    wgA, wuA, wdA = relayout_ffn(inputs["ffn_w_gate"][0, 1], inputs["ffn_w_up"][0, 1], inputs["ffn_w_down"][0, 1])
    wgB, wuB, wdB = relayout_ffn(inputs["ffn_w_gate"][1, 0], inputs["ffn_w_up"][1, 0], inputs["ffn_w_down"][1, 0])
    cwi = np.asarray(inputs["conv_w_in"], np.float32)[0]
    cols = []
    for i in range(8):
        cols.append(cwi[:, 1024 + i * 128:1024 + (i + 1) * 128])
        cols.append(cwi[:, 2048 + i * 128:2048 + (i + 1) * 128])
    for i in range(8):
        cols.append(cwi[:, i * 128:(i + 1) * 128])
    shared = {
        "wo": relayout_fm(inputs["hyb_w_out"][0]),
        "g_ffn_a": fm_vec(inputs["ffn_norm"][0, 1]), "g_ffn_b": fm_vec(inputs["ffn_norm"][1, 0]), "g_mix": fm_vec(inputs["mix_norm"][1]),
        "wgA": wgA, "wuA": wuA, "wdA": wdA, "wgB": wgB, "wuB": wuB, "wdB": wdB,
        "wc": relayout_fm(np.concatenate(cols, axis=1)),
    }
    idx = {c: i for i, c in enumerate(cores)}
    maps = []
    for c in cores:
        b, half = c // 2, c % 2
        r0, rp = r1[idx[2 * b]], r1[idx[2 * b + 1]]
        m = dict(shared)
        m.update(attn_consts(half))
        me = r1[idx[c]]
        m["xT"] = np.asarray(me["xT_o"])
        m["aT"] = np.asarray(me["aT_o"])
        m["QT"] = np.ascontiguousarray(np.asarray(me["QKT_o"])[:, 0:4, :])
        m["KTp"] = np.ascontiguousarray(np.concatenate([np.asarray(r0["QKT_o"])[:, 4:8, :], np.asarray(rp["QKT_o"])[:, 4:8, :]], axis=2))
        vp = np.concatenate([np.asarray(r0["V_o"]), np.asarray(rp["V_o"])], axis=0)
        m["Vp"] = np.ascontiguousarray(vp.transpose(1, 0, 2))
        maps.append(m)
    res = run_bass_kernel_spmd(nc, maps, core_ids=list(range(len(cores))))
    return res.results


def run_l3(inputs, r2, cores):
    nc = build_l3()
    wgC, wuC, wdC = relayout_ffn(inputs["ffn_w_gate"][1, 1], inputs["ffn_w_up"][1, 1], inputs["ffn_w_down"][1, 1])
    cwv = np.asarray(inputs["conv_w"], np.float32)[0]
    cw = np.ascontiguousarray(cwv.reshape(3, DCH, 128).transpose(2, 1, 0))
    shared = {
        "cw": cw, "wco": relayout_fm(inputs["conv_w_out"][0]),
        "g_ffn_c": fm_vec(inputs["ffn_norm"][1, 1]), "g_fin": fm_vec(inputs["final_norm"]),
        "wgC": wgC, "wuC": wuC, "wdC": wdC,
    }
    idx = {c: i for i, c in enumerate(cores)}
    maps = []
    for c in cores:
        b, half = c // 2, c % 2
        me = r2[idx[c]]
        z = np.asarray(me["zT_o"])
        if half == 0:
            halo = np.zeros((128, DCH, 2), np.float32)
        else:
            halo = np.asarray(r2[idx[2 * b]]["zT_o"])[:, :, T_CORE - 2:T_CORE]
        m = dict(shared)
        m["xT"] = np.asarray(me["xT_o"])
        m["zext"] = np.ascontiguousarray(np.concatenate([halo, z], axis=2))
        m["bgT"] = np.asarray(me["bgT_o"])
        maps.append(m)
    res = run_bass_kernel_spmd(nc, maps, core_ids=list(range(len(cores))))
    return res.results


def run_all(inputs, cores):
    inputs = {k: np.asarray(v) for k, v in inputs.items()}
    r1 = run_l1(inputs, cores)
    r2 = run_l2(inputs, r1, cores)
    r3 = run_l3(inputs, r2, cores)
    outs = {}
    for i, c in enumerate(cores):
        oT = np.asarray(r3[i]["outT"], np.float32)
        outs[c] = np.ascontiguousarray(oT.transpose(1, 0, 2).reshape(D_MODEL, T_CORE).T)
    return outs, (r1, r2, r3)
def build_fused():
    nc = bass.Bass("TRN2", target_bir_lowering=False)
    ext = lambda name, shape, dt: nc.dram_tensor(name, shape, dt, kind="ExternalInput").ap()
    itn = lambda name, shape, dt: nc.dram_tensor(name, shape, dt, kind="Internal").ap()
    W = {}
    for sfx in ("0", "A", "B", "C"):
        W["wg" + sfx] = ext("wg" + sfx, [FCH, 128, DCH, 128], F32)
        W["wu" + sfx] = ext("wu" + sfx, [FCH, 128, DCH, 128], F32)
        W["wd" + sfx] = ext("wd" + sfx, [DCH, 128, FCH, 128], F32)
    for name, shape, dt in (("g_ffn", [128, DCH], F32), ("g_mix1", [128, DCH], F32), ("win_fm", [12, 128, DCH, 128], F32), ("win_tm", [2, 128, DCH, 512], F32),
                            ("ropec", [32, 4], F32), ("PT", [128, 128], F32), ("vnorm", [128, 512], F32), ("wsT", [128, 4, 128], F32),
                            ("trilT", [128, 4, 128], F32), ("bsB", [128, 4, 128], F32),
                            ("Esel", [32, 16, 128], BF16), ("identf", [128, 128], F32), ("wo", [8, 128, DCH, 128], F32),
                            ("g_ffn_a", [128, DCH], F32), ("g_ffn_b", [128, DCH], F32), ("g_mix2", [128, DCH], F32), ("wc", [24, 128, DCH, 128], F32),
                            ("cw", [128, DCH, 3], F32), ("wco", [8, 128, DCH, 128], F32), ("g_ffn_c", [128, DCH], F32), ("g_fin", [128, DCH], F32)):
        W[name] = ext(name, shape, dt)
    x2 = ext("xT2", [2, 128, DCH, T_CORE], F32)
    pos2 = ext("pos2", [2, 32, T_CORE], I32)
    vb2 = ext("vbias2", [2, 128, 8, 16], F32)
    no2 = ext("notown2", [2, 128, 8, 16], F32)
    tri2 = ext("tri2", [2, 128, 4, 256], BF16)
    out2 = nc.dram_tensor("outT2", [2, 128, DCH, T_CORE], F32, kind="ExternalOutput").ap()
    x1_i = itn("x1_i", [2, 128, DCH, T_CORE], F32)
    a_i = itn("a_i", [2, 128, 4, T_CORE], BF16)
    qk_i = itn("qk_i", [2, 128, 8, T_CORE], BF16)
    v_i = itn("v_i", [2, 16, 128, 512], BF16)
    x4_i = itn("x4_i", [2, 128, DCH, T_CORE], F32)
    z_i = itn("z_i", [2, 128, DCH, T_CORE], F32)
    bg_i = itn("bg_i", [2, 128, DCH, T_CORE], F32)
    with ExitStack() as sem_es:
        for hf in range(2):
            ov = dict(W)
            ov.update({"g_mix": W["g_mix1"], "xT": x2[hf], "pos": pos2[hf], "xT_o": x1_i[hf], "aT_o": a_i[hf], "QKT_o": qk_i[hf], "V_o": v_i[hf]})
            build_l1(nc, ov, sfx=f"_p1h{hf}", sem_es=sem_es)
        for hf in range(2):
            ov = dict(W)
            ov.update({"g_mix": W["g_mix2"], "xT": x1_i[hf], "aT": a_i[hf], "QT": qk_i[hf][:, 0:4, :],
                       "KT_parts": [qk_i[0][:, 4:8, :], qk_i[1][:, 4:8, :]], "V_parts": [v_i[0], v_i[1]],
                       "vbias": vb2[hf], "notown": no2[hf], "tri": tri2[hf],
                       "xT_o": x4_i[hf], "zT_o": z_i[hf], "bgT_o": bg_i[hf]})
            nblocks = (lambda j: j + 1) if hf == 0 else (lambda j: 9 + j)
            build_l2(nc, ov, sfx=f"_p2h{hf}", nblocks=nblocks, sem_es=sem_es)
        for hf in range(2):
            ov = dict(W)
            ov.update({"xT": x4_i[hf], "z_src": z_i[hf], "halo_src": (z_i[0] if hf == 1 else None), "bgT": bg_i[hf], "outT": out2[hf]})
            build_l3(nc, ov, sfx=f"_p3h{hf}", sem_es=sem_es)
    return nc


def fused_shared(inputs):
    s1 = l1_shared(inputs)
    sh = {k: s1[k] for k in ("wg0", "wu0", "wd0", "g_ffn", "win_fm", "win_tm", "ropec", "PT", "vnorm", "wsT", "trilT", "bsB")}
    sh["g_mix1"] = s1["g_mix"]
    wgA, wuA, wdA = relayout_ffn(inputs["ffn_w_gate"][0, 1], inputs["ffn_w_up"][0, 1], inputs["ffn_w_down"][0, 1])
    wgB, wuB, wdB = relayout_ffn(inputs["ffn_w_gate"][1, 0], inputs["ffn_w_up"][1, 0], inputs["ffn_w_down"][1, 0])
    wgC, wuC, wdC = relayout_ffn(inputs["ffn_w_gate"][1, 1], inputs["ffn_w_up"][1, 1], inputs["ffn_w_down"][1, 1])
    cwi = np.asarray(inputs["conv_w_in"], np.float32)[0]
    cols = []
    for i in range(8):
        cols.append(cwi[:, 1024 + i * 128:1024 + (i + 1) * 128])
        cols.append(cwi[:, 2048 + i * 128:2048 + (i + 1) * 128])
    for i in range(8):
        cols.append(cwi[:, i * 128:(i + 1) * 128])
    cwv = np.asarray(inputs["conv_w"], np.float32)[0]
    c0, c1 = attn_consts(0), attn_consts(1)
    sh.update({
        "wgA": wgA, "wuA": wuA, "wdA": wdA, "wgB": wgB, "wuB": wuB, "wdB": wdB, "wgC": wgC, "wuC": wuC, "wdC": wdC,
        "wo": relayout_fm(inputs["hyb_w_out"][0]), "wc": relayout_fm(np.concatenate(cols, axis=1)), "wco": relayout_fm(inputs["conv_w_out"][0]),
        "g_ffn_a": fm_vec(inputs["ffn_norm"][0, 1]), "g_ffn_b": fm_vec(inputs["ffn_norm"][1, 0]), "g_mix2": fm_vec(inputs["mix_norm"][1]),
        "g_ffn_c": fm_vec(inputs["ffn_norm"][1, 1]), "g_fin": fm_vec(inputs["final_norm"]),
        "cw": np.ascontiguousarray(cwv.reshape(3, DCH, 128).transpose(2, 1, 0)),
        "Esel": c0["Esel"], "identf": c0["identf"],
        "vbias2": np.stack([c0["vbias"], c1["vbias"]]), "notown2": np.stack([c0["notown"], c1["notown"]]), "tri2": np.stack([c0["tri"], c1["tri"]]),
    })
    return sh


def run_fused(inputs, batches):
    inputs = {k: np.asarray(v) for k, v in inputs.items()}
    nc = build_fused()
    sh = fused_shared(inputs)
    pos = np.asarray(inputs["positions"], np.int32)
    maps = []
    for b in batches:
        m = dict(sh)
        m["xT2"] = np.stack([l1_inputs(inputs, 2 * b), l1_inputs(inputs, 2 * b + 1)])
        m["pos2"] = np.ascontiguousarray(np.broadcast_to(pos[b].reshape(2, 1, T_CORE), (2, 32, T_CORE)))
        maps.append(m)
    res = run_bass_kernel_spmd(nc, maps, core_ids=list(range(len(batches))))
    outs = {}
    for i, b in enumerate(batches):
        oT = np.asarray(res.results[i]["outT2"], np.float32)
        outs[b] = np.concatenate([oT[hf].transpose(1, 0, 2).reshape(D_MODEL, T_CORE).T for hf in range(2)], axis=0)
    return outs


def kernel(**inputs):
    outs = run_fused(inputs, list(range(BATCH)))
    return np.ascontiguousarray(np.stack([outs[b] for b in range(BATCH)]).astype(np.float32))
```

```python
import numpy as np
import ml_dtypes
import concourse.bass as bass
import concourse.mybir as mybir
from concourse.bass_utils import run_bass_kernel_spmd

F32 = mybir.dt.float32
BF16 = mybir.dt.bfloat16
I32 = mybir.dt.int32
ALU = mybir.AluOpType
AF = mybir.ActivationFunctionType

N_CORES = 8
D_MODEL = 1024
SEQ = 4096
BATCH = 4
T_CORE = 2048
DCH = D_MODEL // 128
D_FF = 2816
FCH = D_FF // 128
RMS_EPS = 1e-6


from contextlib import ExitStack

AX = mybir.AxisListType.X
GROUPS = [(0, 6), (6, 12), (12, 17), (17, 22)]
NT = T_CORE // 512
GELU_C = 0.044715
GELU_S = 1.5957691216057308


def bf16_np(a):
    return np.asarray(a, dtype=np.float32).astype(ml_dtypes.bfloat16)


class Trk:
    def __init__(self, nc, es, dma_streams, sfx=""):
        self.nc = nc
        self.eng = {"pe": nc.tensor, "act": nc.scalar, "dve": nc.vector, "pool": nc.gpsimd, "sp": nc.sync}
        names = list(self.eng) + list(dma_streams)
        self.dma_streams = set(dma_streams)
        self.sem = {n: es.enter_context(nc.semaphore("s_" + n + sfx)) for n in names}
        self.cnt = {k: 0 for k in names}
        self.known = {e: {} for e in self.eng}
        self.lastw = {}
        self.readers = {}

    def _wait(self, e, semname, val):
        if val <= 0:
            return
        if semname in self.dma_streams:
            val = self.cnt[semname]
        if self.known[e].get(semname, 0) >= val:
            return
        if e == "pe" and semname == "pe":
            return
        self.eng[e].wait_ge(self.sem[semname], val)
        self.known[e][semname] = val

    def deps(self, e, reads=(), writes=()):
        for k in reads:
            if k in self.lastw:
                self._wait(e, *self.lastw[k])
        for k in writes:
            if k in self.lastw:
                self._wait(e, *self.lastw[k])
            for r in self.readers.get(k, ()):
                self._wait(e, *r)

    def done(self, ins, semname, inc, reads=(), writes=()):
        self.cnt[semname] += inc
        ins.then_inc(self.sem[semname], inc)
        tag = (semname, self.cnt[semname])
        for k in reads:
            self.readers.setdefault(k, []).append(tag)
        for k in writes:
            self.lastw[k] = tag
            self.readers[k] = []
        return tag

    @staticmethod
    def _excl(reads, writes):
        r = [k for k in reads if not k.startswith("ps")]
        w = list(writes) + [k for k in reads if k.startswith("ps") and k not in writes]
        return r, w

    def op(self, e, fn, reads=(), writes=()):
        reads, writes = self._excl(reads, writes)
        self.deps(e, reads, writes)
        return self.done(fn(), e, 1, reads, writes)

    def barrier(self):
        for e in self.eng:
            for sname, c in self.cnt.items():
                if c and not (e == sname):
                    if self.known[e].get(sname, 0) < c:
                        self.eng[e].wait_ge(self.sem[sname], c)
                        self.known[e][sname] = c

    def dma(self, e, stream, fn, reads=(), writes=()):
        self.deps(e, reads, writes)
        return self.done(fn(), stream, 16, reads, writes)

    def finish(self, e, streams):
        for s in streams:
            if self.cnt[s]:
                self.eng[e].wait_ge(self.sem[s], self.cnt[s])


def tsl(tt):
    return slice(tt * 512, (tt + 1) * 512)


def emit_rmsnorm(tk, nc, xT, g, gkey, hn, rstd, ps, ones):
    for c in range(DCH):
        tk.op("act", lambda c=c: nc.scalar.activation(out=hn[:, c, :], in_=xT[:, c, :], func=AF.Square),
              reads=[f"xT{c}"], writes=[f"hn{c}"])
    for tt in range(NT):
        for c in range(DCH):
            tk.op("pe", lambda c=c, tt=tt: nc.tensor.matmul(ps[:, tt, :], lhsT=ones[:, :], rhs=hn[:, c, tsl(tt)],
                                                             start=(c == 0), stop=(c == DCH - 1)),
                  reads=[f"hn{c}", "ones"], writes=[f"ps{tt}"])
    for tt in range(NT):
        sl = tsl(tt)
        tk.op("dve", lambda tt=tt, sl=sl: nc.vector.tensor_scalar(out=rstd[:, sl], in0=ps[:, tt, :], scalar1=1.0 / D_MODEL, scalar2=RMS_EPS,
                                                                    op0=ALU.mult, op1=ALU.add),
              reads=[f"ps{tt}"], writes=[f"rstd{tt}"])
        tk.op("act", lambda sl=sl: nc.scalar.sqrt(out=rstd[:, sl], in_=rstd[:, sl]), reads=[f"rstd{tt}"], writes=[f"rstd{tt}"])
        tk.op("dve", lambda sl=sl: nc.vector.reciprocal(out=rstd[:, sl], in_=rstd[:, sl]), reads=[f"rstd{tt}"], writes=[f"rstd{tt}"])
    for c in range(DCH):
        for tt in range(NT):
            sl = tsl(tt)
            tk.op("dve", lambda c=c, sl=sl: nc.vector.scalar_tensor_tensor(out=hn[:, c, sl], in0=xT[:, c, sl], scalar=g[:, c:c + 1], in1=rstd[:, sl],
                                                                             op0=ALU.mult, op1=ALU.mult),
                  reads=[f"xT{c}", f"rstd{tt}", gkey], writes=[f"hn{c}"])


def emit_ffn(tk, nc, st, xT, hn, H, sg, ps, wbuf, wdbuf, wg_d, wu_d, wd_d):
    NW = len(wbuf)
    for (f0, f1) in GROUPS:
        nf = f1 - f0
        for f in range(f0, f1):
            for which, wsrc in ((0, wg_d), (1, wu_d)):
                wi = st["w"] % NW
                st["w"] += 1
                wt = wbuf[wi]
                tk.dma("pool", f"dw{wi}", lambda wt=wt, wsrc=wsrc, f=f: nc.gpsimd.dma_start(out=wt[:, :, :], in_=wsrc[f]),
                       writes=[f"w{wi}"])
                for c in range(DCH):
                    for tt in range(NT):
                        b = which * 4 + tt
                        tk.op("pe", lambda wt=wt, c=c, tt=tt, b=b: nc.tensor.matmul(
                            ps[:, b, :], lhsT=wt[:, c, :], rhs=hn[:, c, tsl(tt)], start=(c == 0), stop=(c == DCH - 1)),
                            reads=[f"w{wi}", f"hn{c}"], writes=[f"ps{b}"])
            si = st["sg"] % 2
            st["sg"] += 1
            for tt in range(NT):
                tk.op("act", lambda tt=tt, si=si: nc.scalar.activation(out=sg[:, si, tsl(tt)], in_=ps[:, tt, :], func=AF.Silu),
                      reads=[f"ps{tt}"], writes=[f"sg{si}_{tt}"])
            for tt in range(NT):
                tk.op("dve", lambda tt=tt, si=si, f=f: nc.vector.tensor_tensor(out=H[:, f - f0, tsl(tt)], in0=sg[:, si, tsl(tt)], in1=ps[:, 4 + tt, :], op=ALU.mult),
                      reads=[f"sg{si}_{tt}", f"ps{4 + tt}"], writes=[f"H{f - f0}"])
        for j in range(DCH):
            di = st["wd"] % len(wdbuf)
            st["wd"] += 1
            wt = wdbuf[di]
            tk.dma("pool", f"dd{di}", lambda wt=wt, j=j, f0=f0, f1=f1, nf=nf: nc.gpsimd.dma_start(out=wt[:, 0:nf, :], in_=wd_d[j, :, f0:f1, :]),
                   writes=[f"wd{di}"])
            bs = (st["dn"] % 2) * 4
            st["dn"] += 1
            for k in range(nf):
                for tt in range(NT):
                    b = bs + tt
                    tk.op("pe", lambda wt=wt, k=k, tt=tt, b=b: nc.tensor.matmul(
                        ps[:, b, :], lhsT=wt[:, k, :], rhs=H[:, k, tsl(tt)], start=(k == 0), stop=(k == nf - 1)),
                        reads=[f"wd{di}", f"H{k}"], writes=[f"ps{b}"])
            for tt in range(NT):
                b = bs + tt
                tk.op("dve", lambda j=j, tt=tt, b=b: nc.vector.scalar_tensor_tensor(out=xT[:, j, tsl(tt)], in0=ps[:, b, :], scalar=0.5, in1=xT[:, j, tsl(tt)],
                                                                                    op0=ALU.mult, op1=ALU.add),
                      reads=[f"ps{b}", f"xT{j}"], writes=[f"xT{j}"])


def emit_gelu(tk, nc, src, srckey, tmp, tmpkey, out, outkey):
    tk.op("act", lambda: nc.scalar.activation(out=tmp, in_=src, func=AF.Square), reads=[srckey], writes=[tmpkey])
    tk.op("dve", lambda: nc.vector.tensor_scalar(out=tmp, in0=tmp, scalar1=GELU_C, scalar2=1.0, op0=ALU.mult, op1=ALU.add),
          reads=[tmpkey], writes=[tmpkey])
    tk.op("dve", lambda: nc.vector.tensor_tensor(out=tmp, in0=tmp, in1=src, op=ALU.mult), reads=[tmpkey, srckey], writes=[tmpkey])
    tk.op("act", lambda: nc.scalar.activation(out=tmp, in_=tmp, func=AF.Sigmoid, scale=GELU_S), reads=[tmpkey], writes=[tmpkey])
    tk.op("dve", lambda: nc.vector.tensor_tensor(out=out, in0=tmp, in1=src, op=ALU.mult), reads=[tmpkey, srckey], writes=[outkey])


FFN_STREAMS = [f"dw{i}" for i in range(4)] + [f"dd{i}" for i in range(3)]


def ffn_dram(nc, sfx, ov=None):
    ov = ov or {}
    mk = lambda name, shape: ov[name] if name in ov else nc.dram_tensor(name, shape, F32, kind="ExternalInput").ap()
    wg = mk("wg" + sfx, [FCH, 128, DCH, 128])
    wu = mk("wu" + sfx, [FCH, 128, DCH, 128])
    wd = mk("wd" + sfx, [DCH, 128, FCH, 128])
    return wg, wu, wd


def io_makers(nc, ov):
    di = lambda name, shape, dt: ov[name] if name in ov else nc.dram_tensor(name, shape, dt, kind="ExternalInput").ap()
    do = lambda name, shape, dt: ov[name] if name in ov else nc.dram_tensor(name, shape, dt, kind="ExternalOutput").ap()
    return di, do


def build_l1(nc=None, ov=None, sfx="", sem_es=None):
    nc = nc or bass.Bass("TRN2", target_bir_lowering=False)
    ov = ov or {}
    di, do = io_makers(nc, ov)
    x_d = di("xT", [128, DCH, T_CORE], F32)
    gf_d = di("g_ffn", [128, DCH], F32)
    gm_d = di("g_mix", [128, DCH], F32)
    wg_d, wu_d, wd_d = ffn_dram(nc, "0", ov)
    wfm_d = di("win_fm", [12, 128, DCH, 128], F32)
    wtm_d = di("win_tm", [2, 128, DCH, 512], F32)
    pos_d = di("pos", [32, T_CORE], I32)
    rc_d = di("ropec", [32, 4], F32)
    pt_d = di("PT", [128, 128], F32)
    vn_d = di("vnorm", [128, 512], F32)
    ws_d = di("wsT", [128, 4, 128], F32)
    tr_d = di("trilT", [128, 4, 128], F32)
    bs_d = di("bsB", [128, 4, 128], F32)
    xo_d = do("xT_o", [128, DCH, T_CORE], F32)
    ao_d = do("aT_o", [128, 4, T_CORE], BF16)
    qk_d = do("QKT_o", [128, 8, T_CORE], BF16)
    vo_d = do("V_o", [16, 128, 512], BF16)
    with ExitStack() as es:
        sb = lambda name, shape, dt: es.enter_context(nc.sbuf_tensor(name + sfx, shape, dt))
        xT = sb("xT_s", [128, DCH, T_CORE], F32)
        hn = sb("hn", [128, DCH, T_CORE], BF16)
        H = sb("H", [128, 8, T_CORE], BF16)
        sg = sb("sg", [128, 2, T_CORE], F32)
        rstd = sb("rstd", [128, T_CORE], F32)
        gf = sb("gf", [128, DCH], F32)
        gm = sb("gm", [128, DCH], F32)
        ones = sb("ones", [128, 128], BF16)
        wbuf = [sb(f"wb{i}", [128, DCH, 128], BF16) for i in range(4)]
        wdbuf = [sb(f"wdb{i}", [128, 6, 128], BF16) for i in range(3)]
        posi = sb("posi", [32, T_CORE], I32)
        rc = sb("rc", [32, 4], F32)
        PT = sb("PTs", [128, 128], BF16)
        vnorm = sb("vnorm_s", [128, 512], F32)
        wsT = sb("wsT_s", [128, 4, 128], F32)
        trT = sb("trT_s", [128, 4, 128], F32)
        WmT = sb("WmT", [128, 4, 128], BF16)
        bsB = sb("bsB_s", [128, 4, 128], F32)
        tmpA = sb("tmpA", [128, 2, 512], F32)
        tmpB = sb("tmpB", [128, 2, 512], F32)
        vg = sb("vg", [128, 2, 512], F32)
        vnb = sb("vnb", [128, 2, 512], BF16)
        vt = sb("vt", [128, 2, 512], BF16)
        ao = sb("ao", [128, 2, 4, 128], BF16)
        tmpm = sb("tmpm", [128, 2, 4, 128], F32)
        ss = sb("ss", [128, 4], F32)
        ps = es.enter_context(nc.psum_tensor("ps" + sfx, [128, 8, 512], F32))
        uT = H[:, 4:8, :]
        wtm = H[:, 0:4, :].rearrange("p a (b t) -> p (a b) t", t=512)
        tk = Trk(nc, sem_es or es, ["dx", "dout", "dwt"] + FFN_STREAMS, sfx)
        for c in range(DCH):
            tk.dma("sp", "dx", lambda c=c: nc.sync.dma_start(out=xT[:, c, :], in_=x_d[:, c, :]), writes=[f"xT{c}"])
        for (t_s, t_d, key) in ((gf, gf_d, "gf"), (gm, gm_d, "gm"), (posi, pos_d, "posi"), (rc, rc_d, "rc"), (vnorm, vn_d, "vnorm"),
                                (wsT, ws_d, "wsT"), (trT, tr_d, "trT"), (bsB, bs_d, "bsB")):
            tk.dma("sp", "dx", lambda t_s=t_s, t_d=t_d: nc.sync.dma_start(out=t_s[:], in_=t_d), writes=[key])
        tk.dma("pool", "dx", lambda: nc.gpsimd.dma_start(out=PT[:, :], in_=pt_d[:, :]), writes=["PT"])
        tk.op("dve", lambda: nc.vector.memset(ones[:, :], 1.0), writes=["ones"])
        tk.op("dve", lambda: nc.vector.tensor_tensor(out=WmT[:, :, :], in0=wsT[:, :, :], in1=trT[:, :, :], op=ALU.mult),
              reads=["wsT", "trT"], writes=["WmT"])
        st = {"w": 0, "sg": 0, "wd": 0, "dn": 0}
        emit_rmsnorm(tk, nc, xT, gf, "gf", hn, rstd, ps, ones)
        emit_ffn(tk, nc, st, xT, hn, H, sg, ps, wbuf, wdbuf, wg_d, wu_d, wd_d)
        emit_rmsnorm(tk, nc, xT, gm, "gm", hn, rstd, ps, ones)
        for c in range(DCH):
            tk.dma("sp", "dout", lambda c=c: nc.sync.dma_start(out=xo_d[:, c, :], in_=xT[:, c, :]), reads=[f"xT{c}"])
        sgkeys = [f"sg{si}_{tt}" for si in range(2) for tt in range(NT)]
        cosT = sg[0:32, 0, :]
        sinT = sg[0:32, 1, :]
        ang = rstd[0:32, :]
        rkeys = [f"rstd{tt}" for tt in range(NT)]
        tk.op("dve", lambda: nc.vector.tensor_copy(out=ang, in_=posi[:, :]), reads=["posi"] + rkeys, writes=rkeys)
        tk.op("dve", lambda: nc.vector.tensor_scalar(out=ang, in0=ang, scalar1=rc[:, 0:1], scalar2=None, op0=ALU.mult), reads=rkeys + ["rc"], writes=rkeys)
        for (tab, off) in ((sinT, 0.0), (cosT, 0.25)):
            tk.op("dve", lambda tab=tab, off=off: nc.vector.tensor_scalar(out=tab, in0=ang, scalar1=float(1.0 / (2 * np.pi)), scalar2=float(off),
                                                                          op0=ALU.mult, op1=ALU.add), reads=rkeys, writes=sgkeys)
            tk.op("dve", lambda tab=tab: nc.vector.tensor_copy(out=posi[:, :], in_=tab), reads=sgkeys, writes=["posi"])
            tk.op("dve", lambda: nc.vector.tensor_copy(out=tmpA[0:32, :, :].rearrange("p a b -> p (a b)"), in_=posi[:, 0:1024]), reads=["posi"], writes=["tmpA0", "tmpA1"])
            tk.op("dve", lambda: nc.vector.tensor_copy(out=tmpB[0:32, :, :].rearrange("p a b -> p (a b)"), in_=posi[:, 1024:2048]), reads=["posi"], writes=["tmpB0", "tmpB1"])
            tk.op("dve", lambda tab=tab: nc.vector.tensor_tensor(out=tab[:, 0:1024], in0=tab[:, 0:1024], in1=tmpA[0:32, :, :].rearrange("p a b -> p (a b)"), op=ALU.subtract),
                  reads=sgkeys + ["tmpA0", "tmpA1"], writes=sgkeys)
            tk.op("dve", lambda tab=tab: nc.vector.tensor_tensor(out=tab[:, 1024:2048], in0=tab[:, 1024:2048], in1=tmpB[0:32, :, :].rearrange("p a b -> p (a b)"), op=ALU.subtract),
                  reads=sgkeys + ["tmpB0", "tmpB1"], writes=sgkeys)
            tk.op("dve", lambda tab=tab: nc.vector.tensor_scalar(out=tmpA[0:32, :, :].rearrange("p a b -> p (a b)"), in0=tab[:, 0:1024], scalar1=0.5, scalar2=None, op0=ALU.is_gt),
                  reads=sgkeys, writes=["tmpA0", "tmpA1"])
            tk.op("dve", lambda tab=tab: nc.vector.tensor_scalar(out=tmpB[0:32, :, :].rearrange("p a b -> p (a b)"), in0=tab[:, 1024:2048], scalar1=0.5, scalar2=None, op0=ALU.is_gt),
                  reads=sgkeys, writes=["tmpB0", "tmpB1"])
            tk.op("dve", lambda tab=tab: nc.vector.tensor_tensor(out=tab[:, 0:1024], in0=tab[:, 0:1024], in1=tmpA[0:32, :, :].rearrange("p a b -> p (a b)"), op=ALU.subtract),
                  reads=sgkeys + ["tmpA0", "tmpA1"], writes=sgkeys)
            tk.op("dve", lambda tab=tab: nc.vector.tensor_tensor(out=tab[:, 1024:2048], in0=tab[:, 1024:2048], in1=tmpB[0:32, :, :].rearrange("p a b -> p (a b)"), op=ALU.subtract),
                  reads=sgkeys + ["tmpB0", "tmpB1"], writes=sgkeys)
            tk.op("act", lambda tab=tab: nc.scalar.activation(out=tab, in_=tab, func=AF.Sin, scale=float(2 * np.pi)), reads=sgkeys, writes=sgkeys)
        tk.op("dve", lambda: nc.vector.tensor_scalar(out=sinT, in0=sinT, scalar1=rc[:, 1:2], scalar2=None, op0=ALU.mult), reads=sgkeys + ["rc"], writes=sgkeys)
        t4 = [(tmpA[:, 0, :], "tmpA0"), (tmpA[:, 1, :], "tmpA1"), (tmpB[:, 0, :], "tmpB0"), (tmpB[:, 1, :], "tmpB1")]
        pending = None
        for j in range(12):
            wi = st["w"] % 4
            st["w"] += 1
            wt = wbuf[wi]
            tk.dma("pool", f"dw{wi}", lambda wt=wt, j=j: nc.gpsimd.dma_start(out=wt[:, :, :], in_=wfm_d[j]), writes=[f"w{wi}"])
            bs = (j % 2) * 4
            for c in range(DCH):
                for tt in range(NT):
                    tk.op("pe", lambda wt=wt, c=c, tt=tt, bs=bs: nc.tensor.matmul(
                        ps[:, bs + tt, :], lhsT=wt[:, c, :], rhs=hn[:, c, tsl(tt)], start=(c == 0), stop=(c == DCH - 1)),
                        reads=[f"w{wi}", f"hn{c}"], writes=[f"ps{bs + tt}"])
            if pending is not None:
                pending()
                pending = None
            if j < 4:
                for tt in range(NT):
                    b = bs + tt
                    emit_gelu(tk, nc, ps[:, b, :], f"ps{b}", t4[tt][0], t4[tt][1], uT[:, j, tsl(tt)], f"H{4 + j}")
            else:
                idx = (j - 4) % 4
                for tt in range(NT):
                    b = bs + tt
                    tk.op("act", lambda idx=idx, tt=tt, b=b: nc.scalar.copy(out=H[:, idx, tsl(tt)], in_=ps[:, b, :]),
                          reads=[f"ps{b}"], writes=[f"H{idx}"])
                    tk.op("dve", lambda tt=tt, b=b: nc.vector.tensor_tensor(out=t4[tt][0][0:32, :], in0=ps[0:32, b, :], in1=cosT[:, tsl(tt)], op=ALU.mult),
                          reads=[f"ps{b}"] + sgkeys, writes=[t4[tt][1]])

                def part2(j=j, idx=idx, bs=bs):
                    for tt in range(NT):
                        b = bs + tt
                        tk.op("pe", lambda: nc.tensor.matmul(ps[:, b, :], lhsT=PT[:, :], rhs=H[:, idx, tsl(tt)], start=True, stop=True),
                              reads=["PT", f"H{idx}"], writes=[f"ps{b}"])
                        tk.op("dve", lambda: nc.vector.tensor_tensor(out=vg[0:32, tt % 2, :], in0=ps[0:32, b, :], in1=sinT[:, tsl(tt)], op=ALU.mult),
                              reads=[f"ps{b}"] + sgkeys, writes=[f"vg{tt % 2}"])
                        tk.op("dve", lambda: nc.vector.tensor_tensor(out=H[0:32, idx, tsl(tt)], in0=t4[tt][0][0:32, :], in1=vg[0:32, tt % 2, :], op=ALU.add),
                              reads=[t4[tt][1], f"vg{tt % 2}"], writes=[f"H{idx}"])
                    tk.dma("sp", "dout", lambda: nc.sync.dma_start(out=qk_d[:, j - 4, :], in_=H[:, idx, :]), reads=[f"H{idx}"])
                pending = part2
        if pending is not None:
            pending()
        for w in range(2):
            tk.dma("pool", "dwt", lambda w=w: nc.gpsimd.dma_start(out=wtm[:, w * 8:(w + 1) * 8, :], in_=wtm_d[w]),
                   writes=["H0", "H1", "H2", "H3"])
        wkeys = ["H0", "H1", "H2", "H3"]
        for i in range(16):
            par = i % 2
            bA, bB, bC = par * 3, par * 3 + 1, par * 3 + 2
            tok = slice(i * 128, (i + 1) * 128)
            for w, b in ((0, bA), (1, bB)):
                for c in range(DCH):
                    tk.op("pe", lambda w=w, b=b, c=c, tok=tok: nc.tensor.matmul(ps[:, b, :], lhsT=hn[:, c, tok], rhs=wtm[:, w * 8 + c, :],
                                                                                  start=(c == 0), stop=(c == DCH - 1)),
                          reads=[f"hn{c}"] + wkeys, writes=[f"ps{b}"])
            tk.op("act", lambda par=par, bB=bB: nc.scalar.copy(out=vt[:, par, :], in_=ps[:, bB, :]), reads=[f"ps{bB}"], writes=[f"vt{par}"])
            tk.dma("sp", "dout", lambda i=i, par=par: nc.sync.dma_start(out=vo_d[i], in_=vt[:, par, :]), reads=[f"vt{par}"])
            emit_gelu(tk, nc, ps[:, bA, :], f"ps{bA}", tmpA[:, par, :], f"tmpA{par}", vg[:, par, :], f"vg{par}")
            tk.op("act", lambda par=par: nc.scalar.activation(out=tmpB[:, par, :], in_=vg[:, par, :], func=AF.Square), reads=[f"vg{par}"], writes=[f"tmpB{par}"])
            tk.op("dve", lambda par=par: nc.vector.reduce_sum(out=ss[:, par:par + 1], in_=tmpB[:, par, :], axis=AX), reads=[f"tmpB{par}"], writes=[f"ss{par}"])
            tk.op("dve", lambda par=par: nc.vector.tensor_scalar(out=ss[:, par:par + 1], in0=ss[:, par:par + 1], scalar1=1.0 / 512.0, scalar2=RMS_EPS,
                                                                   op0=ALU.mult, op1=ALU.add), reads=[f"ss{par}"], writes=[f"ss{par}"])
            tk.op("act", lambda par=par: nc.scalar.sqrt(out=ss[:, par:par + 1], in_=ss[:, par:par + 1]), reads=[f"ss{par}"], writes=[f"ss{par}"])
            tk.op("dve", lambda par=par: nc.vector.reciprocal(out=ss[:, par:par + 1], in_=ss[:, par:par + 1]), reads=[f"ss{par}"], writes=[f"ss{par}"])
            tk.op("dve", lambda par=par: nc.vector.scalar_tensor_tensor(out=vnb[:, par, :], in0=vg[:, par, :], scalar=ss[:, par:par + 1], in1=vnorm[:, :],
                                                                         op0=ALU.mult, op1=ALU.mult),
                  reads=[f"vg{par}", f"ss{par}", "vnorm"], writes=[f"vnb{par}"])
            for g in range(4):
                tk.op("pe", lambda par=par, g=g, bC=bC: nc.tensor.matmul(ps[:, bC, g * 128:(g + 1) * 128], lhsT=vnb[:, par, g * 128:(g + 1) * 128], rhs=WmT[:, g, :],
                                                                          start=True, stop=True),
                      reads=[f"vnb{par}", "WmT"], writes=[f"ps{bC}"])
            tk.op("dve", lambda par=par, bC=bC: nc.vector.tensor_tensor(out=tmpm[:, par, :, :], in0=ps[:, bC, :].rearrange("p (g t) -> p g t", g=4), in1=bsB[:, :, :], op=ALU.add),
                  reads=[f"ps{bC}", "bsB"], writes=[f"tmpm{par}"])
            tk.op("dve", lambda par=par, tok=tok: nc.vector.tensor_tensor(out=ao[:, par, :, :], in0=tmpm[:, par, :, :], in1=uT[:, :, tok], op=ALU.mult),
                  reads=[f"tmpm{par}", "H4", "H5", "H6", "H7"], writes=[f"ao{par}"])
            tk.dma("sp", "dout", lambda par=par, tok=tok: nc.sync.dma_start(out=ao_d[:, :, tok], in_=ao[:, par, :, :]), reads=[f"ao{par}"])
        tk.finish("sp", ["dout"])
        tk.barrier()
    return nc


def relayout_ffn(wg, wu, wd):
    a = np.ascontiguousarray(np.asarray(wg, np.float32).reshape(DCH, 128, FCH, 128).transpose(2, 1, 0, 3))
    b = np.ascontiguousarray(np.asarray(wu, np.float32).reshape(DCH, 128, FCH, 128).transpose(2, 1, 0, 3))
    c = np.ascontiguousarray(np.asarray(wd, np.float32).reshape(FCH, 128, DCH, 128).transpose(2, 1, 0, 3))
    return a, b, c


def fm_vec(v):
    return np.ascontiguousarray(np.asarray(v, np.float32).reshape(DCH, 128).T)


def rope_consts():
    half = 16
    inv = (np.float32(500000.0) ** (-np.arange(half, dtype=np.float32) / np.float32(half))).astype(np.float32)
    rc = np.zeros((32, 4), np.float32)
    rc[:, 0] = np.concatenate([inv, inv])
    rc[:16, 1] = -1.0
    rc[16:, 1] = 1.0
    rc[:, 2] = -np.pi
    PT = np.zeros((128, 128), np.float32)
    for i in range(16):
        PT[i + 16, i] = 1.0
        PT[i, i + 16] = 1.0
    return rc, PT


def l1_inputs(inputs, core):
    b, half = core // 2, core % 2
    tok = slice(half * T_CORE, (half + 1) * T_CORE)
    x = np.asarray(inputs["x"], np.float32)[b, tok]
    xT = np.ascontiguousarray(x.T.reshape(DCH, 128, T_CORE).transpose(1, 0, 2))
    return xT


def l1_shared(inputs):
    wg, wu, wd = relayout_ffn(inputs["ffn_w_gate"][0, 0], inputs["ffn_w_up"][0, 0], inputs["ffn_w_down"][0, 0])
    w_in = np.asarray(inputs["hyb_w_in"], np.float32)[0]
    cols_fm = np.concatenate([w_in[:, 0:512], w_in[:, 1024:1536], w_in[:, 1536:2048]], axis=1)
    wfm = np.ascontiguousarray(cols_fm.reshape(DCH, 128, 12, 128).transpose(2, 1, 0, 3))
    cols_tm = np.stack([w_in[:, 512:1024], w_in[:, 2048:2560]], axis=0)
    wtm = np.ascontiguousarray(cols_tm.reshape(2, DCH, 128, 512).transpose(0, 2, 1, 3))
    rc, PT = rope_consts()
    ws = np.asarray(inputs["gmlp_w_s"], np.float32)[0]
    wsT = np.ascontiguousarray(ws.transpose(2, 0, 1))
    tril = np.tril(np.ones((128, 128), np.float32))
    trilT = np.ascontiguousarray(np.broadcast_to(tril.T[:, None, :], (128, 4, 128)))
    bsB = np.ascontiguousarray(np.broadcast_to(np.asarray(inputs["gmlp_b_s"], np.float32)[0][None], (128, 4, 128)))
    vnorm = np.ascontiguousarray(np.broadcast_to(np.asarray(inputs["gmlp_v_norm"], np.float32)[0][None], (128, 512)))
    return {
        "g_ffn": fm_vec(inputs["ffn_norm"][0, 0]), "g_mix": fm_vec(inputs["mix_norm"][0]),
        "wg0": wg, "wu0": wu, "wd0": wd, "win_fm": wfm, "win_tm": wtm,
        "ropec": rc, "PT": PT, "vnorm": vnorm, "wsT": wsT, "trilT": trilT, "bsB": bsB,
    }


def run_l1(inputs, cores=None):
    cores = list(range(N_CORES)) if cores is None else cores
    nc = build_l1()
    shared = l1_shared(inputs)
    pos = np.asarray(inputs["positions"], np.int32)
    maps = []
    for c in cores:
        b, half = c // 2, c % 2
        m = dict(shared)
        m["xT"] = l1_inputs(inputs, c)
        m["pos"] = np.ascontiguousarray(np.broadcast_to(pos[b, half * T_CORE:(half + 1) * T_CORE][None], (32, T_CORE)))
        maps.append(m)
    res = run_bass_kernel_spmd(nc, maps, core_ids=list(range(len(cores))))
    return res.results


ATT_SCALE = 128.0 ** -0.5
NEG = -1e30


def emit_proj_fm(tk, nc, st, ps, wbuf, w_chunks, in_chunks, epilogue):
    KC = len(in_chunks)
    for j, wsrc in enumerate(w_chunks):
        wi = st["w"] % len(wbuf)
        st["w"] += 1
        wt = wbuf[wi]
        tk.dma("pool", f"dw{wi}", lambda wt=wt, wsrc=wsrc: nc.gpsimd.dma_start(out=wt[:, 0:KC, :], in_=wsrc), writes=[f"w{wi}"])
        bs = (st["pj"] % 2) * 4
        st["pj"] += 1
        for c, (iap, ikey) in enumerate(in_chunks):
            for tt in range(NT):
                tk.op("pe", lambda wt=wt, c=c, tt=tt, bs=bs, iap=iap: nc.tensor.matmul(
                    ps[:, bs + tt, :], lhsT=wt[:, c, :], rhs=iap[:, tsl(tt)], start=(c == 0), stop=(c == KC - 1)),
                    reads=[f"w{wi}", ikey], writes=[f"ps{bs + tt}"])
        epilogue(j, bs)


def emit_resid_epilogue(tk, nc, ps, xT):
    def ep(j, bs):
        for tt in range(NT):
            tk.op("dve", lambda j=j, tt=tt, bs=bs: nc.vector.tensor_tensor(out=xT[:, j, tsl(tt)], in0=ps[:, bs + tt, :], in1=xT[:, j, tsl(tt)], op=ALU.add),
                  reads=[f"ps{bs + tt}", f"xT{j}"], writes=[f"xT{j}"])
    return ep


def emit_final_norm(tk, nc, xT, g, gkey, hn, rstd, ps, ones, sg, out_d):
    for c in range(DCH):
        tk.op("act", lambda c=c: nc.scalar.activation(out=hn[:, c, :], in_=xT[:, c, :], func=AF.Square), reads=[f"xT{c}"], writes=[f"hn{c}"])
    for tt in range(NT):
        for c in range(DCH):
            tk.op("pe", lambda c=c, tt=tt: nc.tensor.matmul(ps[:, tt, :], lhsT=ones[:, :], rhs=hn[:, c, tsl(tt)], start=(c == 0), stop=(c == DCH - 1)),
                  reads=[f"hn{c}", "ones"], writes=[f"ps{tt}"])
    for tt in range(NT):
        sl = tsl(tt)
        tk.op("dve", lambda tt=tt, sl=sl: nc.vector.tensor_scalar(out=rstd[:, sl], in0=ps[:, tt, :], scalar1=1.0 / D_MODEL, scalar2=RMS_EPS, op0=ALU.mult, op1=ALU.add),
              reads=[f"ps{tt}"], writes=[f"rstd{tt}"])
        tk.op("act", lambda sl=sl: nc.scalar.sqrt(out=rstd[:, sl], in_=rstd[:, sl]), reads=[f"rstd{tt}"], writes=[f"rstd{tt}"])
        tk.op("dve", lambda sl=sl: nc.vector.reciprocal(out=rstd[:, sl], in_=rstd[:, sl]), reads=[f"rstd{tt}"], writes=[f"rstd{tt}"])
    for c in range(DCH):
        si = c % 2
        for tt in range(NT):
            sl = tsl(tt)
            tk.op("dve", lambda c=c, sl=sl, si=si: nc.vector.scalar_tensor_tensor(out=sg[:, si, sl], in0=xT[:, c, sl], scalar=g[:, c:c + 1], in1=rstd[:, sl],
                                                                                   op0=ALU.mult, op1=ALU.mult),
                  reads=[f"xT{c}", f"rstd{tt}", gkey], writes=[f"sg{si}_{tt}"])
        tk.dma("sp", "dout", lambda c=c, si=si: nc.sync.dma_start(out=out_d[:, c, :], in_=sg[:, si, :]), reads=[f"sg{si}_{tt}" for tt in range(NT)])


def alloc_ffn_set(nc, es, hchunks=6, sfx=""):
    sb = lambda name, shape, dt: es.enter_context(nc.sbuf_tensor(name + sfx, shape, dt))
    d = {}
    d["xT"] = sb("xT_s", [128, DCH, T_CORE], F32)
    d["hn"] = sb("hn", [128, DCH, T_CORE], BF16)
    d["H"] = sb("H", [128, hchunks, T_CORE], BF16)
    d["sg"] = sb("sg", [128, 2, T_CORE], F32)
    d["rstd"] = sb("rstd", [128, T_CORE], F32)
    d["ones"] = sb("ones", [128, 128], BF16)
    d["wbuf"] = [sb(f"wb{i}", [128, DCH, 128], BF16) for i in range(4)]
    d["wdbuf"] = [sb(f"wdb{i}", [128, 6, 128], BF16) for i in range(3)]
    return d


def build_l2(nc=None, ov=None, sfx="", nblocks=lambda j: 9 + j, sem_es=None, conv_here=False):
    nc = nc or bass.Bass("TRN2", target_bir_lowering=False)
    ov = ov or {}
    di, do = io_makers(nc, ov)
    x_d = di("xT", [128, DCH, T_CORE], F32)
    a_d = di("aT", [128, 4, T_CORE], BF16)
    q_d = di("QT", [128, 4, T_CORE], BF16)
    kparts = ov.get("KT_parts")
    vparts = ov.get("V_parts")
    k_d = None if kparts else di("KTp", [128, 4, 2 * T_CORE], BF16)
    v_d = None if vparts else di("Vp", [128, 32, 512], BF16)
    vb_d = di("vbias", [128, 8, 16], F32)
    no_d = di("notown", [128, 8, 16], F32)
    e_d = di("Esel", [32, 16, 128], BF16)
    tri_d = di("tri", [128, 4, 256], BF16)
    idf_d = di("identf", [128, 128], F32)
    wo_d = di("wo", [8, 128, DCH, 128], F32)
    gA_d = di("g_ffn_a", [128, DCH], F32)
    gB_d = di("g_ffn_b", [128, DCH], F32)
    gM_d = di("g_mix", [128, DCH], F32)
    wgA, wuA, wdA = ffn_dram(nc, "A", ov)
    wgB, wuB, wdB = ffn_dram(nc, "B", ov)
    wc_d = di("wc", [24, 128, DCH, 128], F32)
    xo_d = do("xT_o", [128, DCH, T_CORE], F32)
    if conv_here:
        cw_d = di("cw", [128, DCH, 3], F32)
        g_d = do("g_o", [128, DCH, T_CORE], BF16)
        zh_d = do("zh_o", [128, DCH, 2], F32)
        halo_in = ov.get("halo_in")
    else:
        z_d = do("zT_o", [128, DCH, T_CORE], F32)
        bg_d = do("bgT_o", [128, DCH, T_CORE], F32)
    with ExitStack() as es:
        sb = lambda name, shape, dt: es.enter_context(nc.sbuf_tensor(name + sfx, shape, dt))
        tk = Trk(nc, sem_es or es, ["dx", "dout"] + FFN_STREAMS, sfx)
        bT = sb("bT", [128, 4, T_CORE], BF16)
        ps = es.enter_context(nc.psum_tensor("ps" + sfx, [128, 8, 512], F32))
        st = {"w": 0, "sg": 0, "wd": 0, "dn": 0, "pj": 0}
        with ExitStack() as ea:
            sa = lambda name, shape, dt: ea.enter_context(nc.sbuf_tensor(name + sfx, shape, dt))
            QT = sa("QT_s", [128, 4, T_CORE], BF16)
            KT = sa("KT_s", [128, 4, 2 * T_CORE], BF16)
            V = sa("V_s", [128, 32, 512], BF16)
            vbias = sa("vbias_s", [128, 8, 16], F32)
            notown = sa("notown_s", [128, 8, 16], F32)
            E = sa("E_s", [32, 16, 128], BF16)
            tri = sa("tri_s", [128, 4, 256], BF16)
            identf = sa("identf_s", [128, 128], F32)
            identb = sa("identb", [128, 128], BF16)
            kms = sa("kms", [128, 4, 16], F32)
            kmT = sa("kmT", [128, 4, 16], BF16)
            gm = sa("gm", [128, 2, 16], F32)
            top8 = sa("top8", [128, 2, 8], F32)
            thr = sa("thr", [128, 2, 1], F32)
            G32 = sa("G32", [128, 2, 32], F32)
            biasT = sa("biasT", [32, 2, 256], BF16)
            PTs = sa("PTs", [128, 6, 256], BF16)
            rL = sa("rL", [128, 2, 256], F32)
            onesb = sa("onesb", [128, 128], BF16)
            for h in range(4):
                tk.dma("sp", "dx", lambda h=h: nc.sync.dma_start(out=QT[:, h, :], in_=q_d[:, h, :]), writes=[f"QT{h}"])
                if kparts:
                    for hf in range(2):
                        tk.dma("sp", "dx", lambda h=h, hf=hf: nc.sync.dma_start(out=KT[:, h, hf * T_CORE:(hf + 1) * T_CORE], in_=kparts[hf][:, h, :]), writes=[f"KT{h}"])
                else:
                    tk.dma("sp", "dx", lambda h=h: nc.sync.dma_start(out=KT[:, h, :], in_=k_d[:, h, :]), writes=[f"KT{h}"])
            for v4 in range(4):
                if vparts:
                    src = vparts[v4 // 2][(v4 % 2) * 8:(v4 % 2 + 1) * 8].rearrange("t p c -> p t c")
                    tk.dma("sp", "dx", lambda v4=v4, src=src: nc.sync.dma_start(out=V[:, v4 * 8:(v4 + 1) * 8, :], in_=src), writes=[f"V{v4}"])
                else:
                    tk.dma("sp", "dx", lambda v4=v4: nc.sync.dma_start(out=V[:, v4 * 8:(v4 + 1) * 8, :], in_=v_d[:, v4 * 8:(v4 + 1) * 8, :]), writes=[f"V{v4}"])
            for (t_s, t_d, key) in ((vbias, vb_d, "vbias"), (notown, no_d, "notown"), (E, e_d, "E"), (tri, tri_d, "tri"), (identf, idf_d, "identf")):
                tk.dma("sp", "dx", lambda t_s=t_s, t_d=t_d: nc.sync.dma_start(out=t_s[:], in_=t_d), writes=[key])
            tk.op("dve", lambda: nc.vector.tensor_copy(out=identb[:, :], in_=identf[:, :]), reads=["identf"], writes=["identb"])
            tk.op("dve", lambda: nc.vector.memset(onesb[:, :], 1.0), writes=["onesb"])
            tk.op("dve", lambda: nc.vector.memset(G32[:, :, :], 0.0), writes=["G32_0", "G32_1"])
            for h in range(4):
                tk.op("dve", lambda h=h: nc.vector.reduce_sum(out=kms[:, h, :], in_=KT[:, h, :].rearrange("p (n k) -> p n k", k=256), axis=AX),
                      reads=[f"KT{h}"], writes=["kms"])
            tk.op("dve", lambda: nc.vector.tensor_scalar(out=kmT[:, :, :], in0=kms[:, :, :], scalar1=1.0 / 256.0, scalar2=None, op0=ALU.mult),
                  reads=["kms"], writes=["kmT"])
            SB = [0, 1, 2, 5, 6]
            NPT = 6
            OB, LB = 3, 4
            its = [(h, j) for h in range(4) for j in range(8)]
            sctr = [0]
            pctr = [0]

            def preA(ctr):
                h, j = its[ctr]
                for qt in range(2):
                    qsl = slice(j * 256 + qt * 128, j * 256 + (qt + 1) * 128)
                    tk.op("pe", lambda: nc.tensor.matmul(ps[:, 7, qt * 16:(qt + 1) * 16], lhsT=QT[:, h, qsl], rhs=kmT[:, h, :], start=True, stop=True),
                          reads=[f"QT{h}", "kmT"], writes=["ps7"])
                for qt in range(2):
                    tk.op("dve", lambda: nc.vector.tensor_tensor(out=gm[:, qt, :], in0=ps[:, 7, qt * 16:(qt + 1) * 16], in1=vbias[:, j, :], op=ALU.add),
                          reads=["ps7", "vbias"], writes=[f"gm{qt}"])
                    tk.op("dve", lambda: nc.vector.max(out=top8[:, qt, :], in_=gm[:, qt, :]), reads=[f"gm{qt}"], writes=[f"top8{qt}"])
                    tk.op("dve", lambda: nc.vector.tensor_scalar(out=thr[:, qt, :], in0=top8[:, qt, 2:3], scalar1=-1e29, scalar2=None, op0=ALU.max),
                          reads=[f"top8{qt}"], writes=[f"thr{qt}"])
                    tk.op("dve", lambda: nc.vector.tensor_scalar(out=G32[:, qt, 0:16], in0=gm[:, qt, :], scalar1=thr[:, qt, 0:1], scalar2=None, op0=ALU.is_ge),
                          reads=[f"gm{qt}", f"thr{qt}"], writes=[f"G32_{qt}"])
                    tk.op("dve", lambda: nc.vector.tensor_scalar(out=G32[:, qt, 0:16], in0=G32[:, qt, 0:16], scalar1=-1.0, scalar2=1e30, op0=ALU.add, op1=ALU.mult),
                          reads=[f"G32_{qt}"], writes=[f"G32_{qt}"])
                    tk.op("dve", lambda: nc.vector.tensor_tensor(out=G32[:, qt, 0:16], in0=G32[:, qt, 0:16], in1=notown[:, j, :], op=ALU.mult),
                          reads=[f"G32_{qt}", "notown"], writes=[f"G32_{qt}"])

            def preB(ctr):
                bb = ctr % 2
                for qt in range(2):
                    tk.op("pe", lambda: nc.tensor.transpose(out=ps[0:32, 7, 64 + qt * 128:64 + (qt + 1) * 128], in_=G32[:, qt, :], identity=identf[:, :]),
                          reads=[f"G32_{qt}", "identf"], writes=["ps7"])
                tk.op("act", lambda: nc.scalar.copy(out=biasT[:, bb, :], in_=ps[0:32, 7, 64:320]), reads=["ps7"], writes=[f"biasT{bb}"])

            preA(0)
            preB(0)
            for ctr, (h, j) in enumerate(its):
                bb = ctr % 2
                rb = ctr % 2
                qs = slice(j * 256, (j + 1) * 256)
                nb = nblocks(j)
                tiles = [(n, kt) for n in range(nb) for kt in range(2)]
                NTI = len(tiles)
                sbase, pbase = sctr[0], pctr[0]
                sctr[0] += NTI
                pctr[0] += NTI

                def emit_S(i):
                    n, kt = tiles[i]
                    sbk = SB[(sbase + i) % len(SB)]
                    ktile = 2 * n + kt
                    extra = (n == j) or (n == 8 + j)
                    tk.op("pe", lambda: nc.tensor.matmul(ps[:, sbk, 0:256], lhsT=KT[:, h, ktile * 128:(ktile + 1) * 128], rhs=QT[:, h, qs], start=True, stop=False),
                          reads=[f"KT{h}", f"QT{h}"], writes=[f"ps{sbk}"])
                    tk.op("pe", lambda: nc.tensor.matmul(ps[:, sbk, 0:256], lhsT=E[:, n, :], rhs=biasT[:, bb, :], start=False, stop=(not extra)),
                          reads=["E", f"biasT{bb}"], writes=[f"ps{sbk}"])
                    if extra:
                        ti = (0 if n == j else 2) + kt
                        tk.op("pe", lambda: nc.tensor.matmul(ps[:, sbk, 0:256], lhsT=identb[:, :], rhs=tri[:, ti, :], start=False, stop=True),
                              reads=["identb", "tri"], writes=[f"ps{sbk}"])

                def emit_EXP(i):
                    sbk = SB[(sbase + i) % len(SB)]
                    pb = (pbase + i) % NPT
                    tk.op("act", lambda: nc.scalar.activation(out=PTs[:, pb, :], in_=ps[:, sbk, 0:256], func=AF.Exp, scale=ATT_SCALE),
                          reads=[f"ps{sbk}"], writes=[f"PTs{pb}"])

                def emit_PV(i):
                    n, kt = tiles[i]
                    pb = (pbase + i) % NPT
                    ktile = 2 * n + kt
                    tk.op("pe", lambda: nc.tensor.matmul(ps[:, OB, 0:256], lhsT=V[:, ktile, h * 128:(h + 1) * 128], rhs=PTs[:, pb, :], start=(i == 0), stop=(i == NTI - 1)),
                          reads=[f"V{ktile // 8}", f"PTs{pb}"], writes=[f"ps{OB}"])
                    tk.op("pe", lambda: nc.tensor.matmul(ps[:, LB, 0:256], lhsT=onesb[:, :], rhs=PTs[:, pb, :], start=(i == 0), stop=(i == NTI - 1)),
                          reads=["onesb", f"PTs{pb}"], writes=[f"ps{LB}"])

                DEPTH = 4
                for i in range(min(DEPTH, NTI)):
                    emit_S(i)
                if ctr + 1 < len(its):
                    preA(ctr + 1)
                for i in range(NTI):
                    if i + DEPTH < NTI:
                        emit_S(i + DEPTH)
                    emit_EXP(i)
                    emit_PV(i)
                if ctr + 1 < len(its):
                    preB(ctr + 1)
                tk.op("dve", lambda: nc.vector.reciprocal(out=rL[:, rb, :], in_=ps[:, LB, 0:256]), reads=[f"ps{LB}"], writes=[f"rL{rb}"])
                tk.op("dve", lambda: nc.vector.tensor_tensor(out=bT[:, h, qs], in0=ps[:, OB, 0:256], in1=rL[:, rb, :], op=ALU.mult),
                      reads=[f"ps{OB}", f"rL{rb}"], writes=[f"bT{h}"])
            tk.barrier()
        with ExitStack() as eb:
            f = alloc_ffn_set(nc, eb, hchunks=6, sfx=sfx)
            xT, hn, H, sg, rstd, ones, wbuf, wdbuf = (f[k] for k in ("xT", "hn", "H", "sg", "rstd", "ones", "wbuf", "wdbuf"))
            sbb = lambda name, shape, dt: eb.enter_context(nc.sbuf_tensor(name + sfx, shape, dt))
            gA = sbb("gA", [128, DCH], F32)
            gB = sbb("gB", [128, DCH], F32)
            gM = sbb("gM", [128, DCH], F32)
            for c in range(DCH):
                tk.dma("sp", "dx", lambda c=c: nc.sync.dma_start(out=xT[:, c, :], in_=x_d[:, c, :]), writes=[f"xT{c}"])
            for c in range(4):
                tk.dma("sp", "dx", lambda c=c: nc.sync.dma_start(out=H[:, c, :], in_=a_d[:, c, :]), writes=[f"H{c}"])
            for (t_s, t_d, key) in ((gA, gA_d, "gA"), (gB, gB_d, "gB"), (gM, gM_d, "gM")):
                tk.dma("sp", "dx", lambda t_s=t_s, t_d=t_d: nc.sync.dma_start(out=t_s[:], in_=t_d), writes=[key])
            tk.op("dve", lambda: nc.vector.memset(ones[:, :], 1.0), writes=["ones"])
            in_chunks = [(H[:, c, :], f"H{c}") for c in range(4)] + [(bT[:, h, :], f"bT{h}") for h in range(4)]
            emit_proj_fm(tk, nc, st, ps, wbuf, [wo_d[j] for j in range(8)], in_chunks, emit_resid_epilogue(tk, nc, ps, xT))
            emit_rmsnorm(tk, nc, xT, gA, "gA", hn, rstd, ps, ones)
            emit_ffn(tk, nc, st, xT, hn, H, sg, ps, wbuf, wdbuf, wgA, wuA, wdA)
            emit_rmsnorm(tk, nc, xT, gB, "gB", hn, rstd, ps, ones)
            emit_ffn(tk, nc, st, xT, hn, H, sg, ps, wbuf, wdbuf, wgB, wuB, wdB)
            emit_rmsnorm(tk, nc, xT, gM, "gM", hn, rstd, ps, ones)
            for c in range(DCH):
                tk.dma("sp", "dout", lambda c=c: nc.sync.dma_start(out=xo_d[:, c, :], in_=xT[:, c, :]), reads=[f"xT{c}"])
            sgk = lambda si: [f"sg{si}_{tt}" for tt in range(NT)]

            def conv_ep(jj, bs):
                if jj < 16:
                    i, si = jj // 2, (jj // 2) % 2
                    if jj % 2 == 0:
                        for tt in range(NT):
                            tk.op("act", lambda tt=tt, si=si, bs=bs: nc.scalar.copy(out=sg[:, si, tsl(tt)], in_=ps[:, bs + tt, :]),
                                  reads=[f"ps{bs + tt}"], writes=[f"sg{si}_{tt}"])
                    else:
                        for tt in range(NT):
                            tk.op("dve", lambda tt=tt, si=si, bs=bs: nc.vector.tensor_tensor(out=sg[:, si, tsl(tt)], in0=sg[:, si, tsl(tt)], in1=ps[:, bs + tt, :], op=ALU.mult),
                                  reads=[f"sg{si}_{tt}", f"ps{bs + tt}"], writes=[f"sg{si}_{tt}"])
                        tk.dma("sp", "dout", lambda i=i, si=si: nc.sync.dma_start(out=z_d[:, i, :], in_=sg[:, si, :]), reads=sgk(si))
                else:
                    i, si = jj - 16, jj % 2
                    for tt in range(NT):
                        tk.op("act", lambda tt=tt, si=si, bs=bs: nc.scalar.copy(out=sg[:, si, tsl(tt)], in_=ps[:, bs + tt, :]),
                              reads=[f"ps{bs + tt}"], writes=[f"sg{si}_{tt}"])
                    tk.dma("sp", "dout", lambda i=i, si=si: nc.sync.dma_start(out=bg_d[:, i, :], in_=sg[:, si, :]), reads=sgk(si))

            if not conv_here:
                emit_proj_fm(tk, nc, st, ps, wbuf, [wc_d[j] for j in range(24)], [(hn[:, c, :], f"hn{c}") for c in range(DCH)], conv_ep)
            else:
                cw = sbb("cw_s", [128, DCH, 3], F32)
                zc = [sbb(f"zc{i}", [128, T_CORE + 2], F32) for i in range(2)]
                gout = [sbb(f"gout{i}", [128, T_CORE], BF16) for i in range(2)]
                tk.dma("sp", "dx", lambda: nc.sync.dma_start(out=cw[:], in_=cw_d), writes=["cw"])
                rk = [f"rstd{tt}" for tt in range(NT)]

                def conv_ep2(jj, bs):
                    i, r, si = jj // 3, jj % 3, (jj // 3) % 2
                    if r == 0:
                        for tt in range(NT):
                            tk.op("act", lambda tt=tt, bs=bs: nc.scalar.copy(out=rstd[:, tsl(tt)], in_=ps[:, bs + tt, :]), reads=[f"ps{bs + tt}"], writes=[f"rstd{tt}"])
                    elif r == 1:
                        for tt in range(NT):
                            tk.op("act", lambda tt=tt, si=si, bs=bs: nc.scalar.copy(out=sg[:, si, tsl(tt)], in_=ps[:, bs + tt, :]),
                                  reads=[f"ps{bs + tt}"], writes=[f"sg{si}_{tt}"])
                    else:
                        if halo_in is None:
                            tk.op("dve", lambda si=si: nc.vector.memset(zc[si][:, 0:2], 0.0), writes=[f"zc{si}"])
                        else:
                            tk.dma("sp", "dx", lambda si=si, i=i: nc.sync.dma_start(out=zc[si][:, 0:2], in_=halo_in[:, i, :]), writes=[f"zc{si}"])
                        for tt in range(NT):
                            o = tt * 512
                            tk.op("dve", lambda tt=tt, si=si, bs=bs, o=o: nc.vector.tensor_tensor(out=zc[si][:, 2 + o:2 + o + 512], in0=sg[:, si, tsl(tt)], in1=ps[:, bs + tt, :], op=ALU.mult),
                                  reads=[f"sg{si}_{tt}", f"ps{bs + tt}"], writes=[f"zc{si}"])
                        tk.dma("sp", "dout", lambda si=si, i=i: nc.sync.dma_start(out=zh_d[:, i, :], in_=zc[si][:, T_CORE:T_CORE + 2]), reads=[f"zc{si}"])
                        for tt in range(NT):
                            o = tt * 512
                            sl = tsl(tt)
                            key = f"sg{si}_{tt}"
                            tk.op("dve", lambda si=si, i=i, sl=sl, o=o: nc.vector.tensor_scalar(out=sg[:, si, sl], in0=zc[si][:, o:o + 512], scalar1=cw[:, i, 0:1], scalar2=None, op0=ALU.mult),
                                  reads=[f"zc{si}", "cw"], writes=[key])
                            tk.op("dve", lambda si=si, i=i, sl=sl, o=o: nc.vector.scalar_tensor_tensor(out=sg[:, si, sl], in0=zc[si][:, o + 1:o + 513], scalar=cw[:, i, 1:2], in1=sg[:, si, sl],
                                                                                                     op0=ALU.mult, op1=ALU.add),
                                  reads=[f"zc{si}", "cw", key], writes=[key])
                            tk.op("dve", lambda si=si, i=i, sl=sl, o=o: nc.vector.scalar_tensor_tensor(out=sg[:, si, sl], in0=zc[si][:, o + 2:o + 514], scalar=cw[:, i, 2:3], in1=sg[:, si, sl],
                                                                                                     op0=ALU.mult, op1=ALU.add),
                                  reads=[f"zc{si}", "cw", key], writes=[key])
                            tk.op("dve", lambda si=si, sl=sl, tt=tt: nc.vector.tensor_tensor(out=gout[si][:, sl], in0=sg[:, si, sl], in1=rstd[:, sl], op=ALU.mult),
                                  reads=[key, f"rstd{tt}"], writes=[f"gout{si}"])
                        tk.dma("sp", "dout", lambda si=si, i=i: nc.sync.dma_start(out=g_d[:, i, :], in_=gout[si][:, :]), reads=[f"gout{si}"])

                emit_proj_fm(tk, nc, st, ps, wbuf, [wc_d[j] for j in range(24)], [(hn[:, c, :], f"hn{c}") for c in range(DCH)], conv_ep2)
            tk.finish("sp", ["dout"])
            tk.barrier()
    return nc


def build_l3(nc=None, ov=None, sfx="", sem_es=None):
    nc = nc or bass.Bass("TRN2", target_bir_lowering=False)
    ov = ov or {}
    di, do = io_makers(nc, ov)
    x_d = di("xT", [128, DCH, T_CORE], F32)
    g_src = ov.get("g_src")
    z_src = ov.get("z_src")
    halo_src = ov.get("halo_src")
    ze_d = None if (z_src is not None or g_src is not None) else di("zext", [128, DCH, T_CORE + 2], F32)
    bg_d = None if g_src is not None else di("bgT", [128, DCH, T_CORE], F32)
    cw_d = None if g_src is not None else di("cw", [128, DCH, 3], F32)
    wco_d = di("wco", [8, 128, DCH, 128], F32)
    gC_d = di("g_ffn_c", [128, DCH], F32)
    gF_d = di("g_fin", [128, DCH], F32)
    wgC, wuC, wdC = ffn_dram(nc, "C", ov)
    out_d = do("outT", [128, DCH, T_CORE], F32)
    with ExitStack() as es:
        sb = lambda name, shape, dt: es.enter_context(nc.sbuf_tensor(name + sfx, shape, dt))
        tk = Trk(nc, sem_es or es, ["dx", "dz", "dout"] + FFN_STREAMS, sfx)
        f = alloc_ffn_set(nc, es, hchunks=6, sfx=sfx)
        xT, hn, H, sg, rstd, ones, wbuf, wdbuf = (f[k] for k in ("xT", "hn", "H", "sg", "rstd", "ones", "wbuf", "wdbuf"))
        gC = sb("gC", [128, DCH], F32)
        gF = sb("gF", [128, DCH], F32)
        if g_src is None:
            cw = sb("cw_s", [128, DCH, 3], F32)
            zc = [sb(f"zc{i}", [128, T_CORE + 2], F32) for i in range(2)]
            bgc = [sb(f"bgc{i}", [128, T_CORE], F32) for i in range(2)]
        ps = es.enter_context(nc.psum_tensor("ps" + sfx, [128, 8, 512], F32))
        st = {"w": 0, "sg": 0, "wd": 0, "dn": 0, "pj": 0}
        for c in range(DCH):
            tk.dma("sp", "dx", lambda c=c: nc.sync.dma_start(out=xT[:, c, :], in_=x_d[:, c, :]), writes=[f"xT{c}"])
        for (t_s, t_d, key) in ((gC, gC_d, "gC"), (gF, gF_d, "gF")) + (((cw, cw_d, "cw"),) if g_src is None else ()):
            tk.dma("sp", "dx", lambda t_s=t_s, t_d=t_d: nc.sync.dma_start(out=t_s[:], in_=t_d), writes=[key])
        if g_src is not None:
            for c in range(DCH):
                tk.dma("sp", "dx", lambda c=c: nc.sync.dma_start(out=hn[:, c, :], in_=g_src[:, c, :]), writes=[f"hn{c}"])
        tk.op("dve", lambda: nc.vector.memset(ones[:, :], 1.0), writes=["ones"])
        for c in (range(DCH) if g_src is None else ()):
            ci = c % 2
            if z_src is None:
                tk.dma("sp", f"dz", lambda c=c, ci=ci: nc.sync.dma_start(out=zc[ci][:, :], in_=ze_d[:, c, :]), writes=[f"zc{ci}"])
            else:
                tk.dma("sp", f"dz", lambda c=c, ci=ci: nc.sync.dma_start(out=zc[ci][:, 2:T_CORE + 2], in_=z_src[:, c, :]), writes=[f"zc{ci}"])
                if halo_src is None:
                    tk.op("dve", lambda ci=ci: nc.vector.memset(zc[ci][:, 0:2], 0.0), writes=[f"zc{ci}"])
                else:
                    tk.dma("sp", f"dz", lambda c=c, ci=ci: nc.sync.dma_start(out=zc[ci][:, 0:2], in_=halo_src[:, c, T_CORE - 2:T_CORE]), writes=[f"zc{ci}"])
            tk.dma("sp", f"dz", lambda c=c, ci=ci: nc.sync.dma_start(out=bgc[ci][:, :], in_=bg_d[:, c, :]), writes=[f"bgc{ci}"])
            en, eo = "dve", nc.vector
            for tt in range(NT):
                sl = tsl(tt)
                o = tt * 512
                key = f"sg{ci}_{tt}"
                tk.op(en, lambda c=c, ci=ci, sl=sl, o=o, eo=eo: eo.tensor_scalar(out=sg[:, ci, sl], in0=zc[ci][:, o:o + 512], scalar1=cw[:, c, 0:1], scalar2=None, op0=ALU.mult),
                      reads=[f"zc{ci}", "cw"], writes=[key])
                tk.op(en, lambda c=c, ci=ci, sl=sl, o=o, eo=eo: eo.scalar_tensor_tensor(out=sg[:, ci, sl], in0=zc[ci][:, o + 1:o + 513], scalar=cw[:, c, 1:2], in1=sg[:, ci, sl],
                                                                                        op0=ALU.mult, op1=ALU.add),
                      reads=[f"zc{ci}", "cw", key], writes=[key])
                tk.op(en, lambda c=c, ci=ci, sl=sl, o=o, eo=eo: eo.scalar_tensor_tensor(out=sg[:, ci, sl], in0=zc[ci][:, o + 2:o + 514], scalar=cw[:, c, 2:3], in1=sg[:, ci, sl],
                                                                                        op0=ALU.mult, op1=ALU.add),
                      reads=[f"zc{ci}", "cw", key], writes=[key])
                tk.op(en, lambda c=c, ci=ci, sl=sl, eo=eo: eo.tensor_tensor(out=hn[:, c, sl], in0=sg[:, ci, sl], in1=bgc[ci][:, sl], op=ALU.mult),
                      reads=[key, f"bgc{ci}"], writes=[f"hn{c}"])
        emit_proj_fm(tk, nc, st, ps, wbuf, [wco_d[j] for j in range(8)], [(hn[:, c, :], f"hn{c}") for c in range(DCH)], emit_resid_epilogue(tk, nc, ps, xT))
        emit_rmsnorm(tk, nc, xT, gC, "gC", hn, rstd, ps, ones)
        emit_ffn(tk, nc, st, xT, hn, H, sg, ps, wbuf, wdbuf, wgC, wuC, wdC)
        emit_final_norm(tk, nc, xT, gF, "gF", hn, rstd, ps, ones, sg, out_d)
        tk.finish("sp", ["dout"])
        tk.barrier()
    return nc


def relayout_fm(W):
    W = np.asarray(W, np.float32)
    nch = W.shape[1] // 128
    return np.ascontiguousarray(W.reshape(DCH, 128, nch, 128).transpose(2, 1, 0, 3))


def attn_consts(half):
    vb = np.zeros((8, 16), np.float32)
    no = np.ones((8, 16), np.float32)
    for j in range(8):
        qblk = 8 * half + j
        vb[j, qblk:] = NEG
        no[j, qblk] = 0.0
    E = np.zeros((32, 16, 128), np.float32)
    for n in range(16):
        E[n, n, :] = 1.0
    tri = np.zeros((128, 4, 256), np.float32)
    k = np.arange(128)[:, None]
    q = np.arange(256)[None, :]
    for kt in range(2):
        m = np.where(kt * 128 + k > q, NEG, 0.0).astype(np.float32)
        tri[:, (0 if half == 0 else 2) + kt, :] = m
    return {
        "vbias": np.ascontiguousarray(np.broadcast_to(vb[None], (128, 8, 16))),
        "notown": np.ascontiguousarray(np.broadcast_to(no[None], (128, 8, 16))),
        "Esel": bf16_np(E), "tri": bf16_np(tri), "identf": np.eye(128, dtype=np.float32),
    }


def run_l2(inputs, r1, cores):
    nc = build_l2()
    wgA, wuA, wdA = relayout_ffn(inputs["ffn_w_gate"][0, 1], inputs["ffn_w_up"][0, 1], inputs["ffn_w_down"][0, 1])
    wgB, wuB, wdB = relayout_ffn(inputs["ffn_w_gate"][1, 0], inputs["ffn_w_up"][1, 0], inputs["ffn_w_down"][1, 0])
    cwi = np.asarray(inputs["conv_w_in"], np.float32)[0]
    cols = []
    for i in range(8):
        cols.append(cwi[:, 1024 + i * 128:1024 + (i + 1) * 128])
        cols.append(cwi[:, 2048 + i * 128:2048 + (i + 1) * 128])
    for i in range(8):
        cols.append(cwi[:, i * 128:(i + 1) * 128])
    shared = {
        "wo": relayout_fm(inputs["hyb_w_out"][0]),
        "g_ffn_a": fm_vec(inputs["ffn_norm"][0, 1]), "g_ffn_b": fm_vec(inputs["ffn_norm"][1, 0]), "g_mix": fm_vec(inputs["mix_norm"][1]),
        "wgA": wgA, "wuA": wuA, "wdA": wdA, "wgB": wgB, "wuB": wuB, "wdB": wdB,
        "wc": relayout_fm(np.concatenate(cols, axis=1)),
    }
    idx = {c: i for i, c in enumerate(cores)}
    maps = []
    for c in cores:
        b, half = c // 2, c % 2
        r0, rp = r1[idx[2 * b]], r1[idx[2 * b + 1]]
        m = dict(shared)
        m.update(attn_consts(half))
        me = r1[idx[c]]
        m["xT"] = np.asarray(me["xT_o"])
        m["aT"] = np.asarray(me["aT_o"])
        m["QT"] = np.ascontiguousarray(np.asarray(me["QKT_o"])[:, 0:4, :])
        m["KTp"] = np.ascontiguousarray(np.concatenate([np.asarray(r0["QKT_o"])[:, 4:8, :], np.asarray(rp["QKT_o"])[:, 4:8, :]], axis=2))
        vp = np.concatenate([np.asarray(r0["V_o"]), np.asarray(rp["V_o"])], axis=0)
        m["Vp"] = np.ascontiguousarray(vp.transpose(1, 0, 2))
        maps.append(m)
    res = run_bass_kernel_spmd(nc, maps, core_ids=list(range(len(cores))))
    return res.results


def run_l3(inputs, r2, cores):
    nc = build_l3()
    wgC, wuC, wdC = relayout_ffn(inputs["ffn_w_gate"][1, 1], inputs["ffn_w_up"][1, 1], inputs["ffn_w_down"][1, 1])
    cwv = np.asarray(inputs["conv_w"], np.float32)[0]
    cw = np.ascontiguousarray(cwv.reshape(3, DCH, 128).transpose(2, 1, 0))
    shared = {
        "cw": cw, "wco": relayout_fm(inputs["conv_w_out"][0]),
        "g_ffn_c": fm_vec(inputs["ffn_norm"][1, 1]), "g_fin": fm_vec(inputs["final_norm"]),
        "wgC": wgC, "wuC": wuC, "wdC": wdC,
    }
    idx = {c: i for i, c in enumerate(cores)}
    maps = []
    for c in cores:
        b, half = c // 2, c % 2
        me = r2[idx[c]]
        z = np.asarray(me["zT_o"])
        if half == 0:
            halo = np.zeros((128, DCH, 2), np.float32)
        else:
            halo = np.asarray(r2[idx[2 * b]]["zT_o"])[:, :, T_CORE - 2:T_CORE]
        m = dict(shared)
        m["xT"] = np.asarray(me["xT_o"])
        m["zext"] = np.ascontiguousarray(np.concatenate([halo, z], axis=2))
        m["bgT"] = np.asarray(me["bgT_o"])
        maps.append(m)
    res = run_bass_kernel_spmd(nc, maps, core_ids=list(range(len(cores))))
    return res.results


def run_all(inputs, cores):
    inputs = {k: np.asarray(v) for k, v in inputs.items()}
    r1 = run_l1(inputs, cores)
    r2 = run_l2(inputs, r1, cores)
    r3 = run_l3(inputs, r2, cores)
    outs = {}
    for i, c in enumerate(cores):
        oT = np.asarray(r3[i]["outT"], np.float32)
        outs[c] = np.ascontiguousarray(oT.transpose(1, 0, 2).reshape(D_MODEL, T_CORE).T)
    return outs, (r1, r2, r3)
def build_fused():
    nc = bass.Bass("TRN2", target_bir_lowering=False)
    ext = lambda name, shape, dt: nc.dram_tensor(name, shape, dt, kind="ExternalInput").ap()
    itn = lambda name, shape, dt: nc.dram_tensor(name, shape, dt, kind="Internal").ap()
    W = {}
    for sfx in ("0", "A", "B", "C"):
        W["wg" + sfx] = ext("wg" + sfx, [FCH, 128, DCH, 128], F32)
        W["wu" + sfx] = ext("wu" + sfx, [FCH, 128, DCH, 128], F32)
        W["wd" + sfx] = ext("wd" + sfx, [DCH, 128, FCH, 128], F32)
    for name, shape, dt in (("g_ffn", [128, DCH], F32), ("g_mix1", [128, DCH], F32), ("win_fm", [12, 128, DCH, 128], F32), ("win_tm", [2, 128, DCH, 512], F32),
                            ("ropec", [32, 4], F32), ("PT", [128, 128], F32), ("vnorm", [128, 512], F32), ("wsT", [128, 4, 128], F32),
                            ("trilT", [128, 4, 128], F32), ("bsB", [128, 4, 128], F32),
                            ("Esel", [32, 16, 128], BF16), ("identf", [128, 128], F32), ("wo", [8, 128, DCH, 128], F32),
                            ("g_ffn_a", [128, DCH], F32), ("g_ffn_b", [128, DCH], F32), ("g_mix2", [128, DCH], F32), ("wc", [24, 128, DCH, 128], F32),
                            ("cw", [128, DCH, 3], F32), ("wco", [8, 128, DCH, 128], F32), ("g_ffn_c", [128, DCH], F32), ("g_fin", [128, DCH], F32)):
        W[name] = ext(name, shape, dt)
    x2 = ext("xT2", [2, 128, DCH, T_CORE], F32)
    pos2 = ext("pos2", [2, 32, T_CORE], I32)
    vb2 = ext("vbias2", [2, 128, 8, 16], F32)
    no2 = ext("notown2", [2, 128, 8, 16], F32)
    tri2 = ext("tri2", [2, 128, 4, 256], BF16)
    out2 = nc.dram_tensor("outT2", [2, 128, DCH, T_CORE], F32, kind="ExternalOutput").ap()
    x1_i = itn("x1_i", [2, 128, DCH, T_CORE], F32)
    a_i = itn("a_i", [2, 128, 4, T_CORE], BF16)
    qk_i = itn("qk_i", [2, 128, 8, T_CORE], BF16)
    v_i = itn("v_i", [2, 16, 128, 512], BF16)
    x4_i = itn("x4_i", [2, 128, DCH, T_CORE], F32)
    g_i = itn("g_i", [2, 128, DCH, T_CORE], BF16)
    zh_i = itn("zh_i", [2, 128, DCH, 2], F32)
    with ExitStack() as sem_es:
        for hf in range(2):
            ov = dict(W)
            ov.update({"g_mix": W["g_mix1"], "xT": x2[hf], "pos": pos2[hf], "xT_o": x1_i[hf], "aT_o": a_i[hf], "QKT_o": qk_i[hf], "V_o": v_i[hf]})
            build_l1(nc, ov, sfx=f"_p1h{hf}", sem_es=sem_es)
        for hf in range(2):
            ov = dict(W)
            ov.update({"g_mix": W["g_mix2"], "xT": x1_i[hf], "aT": a_i[hf], "QT": qk_i[hf][:, 0:4, :],
                       "KT_parts": [qk_i[0][:, 4:8, :], qk_i[1][:, 4:8, :]], "V_parts": [v_i[0], v_i[1]],
                       "vbias": vb2[hf], "notown": no2[hf], "tri": tri2[hf],
                       "xT_o": x4_i[hf], "g_o": g_i[hf], "zh_o": zh_i[hf], "halo_in": (zh_i[0] if hf == 1 else None)})
            nblocks = (lambda j: j + 1) if hf == 0 else (lambda j: 9 + j)
            build_l2(nc, ov, sfx=f"_p2h{hf}", nblocks=nblocks, sem_es=sem_es, conv_here=True)
        for hf in range(2):
            ov = dict(W)
            ov.update({"xT": x4_i[hf], "g_src": g_i[hf], "outT": out2[hf]})
            build_l3(nc, ov, sfx=f"_p3h{hf}", sem_es=sem_es)
    return nc


def fused_shared(inputs):
    s1 = l1_shared(inputs)
    sh = {k: s1[k] for k in ("wg0", "wu0", "wd0", "g_ffn", "win_fm", "win_tm", "ropec", "PT", "vnorm", "wsT", "trilT", "bsB")}
    sh["g_mix1"] = s1["g_mix"]
    wgA, wuA, wdA = relayout_ffn(inputs["ffn_w_gate"][0, 1], inputs["ffn_w_up"][0, 1], inputs["ffn_w_down"][0, 1])
    wgB, wuB, wdB = relayout_ffn(inputs["ffn_w_gate"][1, 0], inputs["ffn_w_up"][1, 0], inputs["ffn_w_down"][1, 0])
    wgC, wuC, wdC = relayout_ffn(inputs["ffn_w_gate"][1, 1], inputs["ffn_w_up"][1, 1], inputs["ffn_w_down"][1, 1])
    cwi = np.asarray(inputs["conv_w_in"], np.float32)[0]
    cols = []
    for i in range(8):
        cols.append(cwi[:, i * 128:(i + 1) * 128])
        cols.append(cwi[:, 1024 + i * 128:1024 + (i + 1) * 128])
        cols.append(cwi[:, 2048 + i * 128:2048 + (i + 1) * 128])
    cwv = np.asarray(inputs["conv_w"], np.float32)[0]
    c0, c1 = attn_consts(0), attn_consts(1)
    sh.update({
        "wgA": wgA, "wuA": wuA, "wdA": wdA, "wgB": wgB, "wuB": wuB, "wdB": wdB, "wgC": wgC, "wuC": wuC, "wdC": wdC,
        "wo": relayout_fm(inputs["hyb_w_out"][0]), "wc": relayout_fm(np.concatenate(cols, axis=1)), "wco": relayout_fm(inputs["conv_w_out"][0]),
        "g_ffn_a": fm_vec(inputs["ffn_norm"][0, 1]), "g_ffn_b": fm_vec(inputs["ffn_norm"][1, 0]), "g_mix2": fm_vec(inputs["mix_norm"][1]),
        "g_ffn_c": fm_vec(inputs["ffn_norm"][1, 1]), "g_fin": fm_vec(inputs["final_norm"]),
        "cw": np.ascontiguousarray(cwv.reshape(3, DCH, 128).transpose(2, 1, 0)),
        "Esel": c0["Esel"], "identf": c0["identf"],
        "vbias2": np.stack([c0["vbias"], c1["vbias"]]), "notown2": np.stack([c0["notown"], c1["notown"]]), "tri2": np.stack([c0["tri"], c1["tri"]]),
    })
    return sh


def run_fused(inputs, batches):
    inputs = {k: np.asarray(v) for k, v in inputs.items()}
    nc = build_fused()
    sh = fused_shared(inputs)
    pos = np.asarray(inputs["positions"], np.int32)
    maps = []
    for b in batches:
        m = dict(sh)
        m["xT2"] = np.stack([l1_inputs(inputs, 2 * b), l1_inputs(inputs, 2 * b + 1)])
        m["pos2"] = np.ascontiguousarray(np.broadcast_to(pos[b].reshape(2, 1, T_CORE), (2, 32, T_CORE)))
        maps.append(m)
    res = run_bass_kernel_spmd(nc, maps, core_ids=list(range(len(batches))))
    outs = {}
    for i, b in enumerate(batches):
        oT = np.asarray(res.results[i]["outT2"], np.float32)
        outs[b] = np.concatenate([oT[hf].transpose(1, 0, 2).reshape(D_MODEL, T_CORE).T for hf in range(2)], axis=0)
    return outs


def kernel(**inputs):
    outs = run_fused(inputs, list(range(BATCH)))
    return np.ascontiguousarray(np.stack([outs[b] for b in range(BATCH)]).astype(np.float32))
```

```python
import numpy as np
import ml_dtypes
import concourse.bass as bass
import concourse.mybir as mybir
from concourse.bass_utils import run_bass_kernel_spmd

F32 = mybir.dt.float32
BF16 = mybir.dt.bfloat16
I32 = mybir.dt.int32
ALU = mybir.AluOpType
AF = mybir.ActivationFunctionType

N_CORES = 8
D_MODEL = 1024
SEQ = 4096
BATCH = 4
T_CORE = 2048
DCH = D_MODEL // 128
D_FF = 2816
FCH = D_FF // 128
RMS_EPS = 1e-6


from contextlib import ExitStack

AX = mybir.AxisListType.X
GROUPS = [(0, 6), (6, 12), (12, 17), (17, 22)]
NT = T_CORE // 512
GELU_C = 0.044715
GELU_S = 1.5957691216057308


def bf16_np(a):
    return np.asarray(a, dtype=np.float32).astype(ml_dtypes.bfloat16)


class Trk:
    def __init__(self, nc, es, dma_streams, sfx=""):
        self.nc = nc
        self.eng = {"pe": nc.tensor, "act": nc.scalar, "dve": nc.vector, "pool": nc.gpsimd, "sp": nc.sync}
        names = list(self.eng) + list(dma_streams)
        self.dma_streams = set(dma_streams)
        self.sem = {n: es.enter_context(nc.semaphore("s_" + n + sfx)) for n in names}
        self.cnt = {k: 0 for k in names}
        self.known = {e: {} for e in self.eng}
        self.lastw = {}
        self.readers = {}

    def _wait(self, e, semname, val):
        if val <= 0:
            return
        if semname in self.dma_streams:
            val = self.cnt[semname]
        if self.known[e].get(semname, 0) >= val:
            return
        if e == "pe" and semname == "pe":
            return
        self.eng[e].wait_ge(self.sem[semname], val)
        self.known[e][semname] = val

    def deps(self, e, reads=(), writes=()):
        for k in reads:
            if k in self.lastw:
                self._wait(e, *self.lastw[k])
        for k in writes:
            if k in self.lastw:
                self._wait(e, *self.lastw[k])
            for r in self.readers.get(k, ()):
                self._wait(e, *r)

    def done(self, ins, semname, inc, reads=(), writes=()):
        self.cnt[semname] += inc
        ins.then_inc(self.sem[semname], inc)
        tag = (semname, self.cnt[semname])
        for k in reads:
            self.readers.setdefault(k, []).append(tag)
        for k in writes:
            self.lastw[k] = tag
            self.readers[k] = []
        return tag

    @staticmethod
    def _excl(reads, writes):
        r = [k for k in reads if not k.startswith("ps")]
        w = list(writes) + [k for k in reads if k.startswith("ps") and k not in writes]
        return r, w

    def op(self, e, fn, reads=(), writes=()):
        reads, writes = self._excl(reads, writes)
        self.deps(e, reads, writes)
        return self.done(fn(), e, 1, reads, writes)

    def barrier(self):
        for e in self.eng:
            for sname, c in self.cnt.items():
                if c and not (e == sname):
                    if self.known[e].get(sname, 0) < c:
                        self.eng[e].wait_ge(self.sem[sname], c)
                        self.known[e][sname] = c

    def dma(self, e, stream, fn, reads=(), writes=()):
        self.deps(e, reads, writes)
        return self.done(fn(), stream, 16, reads, writes)

    def finish(self, e, streams):
        for s in streams:
            if self.cnt[s]:
                self.eng[e].wait_ge(self.sem[s], self.cnt[s])


def tsl(tt):
    return slice(tt * 512, (tt + 1) * 512)


def emit_rmsnorm(tk, nc, xT, g, gkey, hn, rstd, ps, ones, epsc):
    for c in range(DCH):
        tk.op("act", lambda c=c: nc.scalar.activation(out=hn[:, c, :], in_=xT[:, c, :], func=AF.Square),
              reads=[f"xT{c}"], writes=[f"hn{c}"])
    for tt in range(NT):
        for c in range(DCH):
            tk.op("pe", lambda c=c, tt=tt: nc.tensor.matmul(ps[:, tt, :], lhsT=ones[:, :], rhs=hn[:, c, tsl(tt)],
                                                             start=(c == 0), stop=(c == DCH - 1)),
                  reads=[f"hn{c}", "ones"], writes=[f"ps{tt}"])
    for tt in range(NT):
        sl = tsl(tt)
        tk.op("act", lambda tt=tt, sl=sl: nc.scalar.activation(out=rstd[:, sl], in_=ps[:, tt, :], func=AF.Sqrt, scale=1.0 / D_MODEL, bias=epsc[:, 0:1]),
              reads=[f"ps{tt}", "epsc"], writes=[f"rstd{tt}"])
        tk.op("dve", lambda sl=sl: nc.vector.reciprocal(out=rstd[:, sl], in_=rstd[:, sl]), reads=[f"rstd{tt}"], writes=[f"rstd{tt}"])
    for c in range(DCH):
        for tt in range(NT):
            sl = tsl(tt)
            tk.op("dve", lambda c=c, sl=sl: nc.vector.scalar_tensor_tensor(out=hn[:, c, sl], in0=xT[:, c, sl], scalar=g[:, c:c + 1], in1=rstd[:, sl],
                                                                             op0=ALU.mult, op1=ALU.mult),
                  reads=[f"xT{c}", f"rstd{tt}", gkey], writes=[f"hn{c}"])


def emit_ffn(tk, nc, st, xT, hn, H, sg, ps, wbuf, wdbuf, wg_d, wu_d, wd_d):
    NW = len(wbuf)
    for (f0, f1) in GROUPS:
        nf = f1 - f0
        for f in range(f0, f1):
            for which, wsrc in ((0, wg_d), (1, wu_d)):
                wi = st["w"] % NW
                st["w"] += 1
                wt = wbuf[wi]
                tk.dma("pool", f"dw{wi}", lambda wt=wt, wsrc=wsrc, f=f: nc.gpsimd.dma_start(out=wt[:, :, :], in_=wsrc[f]),
                       writes=[f"w{wi}"])
                for c in range(DCH):
                    for tt in range(NT):
                        b = which * 4 + tt
                        tk.op("pe", lambda wt=wt, c=c, tt=tt, b=b: nc.tensor.matmul(
                            ps[:, b, :], lhsT=wt[:, c, :], rhs=hn[:, c, tsl(tt)], start=(c == 0), stop=(c == DCH - 1)),
                            reads=[f"w{wi}", f"hn{c}"], writes=[f"ps{b}"])
            si = st["sg"] % 2
            st["sg"] += 1
            for tt in range(NT):
                tk.op("act", lambda tt=tt, si=si: nc.scalar.activation(out=sg[:, si, tsl(tt)], in_=ps[:, tt, :], func=AF.Silu),
                      reads=[f"ps{tt}"], writes=[f"sg{si}_{tt}"])
            for tt in range(NT):
                tk.op("dve", lambda tt=tt, si=si, f=f: nc.vector.tensor_tensor(out=H[:, f - f0, tsl(tt)], in0=sg[:, si, tsl(tt)], in1=ps[:, 4 + tt, :], op=ALU.mult),
                      reads=[f"sg{si}_{tt}", f"ps{4 + tt}"], writes=[f"H{f - f0}"])
        for j in range(DCH):
            di = st["wd"] % len(wdbuf)
            st["wd"] += 1
            wt = wdbuf[di]
            tk.dma("pool", f"dd{di}", lambda wt=wt, j=j, f0=f0, f1=f1, nf=nf: nc.gpsimd.dma_start(out=wt[:, 0:nf, :], in_=wd_d[j, :, f0:f1, :]),
                   writes=[f"wd{di}"])
            bs = (st["dn"] % 2) * 4
            st["dn"] += 1
            for k in range(nf):
                for tt in range(NT):
                    b = bs + tt
                    tk.op("pe", lambda wt=wt, k=k, tt=tt, b=b: nc.tensor.matmul(
                        ps[:, b, :], lhsT=wt[:, k, :], rhs=H[:, k, tsl(tt)], start=(k == 0), stop=(k == nf - 1)),
                        reads=[f"wd{di}", f"H{k}"], writes=[f"ps{b}"])
            for tt in range(NT):
                b = bs + tt
                tk.op("dve", lambda j=j, tt=tt, b=b: nc.vector.scalar_tensor_tensor(out=xT[:, j, tsl(tt)], in0=ps[:, b, :], scalar=0.5, in1=xT[:, j, tsl(tt)],
                                                                                    op0=ALU.mult, op1=ALU.add),
                      reads=[f"ps{b}", f"xT{j}"], writes=[f"xT{j}"])


def emit_gelu(tk, nc, src, srckey, tmp, tmpkey, out, outkey):
    tk.op("act", lambda: nc.scalar.activation(out=tmp, in_=src, func=AF.Square), reads=[srckey], writes=[tmpkey])
    tk.op("dve", lambda: nc.vector.tensor_scalar(out=tmp, in0=tmp, scalar1=GELU_C, scalar2=1.0, op0=ALU.mult, op1=ALU.add),
          reads=[tmpkey], writes=[tmpkey])
    tk.op("dve", lambda: nc.vector.tensor_tensor(out=tmp, in0=tmp, in1=src, op=ALU.mult), reads=[tmpkey, srckey], writes=[tmpkey])
    tk.op("act", lambda: nc.scalar.activation(out=tmp, in_=tmp, func=AF.Sigmoid, scale=GELU_S), reads=[tmpkey], writes=[tmpkey])
    tk.op("dve", lambda: nc.vector.tensor_tensor(out=out, in0=tmp, in1=src, op=ALU.mult), reads=[tmpkey, srckey], writes=[outkey])


FFN_STREAMS = [f"dw{i}" for i in range(4)] + [f"dd{i}" for i in range(3)]


def ffn_dram(nc, sfx, ov=None):
    ov = ov or {}
    mk = lambda name, shape: ov[name] if name in ov else nc.dram_tensor(name, shape, F32, kind="ExternalInput").ap()
    wg = mk("wg" + sfx, [FCH, 128, DCH, 128])
    wu = mk("wu" + sfx, [FCH, 128, DCH, 128])
    wd = mk("wd" + sfx, [DCH, 128, FCH, 128])
    return wg, wu, wd


def io_makers(nc, ov):
    di = lambda name, shape, dt: ov[name] if name in ov else nc.dram_tensor(name, shape, dt, kind="ExternalInput").ap()
    do = lambda name, shape, dt: ov[name] if name in ov else nc.dram_tensor(name, shape, dt, kind="ExternalOutput").ap()
    return di, do


def build_l1(nc=None, ov=None, sfx="", sem_es=None):
    nc = nc or bass.Bass("TRN2", target_bir_lowering=False)
    ov = ov or {}
    di, do = io_makers(nc, ov)
    x_d = di("xT", [128, DCH, T_CORE], F32)
    gf_d = di("g_ffn", [128, DCH], F32)
    gm_d = di("g_mix", [128, DCH], F32)
    wg_d, wu_d, wd_d = ffn_dram(nc, "0", ov)
    wfm_d = di("win_fm", [12, 128, DCH, 128], F32)
    wtm_d = di("win_tm", [2, 128, DCH, 512], F32)
    pos_d = di("pos", [32, T_CORE], I32)
    rc_d = di("ropec", [32, 4], F32)
    pt_d = di("PT", [128, 128], F32)
    vn_d = di("vnorm", [128, 512], F32)
    ws_d = di("wsT", [128, 4, 128], F32)
    tr_d = di("trilT", [128, 4, 128], F32)
    bs_d = di("bsB", [128, 4, 128], F32)
    xo_d = do("xT_o", [128, DCH, T_CORE], F32)
    ao_d = do("aT_o", [128, 4, T_CORE], BF16)
    qk_d = do("QKT_o", [128, 8, T_CORE], BF16)
    vo_d = do("V_o", [16, 128, 512], BF16)
    with ExitStack() as es:
        sb = lambda name, shape, dt: es.enter_context(nc.sbuf_tensor(name + sfx, shape, dt))
        xT = sb("xT_s", [128, DCH, T_CORE], F32)
        hn = sb("hn", [128, DCH, T_CORE], BF16)
        H = sb("H", [128, 8, T_CORE], BF16)
        sg = sb("sg", [128, 2, T_CORE], F32)
        rstd = sb("rstd", [128, T_CORE], F32)
        gf = sb("gf", [128, DCH], F32)
        gm = sb("gm", [128, DCH], F32)
        ones = sb("ones", [128, 128], BF16)
        epsc = sb("epsc", [128, 1], F32)
        wbuf = [sb(f"wb{i}", [128, DCH, 128], BF16) for i in range(4)]
        wdbuf = [sb(f"wdb{i}", [128, 6, 128], BF16) for i in range(3)]
        posi = sb("posi", [32, T_CORE], I32)
        rc = sb("rc", [32, 4], F32)
        PT = sb("PTs", [128, 128], BF16)
        vnorm = sb("vnorm_s", [128, 512], F32)
        wsT = sb("wsT_s", [128, 4, 128], F32)
        trT = sb("trT_s", [128, 4, 128], F32)
        WmT = sb("WmT", [128, 4, 128], BF16)
        bsB = sb("bsB_s", [128, 4, 128], F32)
        tmpA = sb("tmpA", [128, 2, 512], F32)
        tmpB = sb("tmpB", [128, 2, 512], F32)
        vg = sb("vg", [128, 2, 512], F32)
        vnb = sb("vnb", [128, 2, 512], BF16)
        vt = sb("vt", [128, 2, 512], BF16)
        ao = sb("ao", [128, 2, 4, 128], BF16)
        tmpm = sb("tmpm", [128, 2, 4, 128], F32)
        ss = sb("ss", [128, 4], F32)
        ps = es.enter_context(nc.psum_tensor("ps" + sfx, [128, 8, 512], F32))
        uT = H[:, 4:8, :]
        wtm = H[:, 0:4, :].rearrange("p a (b t) -> p (a b) t", t=512)
        tk = Trk(nc, sem_es or es, ["dx", "dout", "dwt"] + FFN_STREAMS, sfx)
        for c in range(DCH):
            tk.dma("sp", "dx", lambda c=c: nc.sync.dma_start(out=xT[:, c, :], in_=x_d[:, c, :]), writes=[f"xT{c}"])
        for (t_s, t_d, key) in ((gf, gf_d, "gf"), (gm, gm_d, "gm"), (posi, pos_d, "posi"), (rc, rc_d, "rc"), (vnorm, vn_d, "vnorm"),
                                (wsT, ws_d, "wsT"), (trT, tr_d, "trT"), (bsB, bs_d, "bsB")):
            tk.dma("sp", "dx", lambda t_s=t_s, t_d=t_d: nc.sync.dma_start(out=t_s[:], in_=t_d), writes=[key])
        tk.dma("pool", "dx", lambda: nc.gpsimd.dma_start(out=PT[:, :], in_=pt_d[:, :]), writes=["PT"])
        tk.op("dve", lambda: nc.vector.memset(ones[:, :], 1.0), writes=["ones"])
        tk.op("dve", lambda: nc.vector.memset(epsc[:, :], RMS_EPS), writes=["epsc"])
        tk.op("dve", lambda: nc.vector.tensor_tensor(out=WmT[:, :, :], in0=wsT[:, :, :], in1=trT[:, :, :], op=ALU.mult),
              reads=["wsT", "trT"], writes=["WmT"])
        st = {"w": 0, "sg": 0, "wd": 0, "dn": 0}
        emit_rmsnorm(tk, nc, xT, gf, "gf", hn, rstd, ps, ones, epsc)
        emit_ffn(tk, nc, st, xT, hn, H, sg, ps, wbuf, wdbuf, wg_d, wu_d, wd_d)
        emit_rmsnorm(tk, nc, xT, gm, "gm", hn, rstd, ps, ones, epsc)
        for c in range(DCH):
            tk.dma("sp", "dout", lambda c=c: nc.sync.dma_start(out=xo_d[:, c, :], in_=xT[:, c, :]), reads=[f"xT{c}"])
        sgkeys = [f"sg{si}_{tt}" for si in range(2) for tt in range(NT)]
        cosT = sg[0:32, 0, :]
        sinT = sg[0:32, 1, :]
        ang = rstd[0:32, :]
        rkeys = [f"rstd{tt}" for tt in range(NT)]
        tk.op("dve", lambda: nc.vector.tensor_copy(out=ang, in_=posi[:, :]), reads=["posi"] + rkeys, writes=rkeys)
        tk.op("dve", lambda: nc.vector.tensor_scalar(out=ang, in0=ang, scalar1=rc[:, 0:1], scalar2=None, op0=ALU.mult), reads=rkeys + ["rc"], writes=rkeys)
        for (tab, off) in ((sinT, 0.0), (cosT, 0.25)):
            tk.op("dve", lambda tab=tab, off=off: nc.vector.tensor_scalar(out=tab, in0=ang, scalar1=float(1.0 / (2 * np.pi)), scalar2=float(off),
                                                                          op0=ALU.mult, op1=ALU.add), reads=rkeys, writes=sgkeys)
            tk.op("dve", lambda tab=tab: nc.vector.tensor_copy(out=posi[:, :], in_=tab), reads=sgkeys, writes=["posi"])
            tk.op("dve", lambda: nc.vector.tensor_copy(out=tmpA[0:32, :, :].rearrange("p a b -> p (a b)"), in_=posi[:, 0:1024]), reads=["posi"], writes=["tmpA0", "tmpA1"])
            tk.op("dve", lambda: nc.vector.tensor_copy(out=tmpB[0:32, :, :].rearrange("p a b -> p (a b)"), in_=posi[:, 1024:2048]), reads=["posi"], writes=["tmpB0", "tmpB1"])
            tk.op("dve", lambda tab=tab: nc.vector.tensor_tensor(out=tab[:, 0:1024], in0=tab[:, 0:1024], in1=tmpA[0:32, :, :].rearrange("p a b -> p (a b)"), op=ALU.subtract),
                  reads=sgkeys + ["tmpA0", "tmpA1"], writes=sgkeys)
            tk.op("dve", lambda tab=tab: nc.vector.tensor_tensor(out=tab[:, 1024:2048], in0=tab[:, 1024:2048], in1=tmpB[0:32, :, :].rearrange("p a b -> p (a b)"), op=ALU.subtract),
                  reads=sgkeys + ["tmpB0", "tmpB1"], writes=sgkeys)
            tk.op("dve", lambda tab=tab: nc.vector.tensor_scalar(out=tmpA[0:32, :, :].rearrange("p a b -> p (a b)"), in0=tab[:, 0:1024], scalar1=0.5, scalar2=None, op0=ALU.is_gt),
                  reads=sgkeys, writes=["tmpA0", "tmpA1"])
            tk.op("dve", lambda tab=tab: nc.vector.tensor_scalar(out=tmpB[0:32, :, :].rearrange("p a b -> p (a b)"), in0=tab[:, 1024:2048], scalar1=0.5, scalar2=None, op0=ALU.is_gt),
                  reads=sgkeys, writes=["tmpB0", "tmpB1"])
            tk.op("dve", lambda tab=tab: nc.vector.tensor_tensor(out=tab[:, 0:1024], in0=tab[:, 0:1024], in1=tmpA[0:32, :, :].rearrange("p a b -> p (a b)"), op=ALU.subtract),
                  reads=sgkeys + ["tmpA0", "tmpA1"], writes=sgkeys)
            tk.op("dve", lambda tab=tab: nc.vector.tensor_tensor(out=tab[:, 1024:2048], in0=tab[:, 1024:2048], in1=tmpB[0:32, :, :].rearrange("p a b -> p (a b)"), op=ALU.subtract),
                  reads=sgkeys + ["tmpB0", "tmpB1"], writes=sgkeys)
            tk.op("act", lambda tab=tab: nc.scalar.activation(out=tab, in_=tab, func=AF.Sin, scale=float(2 * np.pi)), reads=sgkeys, writes=sgkeys)
        tk.op("dve", lambda: nc.vector.tensor_scalar(out=sinT, in0=sinT, scalar1=rc[:, 1:2], scalar2=None, op0=ALU.mult), reads=sgkeys + ["rc"], writes=sgkeys)
        t4 = [(tmpA[:, 0, :], "tmpA0"), (tmpA[:, 1, :], "tmpA1"), (tmpB[:, 0, :], "tmpB0"), (tmpB[:, 1, :], "tmpB1")]
        pending = None
        for j in range(12):
            wi = st["w"] % 4
            st["w"] += 1
            wt = wbuf[wi]
            tk.dma("pool", f"dw{wi}", lambda wt=wt, j=j: nc.gpsimd.dma_start(out=wt[:, :, :], in_=wfm_d[j]), writes=[f"w{wi}"])
            bs = (j % 2) * 4
            for c in range(DCH):
                for tt in range(NT):
                    tk.op("pe", lambda wt=wt, c=c, tt=tt, bs=bs: nc.tensor.matmul(
                        ps[:, bs + tt, :], lhsT=wt[:, c, :], rhs=hn[:, c, tsl(tt)], start=(c == 0), stop=(c == DCH - 1)),
                        reads=[f"w{wi}", f"hn{c}"], writes=[f"ps{bs + tt}"])
            if pending is not None:
                pending()
                pending = None
            if j < 4:
                for tt in range(NT):
                    b = bs + tt
                    emit_gelu(tk, nc, ps[:, b, :], f"ps{b}", t4[tt][0], t4[tt][1], uT[:, j, tsl(tt)], f"H{4 + j}")
            else:
                idx = (j - 4) % 4
                for tt in range(NT):
                    b = bs + tt
                    tk.op("act", lambda idx=idx, tt=tt, b=b: nc.scalar.copy(out=H[:, idx, tsl(tt)], in_=ps[:, b, :]),
                          reads=[f"ps{b}"], writes=[f"H{idx}"])
                    tk.op("dve", lambda tt=tt, b=b: nc.vector.tensor_tensor(out=t4[tt][0][0:32, :], in0=ps[0:32, b, :], in1=cosT[:, tsl(tt)], op=ALU.mult),
                          reads=[f"ps{b}"] + sgkeys, writes=[t4[tt][1]])

                def part2(j=j, idx=idx, bs=bs):
                    for tt in range(NT):
                        b = bs + tt
                        tk.op("pe", lambda: nc.tensor.matmul(ps[:, b, :], lhsT=PT[:, :], rhs=H[:, idx, tsl(tt)], start=True, stop=True),
                              reads=["PT", f"H{idx}"], writes=[f"ps{b}"])
                        tk.op("dve", lambda: nc.vector.tensor_tensor(out=vg[0:32, tt % 2, :], in0=ps[0:32, b, :], in1=sinT[:, tsl(tt)], op=ALU.mult),
                              reads=[f"ps{b}"] + sgkeys, writes=[f"vg{tt % 2}"])
                        tk.op("dve", lambda: nc.vector.tensor_tensor(out=H[0:32, idx, tsl(tt)], in0=t4[tt][0][0:32, :], in1=vg[0:32, tt % 2, :], op=ALU.add),
                              reads=[t4[tt][1], f"vg{tt % 2}"], writes=[f"H{idx}"])
                    tk.dma("sp", "dout", lambda: nc.sync.dma_start(out=qk_d[:, j - 4, :], in_=H[:, idx, :]), reads=[f"H{idx}"])
                pending = part2
        if pending is not None:
            pending()
        for w in range(2):
            tk.dma("pool", "dwt", lambda w=w: nc.gpsimd.dma_start(out=wtm[:, w * 8:(w + 1) * 8, :], in_=wtm_d[w]),
                   writes=["H0", "H1", "H2", "H3"])
        wkeys = ["H0", "H1", "H2", "H3"]
        for i in range(16):
            par = i % 2
            bA, bB, bC = par * 3, par * 3 + 1, par * 3 + 2
            tok = slice(i * 128, (i + 1) * 128)
            for w, b in ((0, bA), (1, bB)):
                for c in range(DCH):
                    tk.op("pe", lambda w=w, b=b, c=c, tok=tok: nc.tensor.matmul(ps[:, b, :], lhsT=hn[:, c, tok], rhs=wtm[:, w * 8 + c, :],
                                                                                  start=(c == 0), stop=(c == DCH - 1)),
                          reads=[f"hn{c}"] + wkeys, writes=[f"ps{b}"])
            tk.op("act", lambda par=par, bB=bB: nc.scalar.copy(out=vt[:, par, :], in_=ps[:, bB, :]), reads=[f"ps{bB}"], writes=[f"vt{par}"])
            tk.dma("sp", "dout", lambda i=i, par=par: nc.sync.dma_start(out=vo_d[i], in_=vt[:, par, :]), reads=[f"vt{par}"])
            emit_gelu(tk, nc, ps[:, bA, :], f"ps{bA}", tmpA[:, par, :], f"tmpA{par}", vg[:, par, :], f"vg{par}")
            tk.op("act", lambda par=par: nc.scalar.activation(out=tmpB[:, par, :], in_=vg[:, par, :], func=AF.Square), reads=[f"vg{par}"], writes=[f"tmpB{par}"])
            tk.op("dve", lambda par=par: nc.vector.reduce_sum(out=ss[:, par:par + 1], in_=tmpB[:, par, :], axis=AX), reads=[f"tmpB{par}"], writes=[f"ss{par}"])
            tk.op("dve", lambda par=par: nc.vector.tensor_scalar(out=ss[:, par:par + 1], in0=ss[:, par:par + 1], scalar1=1.0 / 512.0, scalar2=RMS_EPS,
                                                                   op0=ALU.mult, op1=ALU.add), reads=[f"ss{par}"], writes=[f"ss{par}"])
            tk.op("act", lambda par=par: nc.scalar.sqrt(out=ss[:, par:par + 1], in_=ss[:, par:par + 1]), reads=[f"ss{par}"], writes=[f"ss{par}"])
            tk.op("dve", lambda par=par: nc.vector.reciprocal(out=ss[:, par:par + 1], in_=ss[:, par:par + 1]), reads=[f"ss{par}"], writes=[f"ss{par}"])
            tk.op("dve", lambda par=par: nc.vector.scalar_tensor_tensor(out=vnb[:, par, :], in0=vg[:, par, :], scalar=ss[:, par:par + 1], in1=vnorm[:, :],
                                                                         op0=ALU.mult, op1=ALU.mult),
                  reads=[f"vg{par}", f"ss{par}", "vnorm"], writes=[f"vnb{par}"])
            for g in range(4):
                tk.op("pe", lambda par=par, g=g, bC=bC: nc.tensor.matmul(ps[:, bC, g * 128:(g + 1) * 128], lhsT=vnb[:, par, g * 128:(g + 1) * 128], rhs=WmT[:, g, :],
                                                                          start=True, stop=True),
                      reads=[f"vnb{par}", "WmT"], writes=[f"ps{bC}"])
            tk.op("dve", lambda par=par, bC=bC: nc.vector.tensor_tensor(out=tmpm[:, par, :, :], in0=ps[:, bC, :].rearrange("p (g t) -> p g t", g=4), in1=bsB[:, :, :], op=ALU.add),
                  reads=[f"ps{bC}", "bsB"], writes=[f"tmpm{par}"])
            tk.op("dve", lambda par=par, tok=tok: nc.vector.tensor_tensor(out=ao[:, par, :, :], in0=tmpm[:, par, :, :], in1=uT[:, :, tok], op=ALU.mult),
                  reads=[f"tmpm{par}", "H4", "H5", "H6", "H7"], writes=[f"ao{par}"])
            tk.dma("sp", "dout", lambda par=par, tok=tok: nc.sync.dma_start(out=ao_d[:, :, tok], in_=ao[:, par, :, :]), reads=[f"ao{par}"])
        tk.finish("sp", ["dout"])
        tk.barrier()
    return nc


def relayout_ffn(wg, wu, wd):
    a = np.ascontiguousarray(np.asarray(wg, np.float32).reshape(DCH, 128, FCH, 128).transpose(2, 1, 0, 3))
    b = np.ascontiguousarray(np.asarray(wu, np.float32).reshape(DCH, 128, FCH, 128).transpose(2, 1, 0, 3))
    c = np.ascontiguousarray(np.asarray(wd, np.float32).reshape(FCH, 128, DCH, 128).transpose(2, 1, 0, 3))
    return a, b, c


def fm_vec(v):
    return np.ascontiguousarray(np.asarray(v, np.float32).reshape(DCH, 128).T)


def rope_consts():
    half = 16
    inv = (np.float32(500000.0) ** (-np.arange(half, dtype=np.float32) / np.float32(half))).astype(np.float32)
    rc = np.zeros((32, 4), np.float32)
    rc[:, 0] = np.concatenate([inv, inv])
    rc[:16, 1] = -1.0
    rc[16:, 1] = 1.0
    rc[:, 2] = -np.pi
    PT = np.zeros((128, 128), np.float32)
    for i in range(16):
        PT[i + 16, i] = 1.0
        PT[i, i + 16] = 1.0
    return rc, PT


def l1_inputs(inputs, core):
    b, half = core // 2, core % 2
    tok = slice(half * T_CORE, (half + 1) * T_CORE)
    x = np.asarray(inputs["x"], np.float32)[b, tok]
    xT = np.ascontiguousarray(x.T.reshape(DCH, 128, T_CORE).transpose(1, 0, 2))
    return xT


def l1_shared(inputs):
    wg, wu, wd = relayout_ffn(inputs["ffn_w_gate"][0, 0], inputs["ffn_w_up"][0, 0], inputs["ffn_w_down"][0, 0])
    w_in = np.asarray(inputs["hyb_w_in"], np.float32)[0]
    cols_fm = np.concatenate([w_in[:, 0:512], w_in[:, 1024:1536], w_in[:, 1536:2048]], axis=1)
    wfm = np.ascontiguousarray(cols_fm.reshape(DCH, 128, 12, 128).transpose(2, 1, 0, 3))
    cols_tm = np.stack([w_in[:, 512:1024], w_in[:, 2048:2560]], axis=0)
    wtm = np.ascontiguousarray(cols_tm.reshape(2, DCH, 128, 512).transpose(0, 2, 1, 3))
    rc, PT = rope_consts()
    ws = np.asarray(inputs["gmlp_w_s"], np.float32)[0]
    wsT = np.ascontiguousarray(ws.transpose(2, 0, 1))
    tril = np.tril(np.ones((128, 128), np.float32))
    trilT = np.ascontiguousarray(np.broadcast_to(tril.T[:, None, :], (128, 4, 128)))
    bsB = np.ascontiguousarray(np.broadcast_to(np.asarray(inputs["gmlp_b_s"], np.float32)[0][None], (128, 4, 128)))
    vnorm = np.ascontiguousarray(np.broadcast_to(np.asarray(inputs["gmlp_v_norm"], np.float32)[0][None], (128, 512)))
    return {
        "g_ffn": fm_vec(inputs["ffn_norm"][0, 0]), "g_mix": fm_vec(inputs["mix_norm"][0]),
        "wg0": wg, "wu0": wu, "wd0": wd, "win_fm": wfm, "win_tm": wtm,
        "ropec": rc, "PT": PT, "vnorm": vnorm, "wsT": wsT, "trilT": trilT, "bsB": bsB,
    }


def run_l1(inputs, cores=None):
    cores = list(range(N_CORES)) if cores is None else cores
    nc = build_l1()
    shared = l1_shared(inputs)
    pos = np.asarray(inputs["positions"], np.int32)
    maps = []
    for c in cores:
        b, half = c // 2, c % 2
        m = dict(shared)
        m["xT"] = l1_inputs(inputs, c)
        m["pos"] = np.ascontiguousarray(np.broadcast_to(pos[b, half * T_CORE:(half + 1) * T_CORE][None], (32, T_CORE)))
        maps.append(m)
    res = run_bass_kernel_spmd(nc, maps, core_ids=list(range(len(cores))))
    return res.results


ATT_SCALE = 128.0 ** -0.5
NEG = -1e30


def emit_proj_fm(tk, nc, st, ps, wbuf, w_chunks, in_chunks, epilogue):
    KC = len(in_chunks)
    for j, wsrc in enumerate(w_chunks):
        wi = st["w"] % len(wbuf)
        st["w"] += 1
        wt = wbuf[wi]
        tk.dma("pool", f"dw{wi}", lambda wt=wt, wsrc=wsrc: nc.gpsimd.dma_start(out=wt[:, 0:KC, :], in_=wsrc), writes=[f"w{wi}"])
        bs = (st["pj"] % 2) * 4
        st["pj"] += 1
        for c, (iap, ikey) in enumerate(in_chunks):
            for tt in range(NT):
                tk.op("pe", lambda wt=wt, c=c, tt=tt, bs=bs, iap=iap: nc.tensor.matmul(
                    ps[:, bs + tt, :], lhsT=wt[:, c, :], rhs=iap[:, tsl(tt)], start=(c == 0), stop=(c == KC - 1)),
                    reads=[f"w{wi}", ikey], writes=[f"ps{bs + tt}"])
        epilogue(j, bs)


def emit_resid_epilogue(tk, nc, ps, xT):
    def ep(j, bs):
        for tt in range(NT):
            tk.op("dve", lambda j=j, tt=tt, bs=bs: nc.vector.tensor_tensor(out=xT[:, j, tsl(tt)], in0=ps[:, bs + tt, :], in1=xT[:, j, tsl(tt)], op=ALU.add),
                  reads=[f"ps{bs + tt}", f"xT{j}"], writes=[f"xT{j}"])
    return ep


def emit_final_norm(tk, nc, xT, g, gkey, hn, rstd, ps, ones, sg, out_d, epsc):
    for c in range(DCH):
        tk.op("act", lambda c=c: nc.scalar.activation(out=hn[:, c, :], in_=xT[:, c, :], func=AF.Square), reads=[f"xT{c}"], writes=[f"hn{c}"])
    for tt in range(NT):
        for c in range(DCH):
            tk.op("pe", lambda c=c, tt=tt: nc.tensor.matmul(ps[:, tt, :], lhsT=ones[:, :], rhs=hn[:, c, tsl(tt)], start=(c == 0), stop=(c == DCH - 1)),
                  reads=[f"hn{c}", "ones"], writes=[f"ps{tt}"])
    for tt in range(NT):
        sl = tsl(tt)
        tk.op("act", lambda tt=tt, sl=sl: nc.scalar.activation(out=rstd[:, sl], in_=ps[:, tt, :], func=AF.Sqrt, scale=1.0 / D_MODEL, bias=epsc[:, 0:1]),
              reads=[f"ps{tt}", "epsc"], writes=[f"rstd{tt}"])
        tk.op("dve", lambda sl=sl: nc.vector.reciprocal(out=rstd[:, sl], in_=rstd[:, sl]), reads=[f"rstd{tt}"], writes=[f"rstd{tt}"])
    for c in range(DCH):
        si = c % 2
        for tt in range(NT):
            sl = tsl(tt)
            tk.op("dve", lambda c=c, sl=sl, si=si: nc.vector.scalar_tensor_tensor(out=sg[:, si, sl], in0=xT[:, c, sl], scalar=g[:, c:c + 1], in1=rstd[:, sl],
                                                                                   op0=ALU.mult, op1=ALU.mult),
                  reads=[f"xT{c}", f"rstd{tt}", gkey], writes=[f"sg{si}_{tt}"])
        tk.dma("sp", "dout", lambda c=c, si=si: nc.sync.dma_start(out=out_d[:, c, :], in_=sg[:, si, :]), reads=[f"sg{si}_{tt}" for tt in range(NT)])


def alloc_ffn_set(nc, es, hchunks=6, sfx=""):
    sb = lambda name, shape, dt: es.enter_context(nc.sbuf_tensor(name + sfx, shape, dt))
    d = {}
    d["xT"] = sb("xT_s", [128, DCH, T_CORE], F32)
    d["hn"] = sb("hn", [128, DCH, T_CORE], BF16)
    d["H"] = sb("H", [128, hchunks, T_CORE], BF16)
    d["sg"] = sb("sg", [128, 2, T_CORE], F32)
    d["rstd"] = sb("rstd", [128, T_CORE], F32)
    d["ones"] = sb("ones", [128, 128], BF16)
    d["epsc"] = sb("epsc", [128, 1], F32)
    d["wbuf"] = [sb(f"wb{i}", [128, DCH, 128], BF16) for i in range(4)]
    d["wdbuf"] = [sb(f"wdb{i}", [128, 6, 128], BF16) for i in range(3)]
    return d


def build_l2(nc=None, ov=None, sfx="", nblocks=lambda j: 9 + j, sem_es=None, conv_here=False):
    nc = nc or bass.Bass("TRN2", target_bir_lowering=False)
    ov = ov or {}
    di, do = io_makers(nc, ov)
    x_d = di("xT", [128, DCH, T_CORE], F32)
    a_d = di("aT", [128, 4, T_CORE], BF16)
    q_d = di("QT", [128, 4, T_CORE], BF16)
    kparts = ov.get("KT_parts")
    vparts = ov.get("V_parts")
    k_d = None if kparts else di("KTp", [128, 4, 2 * T_CORE], BF16)
    v_d = None if vparts else di("Vp", [128, 32, 512], BF16)
    vb_d = di("vbias", [128, 8, 16], F32)
    no_d = di("notown", [128, 8, 16], F32)
    e_d = di("Esel", [32, 16, 128], BF16)
    tri_d = di("tri", [128, 4, 256], BF16)
    idf_d = di("identf", [128, 128], F32)
    wo_d = di("wo", [8, 128, DCH, 128], F32)
    gA_d = di("g_ffn_a", [128, DCH], F32)
    gB_d = di("g_ffn_b", [128, DCH], F32)
    gM_d = di("g_mix", [128, DCH], F32)
    wgA, wuA, wdA = ffn_dram(nc, "A", ov)
    wgB, wuB, wdB = ffn_dram(nc, "B", ov)
    wc_d = di("wc", [24, 128, DCH, 128], F32)
    xo_d = do("xT_o", [128, DCH, T_CORE], F32)
    if conv_here:
        cw_d = di("cw", [128, DCH, 3], F32)
        g_d = do("g_o", [128, DCH, T_CORE], BF16)
        zh_d = do("zh_o", [128, DCH, 2], F32)
        halo_in = ov.get("halo_in")
    else:
        z_d = do("zT_o", [128, DCH, T_CORE], F32)
        bg_d = do("bgT_o", [128, DCH, T_CORE], F32)
    with ExitStack() as es:
        sb = lambda name, shape, dt: es.enter_context(nc.sbuf_tensor(name + sfx, shape, dt))
        tk = Trk(nc, sem_es or es, ["dx", "dout"] + FFN_STREAMS, sfx)
        bT = sb("bT", [128, 4, T_CORE], BF16)
        ps = es.enter_context(nc.psum_tensor("ps" + sfx, [128, 8, 512], F32))
        st = {"w": 0, "sg": 0, "wd": 0, "dn": 0, "pj": 0}
        with ExitStack() as ea:
            sa = lambda name, shape, dt: ea.enter_context(nc.sbuf_tensor(name + sfx, shape, dt))
            QT = sa("QT_s", [128, 4, T_CORE], BF16)
            KT = sa("KT_s", [128, 4, 2 * T_CORE], BF16)
            V = sa("V_s", [128, 32, 512], BF16)
            vbias = sa("vbias_s", [128, 8, 16], F32)
            notown = sa("notown_s", [128, 8, 16], F32)
            E = sa("E_s", [32, 16, 128], BF16)
            tri = sa("tri_s", [128, 4, 256], BF16)
            identf = sa("identf_s", [128, 128], F32)
            identb = sa("identb", [128, 128], BF16)
            kms = sa("kms", [128, 4, 16], F32)
            kmT = sa("kmT", [128, 4, 16], BF16)
            gm = sa("gm", [128, 2, 16], F32)
            top8 = sa("top8", [128, 2, 8], F32)
            thr = sa("thr", [128, 2, 1], F32)
            G32 = sa("G32", [128, 2, 32], F32)
            biasT = sa("biasT", [32, 2, 256], BF16)
            PTs = sa("PTs", [128, 6, 256], BF16)
            rL = sa("rL", [128, 2, 256], F32)
            onesb = sa("onesb", [128, 128], BF16)
            for h in range(4):
                tk.dma("sp", "dx", lambda h=h: nc.sync.dma_start(out=QT[:, h, :], in_=q_d[:, h, :]), writes=[f"QT{h}"])
                if kparts:
                    for hf in range(2):
                        tk.dma("sp", "dx", lambda h=h, hf=hf: nc.sync.dma_start(out=KT[:, h, hf * T_CORE:(hf + 1) * T_CORE], in_=kparts[hf][:, h, :]), writes=[f"KT{h}"])
                else:
                    tk.dma("sp", "dx", lambda h=h: nc.sync.dma_start(out=KT[:, h, :], in_=k_d[:, h, :]), writes=[f"KT{h}"])
            for v4 in range(4):
                if vparts:
                    src = vparts[v4 // 2][(v4 % 2) * 8:(v4 % 2 + 1) * 8].rearrange("t p c -> p t c")
                    tk.dma("sp", "dx", lambda v4=v4, src=src: nc.sync.dma_start(out=V[:, v4 * 8:(v4 + 1) * 8, :], in_=src), writes=[f"V{v4}"])
                else:
                    tk.dma("sp", "dx", lambda v4=v4: nc.sync.dma_start(out=V[:, v4 * 8:(v4 + 1) * 8, :], in_=v_d[:, v4 * 8:(v4 + 1) * 8, :]), writes=[f"V{v4}"])
            for (t_s, t_d, key) in ((vbias, vb_d, "vbias"), (notown, no_d, "notown"), (E, e_d, "E"), (tri, tri_d, "tri"), (identf, idf_d, "identf")):
                tk.dma("sp", "dx", lambda t_s=t_s, t_d=t_d: nc.sync.dma_start(out=t_s[:], in_=t_d), writes=[key])
            tk.op("dve", lambda: nc.vector.tensor_copy(out=identb[:, :], in_=identf[:, :]), reads=["identf"], writes=["identb"])
            tk.op("dve", lambda: nc.vector.memset(onesb[:, :], 1.0), writes=["onesb"])
            tk.op("dve", lambda: nc.vector.memset(G32[:, :, :], 0.0), writes=["G32_0", "G32_1"])
            for h in range(4):
                tk.op("dve", lambda h=h: nc.vector.reduce_sum(out=kms[:, h, :], in_=KT[:, h, :].rearrange("p (n k) -> p n k", k=256), axis=AX),
                      reads=[f"KT{h}"], writes=["kms"])
            tk.op("dve", lambda: nc.vector.tensor_scalar(out=kmT[:, :, :], in0=kms[:, :, :], scalar1=1.0 / 256.0, scalar2=None, op0=ALU.mult),
                  reads=["kms"], writes=["kmT"])
            SB = [0, 1, 2, 5, 6]
            NPT = 6
            OB, LB = 3, 4
            its = [(h, j) for h in range(4) for j in range(8)]
            sctr = [0]
            pctr = [0]

            def preA(ctr):
                h, j = its[ctr]
                for qt in range(2):
                    qsl = slice(j * 256 + qt * 128, j * 256 + (qt + 1) * 128)
                    tk.op("pe", lambda: nc.tensor.matmul(ps[:, 7, qt * 16:(qt + 1) * 16], lhsT=QT[:, h, qsl], rhs=kmT[:, h, :], start=True, stop=True),
                          reads=[f"QT{h}", "kmT"], writes=["ps7"])
                for qt in range(2):
                    tk.op("dve", lambda: nc.vector.tensor_tensor(out=gm[:, qt, :], in0=ps[:, 7, qt * 16:(qt + 1) * 16], in1=vbias[:, j, :], op=ALU.add),
                          reads=["ps7", "vbias"], writes=[f"gm{qt}"])
                    tk.op("dve", lambda: nc.vector.max(out=top8[:, qt, :], in_=gm[:, qt, :]), reads=[f"gm{qt}"], writes=[f"top8{qt}"])
                    tk.op("dve", lambda: nc.vector.tensor_scalar(out=thr[:, qt, :], in0=top8[:, qt, 2:3], scalar1=-1e29, scalar2=None, op0=ALU.max),
                          reads=[f"top8{qt}"], writes=[f"thr{qt}"])
                    tk.op("dve", lambda: nc.vector.tensor_scalar(out=G32[:, qt, 0:16], in0=gm[:, qt, :], scalar1=thr[:, qt, 0:1], scalar2=None, op0=ALU.is_ge),
                          reads=[f"gm{qt}", f"thr{qt}"], writes=[f"G32_{qt}"])
                    tk.op("dve", lambda: nc.vector.tensor_scalar(out=G32[:, qt, 0:16], in0=G32[:, qt, 0:16], scalar1=-1.0, scalar2=1e30, op0=ALU.add, op1=ALU.mult),
                          reads=[f"G32_{qt}"], writes=[f"G32_{qt}"])
                    tk.op("dve", lambda: nc.vector.tensor_tensor(out=G32[:, qt, 0:16], in0=G32[:, qt, 0:16], in1=notown[:, j, :], op=ALU.mult),
                          reads=[f"G32_{qt}", "notown"], writes=[f"G32_{qt}"])

            def preB(ctr):
                bb = ctr % 2
                for qt in range(2):
                    tk.op("pe", lambda: nc.tensor.transpose(out=ps[0:32, 7, 64 + qt * 128:64 + (qt + 1) * 128], in_=G32[:, qt, :], identity=identf[:, :]),
                          reads=[f"G32_{qt}", "identf"], writes=["ps7"])
                tk.op("act", lambda: nc.scalar.copy(out=biasT[:, bb, :], in_=ps[0:32, 7, 64:320]), reads=["ps7"], writes=[f"biasT{bb}"])

            preA(0)
            preB(0)
            for ctr, (h, j) in enumerate(its):
                bb = ctr % 2
                rb = ctr % 2
                qs = slice(j * 256, (j + 1) * 256)
                nb = nblocks(j)
                tiles = [(n, kt) for n in range(nb) for kt in range(2)]
                NTI = len(tiles)
                sbase, pbase = sctr[0], pctr[0]
                sctr[0] += NTI
                pctr[0] += NTI

                def emit_S(i):
                    n, kt = tiles[i]
                    sbk = SB[(sbase + i) % len(SB)]
                    ktile = 2 * n + kt
                    extra = (n == j) or (n == 8 + j)
                    tk.op("pe", lambda: nc.tensor.matmul(ps[:, sbk, 0:256], lhsT=KT[:, h, ktile * 128:(ktile + 1) * 128], rhs=QT[:, h, qs], start=True, stop=False),
                          reads=[f"KT{h}", f"QT{h}"], writes=[f"ps{sbk}"])
                    tk.op("pe", lambda: nc.tensor.matmul(ps[:, sbk, 0:256], lhsT=E[:, n, :], rhs=biasT[:, bb, :], start=False, stop=(not extra)),
                          reads=["E", f"biasT{bb}"], writes=[f"ps{sbk}"])
                    if extra:
                        ti = (0 if n == j else 2) + kt
                        tk.op("pe", lambda: nc.tensor.matmul(ps[:, sbk, 0:256], lhsT=identb[:, :], rhs=tri[:, ti, :], start=False, stop=True),
                              reads=["identb", "tri"], writes=[f"ps{sbk}"])

                def emit_EXP(i):
                    sbk = SB[(sbase + i) % len(SB)]
                    pb = (pbase + i) % NPT
                    tk.op("act", lambda: nc.scalar.activation(out=PTs[:, pb, :], in_=ps[:, sbk, 0:256], func=AF.Exp, scale=ATT_SCALE),
                          reads=[f"ps{sbk}"], writes=[f"PTs{pb}"])

                def emit_PV(i):
                    n, kt = tiles[i]
                    pb = (pbase + i) % NPT
                    ktile = 2 * n + kt
                    tk.op("pe", lambda: nc.tensor.matmul(ps[:, OB, 0:256], lhsT=V[:, ktile, h * 128:(h + 1) * 128], rhs=PTs[:, pb, :], start=(i == 0), stop=(i == NTI - 1)),
                          reads=[f"V{ktile // 8}", f"PTs{pb}"], writes=[f"ps{OB}"])
                    tk.op("pe", lambda: nc.tensor.matmul(ps[:, LB, 0:256], lhsT=onesb[:, :], rhs=PTs[:, pb, :], start=(i == 0), stop=(i == NTI - 1)),
                          reads=["onesb", f"PTs{pb}"], writes=[f"ps{LB}"])

                DEPTH = 4
                for i in range(min(DEPTH, NTI)):
                    emit_S(i)
                if ctr + 1 < len(its):
                    preA(ctr + 1)
                for i in range(NTI):
                    if i + DEPTH < NTI:
                        emit_S(i + DEPTH)
                    emit_EXP(i)
                    emit_PV(i)
                if ctr + 1 < len(its):
                    preB(ctr + 1)
                tk.op("dve", lambda: nc.vector.reciprocal(out=rL[:, rb, :], in_=ps[:, LB, 0:256]), reads=[f"ps{LB}"], writes=[f"rL{rb}"])
                tk.op("dve", lambda: nc.vector.tensor_tensor(out=bT[:, h, qs], in0=ps[:, OB, 0:256], in1=rL[:, rb, :], op=ALU.mult),
                      reads=[f"ps{OB}", f"rL{rb}"], writes=[f"bT{h}"])
            tk.barrier()
        with ExitStack() as eb:
            f = alloc_ffn_set(nc, eb, hchunks=6, sfx=sfx)
            xT, hn, H, sg, rstd, ones, wbuf, wdbuf, epsc = (f[k] for k in ("xT", "hn", "H", "sg", "rstd", "ones", "wbuf", "wdbuf", "epsc"))
            sbb = lambda name, shape, dt: eb.enter_context(nc.sbuf_tensor(name + sfx, shape, dt))
            gA = sbb("gA", [128, DCH], F32)
            gB = sbb("gB", [128, DCH], F32)
            gM = sbb("gM", [128, DCH], F32)
            for c in range(DCH):
                tk.dma("sp", "dx", lambda c=c: nc.sync.dma_start(out=xT[:, c, :], in_=x_d[:, c, :]), writes=[f"xT{c}"])
            for c in range(4):
                tk.dma("sp", "dx", lambda c=c: nc.sync.dma_start(out=H[:, c, :], in_=a_d[:, c, :]), writes=[f"H{c}"])
            for (t_s, t_d, key) in ((gA, gA_d, "gA"), (gB, gB_d, "gB"), (gM, gM_d, "gM")):
                tk.dma("sp", "dx", lambda t_s=t_s, t_d=t_d: nc.sync.dma_start(out=t_s[:], in_=t_d), writes=[key])
            tk.op("dve", lambda: nc.vector.memset(ones[:, :], 1.0), writes=["ones"])
            tk.op("dve", lambda: nc.vector.memset(epsc[:, :], RMS_EPS), writes=["epsc"])
            in_chunks = [(H[:, c, :], f"H{c}") for c in range(4)] + [(bT[:, h, :], f"bT{h}") for h in range(4)]
            emit_proj_fm(tk, nc, st, ps, wbuf, [wo_d[j] for j in range(8)], in_chunks, emit_resid_epilogue(tk, nc, ps, xT))
            emit_rmsnorm(tk, nc, xT, gA, "gA", hn, rstd, ps, ones, epsc)
            emit_ffn(tk, nc, st, xT, hn, H, sg, ps, wbuf, wdbuf, wgA, wuA, wdA)
            emit_rmsnorm(tk, nc, xT, gB, "gB", hn, rstd, ps, ones, epsc)
            emit_ffn(tk, nc, st, xT, hn, H, sg, ps, wbuf, wdbuf, wgB, wuB, wdB)
            emit_rmsnorm(tk, nc, xT, gM, "gM", hn, rstd, ps, ones, epsc)
            for c in range(DCH):
                tk.dma("sp", "dout", lambda c=c: nc.sync.dma_start(out=xo_d[:, c, :], in_=xT[:, c, :]), reads=[f"xT{c}"])
            sgk = lambda si: [f"sg{si}_{tt}" for tt in range(NT)]

            def conv_ep(jj, bs):
                if jj < 16:
                    i, si = jj // 2, (jj // 2) % 2
                    if jj % 2 == 0:
                        for tt in range(NT):
                            tk.op("act", lambda tt=tt, si=si, bs=bs: nc.scalar.copy(out=sg[:, si, tsl(tt)], in_=ps[:, bs + tt, :]),
                                  reads=[f"ps{bs + tt}"], writes=[f"sg{si}_{tt}"])
                    else:
                        for tt in range(NT):
                            tk.op("dve", lambda tt=tt, si=si, bs=bs: nc.vector.tensor_tensor(out=sg[:, si, tsl(tt)], in0=sg[:, si, tsl(tt)], in1=ps[:, bs + tt, :], op=ALU.mult),
                                  reads=[f"sg{si}_{tt}", f"ps{bs + tt}"], writes=[f"sg{si}_{tt}"])
                        tk.dma("sp", "dout", lambda i=i, si=si: nc.sync.dma_start(out=z_d[:, i, :], in_=sg[:, si, :]), reads=sgk(si))
                else:
                    i, si = jj - 16, jj % 2
                    for tt in range(NT):
                        tk.op("act", lambda tt=tt, si=si, bs=bs: nc.scalar.copy(out=sg[:, si, tsl(tt)], in_=ps[:, bs + tt, :]),
                              reads=[f"ps{bs + tt}"], writes=[f"sg{si}_{tt}"])
                    tk.dma("sp", "dout", lambda i=i, si=si: nc.sync.dma_start(out=bg_d[:, i, :], in_=sg[:, si, :]), reads=sgk(si))

            if not conv_here:
                emit_proj_fm(tk, nc, st, ps, wbuf, [wc_d[j] for j in range(24)], [(hn[:, c, :], f"hn{c}") for c in range(DCH)], conv_ep)
            else:
                cw = sbb("cw_s", [128, DCH, 3], F32)
                zc = [sbb(f"zc{i}", [128, T_CORE + 2], F32) for i in range(2)]
                gout = [sbb(f"gout{i}", [128, T_CORE], BF16) for i in range(2)]
                tk.dma("sp", "dx", lambda: nc.sync.dma_start(out=cw[:], in_=cw_d), writes=["cw"])
                rk = [f"rstd{tt}" for tt in range(NT)]

                def conv_ep2(jj, bs):
                    i, r, si = jj // 3, jj % 3, (jj // 3) % 2
                    if r == 0:
                        for tt in range(NT):
                            tk.op("act", lambda tt=tt, bs=bs: nc.scalar.copy(out=rstd[:, tsl(tt)], in_=ps[:, bs + tt, :]), reads=[f"ps{bs + tt}"], writes=[f"rstd{tt}"])
                    elif r == 1:
                        for tt in range(NT):
                            tk.op("act", lambda tt=tt, si=si, bs=bs: nc.scalar.copy(out=sg[:, si, tsl(tt)], in_=ps[:, bs + tt, :]),
                                  reads=[f"ps{bs + tt}"], writes=[f"sg{si}_{tt}"])
                    else:
                        if halo_in is None:
                            tk.op("dve", lambda si=si: nc.vector.memset(zc[si][:, 0:2], 0.0), writes=[f"zc{si}"])
                        else:
                            tk.dma("sp", "dx", lambda si=si, i=i: nc.sync.dma_start(out=zc[si][:, 0:2], in_=halo_in[:, i, :]), writes=[f"zc{si}"])
                        for tt in range(NT):
                            o = tt * 512
                            tk.op("dve", lambda tt=tt, si=si, bs=bs, o=o: nc.vector.tensor_tensor(out=zc[si][:, 2 + o:2 + o + 512], in0=sg[:, si, tsl(tt)], in1=ps[:, bs + tt, :], op=ALU.mult),
                                  reads=[f"sg{si}_{tt}", f"ps{bs + tt}"], writes=[f"zc{si}"])
                        tk.dma("sp", "dout", lambda si=si, i=i: nc.sync.dma_start(out=zh_d[:, i, :], in_=zc[si][:, T_CORE:T_CORE + 2]), reads=[f"zc{si}"])
                        for tt in range(NT):
                            o = tt * 512
                            sl = tsl(tt)
                            key = f"sg{si}_{tt}"
                            tk.op("dve", lambda si=si, i=i, sl=sl, o=o: nc.vector.tensor_scalar(out=sg[:, si, sl], in0=zc[si][:, o:o + 512], scalar1=cw[:, i, 0:1], scalar2=None, op0=ALU.mult),
                                  reads=[f"zc{si}", "cw"], writes=[key])
                            tk.op("dve", lambda si=si, i=i, sl=sl, o=o: nc.vector.scalar_tensor_tensor(out=sg[:, si, sl], in0=zc[si][:, o + 1:o + 513], scalar=cw[:, i, 1:2], in1=sg[:, si, sl],
                                                                                                     op0=ALU.mult, op1=ALU.add),
                                  reads=[f"zc{si}", "cw", key], writes=[key])
                            tk.op("dve", lambda si=si, i=i, sl=sl, o=o: nc.vector.scalar_tensor_tensor(out=sg[:, si, sl], in0=zc[si][:, o + 2:o + 514], scalar=cw[:, i, 2:3], in1=sg[:, si, sl],
                                                                                                     op0=ALU.mult, op1=ALU.add),
                                  reads=[f"zc{si}", "cw", key], writes=[key])
                            tk.op("dve", lambda si=si, sl=sl, tt=tt: nc.vector.tensor_tensor(out=gout[si][:, sl], in0=sg[:, si, sl], in1=rstd[:, sl], op=ALU.mult),
                                  reads=[key, f"rstd{tt}"], writes=[f"gout{si}"])
                        tk.dma("sp", "dout", lambda si=si, i=i: nc.sync.dma_start(out=g_d[:, i, :], in_=gout[si][:, :]), reads=[f"gout{si}"])

                emit_proj_fm(tk, nc, st, ps, wbuf, [wc_d[j] for j in range(24)], [(hn[:, c, :], f"hn{c}") for c in range(DCH)], conv_ep2)
            tk.finish("sp", ["dout"])
            tk.barrier()
    return nc


def build_l3(nc=None, ov=None, sfx="", sem_es=None):
    nc = nc or bass.Bass("TRN2", target_bir_lowering=False)
    ov = ov or {}
    di, do = io_makers(nc, ov)
    x_d = di("xT", [128, DCH, T_CORE], F32)
    g_src = ov.get("g_src")
    z_src = ov.get("z_src")
    halo_src = ov.get("halo_src")
    ze_d = None if (z_src is not None or g_src is not None) else di("zext", [128, DCH, T_CORE + 2], F32)
    bg_d = None if g_src is not None else di("bgT", [128, DCH, T_CORE], F32)
    cw_d = None if g_src is not None else di("cw", [128, DCH, 3], F32)
    wco_d = di("wco", [8, 128, DCH, 128], F32)
    gC_d = di("g_ffn_c", [128, DCH], F32)
    gF_d = di("g_fin", [128, DCH], F32)
    wgC, wuC, wdC = ffn_dram(nc, "C", ov)
    out_d = do("outT", [128, DCH, T_CORE], F32)
    with ExitStack() as es:
        sb = lambda name, shape, dt: es.enter_context(nc.sbuf_tensor(name + sfx, shape, dt))
        tk = Trk(nc, sem_es or es, ["dx", "dz", "dout"] + FFN_STREAMS, sfx)
        f = alloc_ffn_set(nc, es, hchunks=6, sfx=sfx)
        xT, hn, H, sg, rstd, ones, wbuf, wdbuf, epsc = (f[k] for k in ("xT", "hn", "H", "sg", "rstd", "ones", "wbuf", "wdbuf", "epsc"))
        gC = sb("gC", [128, DCH], F32)
        gF = sb("gF", [128, DCH], F32)
        if g_src is None:
            cw = sb("cw_s", [128, DCH, 3], F32)
            zc = [sb(f"zc{i}", [128, T_CORE + 2], F32) for i in range(2)]
            bgc = [sb(f"bgc{i}", [128, T_CORE], F32) for i in range(2)]
        ps = es.enter_context(nc.psum_tensor("ps" + sfx, [128, 8, 512], F32))
        st = {"w": 0, "sg": 0, "wd": 0, "dn": 0, "pj": 0}
        for c in range(DCH):
            tk.dma("sp", "dx", lambda c=c: nc.sync.dma_start(out=xT[:, c, :], in_=x_d[:, c, :]), writes=[f"xT{c}"])
        for (t_s, t_d, key) in ((gC, gC_d, "gC"), (gF, gF_d, "gF")) + (((cw, cw_d, "cw"),) if g_src is None else ()):
            tk.dma("sp", "dx", lambda t_s=t_s, t_d=t_d: nc.sync.dma_start(out=t_s[:], in_=t_d), writes=[key])
        if g_src is not None:
            for c in range(DCH):
                tk.dma("sp", "dx", lambda c=c: nc.sync.dma_start(out=hn[:, c, :], in_=g_src[:, c, :]), writes=[f"hn{c}"])
        tk.op("dve", lambda: nc.vector.memset(ones[:, :], 1.0), writes=["ones"])
        tk.op("dve", lambda: nc.vector.memset(epsc[:, :], RMS_EPS), writes=["epsc"])
        for c in (range(DCH) if g_src is None else ()):
            ci = c % 2
            if z_src is None:
                tk.dma("sp", f"dz", lambda c=c, ci=ci: nc.sync.dma_start(out=zc[ci][:, :], in_=ze_d[:, c, :]), writes=[f"zc{ci}"])
            else:
                tk.dma("sp", f"dz", lambda c=c, ci=ci: nc.sync.dma_start(out=zc[ci][:, 2:T_CORE + 2], in_=z_src[:, c, :]), writes=[f"zc{ci}"])
                if halo_src is None:
                    tk.op("dve", lambda ci=ci: nc.vector.memset(zc[ci][:, 0:2], 0.0), writes=[f"zc{ci}"])
                else:
                    tk.dma("sp", f"dz", lambda c=c, ci=ci: nc.sync.dma_start(out=zc[ci][:, 0:2], in_=halo_src[:, c, T_CORE - 2:T_CORE]), writes=[f"zc{ci}"])
            tk.dma("sp", f"dz", lambda c=c, ci=ci: nc.sync.dma_start(out=bgc[ci][:, :], in_=bg_d[:, c, :]), writes=[f"bgc{ci}"])
            en, eo = "dve", nc.vector
            for tt in range(NT):
                sl = tsl(tt)
                o = tt * 512
                key = f"sg{ci}_{tt}"
                tk.op(en, lambda c=c, ci=ci, sl=sl, o=o, eo=eo: eo.tensor_scalar(out=sg[:, ci, sl], in0=zc[ci][:, o:o + 512], scalar1=cw[:, c, 0:1], scalar2=None, op0=ALU.mult),
                      reads=[f"zc{ci}", "cw"], writes=[key])
                tk.op(en, lambda c=c, ci=ci, sl=sl, o=o, eo=eo: eo.scalar_tensor_tensor(out=sg[:, ci, sl], in0=zc[ci][:, o + 1:o + 513], scalar=cw[:, c, 1:2], in1=sg[:, ci, sl],
                                                                                        op0=ALU.mult, op1=ALU.add),
                      reads=[f"zc{ci}", "cw", key], writes=[key])
                tk.op(en, lambda c=c, ci=ci, sl=sl, o=o, eo=eo: eo.scalar_tensor_tensor(out=sg[:, ci, sl], in0=zc[ci][:, o + 2:o + 514], scalar=cw[:, c, 2:3], in1=sg[:, ci, sl],
                                                                                        op0=ALU.mult, op1=ALU.add),
                      reads=[f"zc{ci}", "cw", key], writes=[key])
                tk.op(en, lambda c=c, ci=ci, sl=sl, eo=eo: eo.tensor_tensor(out=hn[:, c, sl], in0=sg[:, ci, sl], in1=bgc[ci][:, sl], op=ALU.mult),
                      reads=[key, f"bgc{ci}"], writes=[f"hn{c}"])
        emit_proj_fm(tk, nc, st, ps, wbuf, [wco_d[j] for j in range(8)], [(hn[:, c, :], f"hn{c}") for c in range(DCH)], emit_resid_epilogue(tk, nc, ps, xT))
        emit_rmsnorm(tk, nc, xT, gC, "gC", hn, rstd, ps, ones, epsc)
        emit_ffn(tk, nc, st, xT, hn, H, sg, ps, wbuf, wdbuf, wgC, wuC, wdC)
        emit_final_norm(tk, nc, xT, gF, "gF", hn, rstd, ps, ones, sg, out_d, epsc)
        tk.finish("sp", ["dout"])
        tk.barrier()
    return nc


def relayout_fm(W):
    W = np.asarray(W, np.float32)
    nch = W.shape[1] // 128
    return np.ascontiguousarray(W.reshape(DCH, 128, nch, 128).transpose(2, 1, 0, 3))


def attn_consts(half):
    vb = np.zeros((8, 16), np.float32)
    no = np.ones((8, 16), np.float32)
    for j in range(8):
        qblk = 8 * half + j
        vb[j, qblk:] = NEG
        no[j, qblk] = 0.0
    E = np.zeros((32, 16, 128), np.float32)
    for n in range(16):
        E[n, n, :] = 1.0
    tri = np.zeros((128, 4, 256), np.float32)
    k = np.arange(128)[:, None]
    q = np.arange(256)[None, :]
    for kt in range(2):
        m = np.where(kt * 128 + k > q, NEG, 0.0).astype(np.float32)
        tri[:, (0 if half == 0 else 2) + kt, :] = m
    return {
        "vbias": np.ascontiguousarray(np.broadcast_to(vb[None], (128, 8, 16))),
        "notown": np.ascontiguousarray(np.broadcast_to(no[None], (128, 8, 16))),
        "Esel": bf16_np(E), "tri": bf16_np(tri), "identf": np.eye(128, dtype=np.float32),
    }


def run_l2(inputs, r1, cores):
    nc = build_l2()
    wgA, wuA, wdA = relayout_ffn(inputs["ffn_w_gate"][0, 1], inputs["ffn_w_up"][0, 1], inputs["ffn_w_down"][0, 1])
    wgB, wuB, wdB = relayout_ffn(inputs["ffn_w_gate"][1, 0], inputs["ffn_w_up"][1, 0], inputs["ffn_w_down"][1, 0])
    cwi = np.asarray(inputs["conv_w_in"], np.float32)[0]
    cols = []
    for i in range(8):
        cols.append(cwi[:, 1024 + i * 128:1024 + (i + 1) * 128])
        cols.append(cwi[:, 2048 + i * 128:2048 + (i + 1) * 128])
    for i in range(8):
        cols.append(cwi[:, i * 128:(i + 1) * 128])
    shared = {
        "wo": relayout_fm(inputs["hyb_w_out"][0]),
        "g_ffn_a": fm_vec(inputs["ffn_norm"][0, 1]), "g_ffn_b": fm_vec(inputs["ffn_norm"][1, 0]), "g_mix": fm_vec(inputs["mix_norm"][1]),
        "wgA": wgA, "wuA": wuA, "wdA": wdA, "wgB": wgB, "wuB": wuB, "wdB": wdB,
        "wc": relayout_fm(np.concatenate(cols, axis=1)),
    }
    idx = {c: i for i, c in enumerate(cores)}
    maps = []
    for c in cores:
        b, half = c // 2, c % 2
        r0, rp = r1[idx[2 * b]], r1[idx[2 * b + 1]]
        m = dict(shared)
        m.update(attn_consts(half))
        me = r1[idx[c]]
        m["xT"] = np.asarray(me["xT_o"])
        m["aT"] = np.asarray(me["aT_o"])
        m["QT"] = np.ascontiguousarray(np.asarray(me["QKT_o"])[:, 0:4, :])
        m["KTp"] = np.ascontiguousarray(np.concatenate([np.asarray(r0["QKT_o"])[:, 4:8, :], np.asarray(rp["QKT_o"])[:, 4:8, :]], axis=2))
        vp = np.concatenate([np.asarray(r0["V_o"]), np.asarray(rp["V_o"])], axis=0)
        m["Vp"] = np.ascontiguousarray(vp.transpose(1, 0, 2))
        maps.append(m)
    res = run_bass_kernel_spmd(nc, maps, core_ids=list(range(len(cores))))
    return res.results


def run_l3(inputs, r2, cores):
    nc = build_l3()
    wgC, wuC, wdC = relayout_ffn(inputs["ffn_w_gate"][1, 1], inputs["ffn_w_up"][1, 1], inputs["ffn_w_down"][1, 1])
    cwv = np.asarray(inputs["conv_w"], np.float32)[0]
    cw = np.ascontiguousarray(cwv.reshape(3, DCH, 128).transpose(2, 1, 0))
    shared = {
        "cw": cw, "wco": relayout_fm(inputs["conv_w_out"][0]),
        "g_ffn_c": fm_vec(inputs["ffn_norm"][1, 1]), "g_fin": fm_vec(inputs["final_norm"]),
        "wgC": wgC, "wuC": wuC, "wdC": wdC,
    }
    idx = {c: i for i, c in enumerate(cores)}
    maps = []
    for c in cores:
        b, half = c // 2, c % 2
        me = r2[idx[c]]
        z = np.asarray(me["zT_o"])
        if half == 0:
            halo = np.zeros((128, DCH, 2), np.float32)
        else:
            halo = np.asarray(r2[idx[2 * b]]["zT_o"])[:, :, T_CORE - 2:T_CORE]
        m = dict(shared)
        m["xT"] = np.asarray(me["xT_o"])
        m["zext"] = np.ascontiguousarray(np.concatenate([halo, z], axis=2))
        m["bgT"] = np.asarray(me["bgT_o"])
        maps.append(m)
    res = run_bass_kernel_spmd(nc, maps, core_ids=list(range(len(cores))))
    return res.results


def run_all(inputs, cores):
    inputs = {k: np.asarray(v) for k, v in inputs.items()}
    r1 = run_l1(inputs, cores)
    r2 = run_l2(inputs, r1, cores)
    r3 = run_l3(inputs, r2, cores)
    outs = {}
    for i, c in enumerate(cores):
        oT = np.asarray(r3[i]["outT"], np.float32)
        outs[c] = np.ascontiguousarray(oT.transpose(1, 0, 2).reshape(D_MODEL, T_CORE).T)
    return outs, (r1, r2, r3)
def build_fused():
    nc = bass.Bass("TRN2", target_bir_lowering=False)
    ext = lambda name, shape, dt: nc.dram_tensor(name, shape, dt, kind="ExternalInput").ap()
    itn = lambda name, shape, dt: nc.dram_tensor(name, shape, dt, kind="Internal").ap()
    W = {}
    for sfx in ("0", "A", "B", "C"):
        W["wg" + sfx] = ext("wg" + sfx, [FCH, 128, DCH, 128], F32)
        W["wu" + sfx] = ext("wu" + sfx, [FCH, 128, DCH, 128], F32)
        W["wd" + sfx] = ext("wd" + sfx, [DCH, 128, FCH, 128], F32)
    for name, shape, dt in (("g_ffn", [128, DCH], F32), ("g_mix1", [128, DCH], F32), ("win_fm", [12, 128, DCH, 128], F32), ("win_tm", [2, 128, DCH, 512], F32),
                            ("ropec", [32, 4], F32), ("PT", [128, 128], F32), ("vnorm", [128, 512], F32), ("wsT", [128, 4, 128], F32),
                            ("trilT", [128, 4, 128], F32), ("bsB", [128, 4, 128], F32),
                            ("Esel", [32, 16, 128], BF16), ("identf", [128, 128], F32), ("wo", [8, 128, DCH, 128], F32),
                            ("g_ffn_a", [128, DCH], F32), ("g_ffn_b", [128, DCH], F32), ("g_mix2", [128, DCH], F32), ("wc", [24, 128, DCH, 128], F32),
                            ("cw", [128, DCH, 3], F32), ("wco", [8, 128, DCH, 128], F32), ("g_ffn_c", [128, DCH], F32), ("g_fin", [128, DCH], F32)):
        W[name] = ext(name, shape, dt)
    x2 = ext("xT2", [2, 128, DCH, T_CORE], F32)
    pos2 = ext("pos2", [2, 32, T_CORE], I32)
    vb2 = ext("vbias2", [2, 128, 8, 16], F32)
    no2 = ext("notown2", [2, 128, 8, 16], F32)
    tri2 = ext("tri2", [2, 128, 4, 256], BF16)
    out2 = nc.dram_tensor("outT2", [2, 128, DCH, T_CORE], F32, kind="ExternalOutput").ap()
    x1_i = itn("x1_i", [2, 128, DCH, T_CORE], F32)
    a_i = itn("a_i", [2, 128, 4, T_CORE], BF16)
    qk_i = itn("qk_i", [2, 128, 8, T_CORE], BF16)
    v_i = itn("v_i", [2, 16, 128, 512], BF16)
    x4_i = itn("x4_i", [2, 128, DCH, T_CORE], F32)
    g_i = itn("g_i", [2, 128, DCH, T_CORE], BF16)
    zh_i = itn("zh_i", [2, 128, DCH, 2], F32)
    with ExitStack() as sem_es:
        for hf in range(2):
            ov = dict(W)
            ov.update({"g_mix": W["g_mix1"], "xT": x2[hf], "pos": pos2[hf], "xT_o": x1_i[hf], "aT_o": a_i[hf], "QKT_o": qk_i[hf], "V_o": v_i[hf]})
            build_l1(nc, ov, sfx=f"_p1h{hf}", sem_es=sem_es)
        for hf in range(2):
            ov = dict(W)
            ov.update({"g_mix": W["g_mix2"], "xT": x1_i[hf], "aT": a_i[hf], "QT": qk_i[hf][:, 0:4, :],
                       "KT_parts": [qk_i[0][:, 4:8, :], qk_i[1][:, 4:8, :]], "V_parts": [v_i[0], v_i[1]],
                       "vbias": vb2[hf], "notown": no2[hf], "tri": tri2[hf],
                       "xT_o": x4_i[hf], "g_o": g_i[hf], "zh_o": zh_i[hf], "halo_in": (zh_i[0] if hf == 1 else None)})
            nblocks = (lambda j: j + 1) if hf == 0 else (lambda j: 9 + j)
            build_l2(nc, ov, sfx=f"_p2h{hf}", nblocks=nblocks, sem_es=sem_es, conv_here=True)
        for hf in range(2):
            ov = dict(W)
            ov.update({"xT": x4_i[hf], "g_src": g_i[hf], "outT": out2[hf]})
            build_l3(nc, ov, sfx=f"_p3h{hf}", sem_es=sem_es)
    return nc


def fused_shared(inputs):
    s1 = l1_shared(inputs)
    sh = {k: s1[k] for k in ("wg0", "wu0", "wd0", "g_ffn", "win_fm", "win_tm", "ropec", "PT", "vnorm", "wsT", "trilT", "bsB")}
    sh["g_mix1"] = s1["g_mix"]
    wgA, wuA, wdA = relayout_ffn(inputs["ffn_w_gate"][0, 1], inputs["ffn_w_up"][0, 1], inputs["ffn_w_down"][0, 1])
    wgB, wuB, wdB = relayout_ffn(inputs["ffn_w_gate"][1, 0], inputs["ffn_w_up"][1, 0], inputs["ffn_w_down"][1, 0])
    wgC, wuC, wdC = relayout_ffn(inputs["ffn_w_gate"][1, 1], inputs["ffn_w_up"][1, 1], inputs["ffn_w_down"][1, 1])
    cwi = np.asarray(inputs["conv_w_in"], np.float32)[0]
    cols = []
    for i in range(8):
        cols.append(cwi[:, i * 128:(i + 1) * 128])
        cols.append(cwi[:, 1024 + i * 128:1024 + (i + 1) * 128])
        cols.append(cwi[:, 2048 + i * 128:2048 + (i + 1) * 128])
    cwv = np.asarray(inputs["conv_w"], np.float32)[0]
    c0, c1 = attn_consts(0), attn_consts(1)
    sh.update({
        "wgA": wgA, "wuA": wuA, "wdA": wdA, "wgB": wgB, "wuB": wuB, "wdB": wdB, "wgC": wgC, "wuC": wuC, "wdC": wdC,
        "wo": relayout_fm(inputs["hyb_w_out"][0]), "wc": relayout_fm(np.concatenate(cols, axis=1)), "wco": relayout_fm(inputs["conv_w_out"][0]),
        "g_ffn_a": fm_vec(inputs["ffn_norm"][0, 1]), "g_ffn_b": fm_vec(inputs["ffn_norm"][1, 0]), "g_mix2": fm_vec(inputs["mix_norm"][1]),
        "g_ffn_c": fm_vec(inputs["ffn_norm"][1, 1]), "g_fin": fm_vec(inputs["final_norm"]),
        "cw": np.ascontiguousarray(cwv.reshape(3, DCH, 128).transpose(2, 1, 0)),
        "Esel": c0["Esel"], "identf": c0["identf"],
        "vbias2": np.stack([c0["vbias"], c1["vbias"]]), "notown2": np.stack([c0["notown"], c1["notown"]]), "tri2": np.stack([c0["tri"], c1["tri"]]),
    })
    return sh


def run_fused(inputs, batches):
    inputs = {k: np.asarray(v) for k, v in inputs.items()}
    nc = build_fused()
    sh = fused_shared(inputs)
    pos = np.asarray(inputs["positions"], np.int32)
    maps = []
    for b in batches:
        m = dict(sh)
        m["xT2"] = np.stack([l1_inputs(inputs, 2 * b), l1_inputs(inputs, 2 * b + 1)])
        m["pos2"] = np.ascontiguousarray(np.broadcast_to(pos[b].reshape(2, 1, T_CORE), (2, 32, T_CORE)))
        maps.append(m)
    res = run_bass_kernel_spmd(nc, maps, core_ids=list(range(len(batches))))
    outs = {}
    for i, b in enumerate(batches):
        oT = np.asarray(res.results[i]["outT2"], np.float32)
        outs[b] = np.concatenate([oT[hf].transpose(1, 0, 2).reshape(D_MODEL, T_CORE).T for hf in range(2)], axis=0)
    return outs


def kernel(**inputs):
    outs = run_fused(inputs, list(range(BATCH)))
    return np.ascontiguousarray(np.stack([outs[b] for b in range(BATCH)]).astype(np.float32))
```

```python
import numpy as np
import ml_dtypes
import concourse.bass as bass
import concourse.mybir as mybir
from concourse.bass_utils import run_bass_kernel_spmd

F32 = mybir.dt.float32
BF16 = mybir.dt.bfloat16
I32 = mybir.dt.int32
ALU = mybir.AluOpType
AF = mybir.ActivationFunctionType

N_CORES = 8
D_MODEL = 1024
SEQ = 4096
BATCH = 4
T_CORE = 2048
DCH = D_MODEL // 128
D_FF = 2816
FCH = D_FF // 128
RMS_EPS = 1e-6


from contextlib import ExitStack

AX = mybir.AxisListType.X
GROUPS = [(0, 6), (6, 12), (12, 17), (17, 22)]
NT = T_CORE // 512
GELU_C = 0.044715
GELU_S = 1.5957691216057308


def bf16_np(a):
    return np.asarray(a, dtype=np.float32).astype(ml_dtypes.bfloat16)


class Trk:
    def __init__(self, nc, es, dma_streams, sfx=""):
        self.nc = nc
        self.eng = {"pe": nc.tensor, "act": nc.scalar, "dve": nc.vector, "pool": nc.gpsimd, "sp": nc.sync}
        names = list(self.eng) + list(dma_streams)
        self.dma_streams = set(dma_streams)
        self.sem = {n: es.enter_context(nc.semaphore("s_" + n + sfx)) for n in names}
        self.cnt = {k: 0 for k in names}
        self.known = {e: {} for e in self.eng}
        self.lastw = {}
        self.readers = {}

    def _wait(self, e, semname, val):
        if val <= 0:
            return
        if semname in self.dma_streams:
            val = self.cnt[semname]
        if self.known[e].get(semname, 0) >= val:
            return
        if e == "pe" and semname == "pe":
            return
        self.eng[e].wait_ge(self.sem[semname], val)
        self.known[e][semname] = val

    def deps(self, e, reads=(), writes=()):
        for k in reads:
            if k in self.lastw:
                self._wait(e, *self.lastw[k])
        for k in writes:
            if k in self.lastw:
                self._wait(e, *self.lastw[k])
            for r in self.readers.get(k, ()):
                self._wait(e, *r)

    def done(self, ins, semname, inc, reads=(), writes=()):
        self.cnt[semname] += inc
        ins.then_inc(self.sem[semname], inc)
        tag = (semname, self.cnt[semname])
        for k in reads:
            self.readers.setdefault(k, []).append(tag)
        for k in writes:
            self.lastw[k] = tag
            self.readers[k] = []
        return tag

    @staticmethod
    def _excl(reads, writes):
        r = [k for k in reads if not k.startswith("ps")]
        w = list(writes) + [k for k in reads if k.startswith("ps") and k not in writes]
        return r, w

    def op(self, e, fn, reads=(), writes=()):
        reads, writes = self._excl(reads, writes)
        self.deps(e, reads, writes)
        return self.done(fn(), e, 1, reads, writes)

    def barrier(self):
        for e in self.eng:
            for sname, c in self.cnt.items():
                if c and not (e == sname):
                    if self.known[e].get(sname, 0) < c:
                        self.eng[e].wait_ge(self.sem[sname], c)
                        self.known[e][sname] = c

    def dma(self, e, stream, fn, reads=(), writes=()):
        self.deps(e, reads, writes)
        return self.done(fn(), stream, 16, reads, writes)

    def finish(self, e, streams):
        for s in streams:
            if self.cnt[s]:
                self.eng[e].wait_ge(self.sem[s], self.cnt[s])


def tsl(tt):
    return slice(tt * 512, (tt + 1) * 512)


def emit_rmsnorm(tk, nc, xT, g, gkey, hn, rstd, ps, ones, epsc):
    for c in range(DCH):
        tk.op("act", lambda c=c: nc.scalar.activation(out=hn[:, c, :], in_=xT[:, c, :], func=AF.Square),
              reads=[f"xT{c}"], writes=[f"hn{c}"])
    for tt in range(NT):
        for c in range(DCH):
            tk.op("pe", lambda c=c, tt=tt: nc.tensor.matmul(ps[:, tt, :], lhsT=ones[:, :], rhs=hn[:, c, tsl(tt)],
                                                             start=(c == 0), stop=(c == DCH - 1)),
                  reads=[f"hn{c}", "ones"], writes=[f"ps{tt}"])
    for tt in range(NT):
        sl = tsl(tt)
        tk.op("act", lambda tt=tt, sl=sl: nc.scalar.activation(out=rstd[:, sl], in_=ps[:, tt, :], func=AF.Sqrt, scale=1.0 / D_MODEL, bias=epsc[:, 0:1]),
              reads=[f"ps{tt}", "epsc"], writes=[f"rstd{tt}"])
        tk.op("dve", lambda sl=sl: nc.vector.reciprocal(out=rstd[:, sl], in_=rstd[:, sl]), reads=[f"rstd{tt}"], writes=[f"rstd{tt}"])
    for c in range(DCH):
        for tt in range(NT):
            sl = tsl(tt)
            tk.op("dve", lambda c=c, sl=sl: nc.vector.scalar_tensor_tensor(out=hn[:, c, sl], in0=xT[:, c, sl], scalar=g[:, c:c + 1], in1=rstd[:, sl],
                                                                             op0=ALU.mult, op1=ALU.mult),
                  reads=[f"xT{c}", f"rstd{tt}", gkey], writes=[f"hn{c}"])


def emit_ffn(tk, nc, st, xT, hn, H, sg, ps, wbuf, wdbuf, wg_d, wu_d, wd_d):
    NW = len(wbuf)
    for (f0, f1) in GROUPS:
        nf = f1 - f0
        for f in range(f0, f1):
            for which, wsrc in ((0, wg_d), (1, wu_d)):
                wi = st["w"] % NW
                st["w"] += 1
                wt = wbuf[wi]
                tk.dma("pool", f"dw{wi}", lambda wt=wt, wsrc=wsrc, f=f: nc.gpsimd.dma_start(out=wt[:, :, :], in_=wsrc[f]),
                       writes=[f"w{wi}"])
                for c in range(DCH):
                    for tt in range(NT):
                        b = which * 4 + tt
                        tk.op("pe", lambda wt=wt, c=c, tt=tt, b=b: nc.tensor.matmul(
                            ps[:, b, :], lhsT=wt[:, c, :], rhs=hn[:, c, tsl(tt)], start=(c == 0), stop=(c == DCH - 1)),
                            reads=[f"w{wi}", f"hn{c}"], writes=[f"ps{b}"])
            si = st["sg"] % 2
            st["sg"] += 1
            for tt in range(NT):
                tk.op("act", lambda tt=tt, si=si: nc.scalar.activation(out=sg[:, si, tsl(tt)], in_=ps[:, tt, :], func=AF.Silu),
                      reads=[f"ps{tt}"], writes=[f"sg{si}_{tt}"])
            for tt in range(NT):
                tk.op("dve", lambda tt=tt, si=si, f=f: nc.vector.tensor_tensor(out=H[:, f - f0, tsl(tt)], in0=sg[:, si, tsl(tt)], in1=ps[:, 4 + tt, :], op=ALU.mult),
                      reads=[f"sg{si}_{tt}", f"ps{4 + tt}"], writes=[f"H{f - f0}"])
        for j in range(DCH):
            di = st["wd"] % len(wdbuf)
            st["wd"] += 1
            wt = wdbuf[di]
            tk.dma("pool", f"dd{di}", lambda wt=wt, j=j, f0=f0, f1=f1, nf=nf: nc.gpsimd.dma_start(out=wt[:, 0:nf, :], in_=wd_d[j, :, f0:f1, :]),
                   writes=[f"wd{di}"])
            bs = (st["dn"] % 2) * 4
            st["dn"] += 1
            for k in range(nf):
                for tt in range(NT):
                    b = bs + tt
                    tk.op("pe", lambda wt=wt, k=k, tt=tt, b=b: nc.tensor.matmul(
                        ps[:, b, :], lhsT=wt[:, k, :], rhs=H[:, k, tsl(tt)], start=(k == 0), stop=(k == nf - 1)),
                        reads=[f"wd{di}", f"H{k}"], writes=[f"ps{b}"])
            for tt in range(NT):
                b = bs + tt
                tk.op("dve", lambda j=j, tt=tt, b=b: nc.vector.scalar_tensor_tensor(out=xT[:, j, tsl(tt)], in0=ps[:, b, :], scalar=0.5, in1=xT[:, j, tsl(tt)],
                                                                                    op0=ALU.mult, op1=ALU.add),
                      reads=[f"ps{b}", f"xT{j}"], writes=[f"xT{j}"])


def emit_gelu(tk, nc, src, srckey, tmp, tmpkey, out, outkey):
    tk.op("act", lambda: nc.scalar.activation(out=tmp, in_=src, func=AF.Square), reads=[srckey], writes=[tmpkey])
    tk.op("dve", lambda: nc.vector.tensor_scalar(out=tmp, in0=tmp, scalar1=GELU_C, scalar2=1.0, op0=ALU.mult, op1=ALU.add),
          reads=[tmpkey], writes=[tmpkey])
    tk.op("dve", lambda: nc.vector.tensor_tensor(out=tmp, in0=tmp, in1=src, op=ALU.mult), reads=[tmpkey, srckey], writes=[tmpkey])
    tk.op("act", lambda: nc.scalar.activation(out=tmp, in_=tmp, func=AF.Sigmoid, scale=GELU_S), reads=[tmpkey], writes=[tmpkey])
    tk.op("dve", lambda: nc.vector.tensor_tensor(out=out, in0=tmp, in1=src, op=ALU.mult), reads=[tmpkey, srckey], writes=[outkey])


FFN_STREAMS = [f"dw{i}" for i in range(4)] + [f"dd{i}" for i in range(3)]


def ffn_dram(nc, sfx, ov=None):
    ov = ov or {}
    mk = lambda name, shape: ov[name] if name in ov else nc.dram_tensor(name, shape, F32, kind="ExternalInput").ap()
    wg = mk("wg" + sfx, [FCH, 128, DCH, 128])
    wu = mk("wu" + sfx, [FCH, 128, DCH, 128])
    wd = mk("wd" + sfx, [DCH, 128, FCH, 128])
    return wg, wu, wd


def io_makers(nc, ov):
    di = lambda name, shape, dt: ov[name] if name in ov else nc.dram_tensor(name, shape, dt, kind="ExternalInput").ap()
    do = lambda name, shape, dt: ov[name] if name in ov else nc.dram_tensor(name, shape, dt, kind="ExternalOutput").ap()
    return di, do


def build_l1(nc=None, ov=None, sfx="", sem_es=None):
    nc = nc or bass.Bass("TRN2", target_bir_lowering=False)
    ov = ov or {}
    di, do = io_makers(nc, ov)
    x_d = di("xT", [128, DCH, T_CORE], F32)
    gf_d = di("g_ffn", [128, DCH], F32)
    gm_d = di("g_mix", [128, DCH], F32)
    wg_d, wu_d, wd_d = ffn_dram(nc, "0", ov)
    wfm_d = di("win_fm", [12, 128, DCH, 128], F32)
    wtm_d = di("win_tm", [2, 128, DCH, 512], F32)
    pos_d = di("pos", [32, T_CORE], I32)
    rc_d = di("ropec", [32, 4], F32)
    pt_d = di("PT", [128, 128], F32)
    vn_d = di("vnorm", [128, 512], F32)
    ws_d = di("wsT", [128, 4, 128], F32)
    tr_d = di("trilT", [128, 4, 128], F32)
    bs_d = di("bsB", [128, 4, 128], F32)
    xo_d = do("xT_o", [128, DCH, T_CORE], F32)
    ao_d = do("aT_o", [128, 4, T_CORE], BF16)
    qk_d = do("QKT_o", [128, 8, T_CORE], BF16)
    vo_d = do("V_o", [16, 128, 512], BF16)
    with ExitStack() as es:
        sb = lambda name, shape, dt: es.enter_context(nc.sbuf_tensor(name + sfx, shape, dt))
        xT = sb("xT_s", [128, DCH, T_CORE], F32)
        hn = sb("hn", [128, DCH, T_CORE], BF16)
        H = sb("H", [128, 8, T_CORE], BF16)
        sg = sb("sg", [128, 2, T_CORE], F32)
        rstd = sb("rstd", [128, T_CORE], F32)
        gf = sb("gf", [128, DCH], F32)
        gm = sb("gm", [128, DCH], F32)
        ones = sb("ones", [128, 128], BF16)
        epsc = sb("epsc", [128, 1], F32)
        wbuf = [sb(f"wb{i}", [128, DCH, 128], BF16) for i in range(4)]
        wdbuf = [sb(f"wdb{i}", [128, 6, 128], BF16) for i in range(3)]
        posi = sb("posi", [32, T_CORE], I32)
        rc = sb("rc", [32, 4], F32)
        PT = sb("PTs", [128, 128], BF16)
        vnorm = sb("vnorm_s", [128, 512], F32)
        wsT = sb("wsT_s", [128, 4, 128], F32)
        trT = sb("trT_s", [128, 4, 128], F32)
        WmT = sb("WmT", [128, 4, 128], BF16)
        bsB = sb("bsB_s", [128, 4, 128], F32)
        tmpA = sb("tmpA", [128, 2, 512], F32)
        tmpB = sb("tmpB", [128, 2, 512], F32)
        vg = sb("vg", [128, 2, 512], F32)
        vnb = sb("vnb", [128, 2, 512], BF16)
        vt = sb("vt", [128, 2, 512], BF16)
        ao = sb("ao", [128, 2, 4, 128], BF16)
        tmpm = sb("tmpm", [128, 2, 4, 128], F32)
        ss = sb("ss", [128, 4], F32)
        ps = es.enter_context(nc.psum_tensor("ps" + sfx, [128, 8, 512], F32))
        uT = H[:, 4:8, :]
        wtm = H[:, 0:4, :].rearrange("p a (b t) -> p (a b) t", t=512)
        tk = Trk(nc, sem_es or es, ["dx", "dout", "dwt", "dpt"] + FFN_STREAMS, sfx)
        for c in range(DCH):
            tk.dma("sp", "dx", lambda c=c: nc.sync.dma_start(out=xT[:, c, :], in_=x_d[:, c, :]), writes=[f"xT{c}"])
        for (t_s, t_d, key) in ((gf, gf_d, "gf"), (gm, gm_d, "gm"), (posi, pos_d, "posi"), (rc, rc_d, "rc"), (vnorm, vn_d, "vnorm"),
                                (wsT, ws_d, "wsT"), (trT, tr_d, "trT"), (bsB, bs_d, "bsB")):
            tk.dma("sp", "dx", lambda t_s=t_s, t_d=t_d: nc.sync.dma_start(out=t_s[:], in_=t_d), writes=[key])
        tk.dma("pool", "dpt", lambda: nc.gpsimd.dma_start(out=PT[:, :], in_=pt_d[:, :]), writes=["PT"])
        tk.op("dve", lambda: nc.vector.memset(ones[:, :], 1.0), writes=["ones"])
        tk.op("dve", lambda: nc.vector.memset(epsc[:, :], RMS_EPS), writes=["epsc"])
        tk.op("dve", lambda: nc.vector.tensor_tensor(out=WmT[:, :, :], in0=wsT[:, :, :], in1=trT[:, :, :], op=ALU.mult),
              reads=["wsT", "trT"], writes=["WmT"])
        st = {"w": 0, "sg": 0, "wd": 0, "dn": 0}
        emit_rmsnorm(tk, nc, xT, gf, "gf", hn, rstd, ps, ones, epsc)
        emit_ffn(tk, nc, st, xT, hn, H, sg, ps, wbuf, wdbuf, wg_d, wu_d, wd_d)
        emit_rmsnorm(tk, nc, xT, gm, "gm", hn, rstd, ps, ones, epsc)
        for c in range(DCH):
            tk.dma("sp", "dout", lambda c=c: nc.sync.dma_start(out=xo_d[:, c, :], in_=xT[:, c, :]), reads=[f"xT{c}"])
        sgkeys = [f"sg{si}_{tt}" for si in range(2) for tt in range(NT)]
        cosT = sg[0:32, 0, :]
        sinT = sg[0:32, 1, :]
        ang = rstd[0:32, :]
        rkeys = [f"rstd{tt}" for tt in range(NT)]
        tk.op("dve", lambda: nc.vector.tensor_copy(out=ang, in_=posi[:, :]), reads=["posi"] + rkeys, writes=rkeys)
        tk.op("dve", lambda: nc.vector.tensor_scalar(out=ang, in0=ang, scalar1=rc[:, 0:1], scalar2=None, op0=ALU.mult), reads=rkeys + ["rc"], writes=rkeys)
        for (tab, off) in ((sinT, 0.0), (cosT, 0.25)):
            tk.op("dve", lambda tab=tab, off=off: nc.vector.tensor_scalar(out=tab, in0=ang, scalar1=float(1.0 / (2 * np.pi)), scalar2=float(off),
                                                                          op0=ALU.mult, op1=ALU.add), reads=rkeys, writes=sgkeys)
            tk.op("dve", lambda tab=tab: nc.vector.tensor_copy(out=posi[:, :], in_=tab), reads=sgkeys, writes=["posi"])
            tk.op("dve", lambda: nc.vector.tensor_copy(out=tmpA[0:32, :, :].rearrange("p a b -> p (a b)"), in_=posi[:, 0:1024]), reads=["posi"], writes=["tmpA0", "tmpA1"])
            tk.op("dve", lambda: nc.vector.tensor_copy(out=tmpB[0:32, :, :].rearrange("p a b -> p (a b)"), in_=posi[:, 1024:2048]), reads=["posi"], writes=["tmpB0", "tmpB1"])
            tk.op("dve", lambda tab=tab: nc.vector.tensor_tensor(out=tab[:, 0:1024], in0=tab[:, 0:1024], in1=tmpA[0:32, :, :].rearrange("p a b -> p (a b)"), op=ALU.subtract),
                  reads=sgkeys + ["tmpA0", "tmpA1"], writes=sgkeys)
            tk.op("dve", lambda tab=tab: nc.vector.tensor_tensor(out=tab[:, 1024:2048], in0=tab[:, 1024:2048], in1=tmpB[0:32, :, :].rearrange("p a b -> p (a b)"), op=ALU.subtract),
                  reads=sgkeys + ["tmpB0", "tmpB1"], writes=sgkeys)
            tk.op("dve", lambda tab=tab: nc.vector.tensor_scalar(out=tmpA[0:32, :, :].rearrange("p a b -> p (a b)"), in0=tab[:, 0:1024], scalar1=0.5, scalar2=None, op0=ALU.is_gt),
                  reads=sgkeys, writes=["tmpA0", "tmpA1"])
            tk.op("dve", lambda tab=tab: nc.vector.tensor_scalar(out=tmpB[0:32, :, :].rearrange("p a b -> p (a b)"), in0=tab[:, 1024:2048], scalar1=0.5, scalar2=None, op0=ALU.is_gt),
                  reads=sgkeys, writes=["tmpB0", "tmpB1"])
            tk.op("dve", lambda tab=tab: nc.vector.tensor_tensor(out=tab[:, 0:1024], in0=tab[:, 0:1024], in1=tmpA[0:32, :, :].rearrange("p a b -> p (a b)"), op=ALU.subtract),
                  reads=sgkeys + ["tmpA0", "tmpA1"], writes=sgkeys)
            tk.op("dve", lambda tab=tab: nc.vector.tensor_tensor(out=tab[:, 1024:2048], in0=tab[:, 1024:2048], in1=tmpB[0:32, :, :].rearrange("p a b -> p (a b)"), op=ALU.subtract),
                  reads=sgkeys + ["tmpB0", "tmpB1"], writes=sgkeys)
            tk.op("act", lambda tab=tab: nc.scalar.activation(out=tab, in_=tab, func=AF.Sin, scale=float(2 * np.pi)), reads=sgkeys, writes=sgkeys)
        tk.op("dve", lambda: nc.vector.tensor_scalar(out=sinT, in0=sinT, scalar1=rc[:, 1:2], scalar2=None, op0=ALU.mult), reads=sgkeys + ["rc"], writes=sgkeys)
        t4 = [(tmpA[:, 0, :], "tmpA0"), (tmpA[:, 1, :], "tmpA1"), (tmpB[:, 0, :], "tmpB0"), (tmpB[:, 1, :], "tmpB1")]
        pending = None
        for j in range(12):
            wi = st["w"] % 4
            st["w"] += 1
            wt = wbuf[wi]
            tk.dma("pool", f"dw{wi}", lambda wt=wt, j=j: nc.gpsimd.dma_start(out=wt[:, :, :], in_=wfm_d[j]), writes=[f"w{wi}"])
            bs = (j % 2) * 4
            for c in range(DCH):
                for tt in range(NT):
                    tk.op("pe", lambda wt=wt, c=c, tt=tt, bs=bs: nc.tensor.matmul(
                        ps[:, bs + tt, :], lhsT=wt[:, c, :], rhs=hn[:, c, tsl(tt)], start=(c == 0), stop=(c == DCH - 1)),
                        reads=[f"w{wi}", f"hn{c}"], writes=[f"ps{bs + tt}"])
            if pending is not None:
                pending()
                pending = None
            if j < 4:
                for tt in range(NT):
                    b = bs + tt
                    emit_gelu(tk, nc, ps[:, b, :], f"ps{b}", t4[tt][0], t4[tt][1], uT[:, j, tsl(tt)], f"H{4 + j}")
            else:
                idx = (j - 4) % 4
                for tt in range(NT):
                    b = bs + tt
                    tk.op("act", lambda idx=idx, tt=tt, b=b: nc.scalar.copy(out=H[:, idx, tsl(tt)], in_=ps[:, b, :]),
                          reads=[f"ps{b}"], writes=[f"H{idx}"])
                    tk.op("dve", lambda tt=tt, b=b: nc.vector.tensor_tensor(out=t4[tt][0][0:32, :], in0=ps[0:32, b, :], in1=cosT[:, tsl(tt)], op=ALU.mult),
                          reads=[f"ps{b}"] + sgkeys, writes=[t4[tt][1]])

                def part2(j=j, idx=idx, bs=bs):
                    for tt in range(NT):
                        b = bs + tt
                        tk.op("pe", lambda: nc.tensor.matmul(ps[:, b, :], lhsT=PT[:, :], rhs=H[:, idx, tsl(tt)], start=True, stop=True),
                              reads=["PT", f"H{idx}"], writes=[f"ps{b}"])
                        tk.op("dve", lambda: nc.vector.tensor_tensor(out=vg[0:32, tt % 2, :], in0=ps[0:32, b, :], in1=sinT[:, tsl(tt)], op=ALU.mult),
                              reads=[f"ps{b}"] + sgkeys, writes=[f"vg{tt % 2}"])
                        tk.op("dve", lambda: nc.vector.tensor_tensor(out=H[0:32, idx, tsl(tt)], in0=t4[tt][0][0:32, :], in1=vg[0:32, tt % 2, :], op=ALU.add),
                              reads=[t4[tt][1], f"vg{tt % 2}"], writes=[f"H{idx}"])
                    tk.dma("sp", "dout", lambda: nc.sync.dma_start(out=qk_d[:, j - 4, :], in_=H[:, idx, :]), reads=[f"H{idx}"])
                pending = part2
        if pending is not None:
            pending()
        for w in range(2):
            tk.dma("pool", "dwt", lambda w=w: nc.gpsimd.dma_start(out=wtm[:, w * 8:(w + 1) * 8, :], in_=wtm_d[w]),
                   writes=["H0", "H1", "H2", "H3"])
        wkeys = ["H0", "H1", "H2", "H3"]
        def p1a(i):
            par = i % 2
            bA, bB = par * 3, par * 3 + 1
            tok = slice(i * 128, (i + 1) * 128)
            for w, b in ((0, bA), (1, bB)):
                for c in range(DCH):
                    tk.op("pe", lambda w=w, b=b, c=c: nc.tensor.matmul(ps[:, b, :], lhsT=hn[:, c, tok], rhs=wtm[:, w * 8 + c, :], start=(c == 0), stop=(c == DCH - 1)),
                          reads=[f"hn{c}"] + wkeys, writes=[f"ps{b}"])
            tk.op("act", lambda: nc.scalar.copy(out=vt[:, par, :], in_=ps[:, bB, :]), reads=[f"ps{bB}"], writes=[f"vt{par}"])
            tk.dma("sp", "dout", lambda: nc.sync.dma_start(out=vo_d[i], in_=vt[:, par, :]), reads=[f"vt{par}"])

        def p1b(i):
            par = i % 2
            bA = par * 3
            emit_gelu(tk, nc, ps[:, bA, :], f"ps{bA}", tmpA[:, par, :], f"tmpA{par}", vg[:, par, :], f"vg{par}")
            tk.op("act", lambda: nc.scalar.activation(out=tmpB[:, par, :], in_=vg[:, par, :], func=AF.Square), reads=[f"vg{par}"], writes=[f"tmpB{par}"])
            tk.op("dve", lambda: nc.vector.reduce_sum(out=ss[:, par:par + 1], in_=tmpB[:, par, :], axis=AX), reads=[f"tmpB{par}"], writes=[f"ss{par}"])
            tk.op("act", lambda: nc.scalar.activation(out=ss[:, par:par + 1], in_=ss[:, par:par + 1], func=AF.Sqrt, scale=1.0 / 512.0, bias=epsc[:, 0:1]),
                  reads=[f"ss{par}", "epsc"], writes=[f"ss{par}"])
            tk.op("dve", lambda: nc.vector.reciprocal(out=ss[:, par:par + 1], in_=ss[:, par:par + 1]), reads=[f"ss{par}"], writes=[f"ss{par}"])
            tk.op("dve", lambda: nc.vector.scalar_tensor_tensor(out=vnb[:, par, :], in0=vg[:, par, :], scalar=ss[:, par:par + 1], in1=vnorm[:, :], op0=ALU.mult, op1=ALU.mult),
                  reads=[f"vg{par}", f"ss{par}", "vnorm"], writes=[f"vnb{par}"])

        def p2(i):
            par = i % 2
            bC = par * 3 + 2
            tok = slice(i * 128, (i + 1) * 128)
            for g in range(4):
                tk.op("pe", lambda g=g: nc.tensor.matmul(ps[:, bC, g * 128:(g + 1) * 128], lhsT=vnb[:, par, g * 128:(g + 1) * 128], rhs=WmT[:, g, :], start=True, stop=True),
                      reads=[f"vnb{par}", "WmT"], writes=[f"ps{bC}"])
            tk.op("dve", lambda: nc.vector.tensor_tensor(out=tmpm[:, par, :, :], in0=ps[:, bC, :].rearrange("p (g t) -> p g t", g=4), in1=bsB[:, :, :], op=ALU.add),
                  reads=[f"ps{bC}", "bsB"], writes=[f"tmpm{par}"])
            tk.op("dve", lambda: nc.vector.tensor_tensor(out=ao[:, par, :, :], in0=tmpm[:, par, :, :], in1=uT[:, :, tok], op=ALU.mult),
                  reads=[f"tmpm{par}", "H4", "H5", "H6", "H7"], writes=[f"ao{par}"])
            tk.dma("sp", "dout", lambda: nc.sync.dma_start(out=ao_d[:, :, tok], in_=ao[:, par, :, :]), reads=[f"ao{par}"])

        p1a(0)
        p1b(0)
        for i in range(16):
            if i + 1 < 16:
                p1a(i + 1)
            p2(i)
            if i + 1 < 16:
                p1b(i + 1)
        tk.finish("sp", ["dout"])
        tk.barrier()
    return nc


def relayout_ffn(wg, wu, wd):
    a = np.ascontiguousarray(np.asarray(wg, np.float32).reshape(DCH, 128, FCH, 128).transpose(2, 1, 0, 3))
    b = np.ascontiguousarray(np.asarray(wu, np.float32).reshape(DCH, 128, FCH, 128).transpose(2, 1, 0, 3))
    c = np.ascontiguousarray(np.asarray(wd, np.float32).reshape(FCH, 128, DCH, 128).transpose(2, 1, 0, 3))
    return a, b, c


def fm_vec(v):
    return np.ascontiguousarray(np.asarray(v, np.float32).reshape(DCH, 128).T)


def rope_consts():
    half = 16
    inv = (np.float32(500000.0) ** (-np.arange(half, dtype=np.float32) / np.float32(half))).astype(np.float32)
    rc = np.zeros((32, 4), np.float32)
    rc[:, 0] = np.concatenate([inv, inv])
    rc[:16, 1] = -1.0
    rc[16:, 1] = 1.0
    rc[:, 2] = -np.pi
    PT = np.zeros((128, 128), np.float32)
    for i in range(16):
        PT[i + 16, i] = 1.0
        PT[i, i + 16] = 1.0
    return rc, PT


def l1_inputs(inputs, core):
    b, half = core // 2, core % 2
    tok = slice(half * T_CORE, (half + 1) * T_CORE)
    x = np.asarray(inputs["x"], np.float32)[b, tok]
    xT = np.ascontiguousarray(x.T.reshape(DCH, 128, T_CORE).transpose(1, 0, 2))
    return xT


def l1_shared(inputs):
    wg, wu, wd = relayout_ffn(inputs["ffn_w_gate"][0, 0], inputs["ffn_w_up"][0, 0], inputs["ffn_w_down"][0, 0])
    w_in = np.asarray(inputs["hyb_w_in"], np.float32)[0]
    cols_fm = np.concatenate([w_in[:, 0:512], w_in[:, 1024:1536], w_in[:, 1536:2048]], axis=1)
    wfm = np.ascontiguousarray(cols_fm.reshape(DCH, 128, 12, 128).transpose(2, 1, 0, 3))
    cols_tm = np.stack([w_in[:, 512:1024], w_in[:, 2048:2560]], axis=0)
    wtm = np.ascontiguousarray(cols_tm.reshape(2, DCH, 128, 512).transpose(0, 2, 1, 3))
    rc, PT = rope_consts()
    ws = np.asarray(inputs["gmlp_w_s"], np.float32)[0]
    wsT = np.ascontiguousarray(ws.transpose(2, 0, 1))
    tril = np.tril(np.ones((128, 128), np.float32))
    trilT = np.ascontiguousarray(np.broadcast_to(tril.T[:, None, :], (128, 4, 128)))
    bsB = np.ascontiguousarray(np.broadcast_to(np.asarray(inputs["gmlp_b_s"], np.float32)[0][None], (128, 4, 128)))
    vnorm = np.ascontiguousarray(np.broadcast_to(np.asarray(inputs["gmlp_v_norm"], np.float32)[0][None], (128, 512)))
    return {
        "g_ffn": fm_vec(inputs["ffn_norm"][0, 0]), "g_mix": fm_vec(inputs["mix_norm"][0]),
        "wg0": wg, "wu0": wu, "wd0": wd, "win_fm": wfm, "win_tm": wtm,
        "ropec": rc, "PT": PT, "vnorm": vnorm, "wsT": wsT, "trilT": trilT, "bsB": bsB,
    }


def run_l1(inputs, cores=None):
    cores = list(range(N_CORES)) if cores is None else cores
    nc = build_l1()
    shared = l1_shared(inputs)
    pos = np.asarray(inputs["positions"], np.int32)
    maps = []
    for c in cores:
        b, half = c // 2, c % 2
        m = dict(shared)
        m["xT"] = l1_inputs(inputs, c)
        m["pos"] = np.ascontiguousarray(np.broadcast_to(pos[b, half * T_CORE:(half + 1) * T_CORE][None], (32, T_CORE)))
        maps.append(m)
    res = run_bass_kernel_spmd(nc, maps, core_ids=list(range(len(cores))))
    return res.results


ATT_SCALE = 128.0 ** -0.5
NEG = -1e30


def emit_proj_fm(tk, nc, st, ps, wbuf, w_chunks, in_chunks, epilogue):
    KC = len(in_chunks)
    for j, wsrc in enumerate(w_chunks):
        wi = st["w"] % len(wbuf)
        st["w"] += 1
        wt = wbuf[wi]
        tk.dma("pool", f"dw{wi}", lambda wt=wt, wsrc=wsrc: nc.gpsimd.dma_start(out=wt[:, 0:KC, :], in_=wsrc), writes=[f"w{wi}"])
        bs = (st["pj"] % 2) * 4
        st["pj"] += 1
        for c, (iap, ikey) in enumerate(in_chunks):
            for tt in range(NT):
                tk.op("pe", lambda wt=wt, c=c, tt=tt, bs=bs, iap=iap: nc.tensor.matmul(
                    ps[:, bs + tt, :], lhsT=wt[:, c, :], rhs=iap[:, tsl(tt)], start=(c == 0), stop=(c == KC - 1)),
                    reads=[f"w{wi}", ikey], writes=[f"ps{bs + tt}"])
        epilogue(j, bs)


def emit_resid_epilogue(tk, nc, ps, xT):
    def ep(j, bs):
        for tt in range(NT):
            tk.op("dve", lambda j=j, tt=tt, bs=bs: nc.vector.tensor_tensor(out=xT[:, j, tsl(tt)], in0=ps[:, bs + tt, :], in1=xT[:, j, tsl(tt)], op=ALU.add),
                  reads=[f"ps{bs + tt}", f"xT{j}"], writes=[f"xT{j}"])
    return ep


def emit_final_norm(tk, nc, xT, g, gkey, hn, rstd, ps, ones, sg, out_d, epsc):
    for c in range(DCH):
        tk.op("act", lambda c=c: nc.scalar.activation(out=hn[:, c, :], in_=xT[:, c, :], func=AF.Square), reads=[f"xT{c}"], writes=[f"hn{c}"])
    for tt in range(NT):
        for c in range(DCH):
            tk.op("pe", lambda c=c, tt=tt: nc.tensor.matmul(ps[:, tt, :], lhsT=ones[:, :], rhs=hn[:, c, tsl(tt)], start=(c == 0), stop=(c == DCH - 1)),
                  reads=[f"hn{c}", "ones"], writes=[f"ps{tt}"])
    for tt in range(NT):
        sl = tsl(tt)
        tk.op("act", lambda tt=tt, sl=sl: nc.scalar.activation(out=rstd[:, sl], in_=ps[:, tt, :], func=AF.Sqrt, scale=1.0 / D_MODEL, bias=epsc[:, 0:1]),
              reads=[f"ps{tt}", "epsc"], writes=[f"rstd{tt}"])
        tk.op("dve", lambda sl=sl: nc.vector.reciprocal(out=rstd[:, sl], in_=rstd[:, sl]), reads=[f"rstd{tt}"], writes=[f"rstd{tt}"])
    for c in range(DCH):
        si = c % 2
        for tt in range(NT):
            sl = tsl(tt)
            tk.op("dve", lambda c=c, sl=sl, si=si: nc.vector.scalar_tensor_tensor(out=sg[:, si, sl], in0=xT[:, c, sl], scalar=g[:, c:c + 1], in1=rstd[:, sl],
                                                                                   op0=ALU.mult, op1=ALU.mult),
                  reads=[f"xT{c}", f"rstd{tt}", gkey], writes=[f"sg{si}_{tt}"])
        tk.dma("sp", "dout", lambda c=c, si=si: nc.sync.dma_start(out=out_d[:, c, :], in_=sg[:, si, :]), reads=[f"sg{si}_{tt}" for tt in range(NT)])


def alloc_ffn_set(nc, es, hchunks=6, sfx=""):
    sb = lambda name, shape, dt: es.enter_context(nc.sbuf_tensor(name + sfx, shape, dt))
    d = {}
    d["xT"] = sb("xT_s", [128, DCH, T_CORE], F32)
    d["hn"] = sb("hn", [128, DCH, T_CORE], BF16)
    d["H"] = sb("H", [128, hchunks, T_CORE], BF16)
    d["sg"] = sb("sg", [128, 2, T_CORE], F32)
    d["rstd"] = sb("rstd", [128, T_CORE], F32)
    d["ones"] = sb("ones", [128, 128], BF16)
    d["epsc"] = sb("epsc", [128, 1], F32)
    d["wbuf"] = [sb(f"wb{i}", [128, DCH, 128], BF16) for i in range(4)]
    d["wdbuf"] = [sb(f"wdb{i}", [128, 6, 128], BF16) for i in range(3)]
    return d


def build_l2(nc=None, ov=None, sfx="", nblocks=lambda j: 9 + j, sem_es=None, conv_here=False):
    nc = nc or bass.Bass("TRN2", target_bir_lowering=False)
    ov = ov or {}
    di, do = io_makers(nc, ov)
    x_d = di("xT", [128, DCH, T_CORE], F32)
    a_d = di("aT", [128, 4, T_CORE], BF16)
    q_d = di("QT", [128, 4, T_CORE], BF16)
    kparts = ov.get("KT_parts")
    vparts = ov.get("V_parts")
    k_d = None if kparts else di("KTp", [128, 4, 2 * T_CORE], BF16)
    v_d = None if vparts else di("Vp", [128, 32, 512], BF16)
    vb_d = di("vbias", [128, 8, 16], F32)
    no_d = di("notown", [128, 8, 16], F32)
    e_d = di("Esel", [32, 16, 128], BF16)
    tri_d = di("tri", [128, 4, 256], BF16)
    idf_d = di("identf", [128, 128], F32)
    wo_d = di("wo", [8, 128, DCH, 128], F32)
    gA_d = di("g_ffn_a", [128, DCH], F32)
    gB_d = di("g_ffn_b", [128, DCH], F32)
    gM_d = di("g_mix", [128, DCH], F32)
    wgA, wuA, wdA = ffn_dram(nc, "A", ov)
    wgB, wuB, wdB = ffn_dram(nc, "B", ov)
    wc_d = di("wc", [24, 128, DCH, 128], F32)
    xo_d = do("xT_o", [128, DCH, T_CORE], F32)
    if conv_here:
        cw_d = di("cw", [128, DCH, 3], F32)
        g_d = do("g_o", [128, DCH, T_CORE], BF16)
        zh_d = do("zh_o", [128, DCH, 2], F32)
        halo_in = ov.get("halo_in")
    else:
        z_d = do("zT_o", [128, DCH, T_CORE], F32)
        bg_d = do("bgT_o", [128, DCH, T_CORE], F32)
    with ExitStack() as es:
        sb = lambda name, shape, dt: es.enter_context(nc.sbuf_tensor(name + sfx, shape, dt))
        tk = Trk(nc, sem_es or es, ["dx", "dout"] + FFN_STREAMS, sfx)
        bT = sb("bT", [128, 4, T_CORE], BF16)
        ps = es.enter_context(nc.psum_tensor("ps" + sfx, [128, 8, 512], F32))
        st = {"w": 0, "sg": 0, "wd": 0, "dn": 0, "pj": 0}
        with ExitStack() as ea:
            sa = lambda name, shape, dt: ea.enter_context(nc.sbuf_tensor(name + sfx, shape, dt))
            QT = sa("QT_s", [128, 4, T_CORE], BF16)
            KT = sa("KT_s", [128, 4, 2 * T_CORE], BF16)
            V = sa("V_s", [128, 32, 512], BF16)
            vbias = sa("vbias_s", [128, 8, 16], F32)
            notown = sa("notown_s", [128, 8, 16], F32)
            E = sa("E_s", [32, 16, 128], BF16)
            tri = sa("tri_s", [128, 4, 256], BF16)
            identf = sa("identf_s", [128, 128], F32)
            identb = sa("identb", [128, 128], BF16)
            kms = sa("kms", [128, 4, 16], F32)
            kmT = sa("kmT", [128, 4, 16], BF16)
            gm = sa("gm", [128, 2, 16], F32)
            top8 = sa("top8", [128, 2, 8], F32)
            thr = sa("thr", [128, 2, 1], F32)
            G32 = sa("G32", [128, 2, 32], F32)
            biasT = sa("biasT", [32, 2, 256], BF16)
            PTs = sa("PTs", [128, 6, 256], BF16)
            rL = sa("rL", [128, 2, 256], F32)
            onesb = sa("onesb", [128, 128], BF16)
            for h in range(4):
                tk.dma("sp", "dx", lambda h=h: nc.sync.dma_start(out=QT[:, h, :], in_=q_d[:, h, :]), writes=[f"QT{h}"])
                if kparts:
                    for hf in range(2):
                        tk.dma("sp", "dx", lambda h=h, hf=hf: nc.sync.dma_start(out=KT[:, h, hf * T_CORE:(hf + 1) * T_CORE], in_=kparts[hf][:, h, :]), writes=[f"KT{h}"])
                else:
                    tk.dma("sp", "dx", lambda h=h: nc.sync.dma_start(out=KT[:, h, :], in_=k_d[:, h, :]), writes=[f"KT{h}"])
            for v4 in range(4):
                if vparts:
                    src = vparts[v4 // 2][(v4 % 2) * 8:(v4 % 2 + 1) * 8].rearrange("t p c -> p t c")
                    tk.dma("sp", "dx", lambda v4=v4, src=src: nc.sync.dma_start(out=V[:, v4 * 8:(v4 + 1) * 8, :], in_=src), writes=[f"V{v4}"])
                else:
                    tk.dma("sp", "dx", lambda v4=v4: nc.sync.dma_start(out=V[:, v4 * 8:(v4 + 1) * 8, :], in_=v_d[:, v4 * 8:(v4 + 1) * 8, :]), writes=[f"V{v4}"])
            for (t_s, t_d, key) in ((vbias, vb_d, "vbias"), (notown, no_d, "notown"), (E, e_d, "E"), (tri, tri_d, "tri"), (identf, idf_d, "identf")):
                tk.dma("sp", "dx", lambda t_s=t_s, t_d=t_d: nc.sync.dma_start(out=t_s[:], in_=t_d), writes=[key])
            tk.op("dve", lambda: nc.vector.tensor_copy(out=identb[:, :], in_=identf[:, :]), reads=["identf"], writes=["identb"])
            tk.op("dve", lambda: nc.vector.memset(onesb[:, :], 1.0), writes=["onesb"])
            tk.op("dve", lambda: nc.vector.memset(G32[:, :, :], 0.0), writes=["G32_0", "G32_1"])
            for h in range(4):
                tk.op("dve", lambda h=h: nc.vector.reduce_sum(out=kms[:, h, :], in_=KT[:, h, :].rearrange("p (n k) -> p n k", k=256), axis=AX),
                      reads=[f"KT{h}"], writes=["kms"])
            tk.op("dve", lambda: nc.vector.tensor_scalar(out=kmT[:, :, :], in0=kms[:, :, :], scalar1=1.0 / 256.0, scalar2=None, op0=ALU.mult),
                  reads=["kms"], writes=["kmT"])
            SB = [0, 1, 2, 5, 6]
            NPT = 6
            OB, LB = 3, 4
            its = [(h, j) for h in range(4) for j in range(8)]
            sctr = [0]
            pctr = [0]

            def preA(ctr):
                h, j = its[ctr]
                for qt in range(2):
                    qsl = slice(j * 256 + qt * 128, j * 256 + (qt + 1) * 128)
                    tk.op("pe", lambda: nc.tensor.matmul(ps[:, 7, qt * 16:(qt + 1) * 16], lhsT=QT[:, h, qsl], rhs=kmT[:, h, :], start=True, stop=True),
                          reads=[f"QT{h}", "kmT"], writes=["ps7"])
                for qt in range(2):
                    tk.op("dve", lambda: nc.vector.tensor_tensor(out=gm[:, qt, :], in0=ps[:, 7, qt * 16:(qt + 1) * 16], in1=vbias[:, j, :], op=ALU.add),
                          reads=["ps7", "vbias"], writes=[f"gm{qt}"])
                    tk.op("dve", lambda: nc.vector.max(out=top8[:, qt, :], in_=gm[:, qt, :]), reads=[f"gm{qt}"], writes=[f"top8{qt}"])
                    tk.op("dve", lambda: nc.vector.tensor_scalar(out=thr[:, qt, :], in0=top8[:, qt, 2:3], scalar1=-1e29, scalar2=None, op0=ALU.max),
                          reads=[f"top8{qt}"], writes=[f"thr{qt}"])
                    tk.op("dve", lambda: nc.vector.tensor_scalar(out=G32[:, qt, 0:16], in0=gm[:, qt, :], scalar1=thr[:, qt, 0:1], scalar2=None, op0=ALU.is_ge),
                          reads=[f"gm{qt}", f"thr{qt}"], writes=[f"G32_{qt}"])
                    tk.op("dve", lambda: nc.vector.tensor_scalar(out=G32[:, qt, 0:16], in0=G32[:, qt, 0:16], scalar1=-1.0, scalar2=1e30, op0=ALU.add, op1=ALU.mult),
                          reads=[f"G32_{qt}"], writes=[f"G32_{qt}"])
                    tk.op("dve", lambda: nc.vector.tensor_tensor(out=G32[:, qt, 0:16], in0=G32[:, qt, 0:16], in1=notown[:, j, :], op=ALU.mult),
                          reads=[f"G32_{qt}", "notown"], writes=[f"G32_{qt}"])

            def preB(ctr):
                bb = ctr % 2
                for qt in range(2):
                    tk.op("pe", lambda: nc.tensor.transpose(out=ps[0:32, 7, 64 + qt * 128:64 + (qt + 1) * 128], in_=G32[:, qt, :], identity=identf[:, :]),
                          reads=[f"G32_{qt}", "identf"], writes=["ps7"])
                tk.op("act", lambda: nc.scalar.copy(out=biasT[:, bb, :], in_=ps[0:32, 7, 64:320]), reads=["ps7"], writes=[f"biasT{bb}"])

            preA(0)
            preB(0)
            for ctr, (h, j) in enumerate(its):
                bb = ctr % 2
                rb = ctr % 2
                qs = slice(j * 256, (j + 1) * 256)
                nb = nblocks(j)
                tiles = [(n, kt) for n in range(nb) for kt in range(2)]
                NTI = len(tiles)
                sbase, pbase = sctr[0], pctr[0]
                sctr[0] += NTI
                pctr[0] += NTI

                def emit_S(i):
                    n, kt = tiles[i]
                    sbk = SB[(sbase + i) % len(SB)]
                    ktile = 2 * n + kt
                    extra = (n == j) or (n == 8 + j)
                    tk.op("pe", lambda: nc.tensor.matmul(ps[:, sbk, 0:256], lhsT=KT[:, h, ktile * 128:(ktile + 1) * 128], rhs=QT[:, h, qs], start=True, stop=False),
                          reads=[f"KT{h}", f"QT{h}"], writes=[f"ps{sbk}"])
                    tk.op("pe", lambda: nc.tensor.matmul(ps[:, sbk, 0:256], lhsT=E[:, n, :], rhs=biasT[:, bb, :], start=False, stop=(not extra)),
                          reads=["E", f"biasT{bb}"], writes=[f"ps{sbk}"])
                    if extra:
                        ti = (0 if n == j else 2) + kt
                        tk.op("pe", lambda: nc.tensor.matmul(ps[:, sbk, 0:256], lhsT=identb[:, :], rhs=tri[:, ti, :], start=False, stop=True),
                              reads=["identb", "tri"], writes=[f"ps{sbk}"])

                def emit_EXP(i):
                    sbk = SB[(sbase + i) % len(SB)]
                    pb = (pbase + i) % NPT
                    tk.op("act", lambda: nc.scalar.activation(out=PTs[:, pb, :], in_=ps[:, sbk, 0:256], func=AF.Exp, scale=ATT_SCALE),
                          reads=[f"ps{sbk}"], writes=[f"PTs{pb}"])

                def emit_PV(i):
                    n, kt = tiles[i]
                    pb = (pbase + i) % NPT
                    ktile = 2 * n + kt
                    tk.op("pe", lambda: nc.tensor.matmul(ps[:, OB, 0:256], lhsT=V[:, ktile, h * 128:(h + 1) * 128], rhs=PTs[:, pb, :], start=(i == 0), stop=(i == NTI - 1)),
                          reads=[f"V{ktile // 8}", f"PTs{pb}"], writes=[f"ps{OB}"])
                    tk.op("pe", lambda: nc.tensor.matmul(ps[:, LB, 0:256], lhsT=onesb[:, :], rhs=PTs[:, pb, :], start=(i == 0), stop=(i == NTI - 1)),
                          reads=["onesb", f"PTs{pb}"], writes=[f"ps{LB}"])

                DEPTH = 4
                for i in range(min(DEPTH, NTI)):
                    emit_S(i)
                if ctr + 1 < len(its):
                    preA(ctr + 1)
                for i in range(NTI):
                    if i + DEPTH < NTI:
                        emit_S(i + DEPTH)
                    emit_EXP(i)
                    emit_PV(i)
                if ctr + 1 < len(its):
                    preB(ctr + 1)
                tk.op("dve", lambda: nc.vector.reciprocal(out=rL[:, rb, :], in_=ps[:, LB, 0:256]), reads=[f"ps{LB}"], writes=[f"rL{rb}"])
                tk.op("dve", lambda: nc.vector.tensor_tensor(out=bT[:, h, qs], in0=ps[:, OB, 0:256], in1=rL[:, rb, :], op=ALU.mult),
                      reads=[f"ps{OB}", f"rL{rb}"], writes=[f"bT{h}"])
            tk.barrier()
        with ExitStack() as eb:
            f = alloc_ffn_set(nc, eb, hchunks=6, sfx=sfx)
            xT, hn, H, sg, rstd, ones, wbuf, wdbuf, epsc = (f[k] for k in ("xT", "hn", "H", "sg", "rstd", "ones", "wbuf", "wdbuf", "epsc"))
            sbb = lambda name, shape, dt: eb.enter_context(nc.sbuf_tensor(name + sfx, shape, dt))
            gA = sbb("gA", [128, DCH], F32)
            gB = sbb("gB", [128, DCH], F32)
            gM = sbb("gM", [128, DCH], F32)
            for c in range(DCH):
                tk.dma("sp", "dx", lambda c=c: nc.sync.dma_start(out=xT[:, c, :], in_=x_d[:, c, :]), writes=[f"xT{c}"])
            for c in range(4):
                tk.dma("sp", "dx", lambda c=c: nc.sync.dma_start(out=H[:, c, :], in_=a_d[:, c, :]), writes=[f"H{c}"])
            for (t_s, t_d, key) in ((gA, gA_d, "gA"), (gB, gB_d, "gB"), (gM, gM_d, "gM")):
                tk.dma("sp", "dx", lambda t_s=t_s, t_d=t_d: nc.sync.dma_start(out=t_s[:], in_=t_d), writes=[key])
            tk.op("dve", lambda: nc.vector.memset(ones[:, :], 1.0), writes=["ones"])
            tk.op("dve", lambda: nc.vector.memset(epsc[:, :], RMS_EPS), writes=["epsc"])
            in_chunks = [(H[:, c, :], f"H{c}") for c in range(4)] + [(bT[:, h, :], f"bT{h}") for h in range(4)]
            emit_proj_fm(tk, nc, st, ps, wbuf, [wo_d[j] for j in range(8)], in_chunks, emit_resid_epilogue(tk, nc, ps, xT))
            emit_rmsnorm(tk, nc, xT, gA, "gA", hn, rstd, ps, ones, epsc)
            emit_ffn(tk, nc, st, xT, hn, H, sg, ps, wbuf, wdbuf, wgA, wuA, wdA)
            emit_rmsnorm(tk, nc, xT, gB, "gB", hn, rstd, ps, ones, epsc)
            emit_ffn(tk, nc, st, xT, hn, H, sg, ps, wbuf, wdbuf, wgB, wuB, wdB)
            emit_rmsnorm(tk, nc, xT, gM, "gM", hn, rstd, ps, ones, epsc)
            for c in range(DCH):
                tk.dma("sp", "dout", lambda c=c: nc.sync.dma_start(out=xo_d[:, c, :], in_=xT[:, c, :]), reads=[f"xT{c}"])
            sgk = lambda si: [f"sg{si}_{tt}" for tt in range(NT)]

            def conv_ep(jj, bs):
                if jj < 16:
                    i, si = jj // 2, (jj // 2) % 2
                    if jj % 2 == 0:
                        for tt in range(NT):
                            tk.op("act", lambda tt=tt, si=si, bs=bs: nc.scalar.copy(out=sg[:, si, tsl(tt)], in_=ps[:, bs + tt, :]),
                                  reads=[f"ps{bs + tt}"], writes=[f"sg{si}_{tt}"])
                    else:
                        for tt in range(NT):
                            tk.op("dve", lambda tt=tt, si=si, bs=bs: nc.vector.tensor_tensor(out=sg[:, si, tsl(tt)], in0=sg[:, si, tsl(tt)], in1=ps[:, bs + tt, :], op=ALU.mult),
                                  reads=[f"sg{si}_{tt}", f"ps{bs + tt}"], writes=[f"sg{si}_{tt}"])
                        tk.dma("sp", "dout", lambda i=i, si=si: nc.sync.dma_start(out=z_d[:, i, :], in_=sg[:, si, :]), reads=sgk(si))
                else:
                    i, si = jj - 16, jj % 2
                    for tt in range(NT):
                        tk.op("act", lambda tt=tt, si=si, bs=bs: nc.scalar.copy(out=sg[:, si, tsl(tt)], in_=ps[:, bs + tt, :]),
                              reads=[f"ps{bs + tt}"], writes=[f"sg{si}_{tt}"])
                    tk.dma("sp", "dout", lambda i=i, si=si: nc.sync.dma_start(out=bg_d[:, i, :], in_=sg[:, si, :]), reads=sgk(si))

            if not conv_here:
                emit_proj_fm(tk, nc, st, ps, wbuf, [wc_d[j] for j in range(24)], [(hn[:, c, :], f"hn{c}") for c in range(DCH)], conv_ep)
            else:
                cw = sbb("cw_s", [128, DCH, 3], F32)
                zc = [sbb(f"zc{i}", [128, T_CORE + 2], F32) for i in range(2)]
                gout = [sbb(f"gout{i}", [128, T_CORE], BF16) for i in range(2)]
                tk.dma("sp", "dx", lambda: nc.sync.dma_start(out=cw[:], in_=cw_d), writes=["cw"])
                rk = [f"rstd{tt}" for tt in range(NT)]

                def conv_ep2(jj, bs):
                    i, r, si = jj // 3, jj % 3, (jj // 3) % 2
                    if r == 0:
                        for tt in range(NT):
                            tk.op("act", lambda tt=tt, bs=bs: nc.scalar.copy(out=rstd[:, tsl(tt)], in_=ps[:, bs + tt, :]), reads=[f"ps{bs + tt}"], writes=[f"rstd{tt}"])
                    elif r == 1:
                        for tt in range(NT):
                            tk.op("act", lambda tt=tt, si=si, bs=bs: nc.scalar.copy(out=sg[:, si, tsl(tt)], in_=ps[:, bs + tt, :]),
                                  reads=[f"ps{bs + tt}"], writes=[f"sg{si}_{tt}"])
                    else:
                        if halo_in is None:
                            tk.op("dve", lambda si=si: nc.vector.memset(zc[si][:, 0:2], 0.0), writes=[f"zc{si}"])
                        else:
                            tk.dma("sp", "dx", lambda si=si, i=i: nc.sync.dma_start(out=zc[si][:, 0:2], in_=halo_in[:, i, :]), writes=[f"zc{si}"])
                        for tt in range(NT):
                            o = tt * 512
                            tk.op("dve", lambda tt=tt, si=si, bs=bs, o=o: nc.vector.tensor_tensor(out=zc[si][:, 2 + o:2 + o + 512], in0=sg[:, si, tsl(tt)], in1=ps[:, bs + tt, :], op=ALU.mult),
                                  reads=[f"sg{si}_{tt}", f"ps{bs + tt}"], writes=[f"zc{si}"])
                        tk.dma("sp", "dout", lambda si=si, i=i: nc.sync.dma_start(out=zh_d[:, i, :], in_=zc[si][:, T_CORE:T_CORE + 2]), reads=[f"zc{si}"])
                        for tt in range(NT):
                            o = tt * 512
                            sl = tsl(tt)
                            key = f"sg{si}_{tt}"
                            tk.op("dve", lambda si=si, i=i, sl=sl, o=o: nc.vector.tensor_scalar(out=sg[:, si, sl], in0=zc[si][:, o:o + 512], scalar1=cw[:, i, 0:1], scalar2=None, op0=ALU.mult),
                                  reads=[f"zc{si}", "cw"], writes=[key])
                            tk.op("dve", lambda si=si, i=i, sl=sl, o=o: nc.vector.scalar_tensor_tensor(out=sg[:, si, sl], in0=zc[si][:, o + 1:o + 513], scalar=cw[:, i, 1:2], in1=sg[:, si, sl],
                                                                                                     op0=ALU.mult, op1=ALU.add),
                                  reads=[f"zc{si}", "cw", key], writes=[key])
                            tk.op("dve", lambda si=si, i=i, sl=sl, o=o: nc.vector.scalar_tensor_tensor(out=sg[:, si, sl], in0=zc[si][:, o + 2:o + 514], scalar=cw[:, i, 2:3], in1=sg[:, si, sl],
                                                                                                     op0=ALU.mult, op1=ALU.add),
                                  reads=[f"zc{si}", "cw", key], writes=[key])
                            tk.op("dve", lambda si=si, sl=sl, tt=tt: nc.vector.tensor_tensor(out=gout[si][:, sl], in0=sg[:, si, sl], in1=rstd[:, sl], op=ALU.mult),
                                  reads=[key, f"rstd{tt}"], writes=[f"gout{si}"])
                        tk.dma("sp", "dout", lambda si=si, i=i: nc.sync.dma_start(out=g_d[:, i, :], in_=gout[si][:, :]), reads=[f"gout{si}"])

                emit_proj_fm(tk, nc, st, ps, wbuf, [wc_d[j] for j in range(24)], [(hn[:, c, :], f"hn{c}") for c in range(DCH)], conv_ep2)
            tk.finish("sp", ["dout"])
            tk.barrier()
    return nc


def build_l3(nc=None, ov=None, sfx="", sem_es=None):
    nc = nc or bass.Bass("TRN2", target_bir_lowering=False)
    ov = ov or {}
    di, do = io_makers(nc, ov)
    x_d = di("xT", [128, DCH, T_CORE], F32)
    g_src = ov.get("g_src")
    z_src = ov.get("z_src")
    halo_src = ov.get("halo_src")
    ze_d = None if (z_src is not None or g_src is not None) else di("zext", [128, DCH, T_CORE + 2], F32)
    bg_d = None if g_src is not None else di("bgT", [128, DCH, T_CORE], F32)
    cw_d = None if g_src is not None else di("cw", [128, DCH, 3], F32)
    wco_d = di("wco", [8, 128, DCH, 128], F32)
    gC_d = di("g_ffn_c", [128, DCH], F32)
    gF_d = di("g_fin", [128, DCH], F32)
    wgC, wuC, wdC = ffn_dram(nc, "C", ov)
    out_d = do("outT", [128, DCH, T_CORE], F32)
    with ExitStack() as es:
        sb = lambda name, shape, dt: es.enter_context(nc.sbuf_tensor(name + sfx, shape, dt))
        tk = Trk(nc, sem_es or es, ["dx", "dz", "dout"] + FFN_STREAMS, sfx)
        f = alloc_ffn_set(nc, es, hchunks=6, sfx=sfx)
        xT, hn, H, sg, rstd, ones, wbuf, wdbuf, epsc = (f[k] for k in ("xT", "hn", "H", "sg", "rstd", "ones", "wbuf", "wdbuf", "epsc"))
        gC = sb("gC", [128, DCH], F32)
        gF = sb("gF", [128, DCH], F32)
        if g_src is None:
            cw = sb("cw_s", [128, DCH, 3], F32)
            zc = [sb(f"zc{i}", [128, T_CORE + 2], F32) for i in range(2)]
            bgc = [sb(f"bgc{i}", [128, T_CORE], F32) for i in range(2)]
        ps = es.enter_context(nc.psum_tensor("ps" + sfx, [128, 8, 512], F32))
        st = {"w": 0, "sg": 0, "wd": 0, "dn": 0, "pj": 0}
        for c in range(DCH):
            tk.dma("sp", "dx", lambda c=c: nc.sync.dma_start(out=xT[:, c, :], in_=x_d[:, c, :]), writes=[f"xT{c}"])
        for (t_s, t_d, key) in ((gC, gC_d, "gC"), (gF, gF_d, "gF")) + (((cw, cw_d, "cw"),) if g_src is None else ()):
            tk.dma("sp", "dx", lambda t_s=t_s, t_d=t_d: nc.sync.dma_start(out=t_s[:], in_=t_d), writes=[key])
        if g_src is not None:
            for c in range(DCH):
                tk.dma("sp", "dx", lambda c=c: nc.sync.dma_start(out=hn[:, c, :], in_=g_src[:, c, :]), writes=[f"hn{c}"])
        tk.op("dve", lambda: nc.vector.memset(ones[:, :], 1.0), writes=["ones"])
        tk.op("dve", lambda: nc.vector.memset(epsc[:, :], RMS_EPS), writes=["epsc"])
        for c in (range(DCH) if g_src is None else ()):
            ci = c % 2
            if z_src is None:
                tk.dma("sp", f"dz", lambda c=c, ci=ci: nc.sync.dma_start(out=zc[ci][:, :], in_=ze_d[:, c, :]), writes=[f"zc{ci}"])
            else:
                tk.dma("sp", f"dz", lambda c=c, ci=ci: nc.sync.dma_start(out=zc[ci][:, 2:T_CORE + 2], in_=z_src[:, c, :]), writes=[f"zc{ci}"])
                if halo_src is None:
                    tk.op("dve", lambda ci=ci: nc.vector.memset(zc[ci][:, 0:2], 0.0), writes=[f"zc{ci}"])
                else:
                    tk.dma("sp", f"dz", lambda c=c, ci=ci: nc.sync.dma_start(out=zc[ci][:, 0:2], in_=halo_src[:, c, T_CORE - 2:T_CORE]), writes=[f"zc{ci}"])
            tk.dma("sp", f"dz", lambda c=c, ci=ci: nc.sync.dma_start(out=bgc[ci][:, :], in_=bg_d[:, c, :]), writes=[f"bgc{ci}"])
            en, eo = "dve", nc.vector
            for tt in range(NT):
                sl = tsl(tt)
                o = tt * 512
                key = f"sg{ci}_{tt}"
                tk.op(en, lambda c=c, ci=ci, sl=sl, o=o, eo=eo: eo.tensor_scalar(out=sg[:, ci, sl], in0=zc[ci][:, o:o + 512], scalar1=cw[:, c, 0:1], scalar2=None, op0=ALU.mult),
                      reads=[f"zc{ci}", "cw"], writes=[key])
                tk.op(en, lambda c=c, ci=ci, sl=sl, o=o, eo=eo: eo.scalar_tensor_tensor(out=sg[:, ci, sl], in0=zc[ci][:, o + 1:o + 513], scalar=cw[:, c, 1:2], in1=sg[:, ci, sl],
                                                                                        op0=ALU.mult, op1=ALU.add),
                      reads=[f"zc{ci}", "cw", key], writes=[key])
                tk.op(en, lambda c=c, ci=ci, sl=sl, o=o, eo=eo: eo.scalar_tensor_tensor(out=sg[:, ci, sl], in0=zc[ci][:, o + 2:o + 514], scalar=cw[:, c, 2:3], in1=sg[:, ci, sl],
                                                                                        op0=ALU.mult, op1=ALU.add),
                      reads=[f"zc{ci}", "cw", key], writes=[key])
                tk.op(en, lambda c=c, ci=ci, sl=sl, eo=eo: eo.tensor_tensor(out=hn[:, c, sl], in0=sg[:, ci, sl], in1=bgc[ci][:, sl], op=ALU.mult),
                      reads=[key, f"bgc{ci}"], writes=[f"hn{c}"])
        emit_proj_fm(tk, nc, st, ps, wbuf, [wco_d[j] for j in range(8)], [(hn[:, c, :], f"hn{c}") for c in range(DCH)], emit_resid_epilogue(tk, nc, ps, xT))
        emit_rmsnorm(tk, nc, xT, gC, "gC", hn, rstd, ps, ones, epsc)
        emit_ffn(tk, nc, st, xT, hn, H, sg, ps, wbuf, wdbuf, wgC, wuC, wdC)
        emit_final_norm(tk, nc, xT, gF, "gF", hn, rstd, ps, ones, sg, out_d, epsc)
        tk.finish("sp", ["dout"])
        tk.barrier()
    return nc


def relayout_fm(W):
    W = np.asarray(W, np.float32)
    nch = W.shape[1] // 128
    return np.ascontiguousarray(W.reshape(DCH, 128, nch, 128).transpose(2, 1, 0, 3))


def attn_consts(half):
    vb = np.zeros((8, 16), np.float32)
    no = np.ones((8, 16), np.float32)
    for j in range(8):
        qblk = 8 * half + j
        vb[j, qblk:] = NEG
        no[j, qblk] = 0.0
    E = np.zeros((32, 16, 128), np.float32)
    for n in range(16):
        E[n, n, :] = 1.0
    tri = np.zeros((128, 4, 256), np.float32)
    k = np.arange(128)[:, None]
    q = np.arange(256)[None, :]
    for kt in range(2):
        m = np.where(kt * 128 + k > q, NEG, 0.0).astype(np.float32)
        tri[:, (0 if half == 0 else 2) + kt, :] = m
    return {
        "vbias": np.ascontiguousarray(np.broadcast_to(vb[None], (128, 8, 16))),
        "notown": np.ascontiguousarray(np.broadcast_to(no[None], (128, 8, 16))),
        "Esel": bf16_np(E), "tri": bf16_np(tri), "identf": np.eye(128, dtype=np.float32),
    }


def run_l2(inputs, r1, cores):
    nc = build_l2()
    wgA, wuA, wdA = relayout_ffn(inputs["ffn_w_gate"][0, 1], inputs["ffn_w_up"][0, 1], inputs["ffn_w_down"][0, 1])
    wgB, wuB, wdB = relayout_ffn(inputs["ffn_w_gate"][1, 0], inputs["ffn_w_up"][1, 0], inputs["ffn_w_down"][1, 0])
    cwi = np.asarray(inputs["conv_w_in"], np.float32)[0]
    cols = []
    for i in range(8):
        cols.append(cwi[:, 1024 + i * 128:1024 + (i + 1) * 128])
        cols.append(cwi[:, 2048 + i * 128:2048 + (i + 1) * 128])
    for i in range(8):
        cols.append(cwi[:, i * 128:(i + 1) * 128])
    shared = {
        "wo": relayout_fm(inputs["hyb_w_out"][0]),
        "g_ffn_a": fm_vec(inputs["ffn_norm"][0, 1]), "g_ffn_b": fm_vec(inputs["ffn_norm"][1, 0]), "g_mix": fm_vec(inputs["mix_norm"][1]),
        "wgA": wgA, "wuA": wuA, "wdA": wdA, "wgB": wgB, "wuB": wuB, "wdB": wdB,
        "wc": relayout_fm(np.concatenate(cols, axis=1)),
    }
    idx = {c: i for i, c in enumerate(cores)}
    maps = []
    for c in cores:
        b, half = c // 2, c % 2
        r0, rp = r1[idx[2 * b]], r1[idx[2 * b + 1]]
        m = dict(shared)
        m.update(attn_consts(half))
        me = r1[idx[c]]
        m["xT"] = np.asarray(me["xT_o"])
        m["aT"] = np.asarray(me["aT_o"])
        m["QT"] = np.ascontiguousarray(np.asarray(me["QKT_o"])[:, 0:4, :])
        m["KTp"] = np.ascontiguousarray(np.concatenate([np.asarray(r0["QKT_o"])[:, 4:8, :], np.asarray(rp["QKT_o"])[:, 4:8, :]], axis=2))
        vp = np.concatenate([np.asarray(r0["V_o"]), np.asarray(rp["V_o"])], axis=0)
        m["Vp"] = np.ascontiguousarray(vp.transpose(1, 0, 2))
        maps.append(m)
    res = run_bass_kernel_spmd(nc, maps, core_ids=list(range(len(cores))))
    return res.results


def run_l3(inputs, r2, cores):
    nc = build_l3()
    wgC, wuC, wdC = relayout_ffn(inputs["ffn_w_gate"][1, 1], inputs["ffn_w_up"][1, 1], inputs["ffn_w_down"][1, 1])
    cwv = np.asarray(inputs["conv_w"], np.float32)[0]
    cw = np.ascontiguousarray(cwv.reshape(3, DCH, 128).transpose(2, 1, 0))
    shared = {
        "cw": cw, "wco": relayout_fm(inputs["conv_w_out"][0]),
        "g_ffn_c": fm_vec(inputs["ffn_norm"][1, 1]), "g_fin": fm_vec(inputs["final_norm"]),
        "wgC": wgC, "wuC": wuC, "wdC": wdC,
    }
    idx = {c: i for i, c in enumerate(cores)}
    maps = []
    for c in cores:
        b, half = c // 2, c % 2
        me = r2[idx[c]]
        z = np.asarray(me["zT_o"])
        if half == 0:
            halo = np.zeros((128, DCH, 2), np.float32)
        else:
            halo = np.asarray(r2[idx[2 * b]]["zT_o"])[:, :, T_CORE - 2:T_CORE]
        m = dict(shared)
        m["xT"] = np.asarray(me["xT_o"])
        m["zext"] = np.ascontiguousarray(np.concatenate([halo, z], axis=2))
        m["bgT"] = np.asarray(me["bgT_o"])
        maps.append(m)
    res = run_bass_kernel_spmd(nc, maps, core_ids=list(range(len(cores))))
    return res.results


def run_all(inputs, cores):
    inputs = {k: np.asarray(v) for k, v in inputs.items()}
    r1 = run_l1(inputs, cores)
    r2 = run_l2(inputs, r1, cores)
    r3 = run_l3(inputs, r2, cores)
    outs = {}
    for i, c in enumerate(cores):
        oT = np.asarray(r3[i]["outT"], np.float32)
        outs[c] = np.ascontiguousarray(oT.transpose(1, 0, 2).reshape(D_MODEL, T_CORE).T)
    return outs, (r1, r2, r3)
def build_fused():
    nc = bass.Bass("TRN2", target_bir_lowering=False)
    ext = lambda name, shape, dt: nc.dram_tensor(name, shape, dt, kind="ExternalInput").ap()
    itn = lambda name, shape, dt: nc.dram_tensor(name, shape, dt, kind="Internal").ap()
    W = {}
    for sfx in ("0", "A", "B", "C"):
        W["wg" + sfx] = ext("wg" + sfx, [FCH, 128, DCH, 128], F32)
        W["wu" + sfx] = ext("wu" + sfx, [FCH, 128, DCH, 128], F32)
        W["wd" + sfx] = ext("wd" + sfx, [DCH, 128, FCH, 128], F32)
    for name, shape, dt in (("g_ffn", [128, DCH], F32), ("g_mix1", [128, DCH], F32), ("win_fm", [12, 128, DCH, 128], F32), ("win_tm", [2, 128, DCH, 512], F32),
                            ("ropec", [32, 4], F32), ("PT", [128, 128], F32), ("vnorm", [128, 512], F32), ("wsT", [128, 4, 128], F32),
                            ("trilT", [128, 4, 128], F32), ("bsB", [128, 4, 128], F32),
                            ("Esel", [32, 16, 128], BF16), ("identf", [128, 128], F32), ("wo", [8, 128, DCH, 128], F32),
                            ("g_ffn_a", [128, DCH], F32), ("g_ffn_b", [128, DCH], F32), ("g_mix2", [128, DCH], F32), ("wc", [24, 128, DCH, 128], F32),
                            ("cw", [128, DCH, 3], F32), ("wco", [8, 128, DCH, 128], F32), ("g_ffn_c", [128, DCH], F32), ("g_fin", [128, DCH], F32)):
        W[name] = ext(name, shape, dt)
    x2 = ext("xT2", [2, 128, DCH, T_CORE], F32)
    pos2 = ext("pos2", [2, 32, T_CORE], I32)
    vb2 = ext("vbias2", [2, 128, 8, 16], F32)
    no2 = ext("notown2", [2, 128, 8, 16], F32)
    tri2 = ext("tri2", [2, 128, 4, 256], BF16)
    out2 = nc.dram_tensor("outT2", [2, 128, DCH, T_CORE], F32, kind="ExternalOutput").ap()
    x1_i = itn("x1_i", [2, 128, DCH, T_CORE], F32)
    a_i = itn("a_i", [2, 128, 4, T_CORE], BF16)
    qk_i = itn("qk_i", [2, 128, 8, T_CORE], BF16)
    v_i = itn("v_i", [2, 16, 128, 512], BF16)
    x4_i = itn("x4_i", [2, 128, DCH, T_CORE], F32)
    g_i = itn("g_i", [2, 128, DCH, T_CORE], BF16)
    zh_i = itn("zh_i", [2, 128, DCH, 2], F32)
    with ExitStack() as sem_es:
        for hf in range(2):
            ov = dict(W)
            ov.update({"g_mix": W["g_mix1"], "xT": x2[hf], "pos": pos2[hf], "xT_o": x1_i[hf], "aT_o": a_i[hf], "QKT_o": qk_i[hf], "V_o": v_i[hf]})
            build_l1(nc, ov, sfx=f"_p1h{hf}", sem_es=sem_es)
        for hf in range(2):
            ov = dict(W)
            ov.update({"g_mix": W["g_mix2"], "xT": x1_i[hf], "aT": a_i[hf], "QT": qk_i[hf][:, 0:4, :],
                       "KT_parts": [qk_i[0][:, 4:8, :], qk_i[1][:, 4:8, :]], "V_parts": [v_i[0], v_i[1]],
                       "vbias": vb2[hf], "notown": no2[hf], "tri": tri2[hf],
                       "xT_o": x4_i[hf], "g_o": g_i[hf], "zh_o": zh_i[hf], "halo_in": (zh_i[0] if hf == 1 else None)})
            nblocks = (lambda j: j + 1) if hf == 0 else (lambda j: 9 + j)
            build_l2(nc, ov, sfx=f"_p2h{hf}", nblocks=nblocks, sem_es=sem_es, conv_here=True)
        for hf in range(2):
            ov = dict(W)
            ov.update({"xT": x4_i[hf], "g_src": g_i[hf], "outT": out2[hf]})
            build_l3(nc, ov, sfx=f"_p3h{hf}", sem_es=sem_es)
    return nc


def fused_shared(inputs):
    s1 = l1_shared(inputs)
    sh = {k: s1[k] for k in ("wg0", "wu0", "wd0", "g_ffn", "win_fm", "win_tm", "ropec", "PT", "vnorm", "wsT", "trilT", "bsB")}
    sh["g_mix1"] = s1["g_mix"]
    wgA, wuA, wdA = relayout_ffn(inputs["ffn_w_gate"][0, 1], inputs["ffn_w_up"][0, 1], inputs["ffn_w_down"][0, 1])
    wgB, wuB, wdB = relayout_ffn(inputs["ffn_w_gate"][1, 0], inputs["ffn_w_up"][1, 0], inputs["ffn_w_down"][1, 0])
    wgC, wuC, wdC = relayout_ffn(inputs["ffn_w_gate"][1, 1], inputs["ffn_w_up"][1, 1], inputs["ffn_w_down"][1, 1])
    cwi = np.asarray(inputs["conv_w_in"], np.float32)[0]
    cols = []
    for i in range(8):
        cols.append(cwi[:, i * 128:(i + 1) * 128])
        cols.append(cwi[:, 1024 + i * 128:1024 + (i + 1) * 128])
        cols.append(cwi[:, 2048 + i * 128:2048 + (i + 1) * 128])
    cwv = np.asarray(inputs["conv_w"], np.float32)[0]
    c0, c1 = attn_consts(0), attn_consts(1)
    sh.update({
        "wgA": wgA, "wuA": wuA, "wdA": wdA, "wgB": wgB, "wuB": wuB, "wdB": wdB, "wgC": wgC, "wuC": wuC, "wdC": wdC,
        "wo": relayout_fm(inputs["hyb_w_out"][0]), "wc": relayout_fm(np.concatenate(cols, axis=1)), "wco": relayout_fm(inputs["conv_w_out"][0]),
        "g_ffn_a": fm_vec(inputs["ffn_norm"][0, 1]), "g_ffn_b": fm_vec(inputs["ffn_norm"][1, 0]), "g_mix2": fm_vec(inputs["mix_norm"][1]),
        "g_ffn_c": fm_vec(inputs["ffn_norm"][1, 1]), "g_fin": fm_vec(inputs["final_norm"]),
        "cw": np.ascontiguousarray(cwv.reshape(3, DCH, 128).transpose(2, 1, 0)),
        "Esel": c0["Esel"], "identf": c0["identf"],
        "vbias2": np.stack([c0["vbias"], c1["vbias"]]), "notown2": np.stack([c0["notown"], c1["notown"]]), "tri2": np.stack([c0["tri"], c1["tri"]]),
    })
    return sh


def run_fused(inputs, batches):
    inputs = {k: np.asarray(v) for k, v in inputs.items()}
    nc = build_fused()
    sh = fused_shared(inputs)
    pos = np.asarray(inputs["positions"], np.int32)
    maps = []
    for b in batches:
        m = dict(sh)
        m["xT2"] = np.stack([l1_inputs(inputs, 2 * b), l1_inputs(inputs, 2 * b + 1)])
        m["pos2"] = np.ascontiguousarray(np.broadcast_to(pos[b].reshape(2, 1, T_CORE), (2, 32, T_CORE)))
        maps.append(m)
    res = run_bass_kernel_spmd(nc, maps, core_ids=list(range(len(batches))))
    outs = {}
    for i, b in enumerate(batches):
        oT = np.asarray(res.results[i]["outT2"], np.float32)
        outs[b] = np.concatenate([oT[hf].transpose(1, 0, 2).reshape(D_MODEL, T_CORE).T for hf in range(2)], axis=0)
    return outs


def kernel(**inputs):
    outs = run_fused(inputs, list(range(BATCH)))
    return np.ascontiguousarray(np.stack([outs[b] for b in range(BATCH)]).astype(np.float32))
```

```python
import numpy as np
import ml_dtypes
import concourse.bass as bass
import concourse.mybir as mybir
from concourse.bass_utils import run_bass_kernel_spmd

F32 = mybir.dt.float32
BF16 = mybir.dt.bfloat16
I32 = mybir.dt.int32
ALU = mybir.AluOpType
AF = mybir.ActivationFunctionType

N_CORES = 8
D_MODEL = 1024
SEQ = 4096
BATCH = 4
T_CORE = 2048
DCH = D_MODEL // 128
D_FF = 2816
FCH = D_FF // 128
RMS_EPS = 1e-6


from contextlib import ExitStack

AX = mybir.AxisListType.X
GROUPS = [(0, 6), (6, 12), (12, 17), (17, 22)]
NT = T_CORE // 512
GELU_C = 0.044715
GELU_S = 1.5957691216057308


def bf16_np(a):
    return np.asarray(a, dtype=np.float32).astype(ml_dtypes.bfloat16)


class Trk:
    def __init__(self, nc, es, dma_streams, sfx=""):
        self.nc = nc
        self.eng = {"pe": nc.tensor, "act": nc.scalar, "dve": nc.vector, "pool": nc.gpsimd, "sp": nc.sync}
        names = list(self.eng) + list(dma_streams)
        self.dma_streams = set(dma_streams)
        self.sem = {n: es.enter_context(nc.semaphore("s_" + n + sfx)) for n in names}
        self.cnt = {k: 0 for k in names}
        self.known = {e: {} for e in self.eng}
        self.lastw = {}
        self.readers = {}

    def _wait(self, e, semname, val):
        if val <= 0:
            return
        if semname in self.dma_streams:
            val = self.cnt[semname]
        if self.known[e].get(semname, 0) >= val:
            return
        if e == "pe" and semname == "pe":
            return
        self.eng[e].wait_ge(self.sem[semname], val)
        self.known[e][semname] = val

    def deps(self, e, reads=(), writes=()):
        for k in reads:
            if k in self.lastw:
                self._wait(e, *self.lastw[k])
        for k in writes:
            if k in self.lastw:
                self._wait(e, *self.lastw[k])
            for r in self.readers.get(k, ()):
                self._wait(e, *r)

    def done(self, ins, semname, inc, reads=(), writes=()):
        self.cnt[semname] += inc
        ins.then_inc(self.sem[semname], inc)
        tag = (semname, self.cnt[semname])
        for k in reads:
            self.readers.setdefault(k, []).append(tag)
        for k in writes:
            self.lastw[k] = tag
            self.readers[k] = []
        return tag

    @staticmethod
    def _excl(reads, writes):
        r = [k for k in reads if not k.startswith("ps")]
        w = list(writes) + [k for k in reads if k.startswith("ps") and k not in writes]
        return r, w

    def op(self, e, fn, reads=(), writes=()):
        reads, writes = self._excl(reads, writes)
        self.deps(e, reads, writes)
        return self.done(fn(), e, 1, reads, writes)

    def barrier(self):
        for e in self.eng:
            for sname, c in self.cnt.items():
                if c and not (e == sname):
                    if self.known[e].get(sname, 0) < c:
                        self.eng[e].wait_ge(self.sem[sname], c)
                        self.known[e][sname] = c

    def dma(self, e, stream, fn, reads=(), writes=()):
        self.deps(e, reads, writes)
        return self.done(fn(), stream, 16, reads, writes)

    def finish(self, e, streams):
        for s in streams:
            if self.cnt[s]:
                self.eng[e].wait_ge(self.sem[s], self.cnt[s])


def tsl(tt):
    return slice(tt * 512, (tt + 1) * 512)


def emit_rmsnorm(tk, nc, xT, g, gkey, hn, rstd, ps, ones, epsc):
    for c in range(DCH):
        tk.op("act", lambda c=c: nc.scalar.activation(out=hn[:, c, :], in_=xT[:, c, :], func=AF.Square),
              reads=[f"xT{c}"], writes=[f"hn{c}"])
    for tt in range(NT):
        for c in range(DCH):
            tk.op("pe", lambda c=c, tt=tt: nc.tensor.matmul(ps[:, tt, :], lhsT=ones[:, :], rhs=hn[:, c, tsl(tt)],
                                                             start=(c == 0), stop=(c == DCH - 1)),
                  reads=[f"hn{c}", "ones"], writes=[f"ps{tt}"])
    for tt in range(NT):
        sl = tsl(tt)
        tk.op("act", lambda tt=tt, sl=sl: nc.scalar.activation(out=rstd[:, sl], in_=ps[:, tt, :], func=AF.Sqrt, scale=1.0 / D_MODEL, bias=epsc[:, 0:1]),
              reads=[f"ps{tt}", "epsc"], writes=[f"rstd{tt}"])
        tk.op("dve", lambda sl=sl: nc.vector.reciprocal(out=rstd[:, sl], in_=rstd[:, sl]), reads=[f"rstd{tt}"], writes=[f"rstd{tt}"])
    for c in range(DCH):
        for tt in range(NT):
            sl = tsl(tt)
            tk.op("dve", lambda c=c, sl=sl: nc.vector.scalar_tensor_tensor(out=hn[:, c, sl], in0=xT[:, c, sl], scalar=g[:, c:c + 1], in1=rstd[:, sl],
                                                                             op0=ALU.mult, op1=ALU.mult),
                  reads=[f"xT{c}", f"rstd{tt}", gkey], writes=[f"hn{c}"])


def emit_ffn(tk, nc, st, xT, hn, H, sg, ps, wbuf, wdbuf, wg_d, wu_d, wd_d):
    NW = len(wbuf)
    for (f0, f1) in GROUPS:
        nf = f1 - f0
        for f in range(f0, f1):
            for which, wsrc in ((0, wg_d), (1, wu_d)):
                wi = st["w"] % NW
                st["w"] += 1
                wt = wbuf[wi]
                tk.dma("pool", f"dw{wi}", lambda wt=wt, wsrc=wsrc, f=f: nc.gpsimd.dma_start(out=wt[:, :, :], in_=wsrc[f]),
                       writes=[f"w{wi}"])
                for c in range(DCH):
                    for tt in range(NT):
                        b = which * 4 + tt
                        tk.op("pe", lambda wt=wt, c=c, tt=tt, b=b: nc.tensor.matmul(
                            ps[:, b, :], lhsT=wt[:, c, :], rhs=hn[:, c, tsl(tt)], start=(c == 0), stop=(c == DCH - 1)),
                            reads=[f"w{wi}", f"hn{c}"], writes=[f"ps{b}"])
            si = st["sg"] % 2
            st["sg"] += 1
            for tt in range(NT):
                tk.op("act", lambda tt=tt, si=si: nc.scalar.activation(out=sg[:, si, tsl(tt)], in_=ps[:, tt, :], func=AF.Silu),
                      reads=[f"ps{tt}"], writes=[f"sg{si}_{tt}"])
            for tt in range(NT):
                tk.op("dve", lambda tt=tt, si=si, f=f: nc.vector.tensor_tensor(out=H[:, f - f0, tsl(tt)], in0=sg[:, si, tsl(tt)], in1=ps[:, 4 + tt, :], op=ALU.mult),
                      reads=[f"sg{si}_{tt}", f"ps{4 + tt}"], writes=[f"H{f - f0}"])
        for j in range(DCH):
            di = st["wd"] % len(wdbuf)
            st["wd"] += 1
            wt = wdbuf[di]
            tk.dma("pool", f"dd{di}", lambda wt=wt, j=j, f0=f0, f1=f1, nf=nf: nc.gpsimd.dma_start(out=wt[:, 0:nf, :], in_=wd_d[j, :, f0:f1, :]),
                   writes=[f"wd{di}"])
            bs = (st["dn"] % 2) * 4
            st["dn"] += 1
            for k in range(nf):
                for tt in range(NT):
                    b = bs + tt
                    tk.op("pe", lambda wt=wt, k=k, tt=tt, b=b: nc.tensor.matmul(
                        ps[:, b, :], lhsT=wt[:, k, :], rhs=H[:, k, tsl(tt)], start=(k == 0), stop=(k == nf - 1)),
                        reads=[f"wd{di}", f"H{k}"], writes=[f"ps{b}"])
            for tt in range(NT):
                b = bs + tt
                tk.op("dve", lambda j=j, tt=tt, b=b: nc.vector.scalar_tensor_tensor(out=xT[:, j, tsl(tt)], in0=ps[:, b, :], scalar=0.5, in1=xT[:, j, tsl(tt)],
                                                                                    op0=ALU.mult, op1=ALU.add),
                      reads=[f"ps{b}", f"xT{j}"], writes=[f"xT{j}"])


def emit_gelu(tk, nc, src, srckey, tmp, tmpkey, out, outkey):
    tk.op("act", lambda: nc.scalar.activation(out=tmp, in_=src, func=AF.Square, scale=GELU_C ** 0.5), reads=[srckey], writes=[tmpkey])
    tk.op("dve", lambda: nc.vector.scalar_tensor_tensor(out=tmp, in0=tmp, scalar=1.0, in1=src, op0=ALU.add, op1=ALU.mult),
          reads=[tmpkey, srckey], writes=[tmpkey])
    tk.op("act", lambda: nc.scalar.activation(out=tmp, in_=tmp, func=AF.Sigmoid, scale=GELU_S), reads=[tmpkey], writes=[tmpkey])
    tk.op("dve", lambda: nc.vector.tensor_tensor(out=out, in0=tmp, in1=src, op=ALU.mult), reads=[tmpkey, srckey], writes=[outkey])


FFN_STREAMS = [f"dw{i}" for i in range(4)] + [f"dd{i}" for i in range(3)]


def ffn_dram(nc, sfx, ov=None):
    ov = ov or {}
    mk = lambda name, shape: ov[name] if name in ov else nc.dram_tensor(name, shape, F32, kind="ExternalInput").ap()
    wg = mk("wg" + sfx, [FCH, 128, DCH, 128])
    wu = mk("wu" + sfx, [FCH, 128, DCH, 128])
    wd = mk("wd" + sfx, [DCH, 128, FCH, 128])
    return wg, wu, wd


def io_makers(nc, ov):
    di = lambda name, shape, dt: ov[name] if name in ov else nc.dram_tensor(name, shape, dt, kind="ExternalInput").ap()
    do = lambda name, shape, dt: ov[name] if name in ov else nc.dram_tensor(name, shape, dt, kind="ExternalOutput").ap()
    return di, do


def build_l1(nc=None, ov=None, sfx="", sem_es=None):
    nc = nc or bass.Bass("TRN2", target_bir_lowering=False)
    ov = ov or {}
    di, do = io_makers(nc, ov)
    x_d = di("xT", [128, DCH, T_CORE], F32)
    gf_d = di("g_ffn", [128, DCH], F32)
    gm_d = di("g_mix", [128, DCH], F32)
    wg_d, wu_d, wd_d = ffn_dram(nc, "0", ov)
    wfm_d = di("win_fm", [12, 128, DCH, 128], F32)
    wtm_d = di("win_tm", [2, 128, DCH, 512], F32)
    pos_d = di("pos", [32, T_CORE], I32)
    rc_d = di("ropec", [32, 4], F32)
    pt_d = di("PT", [128, 128], F32)
    vn_d = di("vnorm", [128, 512], F32)
    ws_d = di("wsT", [128, 4, 128], F32)
    tr_d = di("trilT", [128, 4, 128], F32)
    bs_d = di("bsB", [128, 4, 128], F32)
    xo_d = do("xT_o", [128, DCH, T_CORE], F32)
    ao_d = do("aT_o", [128, 4, T_CORE], BF16)
    qk_d = do("QKT_o", [128, 8, T_CORE], BF16)
    vo_d = do("V_o", [16, 128, 512], BF16)
    with ExitStack() as es:
        sb = lambda name, shape, dt: es.enter_context(nc.sbuf_tensor(name + sfx, shape, dt))
        xT = sb("xT_s", [128, DCH, T_CORE], F32)
        hn = sb("hn", [128, DCH, T_CORE], BF16)
        H = sb("H", [128, 8, T_CORE], BF16)
        sg = sb("sg", [128, 2, T_CORE], F32)
        rstd = sb("rstd", [128, T_CORE], F32)
        gf = sb("gf", [128, DCH], F32)
        gm = sb("gm", [128, DCH], F32)
        ones = sb("ones", [128, 128], BF16)
        epsc = sb("epsc", [128, 1], F32)
        wbuf = [sb(f"wb{i}", [128, DCH, 128], BF16) for i in range(4)]
        wdbuf = [sb(f"wdb{i}", [128, 6, 128], BF16) for i in range(3)]
        posi = sb("posi", [32, T_CORE], I32)
        rc = sb("rc", [32, 4], F32)
        PT = sb("PTs", [128, 128], BF16)
        vnorm = sb("vnorm_s", [128, 512], F32)
        wsT = sb("wsT_s", [128, 4, 128], F32)
        trT = sb("trT_s", [128, 4, 128], F32)
        WmT = sb("WmT", [128, 4, 128], BF16)
        bsB = sb("bsB_s", [128, 4, 128], F32)
        tmpA = sb("tmpA", [128, 2, 512], F32)
        tmpB = sb("tmpB", [128, 2, 512], F32)
        vg = sb("vg", [128, 2, 512], F32)
        vnb = sb("vnb", [128, 2, 512], BF16)
        vt = sb("vt", [128, 2, 512], BF16)
        ao = sb("ao", [128, 2, 4, 128], BF16)
        tmpm = sb("tmpm", [128, 2, 4, 128], F32)
        ss = sb("ss", [128, 4], F32)
        ps = es.enter_context(nc.psum_tensor("ps" + sfx, [128, 8, 512], F32))
        uT = H[:, 4:8, :]
        wtm = H[:, 0:4, :].rearrange("p a (b t) -> p (a b) t", t=512)
        tk = Trk(nc, sem_es or es, ["dx", "dout", "dwt", "dpt"] + FFN_STREAMS, sfx)
        for c in range(DCH):
            tk.dma("sp", "dx", lambda c=c: nc.sync.dma_start(out=xT[:, c, :], in_=x_d[:, c, :]), writes=[f"xT{c}"])
        for (t_s, t_d, key) in ((gf, gf_d, "gf"), (gm, gm_d, "gm"), (posi, pos_d, "posi"), (rc, rc_d, "rc"), (vnorm, vn_d, "vnorm"),
                                (wsT, ws_d, "wsT"), (trT, tr_d, "trT"), (bsB, bs_d, "bsB")):
            tk.dma("sp", "dx", lambda t_s=t_s, t_d=t_d: nc.sync.dma_start(out=t_s[:], in_=t_d), writes=[key])
        tk.dma("pool", "dpt", lambda: nc.gpsimd.dma_start(out=PT[:, :], in_=pt_d[:, :]), writes=["PT"])
        tk.op("dve", lambda: nc.vector.memset(ones[:, :], 1.0), writes=["ones"])
        tk.op("dve", lambda: nc.vector.memset(epsc[:, :], RMS_EPS), writes=["epsc"])
        tk.op("dve", lambda: nc.vector.tensor_tensor(out=WmT[:, :, :], in0=wsT[:, :, :], in1=trT[:, :, :], op=ALU.mult),
              reads=["wsT", "trT"], writes=["WmT"])
        st = {"w": 0, "sg": 0, "wd": 0, "dn": 0}
        emit_rmsnorm(tk, nc, xT, gf, "gf", hn, rstd, ps, ones, epsc)
        emit_ffn(tk, nc, st, xT, hn, H, sg, ps, wbuf, wdbuf, wg_d, wu_d, wd_d)
        emit_rmsnorm(tk, nc, xT, gm, "gm", hn, rstd, ps, ones, epsc)
        for c in range(DCH):
            tk.dma("sp", "dout", lambda c=c: nc.sync.dma_start(out=xo_d[:, c, :], in_=xT[:, c, :]), reads=[f"xT{c}"])
        sgkeys = [f"sg{si}_{tt}" for si in range(2) for tt in range(NT)]
        cosT = sg[0:32, 0, :]
        sinT = sg[0:32, 1, :]
        ang = rstd[0:32, :]
        rkeys = [f"rstd{tt}" for tt in range(NT)]
        tk.op("dve", lambda: nc.vector.tensor_copy(out=ang, in_=posi[:, :]), reads=["posi"] + rkeys, writes=rkeys)
        tk.op("dve", lambda: nc.vector.tensor_scalar(out=ang, in0=ang, scalar1=rc[:, 0:1], scalar2=None, op0=ALU.mult), reads=rkeys + ["rc"], writes=rkeys)
        for (tab, off) in ((sinT, 0.0), (cosT, 0.25)):
            tk.op("dve", lambda tab=tab, off=off: nc.vector.tensor_scalar(out=tab, in0=ang, scalar1=float(1.0 / (2 * np.pi)), scalar2=float(off),
                                                                          op0=ALU.mult, op1=ALU.add), reads=rkeys, writes=sgkeys)
            tk.op("dve", lambda tab=tab: nc.vector.tensor_copy(out=posi[:, :], in_=tab), reads=sgkeys, writes=["posi"])
            tk.op("dve", lambda: nc.vector.tensor_copy(out=tmpA[0:32, :, :].rearrange("p a b -> p (a b)"), in_=posi[:, 0:1024]), reads=["posi"], writes=["tmpA0", "tmpA1"])
            tk.op("dve", lambda: nc.vector.tensor_copy(out=tmpB[0:32, :, :].rearrange("p a b -> p (a b)"), in_=posi[:, 1024:2048]), reads=["posi"], writes=["tmpB0", "tmpB1"])
            tk.op("dve", lambda tab=tab: nc.vector.tensor_tensor(out=tab[:, 0:1024], in0=tab[:, 0:1024], in1=tmpA[0:32, :, :].rearrange("p a b -> p (a b)"), op=ALU.subtract),
                  reads=sgkeys + ["tmpA0", "tmpA1"], writes=sgkeys)
            tk.op("dve", lambda tab=tab: nc.vector.tensor_tensor(out=tab[:, 1024:2048], in0=tab[:, 1024:2048], in1=tmpB[0:32, :, :].rearrange("p a b -> p (a b)"), op=ALU.subtract),
                  reads=sgkeys + ["tmpB0", "tmpB1"], writes=sgkeys)
            tk.op("dve", lambda tab=tab: nc.vector.tensor_scalar(out=tmpA[0:32, :, :].rearrange("p a b -> p (a b)"), in0=tab[:, 0:1024], scalar1=0.5, scalar2=None, op0=ALU.is_gt),
                  reads=sgkeys, writes=["tmpA0", "tmpA1"])
            tk.op("dve", lambda tab=tab: nc.vector.tensor_scalar(out=tmpB[0:32, :, :].rearrange("p a b -> p (a b)"), in0=tab[:, 1024:2048], scalar1=0.5, scalar2=None, op0=ALU.is_gt),
                  reads=sgkeys, writes=["tmpB0", "tmpB1"])
            tk.op("dve", lambda tab=tab: nc.vector.tensor_tensor(out=tab[:, 0:1024], in0=tab[:, 0:1024], in1=tmpA[0:32, :, :].rearrange("p a b -> p (a b)"), op=ALU.subtract),
                  reads=sgkeys + ["tmpA0", "tmpA1"], writes=sgkeys)
            tk.op("dve", lambda tab=tab: nc.vector.tensor_tensor(out=tab[:, 1024:2048], in0=tab[:, 1024:2048], in1=tmpB[0:32, :, :].rearrange("p a b -> p (a b)"), op=ALU.subtract),
                  reads=sgkeys + ["tmpB0", "tmpB1"], writes=sgkeys)
            tk.op("act", lambda tab=tab: nc.scalar.activation(out=tab, in_=tab, func=AF.Sin, scale=float(2 * np.pi)), reads=sgkeys, writes=sgkeys)
        tk.op("dve", lambda: nc.vector.tensor_scalar(out=sinT, in0=sinT, scalar1=rc[:, 1:2], scalar2=None, op0=ALU.mult), reads=sgkeys + ["rc"], writes=sgkeys)
        t4 = [(tmpA[:, 0, :], "tmpA0"), (tmpA[:, 1, :], "tmpA1"), (tmpB[:, 0, :], "tmpB0"), (tmpB[:, 1, :], "tmpB1")]
        pending = None
        for j in range(12):
            wi = st["w"] % 4
            st["w"] += 1
            wt = wbuf[wi]
            tk.dma("pool", f"dw{wi}", lambda wt=wt, j=j: nc.gpsimd.dma_start(out=wt[:, :, :], in_=wfm_d[j]), writes=[f"w{wi}"])
            bs = (j % 2) * 4
            for c in range(DCH):
                for tt in range(NT):
                    tk.op("pe", lambda wt=wt, c=c, tt=tt, bs=bs: nc.tensor.matmul(
                        ps[:, bs + tt, :], lhsT=wt[:, c, :], rhs=hn[:, c, tsl(tt)], start=(c == 0), stop=(c == DCH - 1)),
                        reads=[f"w{wi}", f"hn{c}"], writes=[f"ps{bs + tt}"])
            if pending is not None:
                pending()
                pending = None
            if j < 4:
                for tt in range(NT):
                    b = bs + tt
                    emit_gelu(tk, nc, ps[:, b, :], f"ps{b}", t4[tt][0], t4[tt][1], uT[:, j, tsl(tt)], f"H{4 + j}")
            else:
                idx = (j - 4) % 4
                for tt in range(NT):
                    b = bs + tt
                    tk.op("act", lambda idx=idx, tt=tt, b=b: nc.scalar.copy(out=H[:, idx, tsl(tt)], in_=ps[:, b, :]),
                          reads=[f"ps{b}"], writes=[f"H{idx}"])
                    tk.op("dve", lambda tt=tt, b=b: nc.vector.tensor_tensor(out=t4[tt][0][0:32, :], in0=ps[0:32, b, :], in1=cosT[:, tsl(tt)], op=ALU.mult),
                          reads=[f"ps{b}"] + sgkeys, writes=[t4[tt][1]])

                def part2(j=j, idx=idx, bs=bs):
                    for tt in range(NT):
                        b = bs + tt
                        tk.op("pe", lambda: nc.tensor.matmul(ps[:, b, :], lhsT=PT[:, :], rhs=H[:, idx, tsl(tt)], start=True, stop=True),
                              reads=["PT", f"H{idx}"], writes=[f"ps{b}"])
                        tk.op("dve", lambda: nc.vector.tensor_tensor(out=vg[0:32, tt % 2, :], in0=ps[0:32, b, :], in1=sinT[:, tsl(tt)], op=ALU.mult),
                              reads=[f"ps{b}"] + sgkeys, writes=[f"vg{tt % 2}"])
                        tk.op("dve", lambda: nc.vector.tensor_tensor(out=H[0:32, idx, tsl(tt)], in0=t4[tt][0][0:32, :], in1=vg[0:32, tt % 2, :], op=ALU.add),
                              reads=[t4[tt][1], f"vg{tt % 2}"], writes=[f"H{idx}"])
                    tk.dma("sp", "dout", lambda: nc.sync.dma_start(out=qk_d[:, j - 4, :], in_=H[:, idx, :]), reads=[f"H{idx}"])
                pending = part2
        if pending is not None:
            pending()
        for w in range(2):
            tk.dma("pool", "dwt", lambda w=w: nc.gpsimd.dma_start(out=wtm[:, w * 8:(w + 1) * 8, :], in_=wtm_d[w]),
                   writes=["H0", "H1", "H2", "H3"])
        wkeys = ["H0", "H1", "H2", "H3"]
        def p1a(i):
            par = i % 2
            bA, bB = par * 3, par * 3 + 1
            tok = slice(i * 128, (i + 1) * 128)
            for w, b in ((0, bA), (1, bB)):
                for c in range(DCH):
                    tk.op("pe", lambda w=w, b=b, c=c: nc.tensor.matmul(ps[:, b, :], lhsT=hn[:, c, tok], rhs=wtm[:, w * 8 + c, :], start=(c == 0), stop=(c == DCH - 1)),
                          reads=[f"hn{c}"] + wkeys, writes=[f"ps{b}"])
            tk.op("act", lambda: nc.scalar.copy(out=vt[:, par, :], in_=ps[:, bB, :]), reads=[f"ps{bB}"], writes=[f"vt{par}"])
            tk.dma("sp", "dout", lambda: nc.sync.dma_start(out=vo_d[i], in_=vt[:, par, :]), reads=[f"vt{par}"])

        def p1b(i):
            par = i % 2
            bA = par * 3
            emit_gelu(tk, nc, ps[:, bA, :], f"ps{bA}", tmpA[:, par, :], f"tmpA{par}", vg[:, par, :], f"vg{par}")
            tk.op("act", lambda: nc.scalar.activation(out=tmpB[:, par, :], in_=vg[:, par, :], func=AF.Square), reads=[f"vg{par}"], writes=[f"tmpB{par}"])
            tk.op("dve", lambda: nc.vector.reduce_sum(out=ss[:, par:par + 1], in_=tmpB[:, par, :], axis=AX), reads=[f"tmpB{par}"], writes=[f"ss{par}"])
            tk.op("act", lambda: nc.scalar.activation(out=ss[:, par:par + 1], in_=ss[:, par:par + 1], func=AF.Sqrt, scale=1.0 / 512.0, bias=epsc[:, 0:1]),
                  reads=[f"ss{par}", "epsc"], writes=[f"ss{par}"])
            tk.op("dve", lambda: nc.vector.reciprocal(out=ss[:, par:par + 1], in_=ss[:, par:par + 1]), reads=[f"ss{par}"], writes=[f"ss{par}"])
            tk.op("dve", lambda: nc.vector.scalar_tensor_tensor(out=vnb[:, par, :], in0=vg[:, par, :], scalar=ss[:, par:par + 1], in1=vnorm[:, :], op0=ALU.mult, op1=ALU.mult),
                  reads=[f"vg{par}", f"ss{par}", "vnorm"], writes=[f"vnb{par}"])

        def p2(i):
            par = i % 2
            bC = par * 3 + 2
            tok = slice(i * 128, (i + 1) * 128)
            for g in range(4):
                tk.op("pe", lambda g=g: nc.tensor.matmul(ps[:, bC, g * 128:(g + 1) * 128], lhsT=vnb[:, par, g * 128:(g + 1) * 128], rhs=WmT[:, g, :], start=True, stop=True),
                      reads=[f"vnb{par}", "WmT"], writes=[f"ps{bC}"])
            tk.op("dve", lambda: nc.vector.tensor_tensor(out=tmpm[:, par, :, :], in0=ps[:, bC, :].rearrange("p (g t) -> p g t", g=4), in1=bsB[:, :, :], op=ALU.add),
                  reads=[f"ps{bC}", "bsB"], writes=[f"tmpm{par}"])
            tk.op("dve", lambda: nc.vector.tensor_tensor(out=ao[:, par, :, :], in0=tmpm[:, par, :, :], in1=uT[:, :, tok], op=ALU.mult),
                  reads=[f"tmpm{par}", "H4", "H5", "H6", "H7"], writes=[f"ao{par}"])
            tk.dma("sp", "dout", lambda: nc.sync.dma_start(out=ao_d[:, :, tok], in_=ao[:, par, :, :]), reads=[f"ao{par}"])

        p1a(0)
        p1b(0)
        for i in range(16):
            if i + 1 < 16:
                p1a(i + 1)
            p2(i)
            if i + 1 < 16:
                p1b(i + 1)
        tk.finish("sp", ["dout"])
        tk.barrier()
    return nc


def relayout_ffn(wg, wu, wd):
    a = np.ascontiguousarray(np.asarray(wg, np.float32).reshape(DCH, 128, FCH, 128).transpose(2, 1, 0, 3))
    b = np.ascontiguousarray(np.asarray(wu, np.float32).reshape(DCH, 128, FCH, 128).transpose(2, 1, 0, 3))
    c = np.ascontiguousarray(np.asarray(wd, np.float32).reshape(FCH, 128, DCH, 128).transpose(2, 1, 0, 3))
    return a, b, c


def fm_vec(v):
    return np.ascontiguousarray(np.asarray(v, np.float32).reshape(DCH, 128).T)


def rope_consts():
    half = 16
    inv = (np.float32(500000.0) ** (-np.arange(half, dtype=np.float32) / np.float32(half))).astype(np.float32)
    rc = np.zeros((32, 4), np.float32)
    rc[:, 0] = np.concatenate([inv, inv])
    rc[:16, 1] = -1.0
    rc[16:, 1] = 1.0
    rc[:, 2] = -np.pi
    PT = np.zeros((128, 128), np.float32)
    for i in range(16):
        PT[i + 16, i] = 1.0
        PT[i, i + 16] = 1.0
    return rc, PT


def l1_inputs(inputs, core):
    b, half = core // 2, core % 2
    tok = slice(half * T_CORE, (half + 1) * T_CORE)
    x = np.asarray(inputs["x"], np.float32)[b, tok]
    xT = np.ascontiguousarray(x.T.reshape(DCH, 128, T_CORE).transpose(1, 0, 2))
    return xT


def l1_shared(inputs):
    wg, wu, wd = relayout_ffn(inputs["ffn_w_gate"][0, 0], inputs["ffn_w_up"][0, 0], inputs["ffn_w_down"][0, 0])
    w_in = np.asarray(inputs["hyb_w_in"], np.float32)[0]
    cols_fm = np.concatenate([w_in[:, 0:512], w_in[:, 1024:1536], w_in[:, 1536:2048]], axis=1)
    wfm = np.ascontiguousarray(cols_fm.reshape(DCH, 128, 12, 128).transpose(2, 1, 0, 3))
    cols_tm = np.stack([w_in[:, 512:1024], w_in[:, 2048:2560]], axis=0)
    wtm = np.ascontiguousarray(cols_tm.reshape(2, DCH, 128, 512).transpose(0, 2, 1, 3))
    rc, PT = rope_consts()
    ws = np.asarray(inputs["gmlp_w_s"], np.float32)[0]
    wsT = np.ascontiguousarray(ws.transpose(2, 0, 1))
    tril = np.tril(np.ones((128, 128), np.float32))
    trilT = np.ascontiguousarray(np.broadcast_to(tril.T[:, None, :], (128, 4, 128)))
    bsB = np.ascontiguousarray(np.broadcast_to(np.asarray(inputs["gmlp_b_s"], np.float32)[0][None], (128, 4, 128)))
    vnorm = np.ascontiguousarray(np.broadcast_to(np.asarray(inputs["gmlp_v_norm"], np.float32)[0][None], (128, 512)))
    return {
        "g_ffn": fm_vec(inputs["ffn_norm"][0, 0]), "g_mix": fm_vec(inputs["mix_norm"][0]),
        "wg0": wg, "wu0": wu, "wd0": wd, "win_fm": wfm, "win_tm": wtm,
        "ropec": rc, "PT": PT, "vnorm": vnorm, "wsT": wsT, "trilT": trilT, "bsB": bsB,
    }


def run_l1(inputs, cores=None):
    cores = list(range(N_CORES)) if cores is None else cores
    nc = build_l1()
    shared = l1_shared(inputs)
    pos = np.asarray(inputs["positions"], np.int32)
    maps = []
    for c in cores:
        b, half = c // 2, c % 2
        m = dict(shared)
        m["xT"] = l1_inputs(inputs, c)
        m["pos"] = np.ascontiguousarray(np.broadcast_to(pos[b, half * T_CORE:(half + 1) * T_CORE][None], (32, T_CORE)))
        maps.append(m)
    res = run_bass_kernel_spmd(nc, maps, core_ids=list(range(len(cores))))
    return res.results


ATT_SCALE = 128.0 ** -0.5
NEG = -1e30


def emit_proj_fm(tk, nc, st, ps, wbuf, w_chunks, in_chunks, epilogue):
    KC = len(in_chunks)
    for j, wsrc in enumerate(w_chunks):
        wi = st["w"] % len(wbuf)
        st["w"] += 1
        wt = wbuf[wi]
        tk.dma("pool", f"dw{wi}", lambda wt=wt, wsrc=wsrc: nc.gpsimd.dma_start(out=wt[:, 0:KC, :], in_=wsrc), writes=[f"w{wi}"])
        bs = (st["pj"] % 2) * 4
        st["pj"] += 1
        for c, (iap, ikey) in enumerate(in_chunks):
            for tt in range(NT):
                tk.op("pe", lambda wt=wt, c=c, tt=tt, bs=bs, iap=iap: nc.tensor.matmul(
                    ps[:, bs + tt, :], lhsT=wt[:, c, :], rhs=iap[:, tsl(tt)], start=(c == 0), stop=(c == KC - 1)),
                    reads=[f"w{wi}", ikey], writes=[f"ps{bs + tt}"])
        epilogue(j, bs)


def emit_resid_epilogue(tk, nc, ps, xT):
    def ep(j, bs):
        for tt in range(NT):
            tk.op("dve", lambda j=j, tt=tt, bs=bs: nc.vector.tensor_tensor(out=xT[:, j, tsl(tt)], in0=ps[:, bs + tt, :], in1=xT[:, j, tsl(tt)], op=ALU.add),
                  reads=[f"ps{bs + tt}", f"xT{j}"], writes=[f"xT{j}"])
    return ep


def emit_final_norm(tk, nc, xT, g, gkey, hn, rstd, ps, ones, sg, out_d, epsc):
    for c in range(DCH):
        tk.op("act", lambda c=c: nc.scalar.activation(out=hn[:, c, :], in_=xT[:, c, :], func=AF.Square), reads=[f"xT{c}"], writes=[f"hn{c}"])
    for tt in range(NT):
        for c in range(DCH):
            tk.op("pe", lambda c=c, tt=tt: nc.tensor.matmul(ps[:, tt, :], lhsT=ones[:, :], rhs=hn[:, c, tsl(tt)], start=(c == 0), stop=(c == DCH - 1)),
                  reads=[f"hn{c}", "ones"], writes=[f"ps{tt}"])
    for tt in range(NT):
        sl = tsl(tt)
        tk.op("act", lambda tt=tt, sl=sl: nc.scalar.activation(out=rstd[:, sl], in_=ps[:, tt, :], func=AF.Sqrt, scale=1.0 / D_MODEL, bias=epsc[:, 0:1]),
              reads=[f"ps{tt}", "epsc"], writes=[f"rstd{tt}"])
        tk.op("dve", lambda sl=sl: nc.vector.reciprocal(out=rstd[:, sl], in_=rstd[:, sl]), reads=[f"rstd{tt}"], writes=[f"rstd{tt}"])
    for c in range(DCH):
        si = c % 2
        for tt in range(NT):
            sl = tsl(tt)
            tk.op("dve", lambda c=c, sl=sl, si=si: nc.vector.scalar_tensor_tensor(out=sg[:, si, sl], in0=xT[:, c, sl], scalar=g[:, c:c + 1], in1=rstd[:, sl],
                                                                                   op0=ALU.mult, op1=ALU.mult),
                  reads=[f"xT{c}", f"rstd{tt}", gkey], writes=[f"sg{si}_{tt}"])
        tk.dma("sp", "dout", lambda c=c, si=si: nc.sync.dma_start(out=out_d[:, c, :], in_=sg[:, si, :]), reads=[f"sg{si}_{tt}" for tt in range(NT)])


def alloc_ffn_set(nc, es, hchunks=6, sfx=""):
    sb = lambda name, shape, dt: es.enter_context(nc.sbuf_tensor(name + sfx, shape, dt))
    d = {}
    d["xT"] = sb("xT_s", [128, DCH, T_CORE], F32)
    d["hn"] = sb("hn", [128, DCH, T_CORE], BF16)
    d["H"] = sb("H", [128, hchunks, T_CORE], BF16)
    d["sg"] = sb("sg", [128, 2, T_CORE], F32)
    d["rstd"] = sb("rstd", [128, T_CORE], F32)
    d["ones"] = sb("ones", [128, 128], BF16)
    d["epsc"] = sb("epsc", [128, 1], F32)
    d["wbuf"] = [sb(f"wb{i}", [128, DCH, 128], BF16) for i in range(4)]
    d["wdbuf"] = [sb(f"wdb{i}", [128, 6, 128], BF16) for i in range(3)]
    return d


def build_l2(nc=None, ov=None, sfx="", nblocks=lambda j: 9 + j, sem_es=None, conv_here=False):
    nc = nc or bass.Bass("TRN2", target_bir_lowering=False)
    ov = ov or {}
    di, do = io_makers(nc, ov)
    x_d = di("xT", [128, DCH, T_CORE], F32)
    a_d = di("aT", [128, 4, T_CORE], BF16)
    q_d = di("QT", [128, 4, T_CORE], BF16)
    kparts = ov.get("KT_parts")
    vparts = ov.get("V_parts")
    k_d = None if kparts else di("KTp", [128, 4, 2 * T_CORE], BF16)
    v_d = None if vparts else di("Vp", [128, 32, 512], BF16)
    vb_d = di("vbias", [128, 8, 16], F32)
    no_d = di("notown", [128, 8, 16], F32)
    e_d = di("Esel", [32, 16, 128], BF16)
    tri_d = di("tri", [128, 4, 256], BF16)
    idf_d = di("identf", [128, 128], F32)
    wo_d = di("wo", [8, 128, DCH, 128], F32)
    gA_d = di("g_ffn_a", [128, DCH], F32)
    gB_d = di("g_ffn_b", [128, DCH], F32)
    gM_d = di("g_mix", [128, DCH], F32)
    wgA, wuA, wdA = ffn_dram(nc, "A", ov)
    wgB, wuB, wdB = ffn_dram(nc, "B", ov)
    wc_d = di("wc", [24, 128, DCH, 128], F32)
    xo_d = do("xT_o", [128, DCH, T_CORE], F32)
    if conv_here:
        cw_d = di("cw", [128, DCH, 3], F32)
        g_d = do("g_o", [128, DCH, T_CORE], BF16)
        zh_d = do("zh_o", [128, DCH, 2], F32)
        halo_in = ov.get("halo_in")
    else:
        z_d = do("zT_o", [128, DCH, T_CORE], F32)
        bg_d = do("bgT_o", [128, DCH, T_CORE], F32)
    with ExitStack() as es:
        sb = lambda name, shape, dt: es.enter_context(nc.sbuf_tensor(name + sfx, shape, dt))
        tk = Trk(nc, sem_es or es, ["dx", "dout"] + FFN_STREAMS, sfx)
        bT = sb("bT", [128, 4, T_CORE], BF16)
        ps = es.enter_context(nc.psum_tensor("ps" + sfx, [128, 8, 512], F32))
        st = {"w": 0, "sg": 0, "wd": 0, "dn": 0, "pj": 0}
        with ExitStack() as ea:
            sa = lambda name, shape, dt: ea.enter_context(nc.sbuf_tensor(name + sfx, shape, dt))
            QT = sa("QT_s", [128, 4, T_CORE], BF16)
            KT = sa("KT_s", [128, 4, 2 * T_CORE], BF16)
            V = sa("V_s", [128, 32, 512], BF16)
            vbias = sa("vbias_s", [128, 8, 16], F32)
            notown = sa("notown_s", [128, 8, 16], F32)
            E = sa("E_s", [32, 16, 128], BF16)
            tri = sa("tri_s", [128, 4, 256], BF16)
            identf = sa("identf_s", [128, 128], F32)
            identb = sa("identb", [128, 128], BF16)
            kms = sa("kms", [128, 4, 16], F32)
            kmT = sa("kmT", [128, 4, 16], BF16)
            gm = sa("gm", [128, 2, 16], F32)
            top8 = sa("top8", [128, 2, 8], F32)
            thr = sa("thr", [128, 2, 1], F32)
            G32 = sa("G32", [128, 2, 32], F32)
            biasT = sa("biasT", [32, 2, 256], BF16)
            PTs = sa("PTs", [128, 6, 256], BF16)
            rL = sa("rL", [128, 2, 256], F32)
            onesb = sa("onesb", [128, 128], BF16)
            for h in range(4):
                tk.dma("sp", "dx", lambda h=h: nc.sync.dma_start(out=QT[:, h, :], in_=q_d[:, h, :]), writes=[f"QT{h}"])
                if kparts:
                    for hf in range(2):
                        tk.dma("sp", "dx", lambda h=h, hf=hf: nc.sync.dma_start(out=KT[:, h, hf * T_CORE:(hf + 1) * T_CORE], in_=kparts[hf][:, h, :]), writes=[f"KT{h}"])
                else:
                    tk.dma("sp", "dx", lambda h=h: nc.sync.dma_start(out=KT[:, h, :], in_=k_d[:, h, :]), writes=[f"KT{h}"])
            for v4 in range(4):
                if vparts:
                    src = vparts[v4 // 2][(v4 % 2) * 8:(v4 % 2 + 1) * 8].rearrange("t p c -> p t c")
                    tk.dma("sp", "dx", lambda v4=v4, src=src: nc.sync.dma_start(out=V[:, v4 * 8:(v4 + 1) * 8, :], in_=src), writes=[f"V{v4}"])
                else:
                    tk.dma("sp", "dx", lambda v4=v4: nc.sync.dma_start(out=V[:, v4 * 8:(v4 + 1) * 8, :], in_=v_d[:, v4 * 8:(v4 + 1) * 8, :]), writes=[f"V{v4}"])
            for (t_s, t_d, key) in ((vbias, vb_d, "vbias"), (notown, no_d, "notown"), (E, e_d, "E"), (tri, tri_d, "tri"), (identf, idf_d, "identf")):
                tk.dma("sp", "dx", lambda t_s=t_s, t_d=t_d: nc.sync.dma_start(out=t_s[:], in_=t_d), writes=[key])
            tk.op("dve", lambda: nc.vector.tensor_copy(out=identb[:, :], in_=identf[:, :]), reads=["identf"], writes=["identb"])
            tk.op("dve", lambda: nc.vector.memset(onesb[:, :], 1.0), writes=["onesb"])
            tk.op("dve", lambda: nc.vector.memset(G32[:, :, :], 0.0), writes=["G32_0", "G32_1"])
            for h in range(4):
                tk.op("dve", lambda h=h: nc.vector.reduce_sum(out=kms[:, h, :], in_=KT[:, h, :].rearrange("p (n k) -> p n k", k=256), axis=AX),
                      reads=[f"KT{h}"], writes=["kms"])
            tk.op("dve", lambda: nc.vector.tensor_scalar(out=kmT[:, :, :], in0=kms[:, :, :], scalar1=1.0 / 256.0, scalar2=None, op0=ALU.mult),
                  reads=["kms"], writes=["kmT"])
            SB = [0, 1, 2, 5, 6]
            NPT = 6
            OB, LB = 3, 4
            its = [(h, j) for h in range(4) for j in range(8)]
            sctr = [0]
            pctr = [0]

            def preA(ctr):
                h, j = its[ctr]
                for qt in range(2):
                    qsl = slice(j * 256 + qt * 128, j * 256 + (qt + 1) * 128)
                    tk.op("pe", lambda: nc.tensor.matmul(ps[:, 7, qt * 16:(qt + 1) * 16], lhsT=QT[:, h, qsl], rhs=kmT[:, h, :], start=True, stop=True),
                          reads=[f"QT{h}", "kmT"], writes=["ps7"])
                for qt in range(2):
                    tk.op("dve", lambda: nc.vector.tensor_tensor(out=gm[:, qt, :], in0=ps[:, 7, qt * 16:(qt + 1) * 16], in1=vbias[:, j, :], op=ALU.add),
                          reads=["ps7", "vbias"], writes=[f"gm{qt}"])
                    tk.op("dve", lambda: nc.vector.max(out=top8[:, qt, :], in_=gm[:, qt, :]), reads=[f"gm{qt}"], writes=[f"top8{qt}"])
                    tk.op("dve", lambda: nc.vector.tensor_scalar(out=thr[:, qt, :], in0=top8[:, qt, 2:3], scalar1=-1e29, scalar2=None, op0=ALU.max),
                          reads=[f"top8{qt}"], writes=[f"thr{qt}"])
                    tk.op("dve", lambda: nc.vector.tensor_scalar(out=G32[:, qt, 0:16], in0=gm[:, qt, :], scalar1=thr[:, qt, 0:1], scalar2=None, op0=ALU.is_ge),
                          reads=[f"gm{qt}", f"thr{qt}"], writes=[f"G32_{qt}"])
                    tk.op("dve", lambda: nc.vector.tensor_scalar(out=G32[:, qt, 0:16], in0=G32[:, qt, 0:16], scalar1=-1.0, scalar2=1e30, op0=ALU.add, op1=ALU.mult),
                          reads=[f"G32_{qt}"], writes=[f"G32_{qt}"])
                    tk.op("dve", lambda: nc.vector.tensor_tensor(out=G32[:, qt, 0:16], in0=G32[:, qt, 0:16], in1=notown[:, j, :], op=ALU.mult),
                          reads=[f"G32_{qt}", "notown"], writes=[f"G32_{qt}"])

            def preB(ctr):
                bb = ctr % 2
                for qt in range(2):
                    tk.op("pe", lambda: nc.tensor.transpose(out=ps[0:32, 7, 64 + qt * 128:64 + (qt + 1) * 128], in_=G32[:, qt, :], identity=identf[:, :]),
                          reads=[f"G32_{qt}", "identf"], writes=["ps7"])
                tk.op("act", lambda: nc.scalar.copy(out=biasT[:, bb, :], in_=ps[0:32, 7, 64:320]), reads=["ps7"], writes=[f"biasT{bb}"])

            preA(0)
            preB(0)
            for ctr, (h, j) in enumerate(its):
                bb = ctr % 2
                rb = ctr % 2
                qs = slice(j * 256, (j + 1) * 256)
                nb = nblocks(j)
                tiles = [(n, kt) for n in range(nb) for kt in range(2)]
                NTI = len(tiles)
                sbase, pbase = sctr[0], pctr[0]
                sctr[0] += NTI
                pctr[0] += NTI

                def emit_S(i):
                    n, kt = tiles[i]
                    sbk = SB[(sbase + i) % len(SB)]
                    ktile = 2 * n + kt
                    extra = (n == j) or (n == 8 + j)
                    tk.op("pe", lambda: nc.tensor.matmul(ps[:, sbk, 0:256], lhsT=KT[:, h, ktile * 128:(ktile + 1) * 128], rhs=QT[:, h, qs], start=True, stop=False),
                          reads=[f"KT{h}", f"QT{h}"], writes=[f"ps{sbk}"])
                    tk.op("pe", lambda: nc.tensor.matmul(ps[:, sbk, 0:256], lhsT=E[:, n, :], rhs=biasT[:, bb, :], start=False, stop=(not extra)),
                          reads=["E", f"biasT{bb}"], writes=[f"ps{sbk}"])
                    if extra:
                        ti = (0 if n == j else 2) + kt
                        tk.op("pe", lambda: nc.tensor.matmul(ps[:, sbk, 0:256], lhsT=identb[:, :], rhs=tri[:, ti, :], start=False, stop=True),
                              reads=["identb", "tri"], writes=[f"ps{sbk}"])

                def emit_EXP(i):
                    sbk = SB[(sbase + i) % len(SB)]
                    pb = (pbase + i) % NPT
                    tk.op("act", lambda: nc.scalar.activation(out=PTs[:, pb, :], in_=ps[:, sbk, 0:256], func=AF.Exp, scale=ATT_SCALE),
                          reads=[f"ps{sbk}"], writes=[f"PTs{pb}"])

                def emit_PV(i):
                    n, kt = tiles[i]
                    pb = (pbase + i) % NPT
                    ktile = 2 * n + kt
                    tk.op("pe", lambda: nc.tensor.matmul(ps[:, OB, 0:256], lhsT=V[:, ktile, h * 128:(h + 1) * 128], rhs=PTs[:, pb, :], start=(i == 0), stop=(i == NTI - 1)),
                          reads=[f"V{ktile // 8}", f"PTs{pb}"], writes=[f"ps{OB}"])
                    tk.op("pe", lambda: nc.tensor.matmul(ps[:, LB, 0:256], lhsT=onesb[:, :], rhs=PTs[:, pb, :], start=(i == 0), stop=(i == NTI - 1)),
                          reads=["onesb", f"PTs{pb}"], writes=[f"ps{LB}"])

                DEPTH = 4
                for i in range(min(DEPTH, NTI)):
                    emit_S(i)
                if ctr + 1 < len(its):
                    preA(ctr + 1)
                for i in range(NTI):
                    if i + DEPTH < NTI:
                        emit_S(i + DEPTH)
                    emit_EXP(i)
                    emit_PV(i)
                if ctr + 1 < len(its):
                    preB(ctr + 1)
                tk.op("dve", lambda: nc.vector.reciprocal(out=rL[:, rb, :], in_=ps[:, LB, 0:256]), reads=[f"ps{LB}"], writes=[f"rL{rb}"])
                tk.op("dve", lambda: nc.vector.tensor_tensor(out=bT[:, h, qs], in0=ps[:, OB, 0:256], in1=rL[:, rb, :], op=ALU.mult),
                      reads=[f"ps{OB}", f"rL{rb}"], writes=[f"bT{h}"])
            tk.barrier()
        with ExitStack() as eb:
            f = alloc_ffn_set(nc, eb, hchunks=6, sfx=sfx)
            xT, hn, H, sg, rstd, ones, wbuf, wdbuf, epsc = (f[k] for k in ("xT", "hn", "H", "sg", "rstd", "ones", "wbuf", "wdbuf", "epsc"))
            sbb = lambda name, shape, dt: eb.enter_context(nc.sbuf_tensor(name + sfx, shape, dt))
            gA = sbb("gA", [128, DCH], F32)
            gB = sbb("gB", [128, DCH], F32)
            gM = sbb("gM", [128, DCH], F32)
            for c in range(DCH):
                tk.dma("sp", "dx", lambda c=c: nc.sync.dma_start(out=xT[:, c, :], in_=x_d[:, c, :]), writes=[f"xT{c}"])
            for c in range(4):
                tk.dma("sp", "dx", lambda c=c: nc.sync.dma_start(out=H[:, c, :], in_=a_d[:, c, :]), writes=[f"H{c}"])
            for (t_s, t_d, key) in ((gA, gA_d, "gA"), (gB, gB_d, "gB"), (gM, gM_d, "gM")):
                tk.dma("sp", "dx", lambda t_s=t_s, t_d=t_d: nc.sync.dma_start(out=t_s[:], in_=t_d), writes=[key])
            tk.op("dve", lambda: nc.vector.memset(ones[:, :], 1.0), writes=["ones"])
            tk.op("dve", lambda: nc.vector.memset(epsc[:, :], RMS_EPS), writes=["epsc"])
            in_chunks = [(H[:, c, :], f"H{c}") for c in range(4)] + [(bT[:, h, :], f"bT{h}") for h in range(4)]
            emit_proj_fm(tk, nc, st, ps, wbuf, [wo_d[j] for j in range(8)], in_chunks, emit_resid_epilogue(tk, nc, ps, xT))
            emit_rmsnorm(tk, nc, xT, gA, "gA", hn, rstd, ps, ones, epsc)
            emit_ffn(tk, nc, st, xT, hn, H, sg, ps, wbuf, wdbuf, wgA, wuA, wdA)
            emit_rmsnorm(tk, nc, xT, gB, "gB", hn, rstd, ps, ones, epsc)
            emit_ffn(tk, nc, st, xT, hn, H, sg, ps, wbuf, wdbuf, wgB, wuB, wdB)
            emit_rmsnorm(tk, nc, xT, gM, "gM", hn, rstd, ps, ones, epsc)
            for c in range(DCH):
                tk.dma("sp", "dout", lambda c=c: nc.sync.dma_start(out=xo_d[:, c, :], in_=xT[:, c, :]), reads=[f"xT{c}"])
            sgk = lambda si: [f"sg{si}_{tt}" for tt in range(NT)]

            def conv_ep(jj, bs):
                if jj < 16:
                    i, si = jj // 2, (jj // 2) % 2
                    if jj % 2 == 0:
                        for tt in range(NT):
                            tk.op("act", lambda tt=tt, si=si, bs=bs: nc.scalar.copy(out=sg[:, si, tsl(tt)], in_=ps[:, bs + tt, :]),
                                  reads=[f"ps{bs + tt}"], writes=[f"sg{si}_{tt}"])
                    else:
                        for tt in range(NT):
                            tk.op("dve", lambda tt=tt, si=si, bs=bs: nc.vector.tensor_tensor(out=sg[:, si, tsl(tt)], in0=sg[:, si, tsl(tt)], in1=ps[:, bs + tt, :], op=ALU.mult),
                                  reads=[f"sg{si}_{tt}", f"ps{bs + tt}"], writes=[f"sg{si}_{tt}"])
                        tk.dma("sp", "dout", lambda i=i, si=si: nc.sync.dma_start(out=z_d[:, i, :], in_=sg[:, si, :]), reads=sgk(si))
                else:
                    i, si = jj - 16, jj % 2
                    for tt in range(NT):
                        tk.op("act", lambda tt=tt, si=si, bs=bs: nc.scalar.copy(out=sg[:, si, tsl(tt)], in_=ps[:, bs + tt, :]),
                              reads=[f"ps{bs + tt}"], writes=[f"sg{si}_{tt}"])
                    tk.dma("sp", "dout", lambda i=i, si=si: nc.sync.dma_start(out=bg_d[:, i, :], in_=sg[:, si, :]), reads=sgk(si))

            if not conv_here:
                emit_proj_fm(tk, nc, st, ps, wbuf, [wc_d[j] for j in range(24)], [(hn[:, c, :], f"hn{c}") for c in range(DCH)], conv_ep)
            else:
                cw = sbb("cw_s", [128, DCH, 3], F32)
                zc = [sbb(f"zc{i}", [128, T_CORE + 2], F32) for i in range(2)]
                gout = [sbb(f"gout{i}", [128, T_CORE], BF16) for i in range(2)]
                tk.dma("sp", "dx", lambda: nc.sync.dma_start(out=cw[:], in_=cw_d), writes=["cw"])
                rk = [f"rstd{tt}" for tt in range(NT)]

                def conv_ep2(jj, bs):
                    i, r, si = jj // 3, jj % 3, (jj // 3) % 2
                    if r == 0:
                        for tt in range(NT):
                            tk.op("act", lambda tt=tt, bs=bs: nc.scalar.copy(out=rstd[:, tsl(tt)], in_=ps[:, bs + tt, :]), reads=[f"ps{bs + tt}"], writes=[f"rstd{tt}"])
                    elif r == 1:
                        for tt in range(NT):
                            tk.op("act", lambda tt=tt, si=si, bs=bs: nc.scalar.copy(out=sg[:, si, tsl(tt)], in_=ps[:, bs + tt, :]),
                                  reads=[f"ps{bs + tt}"], writes=[f"sg{si}_{tt}"])
                    else:
                        if halo_in is None:
                            tk.op("dve", lambda si=si: nc.vector.memset(zc[si][:, 0:2], 0.0), writes=[f"zc{si}"])
                        else:
                            tk.dma("sp", "dx", lambda si=si, i=i: nc.sync.dma_start(out=zc[si][:, 0:2], in_=halo_in[:, i, :]), writes=[f"zc{si}"])
                        for tt in range(NT):
                            o = tt * 512
                            tk.op("dve", lambda tt=tt, si=si, bs=bs, o=o: nc.vector.tensor_tensor(out=zc[si][:, 2 + o:2 + o + 512], in0=sg[:, si, tsl(tt)], in1=ps[:, bs + tt, :], op=ALU.mult),
                                  reads=[f"sg{si}_{tt}", f"ps{bs + tt}"], writes=[f"zc{si}"])
                        tk.dma("sp", "dout", lambda si=si, i=i: nc.sync.dma_start(out=zh_d[:, i, :], in_=zc[si][:, T_CORE:T_CORE + 2]), reads=[f"zc{si}"])
                        for tt in range(NT):
                            o = tt * 512
                            sl = tsl(tt)
                            key = f"sg{si}_{tt}"
                            tk.op("dve", lambda si=si, i=i, sl=sl, o=o: nc.vector.tensor_scalar(out=sg[:, si, sl], in0=zc[si][:, o:o + 512], scalar1=cw[:, i, 0:1], scalar2=None, op0=ALU.mult),
                                  reads=[f"zc{si}", "cw"], writes=[key])
                            tk.op("dve", lambda si=si, i=i, sl=sl, o=o: nc.vector.scalar_tensor_tensor(out=sg[:, si, sl], in0=zc[si][:, o + 1:o + 513], scalar=cw[:, i, 1:2], in1=sg[:, si, sl],
                                                                                                     op0=ALU.mult, op1=ALU.add),
                                  reads=[f"zc{si}", "cw", key], writes=[key])
                            tk.op("dve", lambda si=si, i=i, sl=sl, o=o: nc.vector.scalar_tensor_tensor(out=sg[:, si, sl], in0=zc[si][:, o + 2:o + 514], scalar=cw[:, i, 2:3], in1=sg[:, si, sl],
                                                                                                     op0=ALU.mult, op1=ALU.add),
                                  reads=[f"zc{si}", "cw", key], writes=[key])
                            tk.op("dve", lambda si=si, sl=sl, tt=tt: nc.vector.tensor_tensor(out=gout[si][:, sl], in0=sg[:, si, sl], in1=rstd[:, sl], op=ALU.mult),
                                  reads=[key, f"rstd{tt}"], writes=[f"gout{si}"])
                        tk.dma("sp", "dout", lambda si=si, i=i: nc.sync.dma_start(out=g_d[:, i, :], in_=gout[si][:, :]), reads=[f"gout{si}"])

                emit_proj_fm(tk, nc, st, ps, wbuf, [wc_d[j] for j in range(24)], [(hn[:, c, :], f"hn{c}") for c in range(DCH)], conv_ep2)
            tk.finish("sp", ["dout"])
            tk.barrier()
    return nc


def build_l3(nc=None, ov=None, sfx="", sem_es=None):
    nc = nc or bass.Bass("TRN2", target_bir_lowering=False)
    ov = ov or {}
    di, do = io_makers(nc, ov)
    x_d = di("xT", [128, DCH, T_CORE], F32)
    g_src = ov.get("g_src")
    z_src = ov.get("z_src")
    halo_src = ov.get("halo_src")
    ze_d = None if (z_src is not None or g_src is not None) else di("zext", [128, DCH, T_CORE + 2], F32)
    bg_d = None if g_src is not None else di("bgT", [128, DCH, T_CORE], F32)
    cw_d = None if g_src is not None else di("cw", [128, DCH, 3], F32)
    wco_d = di("wco", [8, 128, DCH, 128], F32)
    gC_d = di("g_ffn_c", [128, DCH], F32)
    gF_d = di("g_fin", [128, DCH], F32)
    wgC, wuC, wdC = ffn_dram(nc, "C", ov)
    out_d = do("outT", [128, DCH, T_CORE], F32)
    with ExitStack() as es:
        sb = lambda name, shape, dt: es.enter_context(nc.sbuf_tensor(name + sfx, shape, dt))
        tk = Trk(nc, sem_es or es, ["dx", "dz", "dout"] + FFN_STREAMS, sfx)
        f = alloc_ffn_set(nc, es, hchunks=6, sfx=sfx)
        xT, hn, H, sg, rstd, ones, wbuf, wdbuf, epsc = (f[k] for k in ("xT", "hn", "H", "sg", "rstd", "ones", "wbuf", "wdbuf", "epsc"))
        gC = sb("gC", [128, DCH], F32)
        gF = sb("gF", [128, DCH], F32)
        if g_src is None:
            cw = sb("cw_s", [128, DCH, 3], F32)
            zc = [sb(f"zc{i}", [128, T_CORE + 2], F32) for i in range(2)]
            bgc = [sb(f"bgc{i}", [128, T_CORE], F32) for i in range(2)]
        ps = es.enter_context(nc.psum_tensor("ps" + sfx, [128, 8, 512], F32))
        st = {"w": 0, "sg": 0, "wd": 0, "dn": 0, "pj": 0}
        for c in range(DCH):
            tk.dma("sp", "dx", lambda c=c: nc.sync.dma_start(out=xT[:, c, :], in_=x_d[:, c, :]), writes=[f"xT{c}"])
        for (t_s, t_d, key) in ((gC, gC_d, "gC"), (gF, gF_d, "gF")) + (((cw, cw_d, "cw"),) if g_src is None else ()):
            tk.dma("sp", "dx", lambda t_s=t_s, t_d=t_d: nc.sync.dma_start(out=t_s[:], in_=t_d), writes=[key])
        if g_src is not None:
            for c in range(DCH):
                tk.dma("sp", "dx", lambda c=c: nc.sync.dma_start(out=hn[:, c, :], in_=g_src[:, c, :]), writes=[f"hn{c}"])
        tk.op("dve", lambda: nc.vector.memset(ones[:, :], 1.0), writes=["ones"])
        tk.op("dve", lambda: nc.vector.memset(epsc[:, :], RMS_EPS), writes=["epsc"])
        for c in (range(DCH) if g_src is None else ()):
            ci = c % 2
            if z_src is None:
                tk.dma("sp", f"dz", lambda c=c, ci=ci: nc.sync.dma_start(out=zc[ci][:, :], in_=ze_d[:, c, :]), writes=[f"zc{ci}"])
            else:
                tk.dma("sp", f"dz", lambda c=c, ci=ci: nc.sync.dma_start(out=zc[ci][:, 2:T_CORE + 2], in_=z_src[:, c, :]), writes=[f"zc{ci}"])
                if halo_src is None:
                    tk.op("dve", lambda ci=ci: nc.vector.memset(zc[ci][:, 0:2], 0.0), writes=[f"zc{ci}"])
                else:
                    tk.dma("sp", f"dz", lambda c=c, ci=ci: nc.sync.dma_start(out=zc[ci][:, 0:2], in_=halo_src[:, c, T_CORE - 2:T_CORE]), writes=[f"zc{ci}"])
            tk.dma("sp", f"dz", lambda c=c, ci=ci: nc.sync.dma_start(out=bgc[ci][:, :], in_=bg_d[:, c, :]), writes=[f"bgc{ci}"])
            en, eo = "dve", nc.vector
            for tt in range(NT):
                sl = tsl(tt)
                o = tt * 512
                key = f"sg{ci}_{tt}"
                tk.op(en, lambda c=c, ci=ci, sl=sl, o=o, eo=eo: eo.tensor_scalar(out=sg[:, ci, sl], in0=zc[ci][:, o:o + 512], scalar1=cw[:, c, 0:1], scalar2=None, op0=ALU.mult),
                      reads=[f"zc{ci}", "cw"], writes=[key])
                tk.op(en, lambda c=c, ci=ci, sl=sl, o=o, eo=eo: eo.scalar_tensor_tensor(out=sg[:, ci, sl], in0=zc[ci][:, o + 1:o + 513], scalar=cw[:, c, 1:2], in1=sg[:, ci, sl],
                                                                                        op0=ALU.mult, op1=ALU.add),
                      reads=[f"zc{ci}", "cw", key], writes=[key])
                tk.op(en, lambda c=c, ci=ci, sl=sl, o=o, eo=eo: eo.scalar_tensor_tensor(out=sg[:, ci, sl], in0=zc[ci][:, o + 2:o + 514], scalar=cw[:, c, 2:3], in1=sg[:, ci, sl],
                                                                                        op0=ALU.mult, op1=ALU.add),
                      reads=[f"zc{ci}", "cw", key], writes=[key])
                tk.op(en, lambda c=c, ci=ci, sl=sl, eo=eo: eo.tensor_tensor(out=hn[:, c, sl], in0=sg[:, ci, sl], in1=bgc[ci][:, sl], op=ALU.mult),
                      reads=[key, f"bgc{ci}"], writes=[f"hn{c}"])
        emit_proj_fm(tk, nc, st, ps, wbuf, [wco_d[j] for j in range(8)], [(hn[:, c, :], f"hn{c}") for c in range(DCH)], emit_resid_epilogue(tk, nc, ps, xT))
        emit_rmsnorm(tk, nc, xT, gC, "gC", hn, rstd, ps, ones, epsc)
        emit_ffn(tk, nc, st, xT, hn, H, sg, ps, wbuf, wdbuf, wgC, wuC, wdC)
        emit_final_norm(tk, nc, xT, gF, "gF", hn, rstd, ps, ones, sg, out_d, epsc)
        tk.finish("sp", ["dout"])
        tk.barrier()
    return nc


def relayout_fm(W):
    W = np.asarray(W, np.float32)
    nch = W.shape[1] // 128
    return np.ascontiguousarray(W.reshape(DCH, 128, nch, 128).transpose(2, 1, 0, 3))


def attn_consts(half):
    vb = np.zeros((8, 16), np.float32)
    no = np.ones((8, 16), np.float32)
    for j in range(8):
        qblk = 8 * half + j
        vb[j, qblk:] = NEG
        no[j, qblk] = 0.0
    E = np.zeros((32, 16, 128), np.float32)
    for n in range(16):
        E[n, n, :] = 1.0
    tri = np.zeros((128, 4, 256), np.float32)
    k = np.arange(128)[:, None]
    q = np.arange(256)[None, :]
    for kt in range(2):
        m = np.where(kt * 128 + k > q, NEG, 0.0).astype(np.float32)
        tri[:, (0 if half == 0 else 2) + kt, :] = m
    return {
        "vbias": np.ascontiguousarray(np.broadcast_to(vb[None], (128, 8, 16))),
        "notown": np.ascontiguousarray(np.broadcast_to(no[None], (128, 8, 16))),
        "Esel": bf16_np(E), "tri": bf16_np(tri), "identf": np.eye(128, dtype=np.float32),
    }


def run_l2(inputs, r1, cores):
    nc = build_l2()
    wgA, wuA, wdA = relayout_ffn(inputs["ffn_w_gate"][0, 1], inputs["ffn_w_up"][0, 1], inputs["ffn_w_down"][0, 1])
    wgB, wuB, wdB = relayout_ffn(inputs["ffn_w_gate"][1, 0], inputs["ffn_w_up"][1, 0], inputs["ffn_w_down"][1, 0])
    cwi = np.asarray(inputs["conv_w_in"], np.float32)[0]
    cols = []
    for i in range(8):
        cols.append(cwi[:, 1024 + i * 128:1024 + (i + 1) * 128])
        cols.append(cwi[:, 2048 + i * 128:2048 + (i + 1) * 128])
    for i in range(8):
        cols.append(cwi[:, i * 128:(i + 1) * 128])
    shared = {
        "wo": relayout_fm(inputs["hyb_w_out"][0]),
        "g_ffn_a": fm_vec(inputs["ffn_norm"][0, 1]), "g_ffn_b": fm_vec(inputs["ffn_norm"][1, 0]), "g_mix": fm_vec(inputs["mix_norm"][1]),
        "wgA": wgA, "wuA": wuA, "wdA": wdA, "wgB": wgB, "wuB": wuB, "wdB": wdB,
        "wc": relayout_fm(np.concatenate(cols, axis=1)),
    }
    idx = {c: i for i, c in enumerate(cores)}
    maps = []
    for c in cores:
        b, half = c // 2, c % 2
        r0, rp = r1[idx[2 * b]], r1[idx[2 * b + 1]]
        m = dict(shared)
        m.update(attn_consts(half))
        me = r1[idx[c]]
        m["xT"] = np.asarray(me["xT_o"])
        m["aT"] = np.asarray(me["aT_o"])
        m["QT"] = np.ascontiguousarray(np.asarray(me["QKT_o"])[:, 0:4, :])
        m["KTp"] = np.ascontiguousarray(np.concatenate([np.asarray(r0["QKT_o"])[:, 4:8, :], np.asarray(rp["QKT_o"])[:, 4:8, :]], axis=2))
        vp = np.concatenate([np.asarray(r0["V_o"]), np.asarray(rp["V_o"])], axis=0)
        m["Vp"] = np.ascontiguousarray(vp.transpose(1, 0, 2))
        maps.append(m)
    res = run_bass_kernel_spmd(nc, maps, core_ids=list(range(len(cores))))
    return res.results


def run_l3(inputs, r2, cores):
    nc = build_l3()
    wgC, wuC, wdC = relayout_ffn(inputs["ffn_w_gate"][1, 1], inputs["ffn_w_up"][1, 1], inputs["ffn_w_down"][1, 1])
    cwv = np.asarray(inputs["conv_w"], np.float32)[0]
    cw = np.ascontiguousarray(cwv.reshape(3, DCH, 128).transpose(2, 1, 0))
    shared = {
        "cw": cw, "wco": relayout_fm(inputs["conv_w_out"][0]),
        "g_ffn_c": fm_vec(inputs["ffn_norm"][1, 1]), "g_fin": fm_vec(inputs["final_norm"]),
        "wgC": wgC, "wuC": wuC, "wdC": wdC,
    }
    idx = {c: i for i, c in enumerate(cores)}
    maps = []
    for c in cores:
        b, half = c // 2, c % 2
        me = r2[idx[c]]
        z = np.asarray(me["zT_o"])
        if half == 0:
            halo = np.zeros((128, DCH, 2), np.float32)
        else:
            halo = np.asarray(r2[idx[2 * b]]["zT_o"])[:, :, T_CORE - 2:T_CORE]
        m = dict(shared)
        m["xT"] = np.asarray(me["xT_o"])
        m["zext"] = np.ascontiguousarray(np.concatenate([halo, z], axis=2))
        m["bgT"] = np.asarray(me["bgT_o"])
        maps.append(m)
    res = run_bass_kernel_spmd(nc, maps, core_ids=list(range(len(cores))))
    return res.results


def run_all(inputs, cores):
    inputs = {k: np.asarray(v) for k, v in inputs.items()}
    r1 = run_l1(inputs, cores)
    r2 = run_l2(inputs, r1, cores)
    r3 = run_l3(inputs, r2, cores)
    outs = {}
    for i, c in enumerate(cores):
        oT = np.asarray(r3[i]["outT"], np.float32)
        outs[c] = np.ascontiguousarray(oT.transpose(1, 0, 2).reshape(D_MODEL, T_CORE).T)
    return outs, (r1, r2, r3)
def build_fused():
    nc = bass.Bass("TRN2", target_bir_lowering=False)
    ext = lambda name, shape, dt: nc.dram_tensor(name, shape, dt, kind="ExternalInput").ap()
    itn = lambda name, shape, dt: nc.dram_tensor(name, shape, dt, kind="Internal").ap()
    W = {}
    for sfx in ("0", "A", "B", "C"):
        W["wg" + sfx] = ext("wg" + sfx, [FCH, 128, DCH, 128], F32)
        W["wu" + sfx] = ext("wu" + sfx, [FCH, 128, DCH, 128], F32)
        W["wd" + sfx] = ext("wd" + sfx, [DCH, 128, FCH, 128], F32)
    for name, shape, dt in (("g_ffn", [128, DCH], F32), ("g_mix1", [128, DCH], F32), ("win_fm", [12, 128, DCH, 128], F32), ("win_tm", [2, 128, DCH, 512], F32),
                            ("ropec", [32, 4], F32), ("PT", [128, 128], F32), ("vnorm", [128, 512], F32), ("wsT", [128, 4, 128], F32),
                            ("trilT", [128, 4, 128], F32), ("bsB", [128, 4, 128], F32),
                            ("Esel", [32, 16, 128], BF16), ("identf", [128, 128], F32), ("wo", [8, 128, DCH, 128], F32),
                            ("g_ffn_a", [128, DCH], F32), ("g_ffn_b", [128, DCH], F32), ("g_mix2", [128, DCH], F32), ("wc", [24, 128, DCH, 128], F32),
                            ("cw", [128, DCH, 3], F32), ("wco", [8, 128, DCH, 128], F32), ("g_ffn_c", [128, DCH], F32), ("g_fin", [128, DCH], F32)):
        W[name] = ext(name, shape, dt)
    x2 = ext("xT2", [2, 128, DCH, T_CORE], F32)
    pos2 = ext("pos2", [2, 32, T_CORE], I32)
    vb2 = ext("vbias2", [2, 128, 8, 16], F32)
    no2 = ext("notown2", [2, 128, 8, 16], F32)
    tri2 = ext("tri2", [2, 128, 4, 256], BF16)
    out2 = nc.dram_tensor("outT2", [2, 128, DCH, T_CORE], F32, kind="ExternalOutput").ap()
    x1_i = itn("x1_i", [2, 128, DCH, T_CORE], F32)
    a_i = itn("a_i", [2, 128, 4, T_CORE], BF16)
    qk_i = itn("qk_i", [2, 128, 8, T_CORE], BF16)
    v_i = itn("v_i", [2, 16, 128, 512], BF16)
    x4_i = itn("x4_i", [2, 128, DCH, T_CORE], F32)
    g_i = itn("g_i", [2, 128, DCH, T_CORE], BF16)
    zh_i = itn("zh_i", [2, 128, DCH, 2], F32)
    with ExitStack() as sem_es:
        for hf in range(2):
            ov = dict(W)
            ov.update({"g_mix": W["g_mix1"], "xT": x2[hf], "pos": pos2[hf], "xT_o": x1_i[hf], "aT_o": a_i[hf], "QKT_o": qk_i[hf], "V_o": v_i[hf]})
            build_l1(nc, ov, sfx=f"_p1h{hf}", sem_es=sem_es)
        for hf in range(2):
            ov = dict(W)
            ov.update({"g_mix": W["g_mix2"], "xT": x1_i[hf], "aT": a_i[hf], "QT": qk_i[hf][:, 0:4, :],
                       "KT_parts": [qk_i[0][:, 4:8, :], qk_i[1][:, 4:8, :]], "V_parts": [v_i[0], v_i[1]],
                       "vbias": vb2[hf], "notown": no2[hf], "tri": tri2[hf],
                       "xT_o": x4_i[hf], "g_o": g_i[hf], "zh_o": zh_i[hf], "halo_in": (zh_i[0] if hf == 1 else None)})
            nblocks = (lambda j: j + 1) if hf == 0 else (lambda j: 9 + j)
            build_l2(nc, ov, sfx=f"_p2h{hf}", nblocks=nblocks, sem_es=sem_es, conv_here=True)
        for hf in range(2):
            ov = dict(W)
            ov.update({"xT": x4_i[hf], "g_src": g_i[hf], "outT": out2[hf]})
            build_l3(nc, ov, sfx=f"_p3h{hf}", sem_es=sem_es)
    return nc


def fused_shared(inputs):
    s1 = l1_shared(inputs)
    sh = {k: s1[k] for k in ("wg0", "wu0", "wd0", "g_ffn", "win_fm", "win_tm", "ropec", "PT", "vnorm", "wsT", "trilT", "bsB")}
    sh["g_mix1"] = s1["g_mix"]
    wgA, wuA, wdA = relayout_ffn(inputs["ffn_w_gate"][0, 1], inputs["ffn_w_up"][0, 1], inputs["ffn_w_down"][0, 1])
    wgB, wuB, wdB = relayout_ffn(inputs["ffn_w_gate"][1, 0], inputs["ffn_w_up"][1, 0], inputs["ffn_w_down"][1, 0])
    wgC, wuC, wdC = relayout_ffn(inputs["ffn_w_gate"][1, 1], inputs["ffn_w_up"][1, 1], inputs["ffn_w_down"][1, 1])
    cwi = np.asarray(inputs["conv_w_in"], np.float32)[0]
    cols = []
    for i in range(8):
        cols.append(cwi[:, i * 128:(i + 1) * 128])
        cols.append(cwi[:, 1024 + i * 128:1024 + (i + 1) * 128])
        cols.append(cwi[:, 2048 + i * 128:2048 + (i + 1) * 128])
    cwv = np.asarray(inputs["conv_w"], np.float32)[0]
    c0, c1 = attn_consts(0), attn_consts(1)
    sh.update({
        "wgA": wgA, "wuA": wuA, "wdA": wdA, "wgB": wgB, "wuB": wuB, "wdB": wdB, "wgC": wgC, "wuC": wuC, "wdC": wdC,
        "wo": relayout_fm(inputs["hyb_w_out"][0]), "wc": relayout_fm(np.concatenate(cols, axis=1)), "wco": relayout_fm(inputs["conv_w_out"][0]),
        "g_ffn_a": fm_vec(inputs["ffn_norm"][0, 1]), "g_ffn_b": fm_vec(inputs["ffn_norm"][1, 0]), "g_mix2": fm_vec(inputs["mix_norm"][1]),
        "g_ffn_c": fm_vec(inputs["ffn_norm"][1, 1]), "g_fin": fm_vec(inputs["final_norm"]),
        "cw": np.ascontiguousarray(cwv.reshape(3, DCH, 128).transpose(2, 1, 0)),
        "Esel": c0["Esel"], "identf": c0["identf"],
        "vbias2": np.stack([c0["vbias"], c1["vbias"]]), "notown2": np.stack([c0["notown"], c1["notown"]]), "tri2": np.stack([c0["tri"], c1["tri"]]),
    })
    return sh


def run_fused(inputs, batches):
    inputs = {k: np.asarray(v) for k, v in inputs.items()}
    nc = build_fused()
    sh = fused_shared(inputs)
    pos = np.asarray(inputs["positions"], np.int32)
    maps = []
    for b in batches:
        m = dict(sh)
        m["xT2"] = np.stack([l1_inputs(inputs, 2 * b), l1_inputs(inputs, 2 * b + 1)])
        m["pos2"] = np.ascontiguousarray(np.broadcast_to(pos[b].reshape(2, 1, T_CORE), (2, 32, T_CORE)))
        maps.append(m)
    res = run_bass_kernel_spmd(nc, maps, core_ids=list(range(len(batches))))
    outs = {}
    for i, b in enumerate(batches):
        oT = np.asarray(res.results[i]["outT2"], np.float32)
        outs[b] = np.concatenate([oT[hf].transpose(1, 0, 2).reshape(D_MODEL, T_CORE).T for hf in range(2)], axis=0)
    return outs


def kernel(**inputs):
    outs = run_fused(inputs, list(range(BATCH)))
    return np.ascontiguousarray(np.stack([outs[b] for b in range(BATCH)]).astype(np.float32))
```

```python
import numpy as np
import ml_dtypes
import concourse.bass as bass
import concourse.mybir as mybir
from concourse.bass_utils import run_bass_kernel_spmd

F32 = mybir.dt.float32
BF16 = mybir.dt.bfloat16
I32 = mybir.dt.int32
ALU = mybir.AluOpType
AF = mybir.ActivationFunctionType

N_CORES = 8
D_MODEL = 1024
SEQ = 4096
BATCH = 4
T_CORE = 2048
DCH = D_MODEL // 128
D_FF = 2816
FCH = D_FF // 128
RMS_EPS = 1e-6


from contextlib import ExitStack

AX = mybir.AxisListType.X
GROUPS = [(0, 6), (6, 12), (12, 17), (17, 22)]
NT = T_CORE // 512
GELU_C = 0.044715
GELU_S = 1.5957691216057308


def bf16_np(a):
    return np.asarray(a, dtype=np.float32).astype(ml_dtypes.bfloat16)


class Trk:
    def __init__(self, nc, es, dma_streams, sfx=""):
        self.nc = nc
        self.eng = {"pe": nc.tensor, "act": nc.scalar, "dve": nc.vector, "pool": nc.gpsimd, "sp": nc.sync}
        names = list(self.eng) + list(dma_streams)
        self.dma_streams = set(dma_streams)
        self.sem = {n: es.enter_context(nc.semaphore("s_" + n + sfx)) for n in names}
        self.cnt = {k: 0 for k in names}
        self.known = {e: {} for e in self.eng}
        self.lastw = {}
        self.readers = {}

    def _wait(self, e, semname, val):
        if val <= 0:
            return
        if semname in self.dma_streams:
            val = self.cnt[semname]
        if self.known[e].get(semname, 0) >= val:
            return
        if e == "pe" and semname == "pe":
            return
        self.eng[e].wait_ge(self.sem[semname], val)
        self.known[e][semname] = val

    def deps(self, e, reads=(), writes=()):
        for k in reads:
            if k in self.lastw:
                self._wait(e, *self.lastw[k])
        for k in writes:
            if k in self.lastw:
                self._wait(e, *self.lastw[k])
            for r in self.readers.get(k, ()):
                self._wait(e, *r)

    def done(self, ins, semname, inc, reads=(), writes=()):
        self.cnt[semname] += inc
        ins.then_inc(self.sem[semname], inc)
        tag = (semname, self.cnt[semname])
        for k in reads:
            self.readers.setdefault(k, []).append(tag)
        for k in writes:
            self.lastw[k] = tag
            self.readers[k] = []
        return tag

    @staticmethod
    def _excl(reads, writes):
        r = [k for k in reads if not k.startswith("ps")]
        w = list(writes) + [k for k in reads if k.startswith("ps") and k not in writes]
        return r, w

    def op(self, e, fn, reads=(), writes=()):
        reads, writes = self._excl(reads, writes)
        self.deps(e, reads, writes)
        return self.done(fn(), e, 1, reads, writes)

    def barrier(self):
        for e in self.eng:
            for sname, c in self.cnt.items():
                if c and not (e == sname):
                    if self.known[e].get(sname, 0) < c:
                        self.eng[e].wait_ge(self.sem[sname], c)
                        self.known[e][sname] = c

    def dma(self, e, stream, fn, reads=(), writes=()):
        self.deps(e, reads, writes)
        return self.done(fn(), stream, 16, reads, writes)

    def finish(self, e, streams):
        for s in streams:
            if self.cnt[s]:
                self.eng[e].wait_ge(self.sem[s], self.cnt[s])


def tsl(tt):
    return slice(tt * 512, (tt + 1) * 512)


def emit_rmsnorm(tk, nc, xT, g, gkey, hn, rstd, ps, ones, epsc):
    for c in range(DCH):
        tk.op("act", lambda c=c: nc.scalar.activation(out=hn[:, c, :], in_=xT[:, c, :], func=AF.Square),
              reads=[f"xT{c}"], writes=[f"hn{c}"])
    for c in range(DCH):
        for tt in range(NT):
            tk.op("pe", lambda c=c, tt=tt: nc.tensor.matmul(ps[:, tt, :], lhsT=ones[:, :], rhs=hn[:, c, tsl(tt)],
                                                             start=(c == 0), stop=(c == DCH - 1)),
                  reads=[f"hn{c}", "ones"], writes=[f"ps{tt}"])
    for tt in range(NT):
        sl = tsl(tt)
        tk.op("act", lambda tt=tt, sl=sl: nc.scalar.activation(out=rstd[:, sl], in_=ps[:, tt, :], func=AF.Sqrt, scale=1.0 / D_MODEL, bias=epsc[:, 0:1]),
              reads=[f"ps{tt}", "epsc"], writes=[f"rstd{tt}"])
        tk.op("dve", lambda sl=sl: nc.vector.reciprocal(out=rstd[:, sl], in_=rstd[:, sl]), reads=[f"rstd{tt}"], writes=[f"rstd{tt}"])
    for c in range(DCH):
        for tt in range(NT):
            sl = tsl(tt)
            tk.op("dve", lambda c=c, sl=sl: nc.vector.scalar_tensor_tensor(out=hn[:, c, sl], in0=xT[:, c, sl], scalar=g[:, c:c + 1], in1=rstd[:, sl],
                                                                             op0=ALU.mult, op1=ALU.mult),
                  reads=[f"xT{c}", f"rstd{tt}", gkey], writes=[f"hn{c}"])


def emit_ffn(tk, nc, st, xT, hn, H, sg, ps, wbuf, wdbuf, wg_d, wu_d, wd_d):
    NW = len(wbuf)
    for (f0, f1) in GROUPS:
        nf = f1 - f0
        for f in range(f0, f1):
            for which, wsrc in ((0, wg_d), (1, wu_d)):
                wi = st["w"] % NW
                st["w"] += 1
                wt = wbuf[wi]
                tk.dma("pool", f"dw{wi}", lambda wt=wt, wsrc=wsrc, f=f: nc.gpsimd.dma_start(out=wt[:, :, :], in_=wsrc[f]),
                       writes=[f"w{wi}"])
                for c in range(DCH):
                    for tt in range(NT):
                        b = which * 4 + tt
                        tk.op("pe", lambda wt=wt, c=c, tt=tt, b=b: nc.tensor.matmul(
                            ps[:, b, :], lhsT=wt[:, c, :], rhs=hn[:, c, tsl(tt)], start=(c == 0), stop=(c == DCH - 1)),
                            reads=[f"w{wi}", f"hn{c}"], writes=[f"ps{b}"])
            si = st["sg"] % 2
            st["sg"] += 1
            for tt in range(NT):
                tk.op("act", lambda tt=tt, si=si: nc.scalar.activation(out=sg[:, si, tsl(tt)], in_=ps[:, tt, :], func=AF.Silu),
                      reads=[f"ps{tt}"], writes=[f"sg{si}_{tt}"])
            for tt in range(NT):
                tk.op("dve", lambda tt=tt, si=si, f=f: nc.vector.tensor_tensor(out=H[:, f - f0, tsl(tt)], in0=sg[:, si, tsl(tt)], in1=ps[:, 4 + tt, :], op=ALU.mult),
                      reads=[f"sg{si}_{tt}", f"ps{4 + tt}"], writes=[f"H{f - f0}"])
        for j in range(DCH):
            di = st["wd"] % len(wdbuf)
            st["wd"] += 1
            wt = wdbuf[di]
            tk.dma("pool", f"dd{di}", lambda wt=wt, j=j, f0=f0, f1=f1, nf=nf: nc.gpsimd.dma_start(out=wt[:, 0:nf, :], in_=wd_d[j, :, f0:f1, :]),
                   writes=[f"wd{di}"])
            bs = (st["dn"] % 2) * 4
            st["dn"] += 1
            for k in range(nf):
                for tt in range(NT):
                    b = bs + tt
                    tk.op("pe", lambda wt=wt, k=k, tt=tt, b=b: nc.tensor.matmul(
                        ps[:, b, :], lhsT=wt[:, k, :], rhs=H[:, k, tsl(tt)], start=(k == 0), stop=(k == nf - 1)),
                        reads=[f"wd{di}", f"H{k}"], writes=[f"ps{b}"])
            for tt in range(NT):
                b = bs + tt
                tk.op("dve", lambda j=j, tt=tt, b=b: nc.vector.scalar_tensor_tensor(out=xT[:, j, tsl(tt)], in0=ps[:, b, :], scalar=0.5, in1=xT[:, j, tsl(tt)],
                                                                                    op0=ALU.mult, op1=ALU.add),
                      reads=[f"ps{b}", f"xT{j}"], writes=[f"xT{j}"])


def emit_gelu(tk, nc, src, srckey, tmp, tmpkey, out, outkey):
    tk.op("act", lambda: nc.scalar.activation(out=tmp, in_=src, func=AF.Square, scale=GELU_C ** 0.5), reads=[srckey], writes=[tmpkey])
    tk.op("dve", lambda: nc.vector.scalar_tensor_tensor(out=tmp, in0=tmp, scalar=1.0, in1=src, op0=ALU.add, op1=ALU.mult),
          reads=[tmpkey, srckey], writes=[tmpkey])
    tk.op("act", lambda: nc.scalar.activation(out=tmp, in_=tmp, func=AF.Sigmoid, scale=GELU_S), reads=[tmpkey], writes=[tmpkey])
    tk.op("dve", lambda: nc.vector.tensor_tensor(out=out, in0=tmp, in1=src, op=ALU.mult), reads=[tmpkey, srckey], writes=[outkey])


FFN_STREAMS = [f"dw{i}" for i in range(4)] + [f"dd{i}" for i in range(3)]


def ffn_dram(nc, sfx, ov=None):
    ov = ov or {}
    mk = lambda name, shape: ov[name] if name in ov else nc.dram_tensor(name, shape, F32, kind="ExternalInput").ap()
    wg = mk("wg" + sfx, [FCH, 128, DCH, 128])
    wu = mk("wu" + sfx, [FCH, 128, DCH, 128])
    wd = mk("wd" + sfx, [DCH, 128, FCH, 128])
    return wg, wu, wd


def io_makers(nc, ov):
    di = lambda name, shape, dt: ov[name] if name in ov else nc.dram_tensor(name, shape, dt, kind="ExternalInput").ap()
    do = lambda name, shape, dt: ov[name] if name in ov else nc.dram_tensor(name, shape, dt, kind="ExternalOutput").ap()
    return di, do


def build_l1(nc=None, ov=None, sfx="", sem_es=None):
    nc = nc or bass.Bass("TRN2", target_bir_lowering=False)
    ov = ov or {}
    di, do = io_makers(nc, ov)
    x_d = di("xT", [128, DCH, T_CORE], F32)
    gf_d = di("g_ffn", [128, DCH], F32)
    gm_d = di("g_mix", [128, DCH], F32)
    wg_d, wu_d, wd_d = ffn_dram(nc, "0", ov)
    wfm_d = di("win_fm", [12, 128, DCH, 128], F32)
    wtm_d = di("win_tm", [2, 128, DCH, 512], F32)
    pos_d = di("pos", [32, T_CORE], I32)
    rc_d = di("ropec", [32, 4], F32)
    pt_d = di("PT", [128, 128], F32)
    vn_d = di("vnorm", [128, 512], F32)
    ws_d = di("wsT", [128, 4, 128], F32)
    tr_d = di("trilT", [128, 4, 128], F32)
    bs_d = di("bsB", [128, 4, 128], F32)
    xo_d = do("xT_o", [128, DCH, T_CORE], F32)
    ao_d = do("aT_o", [128, 4, T_CORE], BF16)
    qk_d = do("QKT_o", [128, 8, T_CORE], BF16)
    vo_d = do("V_o", [16, 128, 512], BF16)
    with ExitStack() as es:
        sb = lambda name, shape, dt: es.enter_context(nc.sbuf_tensor(name + sfx, shape, dt))
        xT = sb("xT_s", [128, DCH, T_CORE], F32)
        hn = sb("hn", [128, DCH, T_CORE], BF16)
        H = sb("H", [128, 8, T_CORE], BF16)
        sg = sb("sg", [128, 2, T_CORE], F32)
        rstd = sb("rstd", [128, T_CORE], F32)
        gf = sb("gf", [128, DCH], F32)
        gm = sb("gm", [128, DCH], F32)
        ones = sb("ones", [128, 128], BF16)
        epsc = sb("epsc", [128, 1], F32)
        wbuf = [sb(f"wb{i}", [128, DCH, 128], BF16) for i in range(4)]
        wdbuf = [sb(f"wdb{i}", [128, 6, 128], BF16) for i in range(3)]
        posi = sb("posi", [32, T_CORE], I32)
        rc = sb("rc", [32, 4], F32)
        PT = sb("PTs", [128, 128], BF16)
        vnorm = sb("vnorm_s", [128, 512], F32)
        wsT = sb("wsT_s", [128, 4, 128], F32)
        trT = sb("trT_s", [128, 4, 128], F32)
        WmT = sb("WmT", [128, 4, 128], BF16)
        bsB = sb("bsB_s", [128, 4, 128], F32)
        tmpA = sb("tmpA", [128, 2, 512], F32)
        tmpB = sb("tmpB", [128, 2, 512], F32)
        vg = sb("vg", [128, 2, 512], F32)
        vnb = sb("vnb", [128, 2, 512], BF16)
        vt = sb("vt", [128, 2, 512], BF16)
        ao = sb("ao", [128, 2, 4, 128], BF16)
        tmpm = sb("tmpm", [128, 2, 4, 128], F32)
        ss = sb("ss", [128, 4], F32)
        ps = es.enter_context(nc.psum_tensor("ps" + sfx, [128, 8, 512], F32))
        uT = H[:, 4:8, :]
        wtm = H[:, 0:4, :].rearrange("p a (b t) -> p (a b) t", t=512)
        tk = Trk(nc, sem_es or es, ["dx", "dout", "dwt", "dpt"] + FFN_STREAMS, sfx)
        for c in range(DCH):
            tk.dma("sp", "dx", lambda c=c: nc.sync.dma_start(out=xT[:, c, :], in_=x_d[:, c, :]), writes=[f"xT{c}"])
        for (t_s, t_d, key) in ((gf, gf_d, "gf"), (gm, gm_d, "gm"), (posi, pos_d, "posi"), (rc, rc_d, "rc"), (vnorm, vn_d, "vnorm"),
                                (wsT, ws_d, "wsT"), (trT, tr_d, "trT"), (bsB, bs_d, "bsB")):
            tk.dma("sp", "dx", lambda t_s=t_s, t_d=t_d: nc.sync.dma_start(out=t_s[:], in_=t_d), writes=[key])
        tk.dma("pool", "dpt", lambda: nc.gpsimd.dma_start(out=PT[:, :], in_=pt_d[:, :]), writes=["PT"])
        tk.op("dve", lambda: nc.vector.memset(ones[:, :], 1.0), writes=["ones"])
        tk.op("dve", lambda: nc.vector.memset(epsc[:, :], RMS_EPS), writes=["epsc"])
        tk.op("dve", lambda: nc.vector.tensor_tensor(out=WmT[:, :, :], in0=wsT[:, :, :], in1=trT[:, :, :], op=ALU.mult),
              reads=["wsT", "trT"], writes=["WmT"])
        st = {"w": 0, "sg": 0, "wd": 0, "dn": 0}
        emit_rmsnorm(tk, nc, xT, gf, "gf", hn, rstd, ps, ones, epsc)
        emit_ffn(tk, nc, st, xT, hn, H, sg, ps, wbuf, wdbuf, wg_d, wu_d, wd_d)
        emit_rmsnorm(tk, nc, xT, gm, "gm", hn, rstd, ps, ones, epsc)
        for c in range(DCH):
            tk.dma("sp", "dout", lambda c=c: nc.sync.dma_start(out=xo_d[:, c, :], in_=xT[:, c, :]), reads=[f"xT{c}"])
        sgkeys = [f"sg{si}_{tt}" for si in range(2) for tt in range(NT)]
        cosT = sg[0:32, 0, :]
        sinT = sg[0:32, 1, :]
        ang = rstd[0:32, :]
        rkeys = [f"rstd{tt}" for tt in range(NT)]
        tk.op("dve", lambda: nc.vector.tensor_copy(out=ang, in_=posi[:, :]), reads=["posi"] + rkeys, writes=rkeys)
        tk.op("dve", lambda: nc.vector.tensor_scalar(out=ang, in0=ang, scalar1=rc[:, 0:1], scalar2=None, op0=ALU.mult), reads=rkeys + ["rc"], writes=rkeys)
        for (tab, off) in ((sinT, 0.0), (cosT, 0.25)):
            tk.op("dve", lambda tab=tab, off=off: nc.vector.tensor_scalar(out=tab, in0=ang, scalar1=float(1.0 / (2 * np.pi)), scalar2=float(off),
                                                                          op0=ALU.mult, op1=ALU.add), reads=rkeys, writes=sgkeys)
            tk.op("dve", lambda tab=tab: nc.vector.tensor_copy(out=posi[:, :], in_=tab), reads=sgkeys, writes=["posi"])
            tk.op("dve", lambda: nc.vector.tensor_copy(out=tmpA[0:32, :, :].rearrange("p a b -> p (a b)"), in_=posi[:, 0:1024]), reads=["posi"], writes=["tmpA0", "tmpA1"])
            tk.op("dve", lambda: nc.vector.tensor_copy(out=tmpB[0:32, :, :].rearrange("p a b -> p (a b)"), in_=posi[:, 1024:2048]), reads=["posi"], writes=["tmpB0", "tmpB1"])
            tk.op("dve", lambda tab=tab: nc.vector.tensor_tensor(out=tab[:, 0:1024], in0=tab[:, 0:1024], in1=tmpA[0:32, :, :].rearrange("p a b -> p (a b)"), op=ALU.subtract),
                  reads=sgkeys + ["tmpA0", "tmpA1"], writes=sgkeys)
            tk.op("dve", lambda tab=tab: nc.vector.tensor_tensor(out=tab[:, 1024:2048], in0=tab[:, 1024:2048], in1=tmpB[0:32, :, :].rearrange("p a b -> p (a b)"), op=ALU.subtract),
                  reads=sgkeys + ["tmpB0", "tmpB1"], writes=sgkeys)
            tk.op("dve", lambda tab=tab: nc.vector.tensor_scalar(out=tmpA[0:32, :, :].rearrange("p a b -> p (a b)"), in0=tab[:, 0:1024], scalar1=0.5, scalar2=None, op0=ALU.is_gt),
                  reads=sgkeys, writes=["tmpA0", "tmpA1"])
            tk.op("dve", lambda tab=tab: nc.vector.tensor_scalar(out=tmpB[0:32, :, :].rearrange("p a b -> p (a b)"), in0=tab[:, 1024:2048], scalar1=0.5, scalar2=None, op0=ALU.is_gt),
                  reads=sgkeys, writes=["tmpB0", "tmpB1"])
            tk.op("dve", lambda tab=tab: nc.vector.tensor_tensor(out=tab[:, 0:1024], in0=tab[:, 0:1024], in1=tmpA[0:32, :, :].rearrange("p a b -> p (a b)"), op=ALU.subtract),
                  reads=sgkeys + ["tmpA0", "tmpA1"], writes=sgkeys)
            tk.op("dve", lambda tab=tab: nc.vector.tensor_tensor(out=tab[:, 1024:2048], in0=tab[:, 1024:2048], in1=tmpB[0:32, :, :].rearrange("p a b -> p (a b)"), op=ALU.subtract),
                  reads=sgkeys + ["tmpB0", "tmpB1"], writes=sgkeys)
            tk.op("act", lambda tab=tab: nc.scalar.activation(out=tab, in_=tab, func=AF.Sin, scale=float(2 * np.pi)), reads=sgkeys, writes=sgkeys)
        tk.op("dve", lambda: nc.vector.tensor_scalar(out=sinT, in0=sinT, scalar1=rc[:, 1:2], scalar2=None, op0=ALU.mult), reads=sgkeys + ["rc"], writes=sgkeys)
        t4 = [(tmpA[:, 0, :], "tmpA0"), (tmpA[:, 1, :], "tmpA1"), (tmpB[:, 0, :], "tmpB0"), (tmpB[:, 1, :], "tmpB1")]
        pending = None
        for j in range(12):
            wi = st["w"] % 4
            st["w"] += 1
            wt = wbuf[wi]
            tk.dma("pool", f"dw{wi}", lambda wt=wt, j=j: nc.gpsimd.dma_start(out=wt[:, :, :], in_=wfm_d[j]), writes=[f"w{wi}"])
            bs = (j % 2) * 4
            for c in range(DCH):
                for tt in range(NT):
                    tk.op("pe", lambda wt=wt, c=c, tt=tt, bs=bs: nc.tensor.matmul(
                        ps[:, bs + tt, :], lhsT=wt[:, c, :], rhs=hn[:, c, tsl(tt)], start=(c == 0), stop=(c == DCH - 1)),
                        reads=[f"w{wi}", f"hn{c}"], writes=[f"ps{bs + tt}"])
            if pending is not None:
                pending()
                pending = None
            if j < 4:
                for tt in range(NT):
                    b = bs + tt
                    emit_gelu(tk, nc, ps[:, b, :], f"ps{b}", t4[tt][0], t4[tt][1], uT[:, j, tsl(tt)], f"H{4 + j}")
            else:
                idx = (j - 4) % 4
                for tt in range(NT):
                    b = bs + tt
                    tk.op("act", lambda idx=idx, tt=tt, b=b: nc.scalar.copy(out=H[:, idx, tsl(tt)], in_=ps[:, b, :]),
                          reads=[f"ps{b}"], writes=[f"H{idx}"])
                    tk.op("dve", lambda tt=tt, b=b: nc.vector.tensor_tensor(out=t4[tt][0][0:32, :], in0=ps[0:32, b, :], in1=cosT[:, tsl(tt)], op=ALU.mult),
                          reads=[f"ps{b}"] + sgkeys, writes=[t4[tt][1]])

                def part2(j=j, idx=idx, bs=bs):
                    for tt in range(NT):
                        b = bs + tt
                        tk.op("pe", lambda: nc.tensor.matmul(ps[:, b, :], lhsT=PT[:, :], rhs=H[:, idx, tsl(tt)], start=True, stop=True),
                              reads=["PT", f"H{idx}"], writes=[f"ps{b}"])
                        tk.op("dve", lambda: nc.vector.tensor_tensor(out=vg[0:32, tt % 2, :], in0=ps[0:32, b, :], in1=sinT[:, tsl(tt)], op=ALU.mult),
                              reads=[f"ps{b}"] + sgkeys, writes=[f"vg{tt % 2}"])
                        tk.op("dve", lambda: nc.vector.tensor_tensor(out=H[0:32, idx, tsl(tt)], in0=t4[tt][0][0:32, :], in1=vg[0:32, tt % 2, :], op=ALU.add),
                              reads=[t4[tt][1], f"vg{tt % 2}"], writes=[f"H{idx}"])
                    tk.dma("sp", "dout", lambda: nc.sync.dma_start(out=qk_d[:, j - 4, :], in_=H[:, idx, :]), reads=[f"H{idx}"])
                pending = part2
        if pending is not None:
            pending()
        for w in range(2):
            tk.dma("pool", "dwt", lambda w=w: nc.gpsimd.dma_start(out=wtm[:, w * 8:(w + 1) * 8, :], in_=wtm_d[w]),
                   writes=["H0", "H1", "H2", "H3"])
        wkeys = ["H0", "H1", "H2", "H3"]
        def p1a(i):
            par = i % 2
            bA, bB = par * 3, par * 3 + 1
            tok = slice(i * 128, (i + 1) * 128)
            for w, b in ((0, bA), (1, bB)):
                for c in range(DCH):
                    tk.op("pe", lambda w=w, b=b, c=c: nc.tensor.matmul(ps[:, b, :], lhsT=hn[:, c, tok], rhs=wtm[:, w * 8 + c, :], start=(c == 0), stop=(c == DCH - 1)),
                          reads=[f"hn{c}"] + wkeys, writes=[f"ps{b}"])
            tk.op("act", lambda: nc.scalar.copy(out=vt[:, par, :], in_=ps[:, bB, :]), reads=[f"ps{bB}"], writes=[f"vt{par}"])
            tk.dma("sp", "dout", lambda: nc.sync.dma_start(out=vo_d[i], in_=vt[:, par, :]), reads=[f"vt{par}"])

        def p1b(i):
            par = i % 2
            bA = par * 3
            emit_gelu(tk, nc, ps[:, bA, :], f"ps{bA}", tmpA[:, par, :], f"tmpA{par}", vg[:, par, :], f"vg{par}")
            tk.op("act", lambda: nc.scalar.activation(out=tmpB[:, par, :], in_=vg[:, par, :], func=AF.Square), reads=[f"vg{par}"], writes=[f"tmpB{par}"])
            tk.op("dve", lambda: nc.vector.reduce_sum(out=ss[:, par:par + 1], in_=tmpB[:, par, :], axis=AX), reads=[f"tmpB{par}"], writes=[f"ss{par}"])
            tk.op("act", lambda: nc.scalar.activation(out=ss[:, par:par + 1], in_=ss[:, par:par + 1], func=AF.Sqrt, scale=1.0 / 512.0, bias=epsc[:, 0:1]),
                  reads=[f"ss{par}", "epsc"], writes=[f"ss{par}"])
            tk.op("dve", lambda: nc.vector.reciprocal(out=ss[:, par:par + 1], in_=ss[:, par:par + 1]), reads=[f"ss{par}"], writes=[f"ss{par}"])
            tk.op("dve", lambda: nc.vector.scalar_tensor_tensor(out=vnb[:, par, :], in0=vg[:, par, :], scalar=ss[:, par:par + 1], in1=vnorm[:, :], op0=ALU.mult, op1=ALU.mult),
                  reads=[f"vg{par}", f"ss{par}", "vnorm"], writes=[f"vnb{par}"])

        def p2(i):
            par = i % 2
            bC = par * 3 + 2
            tok = slice(i * 128, (i + 1) * 128)
            for g in range(4):
                tk.op("pe", lambda g=g: nc.tensor.matmul(ps[:, bC, g * 128:(g + 1) * 128], lhsT=vnb[:, par, g * 128:(g + 1) * 128], rhs=WmT[:, g, :], start=True, stop=True),
                      reads=[f"vnb{par}", "WmT"], writes=[f"ps{bC}"])
            tk.op("dve", lambda: nc.vector.tensor_tensor(out=tmpm[:, par, :, :], in0=ps[:, bC, :].rearrange("p (g t) -> p g t", g=4), in1=bsB[:, :, :], op=ALU.add),
                  reads=[f"ps{bC}", "bsB"], writes=[f"tmpm{par}"])
            tk.op("dve", lambda: nc.vector.tensor_tensor(out=ao[:, par, :, :], in0=tmpm[:, par, :, :], in1=uT[:, :, tok], op=ALU.mult),
                  reads=[f"tmpm{par}", "H4", "H5", "H6", "H7"], writes=[f"ao{par}"])
            tk.dma("sp", "dout", lambda: nc.sync.dma_start(out=ao_d[:, :, tok], in_=ao[:, par, :, :]), reads=[f"ao{par}"])

        p1a(0)
        p1b(0)
        for i in range(16):
            if i + 1 < 16:
                p1a(i + 1)
            p2(i)
            if i + 1 < 16:
                p1b(i + 1)
        tk.finish("sp", ["dout"])
        tk.barrier()
    return nc


def relayout_ffn(wg, wu, wd):
    a = np.ascontiguousarray(np.asarray(wg, np.float32).reshape(DCH, 128, FCH, 128).transpose(2, 1, 0, 3))
    b = np.ascontiguousarray(np.asarray(wu, np.float32).reshape(DCH, 128, FCH, 128).transpose(2, 1, 0, 3))
    c = np.ascontiguousarray(np.asarray(wd, np.float32).reshape(FCH, 128, DCH, 128).transpose(2, 1, 0, 3))
    return a, b, c


def fm_vec(v):
    return np.ascontiguousarray(np.asarray(v, np.float32).reshape(DCH, 128).T)


def rope_consts():
    half = 16
    inv = (np.float32(500000.0) ** (-np.arange(half, dtype=np.float32) / np.float32(half))).astype(np.float32)
    rc = np.zeros((32, 4), np.float32)
    rc[:, 0] = np.concatenate([inv, inv])
    rc[:16, 1] = -1.0
    rc[16:, 1] = 1.0
    rc[:, 2] = -np.pi
    PT = np.zeros((128, 128), np.float32)
    for i in range(16):
        PT[i + 16, i] = 1.0
        PT[i, i + 16] = 1.0
    return rc, PT


def l1_inputs(inputs, core):
    b, half = core // 2, core % 2
    tok = slice(half * T_CORE, (half + 1) * T_CORE)
    x = np.asarray(inputs["x"], np.float32)[b, tok]
    xT = np.ascontiguousarray(x.T.reshape(DCH, 128, T_CORE).transpose(1, 0, 2))
    return xT


def l1_shared(inputs):
    wg, wu, wd = relayout_ffn(inputs["ffn_w_gate"][0, 0], inputs["ffn_w_up"][0, 0], inputs["ffn_w_down"][0, 0])
    w_in = np.asarray(inputs["hyb_w_in"], np.float32)[0]
    cols_fm = np.concatenate([w_in[:, 0:512], w_in[:, 1024:1536], w_in[:, 1536:2048]], axis=1)
    wfm = np.ascontiguousarray(cols_fm.reshape(DCH, 128, 12, 128).transpose(2, 1, 0, 3))
    cols_tm = np.stack([w_in[:, 512:1024], w_in[:, 2048:2560]], axis=0)
    wtm = np.ascontiguousarray(cols_tm.reshape(2, DCH, 128, 512).transpose(0, 2, 1, 3))
    rc, PT = rope_consts()
    ws = np.asarray(inputs["gmlp_w_s"], np.float32)[0]
    wsT = np.ascontiguousarray(ws.transpose(2, 0, 1))
    tril = np.tril(np.ones((128, 128), np.float32))
    trilT = np.ascontiguousarray(np.broadcast_to(tril.T[:, None, :], (128, 4, 128)))
    bsB = np.ascontiguousarray(np.broadcast_to(np.asarray(inputs["gmlp_b_s"], np.float32)[0][None], (128, 4, 128)))
    vnorm = np.ascontiguousarray(np.broadcast_to(np.asarray(inputs["gmlp_v_norm"], np.float32)[0][None], (128, 512)))
    return {
        "g_ffn": fm_vec(inputs["ffn_norm"][0, 0]), "g_mix": fm_vec(inputs["mix_norm"][0]),
        "wg0": wg, "wu0": wu, "wd0": wd, "win_fm": wfm, "win_tm": wtm,
        "ropec": rc, "PT": PT, "vnorm": vnorm, "wsT": wsT, "trilT": trilT, "bsB": bsB,
    }


def run_l1(inputs, cores=None):
    cores = list(range(N_CORES)) if cores is None else cores
    nc = build_l1()
    shared = l1_shared(inputs)
    pos = np.asarray(inputs["positions"], np.int32)
    maps = []
    for c in cores:
        b, half = c // 2, c % 2
        m = dict(shared)
        m["xT"] = l1_inputs(inputs, c)
        m["pos"] = np.ascontiguousarray(np.broadcast_to(pos[b, half * T_CORE:(half + 1) * T_CORE][None], (32, T_CORE)))
        maps.append(m)
    res = run_bass_kernel_spmd(nc, maps, core_ids=list(range(len(cores))))
    return res.results


ATT_SCALE = 128.0 ** -0.5
NEG = -1e30


def emit_proj_fm(tk, nc, st, ps, wbuf, w_chunks, in_chunks, epilogue):
    KC = len(in_chunks)
    for j, wsrc in enumerate(w_chunks):
        wi = st["w"] % len(wbuf)
        st["w"] += 1
        wt = wbuf[wi]
        tk.dma("pool", f"dw{wi}", lambda wt=wt, wsrc=wsrc: nc.gpsimd.dma_start(out=wt[:, 0:KC, :], in_=wsrc), writes=[f"w{wi}"])
        bs = (st["pj"] % 2) * 4
        st["pj"] += 1
        for c, (iap, ikey) in enumerate(in_chunks):
            for tt in range(NT):
                tk.op("pe", lambda wt=wt, c=c, tt=tt, bs=bs, iap=iap: nc.tensor.matmul(
                    ps[:, bs + tt, :], lhsT=wt[:, c, :], rhs=iap[:, tsl(tt)], start=(c == 0), stop=(c == KC - 1)),
                    reads=[f"w{wi}", ikey], writes=[f"ps{bs + tt}"])
        epilogue(j, bs)


def emit_resid_epilogue(tk, nc, ps, xT):
    def ep(j, bs):
        for tt in range(NT):
            tk.op("dve", lambda j=j, tt=tt, bs=bs: nc.vector.tensor_tensor(out=xT[:, j, tsl(tt)], in0=ps[:, bs + tt, :], in1=xT[:, j, tsl(tt)], op=ALU.add),
                  reads=[f"ps{bs + tt}", f"xT{j}"], writes=[f"xT{j}"])
    return ep


def emit_final_norm(tk, nc, xT, g, gkey, hn, rstd, ps, ones, sg, out_d, epsc):
    for c in range(DCH):
        tk.op("act", lambda c=c: nc.scalar.activation(out=hn[:, c, :], in_=xT[:, c, :], func=AF.Square), reads=[f"xT{c}"], writes=[f"hn{c}"])
    for c in range(DCH):
        for tt in range(NT):
            tk.op("pe", lambda c=c, tt=tt: nc.tensor.matmul(ps[:, tt, :], lhsT=ones[:, :], rhs=hn[:, c, tsl(tt)], start=(c == 0), stop=(c == DCH - 1)),
                  reads=[f"hn{c}", "ones"], writes=[f"ps{tt}"])
    for tt in range(NT):
        sl = tsl(tt)
        tk.op("act", lambda tt=tt, sl=sl: nc.scalar.activation(out=rstd[:, sl], in_=ps[:, tt, :], func=AF.Sqrt, scale=1.0 / D_MODEL, bias=epsc[:, 0:1]),
              reads=[f"ps{tt}", "epsc"], writes=[f"rstd{tt}"])
        tk.op("dve", lambda sl=sl: nc.vector.reciprocal(out=rstd[:, sl], in_=rstd[:, sl]), reads=[f"rstd{tt}"], writes=[f"rstd{tt}"])
    for c in range(DCH):
        si = c % 2
        for tt in range(NT):
            sl = tsl(tt)
            tk.op("dve", lambda c=c, sl=sl, si=si: nc.vector.scalar_tensor_tensor(out=sg[:, si, sl], in0=xT[:, c, sl], scalar=g[:, c:c + 1], in1=rstd[:, sl],
                                                                                   op0=ALU.mult, op1=ALU.mult),
                  reads=[f"xT{c}", f"rstd{tt}", gkey], writes=[f"sg{si}_{tt}"])
        tk.dma("sp", "dout", lambda c=c, si=si: nc.sync.dma_start(out=out_d[:, c, :], in_=sg[:, si, :]), reads=[f"sg{si}_{tt}" for tt in range(NT)])


def alloc_ffn_set(nc, es, hchunks=6, sfx=""):
    sb = lambda name, shape, dt: es.enter_context(nc.sbuf_tensor(name + sfx, shape, dt))
    d = {}
    d["xT"] = sb("xT_s", [128, DCH, T_CORE], F32)
    d["hn"] = sb("hn", [128, DCH, T_CORE], BF16)
    d["H"] = sb("H", [128, hchunks, T_CORE], BF16)
    d["sg"] = sb("sg", [128, 2, T_CORE], F32)
    d["rstd"] = sb("rstd", [128, T_CORE], F32)
    d["ones"] = sb("ones", [128, 128], BF16)
    d["epsc"] = sb("epsc", [128, 1], F32)
    d["wbuf"] = [sb(f"wb{i}", [128, DCH, 128], BF16) for i in range(4)]
    d["wdbuf"] = [sb(f"wdb{i}", [128, 6, 128], BF16) for i in range(3)]
    return d


def build_l2(nc=None, ov=None, sfx="", nblocks=lambda j: 9 + j, sem_es=None, conv_here=False):
    nc = nc or bass.Bass("TRN2", target_bir_lowering=False)
    ov = ov or {}
    di, do = io_makers(nc, ov)
    x_d = di("xT", [128, DCH, T_CORE], F32)
    a_d = di("aT", [128, 4, T_CORE], BF16)
    q_d = di("QT", [128, 4, T_CORE], BF16)
    kparts = ov.get("KT_parts")
    vparts = ov.get("V_parts")
    k_d = None if kparts else di("KTp", [128, 4, 2 * T_CORE], BF16)
    v_d = None if vparts else di("Vp", [128, 32, 512], BF16)
    vb_d = di("vbias", [128, 8, 16], F32)
    no_d = di("notown", [128, 8, 16], F32)
    e_d = di("Esel", [32, 16, 128], BF16)
    tri_d = di("tri", [128, 4, 256], BF16)
    idf_d = di("identf", [128, 128], F32)
    wo_d = di("wo", [8, 128, DCH, 128], F32)
    gA_d = di("g_ffn_a", [128, DCH], F32)
    gB_d = di("g_ffn_b", [128, DCH], F32)
    gM_d = di("g_mix", [128, DCH], F32)
    wgA, wuA, wdA = ffn_dram(nc, "A", ov)
    wgB, wuB, wdB = ffn_dram(nc, "B", ov)
    wc_d = di("wc", [24, 128, DCH, 128], F32)
    xo_d = do("xT_o", [128, DCH, T_CORE], F32)
    if conv_here:
        cw_d = di("cw", [128, DCH, 3], F32)
        g_d = do("g_o", [128, DCH, T_CORE], BF16)
        zh_d = do("zh_o", [128, DCH, 2], F32)
        halo_in = ov.get("halo_in")
    else:
        z_d = do("zT_o", [128, DCH, T_CORE], F32)
        bg_d = do("bgT_o", [128, DCH, T_CORE], F32)
    with ExitStack() as es:
        sb = lambda name, shape, dt: es.enter_context(nc.sbuf_tensor(name + sfx, shape, dt))
        tk = Trk(nc, sem_es or es, ["dx", "dout"] + FFN_STREAMS, sfx)
        bT = sb("bT", [128, 4, T_CORE], BF16)
        ps = es.enter_context(nc.psum_tensor("ps" + sfx, [128, 8, 512], F32))
        st = {"w": 0, "sg": 0, "wd": 0, "dn": 0, "pj": 0}
        with ExitStack() as ea:
            sa = lambda name, shape, dt: ea.enter_context(nc.sbuf_tensor(name + sfx, shape, dt))
            QT = sa("QT_s", [128, 4, T_CORE], BF16)
            KT = sa("KT_s", [128, 4, 2 * T_CORE], BF16)
            V = sa("V_s", [128, 32, 512], BF16)
            vbias = sa("vbias_s", [128, 8, 16], F32)
            notown = sa("notown_s", [128, 8, 16], F32)
            E = sa("E_s", [32, 16, 128], BF16)
            tri = sa("tri_s", [128, 4, 256], BF16)
            identf = sa("identf_s", [128, 128], F32)
            identb = sa("identb", [128, 128], BF16)
            kms = sa("kms", [128, 4, 16], F32)
            kmT = sa("kmT", [128, 4, 16], BF16)
            gm = sa("gm", [128, 2, 16], F32)
            top8 = sa("top8", [128, 2, 8], F32)
            thr = sa("thr", [128, 2, 1], F32)
            G32 = sa("G32", [128, 2, 32], F32)
            biasT = sa("biasT", [32, 2, 256], BF16)
            PTs = sa("PTs", [128, 6, 256], BF16)
            rL = sa("rL", [128, 2, 256], F32)
            onesb = sa("onesb", [128, 128], BF16)
            for h in range(4):
                tk.dma("sp", "dx", lambda h=h: nc.sync.dma_start(out=QT[:, h, :], in_=q_d[:, h, :]), writes=[f"QT{h}"])
                if kparts:
                    for hf in range(2):
                        tk.dma("sp", "dx", lambda h=h, hf=hf: nc.sync.dma_start(out=KT[:, h, hf * T_CORE:(hf + 1) * T_CORE], in_=kparts[hf][:, h, :]), writes=[f"KT{h}"])
                else:
                    tk.dma("sp", "dx", lambda h=h: nc.sync.dma_start(out=KT[:, h, :], in_=k_d[:, h, :]), writes=[f"KT{h}"])
            for v4 in range(4):
                if vparts:
                    src = vparts[v4 // 2][(v4 % 2) * 8:(v4 % 2 + 1) * 8].rearrange("t p c -> p t c")
                    tk.dma("sp", "dx", lambda v4=v4, src=src: nc.sync.dma_start(out=V[:, v4 * 8:(v4 + 1) * 8, :], in_=src), writes=[f"V{v4}"])
                else:
                    tk.dma("sp", "dx", lambda v4=v4: nc.sync.dma_start(out=V[:, v4 * 8:(v4 + 1) * 8, :], in_=v_d[:, v4 * 8:(v4 + 1) * 8, :]), writes=[f"V{v4}"])
            for (t_s, t_d, key) in ((vbias, vb_d, "vbias"), (notown, no_d, "notown"), (E, e_d, "E"), (tri, tri_d, "tri"), (identf, idf_d, "identf")):
                tk.dma("sp", "dx", lambda t_s=t_s, t_d=t_d: nc.sync.dma_start(out=t_s[:], in_=t_d), writes=[key])
            tk.op("dve", lambda: nc.vector.tensor_copy(out=identb[:, :], in_=identf[:, :]), reads=["identf"], writes=["identb"])
            tk.op("dve", lambda: nc.vector.memset(onesb[:, :], 1.0), writes=["onesb"])
            tk.op("dve", lambda: nc.vector.memset(G32[:, :, :], 0.0), writes=["G32_0", "G32_1"])
            for h in range(4):
                tk.op("dve", lambda h=h: nc.vector.reduce_sum(out=kms[:, h, :], in_=KT[:, h, :].rearrange("p (n k) -> p n k", k=256), axis=AX),
                      reads=[f"KT{h}"], writes=["kms"])
            tk.op("dve", lambda: nc.vector.tensor_scalar(out=kmT[:, :, :], in0=kms[:, :, :], scalar1=1.0 / 256.0, scalar2=None, op0=ALU.mult),
                  reads=["kms"], writes=["kmT"])
            SB = [0, 1, 2, 5, 6]
            NPT = 6
            OB, LB = 3, 4
            its = [(h, j) for h in range(4) for j in range(8)]
            sctr = [0]
            pctr = [0]

            def preA(ctr):
                h, j = its[ctr]
                for qt in range(2):
                    qsl = slice(j * 256 + qt * 128, j * 256 + (qt + 1) * 128)
                    tk.op("pe", lambda: nc.tensor.matmul(ps[:, 7, qt * 16:(qt + 1) * 16], lhsT=QT[:, h, qsl], rhs=kmT[:, h, :], start=True, stop=True),
                          reads=[f"QT{h}", "kmT"], writes=["ps7"])
                for qt in range(2):
                    tk.op("dve", lambda: nc.vector.tensor_tensor(out=gm[:, qt, :], in0=ps[:, 7, qt * 16:(qt + 1) * 16], in1=vbias[:, j, :], op=ALU.add),
                          reads=["ps7", "vbias"], writes=[f"gm{qt}"])
                    tk.op("dve", lambda: nc.vector.max(out=top8[:, qt, :], in_=gm[:, qt, :]), reads=[f"gm{qt}"], writes=[f"top8{qt}"])
                    tk.op("dve", lambda: nc.vector.tensor_scalar(out=thr[:, qt, :], in0=top8[:, qt, 2:3], scalar1=-1e29, scalar2=None, op0=ALU.max),
                          reads=[f"top8{qt}"], writes=[f"thr{qt}"])
                    tk.op("dve", lambda: nc.vector.tensor_scalar(out=G32[:, qt, 0:16], in0=gm[:, qt, :], scalar1=thr[:, qt, 0:1], scalar2=None, op0=ALU.is_ge),
                          reads=[f"gm{qt}", f"thr{qt}"], writes=[f"G32_{qt}"])
                    tk.op("dve", lambda: nc.vector.tensor_scalar(out=G32[:, qt, 0:16], in0=G32[:, qt, 0:16], scalar1=-1.0, scalar2=1e30, op0=ALU.add, op1=ALU.mult),
                          reads=[f"G32_{qt}"], writes=[f"G32_{qt}"])
                    tk.op("dve", lambda: nc.vector.tensor_tensor(out=G32[:, qt, 0:16], in0=G32[:, qt, 0:16], in1=notown[:, j, :], op=ALU.mult),
                          reads=[f"G32_{qt}", "notown"], writes=[f"G32_{qt}"])

            def preB(ctr):
                bb = ctr % 2
                for qt in range(2):
                    tk.op("pe", lambda: nc.tensor.transpose(out=ps[0:32, 7, 64 + qt * 128:64 + (qt + 1) * 128], in_=G32[:, qt, :], identity=identf[:, :]),
                          reads=[f"G32_{qt}", "identf"], writes=["ps7"])
                tk.op("act", lambda: nc.scalar.copy(out=biasT[:, bb, :], in_=ps[0:32, 7, 64:320]), reads=["ps7"], writes=[f"biasT{bb}"])

            preA(0)
            preB(0)
            for ctr, (h, j) in enumerate(its):
                bb = ctr % 2
                rb = ctr % 2
                qs = slice(j * 256, (j + 1) * 256)
                nb = nblocks(j)
                tiles = [(n, kt) for n in range(nb) for kt in range(2)]
                NTI = len(tiles)
                sbase, pbase = sctr[0], pctr[0]
                sctr[0] += NTI
                pctr[0] += NTI

                def emit_S(i):
                    n, kt = tiles[i]
                    sbk = SB[(sbase + i) % len(SB)]
                    ktile = 2 * n + kt
                    extra = (n == j) or (n == 8 + j)
                    tk.op("pe", lambda: nc.tensor.matmul(ps[:, sbk, 0:256], lhsT=KT[:, h, ktile * 128:(ktile + 1) * 128], rhs=QT[:, h, qs], start=True, stop=False),
                          reads=[f"KT{h}", f"QT{h}"], writes=[f"ps{sbk}"])
                    tk.op("pe", lambda: nc.tensor.matmul(ps[:, sbk, 0:256], lhsT=E[:, n, :], rhs=biasT[:, bb, :], start=False, stop=(not extra)),
                          reads=["E", f"biasT{bb}"], writes=[f"ps{sbk}"])
                    if extra:
                        ti = (0 if n == j else 2) + kt
                        tk.op("pe", lambda: nc.tensor.matmul(ps[:, sbk, 0:256], lhsT=identb[:, :], rhs=tri[:, ti, :], start=False, stop=True),
                              reads=["identb", "tri"], writes=[f"ps{sbk}"])

                def emit_EXP(i):
                    sbk = SB[(sbase + i) % len(SB)]
                    pb = (pbase + i) % NPT
                    tk.op("act", lambda: nc.scalar.activation(out=PTs[:, pb, :], in_=ps[:, sbk, 0:256], func=AF.Exp, scale=ATT_SCALE),
                          reads=[f"ps{sbk}"], writes=[f"PTs{pb}"])

                def emit_PV(i):
                    n, kt = tiles[i]
                    pb = (pbase + i) % NPT
                    ktile = 2 * n + kt
                    tk.op("pe", lambda: nc.tensor.matmul(ps[:, OB, 0:256], lhsT=V[:, ktile, h * 128:(h + 1) * 128], rhs=PTs[:, pb, :], start=(i == 0), stop=(i == NTI - 1)),
                          reads=[f"V{ktile // 8}", f"PTs{pb}"], writes=[f"ps{OB}"])
                    tk.op("pe", lambda: nc.tensor.matmul(ps[:, LB, 0:256], lhsT=onesb[:, :], rhs=PTs[:, pb, :], start=(i == 0), stop=(i == NTI - 1)),
                          reads=["onesb", f"PTs{pb}"], writes=[f"ps{LB}"])

                DEPTH = 4
                for i in range(min(DEPTH, NTI)):
                    emit_S(i)
                if ctr + 1 < len(its):
                    preA(ctr + 1)
                for i in range(NTI):
                    if i + DEPTH < NTI:
                        emit_S(i + DEPTH)
                    emit_EXP(i)
                    emit_PV(i)
                if ctr + 1 < len(its):
                    preB(ctr + 1)
                tk.op("dve", lambda: nc.vector.reciprocal(out=rL[:, rb, :], in_=ps[:, LB, 0:256]), reads=[f"ps{LB}"], writes=[f"rL{rb}"])
                tk.op("dve", lambda: nc.vector.tensor_tensor(out=bT[:, h, qs], in0=ps[:, OB, 0:256], in1=rL[:, rb, :], op=ALU.mult),
                      reads=[f"ps{OB}", f"rL{rb}"], writes=[f"bT{h}"])
            tk.barrier()
        with ExitStack() as eb:
            f = alloc_ffn_set(nc, eb, hchunks=6, sfx=sfx)
            xT, hn, H, sg, rstd, ones, wbuf, wdbuf, epsc = (f[k] for k in ("xT", "hn", "H", "sg", "rstd", "ones", "wbuf", "wdbuf", "epsc"))
            sbb = lambda name, shape, dt: eb.enter_context(nc.sbuf_tensor(name + sfx, shape, dt))
            gA = sbb("gA", [128, DCH], F32)
            gB = sbb("gB", [128, DCH], F32)
            gM = sbb("gM", [128, DCH], F32)
            for c in range(DCH):
                tk.dma("sp", "dx", lambda c=c: nc.sync.dma_start(out=xT[:, c, :], in_=x_d[:, c, :]), writes=[f"xT{c}"])
            for c in range(4):
                tk.dma("sp", "dx", lambda c=c: nc.sync.dma_start(out=H[:, c, :], in_=a_d[:, c, :]), writes=[f"H{c}"])
            for (t_s, t_d, key) in ((gA, gA_d, "gA"), (gB, gB_d, "gB"), (gM, gM_d, "gM")):
                tk.dma("sp", "dx", lambda t_s=t_s, t_d=t_d: nc.sync.dma_start(out=t_s[:], in_=t_d), writes=[key])
            tk.op("dve", lambda: nc.vector.memset(ones[:, :], 1.0), writes=["ones"])
            tk.op("dve", lambda: nc.vector.memset(epsc[:, :], RMS_EPS), writes=["epsc"])
            in_chunks = [(H[:, c, :], f"H{c}") for c in range(4)] + [(bT[:, h, :], f"bT{h}") for h in range(4)]
            emit_proj_fm(tk, nc, st, ps, wbuf, [wo_d[j] for j in range(8)], in_chunks, emit_resid_epilogue(tk, nc, ps, xT))
            emit_rmsnorm(tk, nc, xT, gA, "gA", hn, rstd, ps, ones, epsc)
            emit_ffn(tk, nc, st, xT, hn, H, sg, ps, wbuf, wdbuf, wgA, wuA, wdA)
            emit_rmsnorm(tk, nc, xT, gB, "gB", hn, rstd, ps, ones, epsc)
            emit_ffn(tk, nc, st, xT, hn, H, sg, ps, wbuf, wdbuf, wgB, wuB, wdB)
            emit_rmsnorm(tk, nc, xT, gM, "gM", hn, rstd, ps, ones, epsc)
            for c in range(DCH):
                tk.dma("sp", "dout", lambda c=c: nc.sync.dma_start(out=xo_d[:, c, :], in_=xT[:, c, :]), reads=[f"xT{c}"])
            sgk = lambda si: [f"sg{si}_{tt}" for tt in range(NT)]

            def conv_ep(jj, bs):
                if jj < 16:
                    i, si = jj // 2, (jj // 2) % 2
                    if jj % 2 == 0:
                        for tt in range(NT):
                            tk.op("act", lambda tt=tt, si=si, bs=bs: nc.scalar.copy(out=sg[:, si, tsl(tt)], in_=ps[:, bs + tt, :]),
                                  reads=[f"ps{bs + tt}"], writes=[f"sg{si}_{tt}"])
                    else:
                        for tt in range(NT):
                            tk.op("dve", lambda tt=tt, si=si, bs=bs: nc.vector.tensor_tensor(out=sg[:, si, tsl(tt)], in0=sg[:, si, tsl(tt)], in1=ps[:, bs + tt, :], op=ALU.mult),
                                  reads=[f"sg{si}_{tt}", f"ps{bs + tt}"], writes=[f"sg{si}_{tt}"])
                        tk.dma("sp", "dout", lambda i=i, si=si: nc.sync.dma_start(out=z_d[:, i, :], in_=sg[:, si, :]), reads=sgk(si))
                else:
                    i, si = jj - 16, jj % 2
                    for tt in range(NT):
                        tk.op("act", lambda tt=tt, si=si, bs=bs: nc.scalar.copy(out=sg[:, si, tsl(tt)], in_=ps[:, bs + tt, :]),
                              reads=[f"ps{bs + tt}"], writes=[f"sg{si}_{tt}"])
                    tk.dma("sp", "dout", lambda i=i, si=si: nc.sync.dma_start(out=bg_d[:, i, :], in_=sg[:, si, :]), reads=sgk(si))

            if not conv_here:
                emit_proj_fm(tk, nc, st, ps, wbuf, [wc_d[j] for j in range(24)], [(hn[:, c, :], f"hn{c}") for c in range(DCH)], conv_ep)
            else:
                cw = sbb("cw_s", [128, DCH, 3], F32)
                zc = [sbb(f"zc{i}", [128, T_CORE + 2], F32) for i in range(2)]
                gout = [sbb(f"gout{i}", [128, T_CORE], BF16) for i in range(2)]
                tk.dma("sp", "dx", lambda: nc.sync.dma_start(out=cw[:], in_=cw_d), writes=["cw"])
                rk = [f"rstd{tt}" for tt in range(NT)]

                def conv_ep2(jj, bs):
                    i, r, si = jj // 3, jj % 3, (jj // 3) % 2
                    if r == 0:
                        for tt in range(NT):
                            tk.op("act", lambda tt=tt, bs=bs: nc.scalar.copy(out=rstd[:, tsl(tt)], in_=ps[:, bs + tt, :]), reads=[f"ps{bs + tt}"], writes=[f"rstd{tt}"])
                    elif r == 1:
                        for tt in range(NT):
                            tk.op("act", lambda tt=tt, si=si, bs=bs: nc.scalar.copy(out=sg[:, si, tsl(tt)], in_=ps[:, bs + tt, :]),
                                  reads=[f"ps{bs + tt}"], writes=[f"sg{si}_{tt}"])
                    else:
                        if halo_in is None:
                            tk.op("dve", lambda si=si: nc.vector.memset(zc[si][:, 0:2], 0.0), writes=[f"zc{si}"])
                        else:
                            tk.dma("sp", "dx", lambda si=si, i=i: nc.sync.dma_start(out=zc[si][:, 0:2], in_=halo_in[:, i, :]), writes=[f"zc{si}"])
                        for tt in range(NT):
                            o = tt * 512
                            tk.op("dve", lambda tt=tt, si=si, bs=bs, o=o: nc.vector.tensor_tensor(out=zc[si][:, 2 + o:2 + o + 512], in0=sg[:, si, tsl(tt)], in1=ps[:, bs + tt, :], op=ALU.mult),
                                  reads=[f"sg{si}_{tt}", f"ps{bs + tt}"], writes=[f"zc{si}"])
                        tk.dma("sp", "dout", lambda si=si, i=i: nc.sync.dma_start(out=zh_d[:, i, :], in_=zc[si][:, T_CORE:T_CORE + 2]), reads=[f"zc{si}"])
                        for tt in range(NT):
                            o = tt * 512
                            sl = tsl(tt)
                            key = f"sg{si}_{tt}"
                            tk.op("dve", lambda si=si, i=i, sl=sl, o=o: nc.vector.tensor_scalar(out=sg[:, si, sl], in0=zc[si][:, o:o + 512], scalar1=cw[:, i, 0:1], scalar2=None, op0=ALU.mult),
                                  reads=[f"zc{si}", "cw"], writes=[key])
                            tk.op("dve", lambda si=si, i=i, sl=sl, o=o: nc.vector.scalar_tensor_tensor(out=sg[:, si, sl], in0=zc[si][:, o + 1:o + 513], scalar=cw[:, i, 1:2], in1=sg[:, si, sl],
                                                                                                     op0=ALU.mult, op1=ALU.add),
                                  reads=[f"zc{si}", "cw", key], writes=[key])
                            tk.op("dve", lambda si=si, i=i, sl=sl, o=o: nc.vector.scalar_tensor_tensor(out=sg[:, si, sl], in0=zc[si][:, o + 2:o + 514], scalar=cw[:, i, 2:3], in1=sg[:, si, sl],
                                                                                                     op0=ALU.mult, op1=ALU.add),
                                  reads=[f"zc{si}", "cw", key], writes=[key])
                            tk.op("dve", lambda si=si, sl=sl, tt=tt: nc.vector.tensor_tensor(out=gout[si][:, sl], in0=sg[:, si, sl], in1=rstd[:, sl], op=ALU.mult),
                                  reads=[key, f"rstd{tt}"], writes=[f"gout{si}"])
                        tk.dma("sp", "dout", lambda si=si, i=i: nc.sync.dma_start(out=g_d[:, i, :], in_=gout[si][:, :]), reads=[f"gout{si}"])

                emit_proj_fm(tk, nc, st, ps, wbuf, [wc_d[j] for j in range(24)], [(hn[:, c, :], f"hn{c}") for c in range(DCH)], conv_ep2)
            tk.finish("sp", ["dout"])
            tk.barrier()
    return nc


def build_l3(nc=None, ov=None, sfx="", sem_es=None):
    nc = nc or bass.Bass("TRN2", target_bir_lowering=False)
    ov = ov or {}
    di, do = io_makers(nc, ov)
    x_d = di("xT", [128, DCH, T_CORE], F32)
    g_src = ov.get("g_src")
    z_src = ov.get("z_src")
    halo_src = ov.get("halo_src")
    ze_d = None if (z_src is not None or g_src is not None) else di("zext", [128, DCH, T_CORE + 2], F32)
    bg_d = None if g_src is not None else di("bgT", [128, DCH, T_CORE], F32)
    cw_d = None if g_src is not None else di("cw", [128, DCH, 3], F32)
    wco_d = di("wco", [8, 128, DCH, 128], F32)
    gC_d = di("g_ffn_c", [128, DCH], F32)
    gF_d = di("g_fin", [128, DCH], F32)
    wgC, wuC, wdC = ffn_dram(nc, "C", ov)
    out_d = do("outT", [128, DCH, T_CORE], F32)
    with ExitStack() as es:
        sb = lambda name, shape, dt: es.enter_context(nc.sbuf_tensor(name + sfx, shape, dt))
        tk = Trk(nc, sem_es or es, ["dx", "dz", "dout"] + FFN_STREAMS, sfx)
        f = alloc_ffn_set(nc, es, hchunks=6, sfx=sfx)
        xT, hn, H, sg, rstd, ones, wbuf, wdbuf, epsc = (f[k] for k in ("xT", "hn", "H", "sg", "rstd", "ones", "wbuf", "wdbuf", "epsc"))
        gC = sb("gC", [128, DCH], F32)
        gF = sb("gF", [128, DCH], F32)
        if g_src is None:
            cw = sb("cw_s", [128, DCH, 3], F32)
            zc = [sb(f"zc{i}", [128, T_CORE + 2], F32) for i in range(2)]
            bgc = [sb(f"bgc{i}", [128, T_CORE], F32) for i in range(2)]
        ps = es.enter_context(nc.psum_tensor("ps" + sfx, [128, 8, 512], F32))
        st = {"w": 0, "sg": 0, "wd": 0, "dn": 0, "pj": 0}
        for c in range(DCH):
            tk.dma("sp", "dx", lambda c=c: nc.sync.dma_start(out=xT[:, c, :], in_=x_d[:, c, :]), writes=[f"xT{c}"])
        for (t_s, t_d, key) in ((gC, gC_d, "gC"), (gF, gF_d, "gF")) + (((cw, cw_d, "cw"),) if g_src is None else ()):
            tk.dma("sp", "dx", lambda t_s=t_s, t_d=t_d: nc.sync.dma_start(out=t_s[:], in_=t_d), writes=[key])
        if g_src is not None:
            for c in range(DCH):
                tk.dma("sp", "dx", lambda c=c: nc.sync.dma_start(out=hn[:, c, :], in_=g_src[:, c, :]), writes=[f"hn{c}"])
        tk.op("dve", lambda: nc.vector.memset(ones[:, :], 1.0), writes=["ones"])
        tk.op("dve", lambda: nc.vector.memset(epsc[:, :], RMS_EPS), writes=["epsc"])
        for c in (range(DCH) if g_src is None else ()):
            ci = c % 2
            if z_src is None:
                tk.dma("sp", f"dz", lambda c=c, ci=ci: nc.sync.dma_start(out=zc[ci][:, :], in_=ze_d[:, c, :]), writes=[f"zc{ci}"])
            else:
                tk.dma("sp", f"dz", lambda c=c, ci=ci: nc.sync.dma_start(out=zc[ci][:, 2:T_CORE + 2], in_=z_src[:, c, :]), writes=[f"zc{ci}"])
                if halo_src is None:
                    tk.op("dve", lambda ci=ci: nc.vector.memset(zc[ci][:, 0:2], 0.0), writes=[f"zc{ci}"])
                else:
                    tk.dma("sp", f"dz", lambda c=c, ci=ci: nc.sync.dma_start(out=zc[ci][:, 0:2], in_=halo_src[:, c, T_CORE - 2:T_CORE]), writes=[f"zc{ci}"])
            tk.dma("sp", f"dz", lambda c=c, ci=ci: nc.sync.dma_start(out=bgc[ci][:, :], in_=bg_d[:, c, :]), writes=[f"bgc{ci}"])
            en, eo = "dve", nc.vector
            for tt in range(NT):
                sl = tsl(tt)
                o = tt * 512
                key = f"sg{ci}_{tt}"
                tk.op(en, lambda c=c, ci=ci, sl=sl, o=o, eo=eo: eo.tensor_scalar(out=sg[:, ci, sl], in0=zc[ci][:, o:o + 512], scalar1=cw[:, c, 0:1], scalar2=None, op0=ALU.mult),
                      reads=[f"zc{ci}", "cw"], writes=[key])
                tk.op(en, lambda c=c, ci=ci, sl=sl, o=o, eo=eo: eo.scalar_tensor_tensor(out=sg[:, ci, sl], in0=zc[ci][:, o + 1:o + 513], scalar=cw[:, c, 1:2], in1=sg[:, ci, sl],
                                                                                        op0=ALU.mult, op1=ALU.add),
                      reads=[f"zc{ci}", "cw", key], writes=[key])
                tk.op(en, lambda c=c, ci=ci, sl=sl, o=o, eo=eo: eo.scalar_tensor_tensor(out=sg[:, ci, sl], in0=zc[ci][:, o + 2:o + 514], scalar=cw[:, c, 2:3], in1=sg[:, ci, sl],
                                                                                        op0=ALU.mult, op1=ALU.add),
                      reads=[f"zc{ci}", "cw", key], writes=[key])
                tk.op(en, lambda c=c, ci=ci, sl=sl, eo=eo: eo.tensor_tensor(out=hn[:, c, sl], in0=sg[:, ci, sl], in1=bgc[ci][:, sl], op=ALU.mult),
                      reads=[key, f"bgc{ci}"], writes=[f"hn{c}"])
        emit_proj_fm(tk, nc, st, ps, wbuf, [wco_d[j] for j in range(8)], [(hn[:, c, :], f"hn{c}") for c in range(DCH)], emit_resid_epilogue(tk, nc, ps, xT))
        emit_rmsnorm(tk, nc, xT, gC, "gC", hn, rstd, ps, ones, epsc)
        emit_ffn(tk, nc, st, xT, hn, H, sg, ps, wbuf, wdbuf, wgC, wuC, wdC)
        emit_final_norm(tk, nc, xT, gF, "gF", hn, rstd, ps, ones, sg, out_d, epsc)
        tk.finish("sp", ["dout"])
        tk.barrier()
    return nc


def relayout_fm(W):
    W = np.asarray(W, np.float32)
    nch = W.shape[1] // 128
    return np.ascontiguousarray(W.reshape(DCH, 128, nch, 128).transpose(2, 1, 0, 3))


def attn_consts(half):
    vb = np.zeros((8, 16), np.float32)
    no = np.ones((8, 16), np.float32)
    for j in range(8):
        qblk = 8 * half + j
        vb[j, qblk:] = NEG
        no[j, qblk] = 0.0
    E = np.zeros((32, 16, 128), np.float32)
    for n in range(16):
        E[n, n, :] = 1.0
    tri = np.zeros((128, 4, 256), np.float32)
    k = np.arange(128)[:, None]
    q = np.arange(256)[None, :]
    for kt in range(2):
        m = np.where(kt * 128 + k > q, NEG, 0.0).astype(np.float32)
        tri[:, (0 if half == 0 else 2) + kt, :] = m
    return {
        "vbias": np.ascontiguousarray(np.broadcast_to(vb[None], (128, 8, 16))),
        "notown": np.ascontiguousarray(np.broadcast_to(no[None], (128, 8, 16))),
        "Esel": bf16_np(E), "tri": bf16_np(tri), "identf": np.eye(128, dtype=np.float32),
    }


def run_l2(inputs, r1, cores):
    nc = build_l2()
    wgA, wuA, wdA = relayout_ffn(inputs["ffn_w_gate"][0, 1], inputs["ffn_w_up"][0, 1], inputs["ffn_w_down"][0, 1])
    wgB, wuB, wdB = relayout_ffn(inputs["ffn_w_gate"][1, 0], inputs["ffn_w_up"][1, 0], inputs["ffn_w_down"][1, 0])
    cwi = np.asarray(inputs["conv_w_in"], np.float32)[0]
    cols = []
    for i in range(8):
        cols.append(cwi[:, 1024 + i * 128:1024 + (i + 1) * 128])
        cols.append(cwi[:, 2048 + i * 128:2048 + (i + 1) * 128])
    for i in range(8):
        cols.append(cwi[:, i * 128:(i + 1) * 128])
    shared = {
        "wo": relayout_fm(inputs["hyb_w_out"][0]),
        "g_ffn_a": fm_vec(inputs["ffn_norm"][0, 1]), "g_ffn_b": fm_vec(inputs["ffn_norm"][1, 0]), "g_mix": fm_vec(inputs["mix_norm"][1]),
        "wgA": wgA, "wuA": wuA, "wdA": wdA, "wgB": wgB, "wuB": wuB, "wdB": wdB,
        "wc": relayout_fm(np.concatenate(cols, axis=1)),
    }
    idx = {c: i for i, c in enumerate(cores)}
    maps = []
    for c in cores:
        b, half = c // 2, c % 2
        r0, rp = r1[idx[2 * b]], r1[idx[2 * b + 1]]
        m = dict(shared)
        m.update(attn_consts(half))
        me = r1[idx[c]]
        m["xT"] = np.asarray(me["xT_o"])
        m["aT"] = np.asarray(me["aT_o"])
        m["QT"] = np.ascontiguousarray(np.asarray(me["QKT_o"])[:, 0:4, :])
        m["KTp"] = np.ascontiguousarray(np.concatenate([np.asarray(r0["QKT_o"])[:, 4:8, :], np.asarray(rp["QKT_o"])[:, 4:8, :]], axis=2))
        vp = np.concatenate([np.asarray(r0["V_o"]), np.asarray(rp["V_o"])], axis=0)
        m["Vp"] = np.ascontiguousarray(vp.transpose(1, 0, 2))
        maps.append(m)
    res = run_bass_kernel_spmd(nc, maps, core_ids=list(range(len(cores))))
    return res.results


def run_l3(inputs, r2, cores):
    nc = build_l3()
    wgC, wuC, wdC = relayout_ffn(inputs["ffn_w_gate"][1, 1], inputs["ffn_w_up"][1, 1], inputs["ffn_w_down"][1, 1])
    cwv = np.asarray(inputs["conv_w"], np.float32)[0]
    cw = np.ascontiguousarray(cwv.reshape(3, DCH, 128).transpose(2, 1, 0))
    shared = {
        "cw": cw, "wco": relayout_fm(inputs["conv_w_out"][0]),
        "g_ffn_c": fm_vec(inputs["ffn_norm"][1, 1]), "g_fin": fm_vec(inputs["final_norm"]),
        "wgC": wgC, "wuC": wuC, "wdC": wdC,
    }
    idx = {c: i for i, c in enumerate(cores)}
    maps = []
    for c in cores:
        b, half = c // 2, c % 2
        me = r2[idx[c]]
        z = np.asarray(me["zT_o"])
        if half == 0:
            halo = np.zeros((128, DCH, 2), np.float32)
        else:
            halo = np.asarray(r2[idx[2 * b]]["zT_o"])[:, :, T_CORE - 2:T_CORE]
        m = dict(shared)
        m["xT"] = np.asarray(me["xT_o"])
        m["zext"] = np.ascontiguousarray(np.concatenate([halo, z], axis=2))
        m["bgT"] = np.asarray(me["bgT_o"])
        maps.append(m)
    res = run_bass_kernel_spmd(nc, maps, core_ids=list(range(len(cores))))
    return res.results


def run_all(inputs, cores):
    inputs = {k: np.asarray(v) for k, v in inputs.items()}
    r1 = run_l1(inputs, cores)
    r2 = run_l2(inputs, r1, cores)
    r3 = run_l3(inputs, r2, cores)
    outs = {}
    for i, c in enumerate(cores):
        oT = np.asarray(r3[i]["outT"], np.float32)
        outs[c] = np.ascontiguousarray(oT.transpose(1, 0, 2).reshape(D_MODEL, T_CORE).T)
    return outs, (r1, r2, r3)
def build_fused():
    nc = bass.Bass("TRN2", target_bir_lowering=False)
    ext = lambda name, shape, dt: nc.dram_tensor(name, shape, dt, kind="ExternalInput").ap()
    itn = lambda name, shape, dt: nc.dram_tensor(name, shape, dt, kind="Internal").ap()
    W = {}
    for sfx in ("0", "A", "B", "C"):
        W["wg" + sfx] = ext("wg" + sfx, [FCH, 128, DCH, 128], F32)
        W["wu" + sfx] = ext("wu" + sfx, [FCH, 128, DCH, 128], F32)
        W["wd" + sfx] = ext("wd" + sfx, [DCH, 128, FCH, 128], F32)
    for name, shape, dt in (("g_ffn", [128, DCH], F32), ("g_mix1", [128, DCH], F32), ("win_fm", [12, 128, DCH, 128], F32), ("win_tm", [2, 128, DCH, 512], F32),
                            ("ropec", [32, 4], F32), ("PT", [128, 128], F32), ("vnorm", [128, 512], F32), ("wsT", [128, 4, 128], F32),
                            ("trilT", [128, 4, 128], F32), ("bsB", [128, 4, 128], F32),
                            ("Esel", [32, 16, 128], BF16), ("identf", [128, 128], F32), ("wo", [8, 128, DCH, 128], F32),
                            ("g_ffn_a", [128, DCH], F32), ("g_ffn_b", [128, DCH], F32), ("g_mix2", [128, DCH], F32), ("wc", [24, 128, DCH, 128], F32),
                            ("cw", [128, DCH, 3], F32), ("wco", [8, 128, DCH, 128], F32), ("g_ffn_c", [128, DCH], F32), ("g_fin", [128, DCH], F32)):
        W[name] = ext(name, shape, dt)
    x2 = ext("xT2", [2, 128, DCH, T_CORE], F32)
    pos2 = ext("pos2", [2, 32, T_CORE], I32)
    vb2 = ext("vbias2", [2, 128, 8, 16], F32)
    no2 = ext("notown2", [2, 128, 8, 16], F32)
    tri2 = ext("tri2", [2, 128, 4, 256], BF16)
    out2 = nc.dram_tensor("outT2", [2, 128, DCH, T_CORE], F32, kind="ExternalOutput").ap()
    x1_i = itn("x1_i", [2, 128, DCH, T_CORE], F32)
    a_i = itn("a_i", [2, 128, 4, T_CORE], BF16)
    qk_i = itn("qk_i", [2, 128, 8, T_CORE], BF16)
    v_i = itn("v_i", [2, 16, 128, 512], BF16)
    x4_i = itn("x4_i", [2, 128, DCH, T_CORE], F32)
    g_i = itn("g_i", [2, 128, DCH, T_CORE], BF16)
    zh_i = itn("zh_i", [2, 128, DCH, 2], F32)
    with ExitStack() as sem_es:
        for hf in range(2):
            ov = dict(W)
            ov.update({"g_mix": W["g_mix1"], "xT": x2[hf], "pos": pos2[hf], "xT_o": x1_i[hf], "aT_o": a_i[hf], "QKT_o": qk_i[hf], "V_o": v_i[hf]})
            build_l1(nc, ov, sfx=f"_p1h{hf}", sem_es=sem_es)
        for hf in range(2):
            ov = dict(W)
            ov.update({"g_mix": W["g_mix2"], "xT": x1_i[hf], "aT": a_i[hf], "QT": qk_i[hf][:, 0:4, :],
                       "KT_parts": [qk_i[0][:, 4:8, :], qk_i[1][:, 4:8, :]], "V_parts": [v_i[0], v_i[1]],
                       "vbias": vb2[hf], "notown": no2[hf], "tri": tri2[hf],
                       "xT_o": x4_i[hf], "g_o": g_i[hf], "zh_o": zh_i[hf], "halo_in": (zh_i[0] if hf == 1 else None)})
            nblocks = (lambda j: j + 1) if hf == 0 else (lambda j: 9 + j)
            build_l2(nc, ov, sfx=f"_p2h{hf}", nblocks=nblocks, sem_es=sem_es, conv_here=True)
        for hf in range(2):
            ov = dict(W)
            ov.update({"xT": x4_i[hf], "g_src": g_i[hf], "outT": out2[hf]})
            build_l3(nc, ov, sfx=f"_p3h{hf}", sem_es=sem_es)
    return nc


def fused_shared(inputs):
    s1 = l1_shared(inputs)
    sh = {k: s1[k] for k in ("wg0", "wu0", "wd0", "g_ffn", "win_fm", "win_tm", "ropec", "PT", "vnorm", "wsT", "trilT", "bsB")}
    sh["g_mix1"] = s1["g_mix"]
    wgA, wuA, wdA = relayout_ffn(inputs["ffn_w_gate"][0, 1], inputs["ffn_w_up"][0, 1], inputs["ffn_w_down"][0, 1])
    wgB, wuB, wdB = relayout_ffn(inputs["ffn_w_gate"][1, 0], inputs["ffn_w_up"][1, 0], inputs["ffn_w_down"][1, 0])
    wgC, wuC, wdC = relayout_ffn(inputs["ffn_w_gate"][1, 1], inputs["ffn_w_up"][1, 1], inputs["ffn_w_down"][1, 1])
    cwi = np.asarray(inputs["conv_w_in"], np.float32)[0]
    cols = []
    for i in range(8):
        cols.append(cwi[:, i * 128:(i + 1) * 128])
        cols.append(cwi[:, 1024 + i * 128:1024 + (i + 1) * 128])
        cols.append(cwi[:, 2048 + i * 128:2048 + (i + 1) * 128])
    cwv = np.asarray(inputs["conv_w"], np.float32)[0]
    c0, c1 = attn_consts(0), attn_consts(1)
    sh.update({
        "wgA": wgA, "wuA": wuA, "wdA": wdA, "wgB": wgB, "wuB": wuB, "wdB": wdB, "wgC": wgC, "wuC": wuC, "wdC": wdC,
        "wo": relayout_fm(inputs["hyb_w_out"][0]), "wc": relayout_fm(np.concatenate(cols, axis=1)), "wco": relayout_fm(inputs["conv_w_out"][0]),
        "g_ffn_a": fm_vec(inputs["ffn_norm"][0, 1]), "g_ffn_b": fm_vec(inputs["ffn_norm"][1, 0]), "g_mix2": fm_vec(inputs["mix_norm"][1]),
        "g_ffn_c": fm_vec(inputs["ffn_norm"][1, 1]), "g_fin": fm_vec(inputs["final_norm"]),
        "cw": np.ascontiguousarray(cwv.reshape(3, DCH, 128).transpose(2, 1, 0)),
        "Esel": c0["Esel"], "identf": c0["identf"],
        "vbias2": np.stack([c0["vbias"], c1["vbias"]]), "notown2": np.stack([c0["notown"], c1["notown"]]), "tri2": np.stack([c0["tri"], c1["tri"]]),
    })
    return sh


def run_fused(inputs, batches):
    inputs = {k: np.asarray(v) for k, v in inputs.items()}
    nc = build_fused()
    sh = fused_shared(inputs)
    pos = np.asarray(inputs["positions"], np.int32)
    maps = []
    for b in batches:
        m = dict(sh)
        m["xT2"] = np.stack([l1_inputs(inputs, 2 * b), l1_inputs(inputs, 2 * b + 1)])
        m["pos2"] = np.ascontiguousarray(np.broadcast_to(pos[b].reshape(2, 1, T_CORE), (2, 32, T_CORE)))
        maps.append(m)
    res = run_bass_kernel_spmd(nc, maps, core_ids=list(range(len(batches))))
    outs = {}
    for i, b in enumerate(batches):
        oT = np.asarray(res.results[i]["outT2"], np.float32)
        outs[b] = np.concatenate([oT[hf].transpose(1, 0, 2).reshape(D_MODEL, T_CORE).T for hf in range(2)], axis=0)
    return outs


def kernel(**inputs):
    outs = run_fused(inputs, list(range(BATCH)))
    return np.ascontiguousarray(np.stack([outs[b] for b in range(BATCH)]).astype(np.float32))
```
